# Optimizing a Trainium2 kernel written in Bass

```python
import math
import jax, jax.numpy as jnp
from jax import lax
import numpy as np

D_MODEL = 1024
BATCH = 16
SEQ = 256
DEPTH = 2
DEC_BATCH = 2
DEC_SEQ = 4096
PAST_LEN = 256

GRID_W = 64
H_A = 4
DK_A = 64
DV_A = 128
RET_CHUNK = 128
H_B = 4
DK_B = 64
DV_B = 128
GLA_CHUNK = 16
GLA_RANK = 16
GLA_TAU = 16.0
H_C = 4
DH_C = 64
Q_BLOCK = 128
D_FF = 2816
CONV_W = 3
ROPE_BASE = 10000.0
N_BRANCH = 3
BRANCH_W = 512
ALPHA = (2 * DEPTH) ** 0.25
BETA = (8 * DEPTH) ** -0.25
EPS = 1e-5
SPLIT_SIZES = (H_A * DK_A, H_A * DK_A, H_A * DV_A, H_A * DV_A,
               H_B * DK_B, H_B * DK_B, H_B * DV_B, H_B * DV_B,
               2 * H_C * DH_C, 2 * H_C * DH_C, 2 * H_C * DH_C,
               D_MODEL, D_MODEL, D_MODEL)
D_IN = sum(SPLIT_SIZES)

kernel_name = 'hybrid_ret_gla_diffattn_prefix_dit_step'


def _normalize(x):
    xf = x.astype(jnp.float32)
    mu = xf.mean(-1, keepdims=True)
    var = jnp.square(xf - mu).mean(-1, keepdims=True)
    return (xf - mu) * lax.rsqrt(var + EPS)


def layer_norm(x, g, b):
    return (_normalize(x) * g + b).astype(x.dtype)


def rms_norm(x, g):
    xf = x.astype(jnp.float32)
    return (xf * lax.rsqrt(jnp.mean(xf * xf, -1, keepdims=True) + EPS) * g).astype(x.dtype)


def heads(x, g):
    B, T, _ = x.shape
    return x.reshape(B, T, g, -1).transpose(0, 2, 1, 3)


def merge_heads(x):
    B, G, T, d = x.shape
    return x.transpose(0, 2, 1, 3).reshape(B, T, G * d)


def rev(x):
    return jnp.flip(x, axis=2)


def grid_positions(T):
    rows = T // GRID_W
    row = jnp.repeat(jnp.arange(rows, dtype=jnp.float32), GRID_W)
    col = (jnp.arange(T) % GRID_W).astype(jnp.float32)
    return row, col


def rope_1d(x, p):
    half = x.shape[-1] // 2
    inv = ROPE_BASE ** (-jnp.arange(half, dtype=jnp.float32) / half)
    ang = p[:, None] * inv[None, :]
    cos, sin = jnp.cos(ang), jnp.sin(ang)
    x1, x2 = x[..., :half], x[..., half:]
    return jnp.concatenate([x1 * cos - x2 * sin, x2 * cos + x1 * sin], axis=-1)


def rope_2d(x, pos):
    row, col = pos
    d = x.shape[-1] // 2
    xf = x.astype(jnp.float32)
    return jnp.concatenate([rope_1d(xf[..., :d], row), rope_1d(xf[..., d:], col)], axis=-1).astype(x.dtype)


def chunk_state_scan(q_dec, k_dec, v, chunk_decay, s0):
    delta = jnp.einsum('bgnck,bgncv->bgnkv', k_dec, v)

    def step(s, inp):
        dec, d = inp
        return dec[..., None] * s + d, s

    s_final, s_before = lax.scan(step, s0.astype(jnp.float32),
                                 (jnp.moveaxis(chunk_decay, 2, 0), jnp.moveaxis(delta, 2, 0)))
    s_before = jnp.moveaxis(s_before, 0, 2)
    inter = jnp.einsum('bgnck,bgnkv->bgncv', q_dec, s_before)
    return inter, s_final


def retention_dir(q, k, v, log_gamma, s0):
    B, H, T, _ = q.shape
    C = RET_CHUNK
    n = T // C
    qc = q.reshape(B, H, n, C, -1)
    kc = k.reshape(B, H, n, C, -1)
    vc = v.reshape(B, H, n, C, -1)
    idx = jnp.arange(C, dtype=jnp.float32)
    lg = log_gamma[:, None]
    b = (idx + 1.0)[None, :] * lg
    diff = idx[:, None] - idx[None, :]
    dmat = jnp.where(diff >= 0, jnp.exp(jnp.maximum(diff, 0.0)[None] * lg[:, :, None]), 0.0)
    scores = jnp.einsum('bhnik,bhnjk->bhnij', qc, kc) * dmat[None, :, None]
    intra = jnp.einsum('bhnij,bhnjv->bhniv', scores, vc)
    q_dec = qc * jnp.exp(b)[None, :, None, :, None]
    k_dec = kc * jnp.exp(C * lg - b)[None, :, None, :, None]
    chunk_decay = jnp.broadcast_to(jnp.exp(C * lg)[None, :, None, :], (B, H, n, qc.shape[-1]))
    inter, s_final = chunk_state_scan(q_dec, k_dec, vc, chunk_decay, s0)
    return (intra + inter).reshape(B, H, T, -1), s_final


def gla_dir(q, k, v, log_a, s0):
    B, H, T, _ = q.shape
    C = GLA_CHUNK
    n = T // C
    qc = q.reshape(B, H, n, C, -1)
    kc = k.reshape(B, H, n, C, -1)
    vc = v.reshape(B, H, n, C, -1)
    b = jnp.cumsum(log_a.reshape(B, H, n, C, -1), axis=3)
    b_last = b[:, :, :, -1:, :]
    causal = jnp.tril(jnp.ones((C, C), dtype=bool))
    pair = jnp.where(causal[:, :, None], b[:, :, :, :, None, :] - b[:, :, :, None, :, :], -jnp.inf)
    attn = jnp.einsum('bhntk,bhnsk,bhntsk->bhnts', qc, kc, jnp.exp(pair))
    intra = jnp.einsum('bhnts,bhnsv->bhntv', attn, vc)
    q_dec = qc * jnp.exp(b)
    k_dec = kc * jnp.exp(b_last - b)
    inter, s_final = chunk_state_scan(q_dec, k_dec, vc, jnp.exp(b_last[:, :, :, 0]), s0)
    return (intra + inter).reshape(B, H, T, -1), s_final


def diff_attention(q, k, v, lam, lam_init, subln_g):
    B, G, Tq, dh = q.shape
    nb = Tq // Q_BLOCK
    qb = jnp.moveaxis(q.reshape(B, G, nb, Q_BLOCK, dh), 2, 0)
    scale = dh ** -0.5

    def block(qi):
        s = jnp.einsum('bgqd,bgkd->bgqk', qi, k).astype(jnp.float32) * scale
        p = jax.nn.softmax(s, axis=-1).reshape(B, H_C, 2, Q_BLOCK, -1)
        a = (p[:, :, 0] - lam * p[:, :, 1]).astype(v.dtype)
        return jnp.einsum('bhqk,bhkv->bhqv', a, v)

    o = lax.map(block, qb)
    o = jnp.moveaxis(o, 0, 2).reshape(B, H_C, Tq, -1)
    o = rms_norm(o, subln_g) * (1.0 - lam_init)
    return merge_heads(o)


def token_mixer(h, l, P, pos, ctx):
    B, T, _ = h.shape
    f32 = jnp.float32
    dt = h.dtype
    points = np.cumsum(SPLIT_SIZES)[:-1].tolist()
    (a_q, a_k, a_v, a_g, b_q, b_k, b_v, b_r, c_q, c_k, c_v,
     m_a, m_b, m_c) = jnp.split(h @ P['w_in'][l], points, axis=-1)

    q = heads(a_q, H_A).astype(f32)
    k = heads(a_k, H_A).astype(f32) * DK_A ** -0.5
    if pos is not None:
        q, k = rope_2d(q, pos), rope_2d(k, pos)
    v = heads(a_v, H_A).astype(f32)
    s0 = jnp.zeros((B, 2, H_A, DK_A, DV_A), f32) if ctx is None else ctx['ret'].astype(f32)
    lg = jax.nn.log_sigmoid(P['ret_decay'][l].astype(f32))
    ya_f, sa_f = retention_dir(q, k, v, lg[0], s0[:, 0])
    ya_b, sa_b = retention_dir(rev(q), rev(k), rev(v), lg[1], s0[:, 1])
    y_a = merge_heads(_normalize(ya_f + rev(ya_b))) * jax.nn.silu(a_g.astype(f32))
    ret_state = jnp.stack([sa_f, sa_b], axis=1)

    q = heads(b_q, H_B).astype(f32) * DK_B ** -0.5
    k = heads(b_k, H_B).astype(f32)
    v = heads(b_v, H_B).astype(f32)

    def log_gate(e):
        z = (h @ P['gla_wa1'][l, e]) @ P['gla_wa2'][l, e] + P['gla_ba'][l, e]
        return heads(jax.nn.log_sigmoid(z.astype(f32)) / GLA_TAU, H_B)

    s0 = jnp.zeros((B, 2, H_B, DK_B, DV_B), f32) if ctx is None else ctx['gla'].astype(f32)
    yb_f, sb_f = gla_dir(q, k, v, log_gate(0), s0[:, 0])
    yb_b, sb_b = gla_dir(rev(q), rev(k), rev(v), rev(log_gate(1)), s0[:, 1])
    y_b = merge_heads(rms_norm(yb_f + rev(yb_b), P['gla_norm_g'][l])) * jax.nn.silu(b_r.astype(f32))
    gla_state = jnp.stack([sb_f, sb_b], axis=1)

    cq = heads(c_q, 2 * H_C)
    ck = heads(c_k, 2 * H_C)
    cv = heads(c_v, H_C)
    if pos is not None:
        cq, ck = rope_2d(cq, pos), rope_2d(ck, pos)
    if ctx is None:
        k_all, v_all = ck, cv
    else:
        k_all = jnp.concatenate([ctx['dk'].astype(ck.dtype), ck], axis=2)
        v_all = jnp.concatenate([ctx['dv'].astype(cv.dtype), cv], axis=2)
    lam_init = 0.8 - 0.6 * math.exp(-0.3 * l)
    lp = P['diff_lam'][l].astype(f32)
    lam = jnp.exp(jnp.sum(lp[0] * lp[1])) - jnp.exp(jnp.sum(lp[2] * lp[3])) + lam_init
    y_c = diff_attention(cq, k_all, v_all, lam, lam_init, P['diff_subln_g'][l])

    wb = P['w_branch'][l]
    merged = (jax.nn.sigmoid(m_a) * (y_a.astype(dt) @ wb[0])
              + jax.nn.sigmoid(m_b) * (y_b.astype(dt) @ wb[1])
              + jax.nn.sigmoid(m_c) * (y_c.astype(dt) @ wb[2]))
    out = merged @ P['w_out'][l]
    return out, (ck, cv, ret_state.astype(dt), gla_state.astype(dt))


def conv_ffn(h, l, P):
    a, b = jnp.split(h @ P['ffn_w_up'][l], 2, axis=-1)
    w = P['ffn_conv_w'][l]
    ap = jnp.pad(a, ((0, 0), (1, 1), (0, 0)))
    a = ap[:, :-2] * w[0] + ap[:, 1:-1] * w[1] + ap[:, 2:] * w[2] + P['ffn_conv_b'][l]
    return (jax.nn.gelu(a) * b) @ P['ffn_w_down'][l]


def trunk_layer(x, cvec, l, P, pos, ctx):
    mod = jax.nn.silu(cvec) @ P['ada_w'][l] + P['ada_b'][l]
    sh1, sc1, g1, sh2, sc2, g2 = jnp.split(mod, 6, axis=-1)
    mix, ctx_out = token_mixer(x * (1.0 + sc1) + sh1, l, P, pos, ctx)
    x = layer_norm(ALPHA * x + g1 * mix, P['ln_g'][l, 0], P['ln_b'][l, 0])
    ff = conv_ffn(x * (1.0 + sc2) + sh2, l, P)
    x = layer_norm(ALPHA * x + g2 * ff, P['ln_g'][l, 1], P['ln_b'][l, 1])
    return x, ctx_out


def setup_inputs(seed: int = 0) -> dict:
    key = jax.random.key(seed)
    ks = jax.random.split(key, 26)
    f32 = jnp.float32

    def nrm(k, shape, s):
        return s * jax.random.normal(k, shape, f32)

    gam = 1.0 - 2.0 ** (-5.0 - jnp.arange(H_A, dtype=f32))
    return {
        'x_prompt': nrm(ks[0], (BATCH, SEQ, D_MODEL), 1.0),
        'x_sample': nrm(ks[1], (DEC_BATCH, DEC_SEQ, D_MODEL), 1.0),
        'cache_diff_k': nrm(ks[2], (DEC_BATCH, DEPTH, 2 * H_C, PAST_LEN, DH_C), 1.0),
        'cache_diff_v': nrm(ks[3], (DEC_BATCH, DEPTH, H_C, PAST_LEN, 2 * DH_C), 1.0),
        'state_ret': nrm(ks[4], (DEC_BATCH, DEPTH, 2, H_A, DK_A, DV_A), 0.5),
        'state_gla': nrm(ks[5], (DEC_BATCH, DEPTH, 2, H_B, DK_B, DV_B), 0.5),
        'c': nrm(ks[6], (DEC_BATCH, D_MODEL), 1.0),
        'c_ctx': nrm(ks[7], (D_MODEL,), 1.0),
        'ada_w': nrm(ks[8], (DEPTH, D_MODEL, 6 * D_MODEL), 0.5 * D_MODEL ** -0.5),
        'ada_b': nrm(ks[9], (DEPTH, 6 * D_MODEL), 0.02),
        'w_in': nrm(ks[10], (DEPTH, D_MODEL, D_IN), D_MODEL ** -0.5),
        'ret_decay': jnp.log(gam / (1.0 - gam))[None, None, :] + nrm(ks[11], (DEPTH, 2, H_A), 0.1),
        'gla_wa1': nrm(ks[12], (DEPTH, 2, D_MODEL, GLA_RANK), D_MODEL ** -0.5),
        'gla_wa2': nrm(ks[13], (DEPTH, 2, GLA_RANK, H_B * DK_B), GLA_RANK ** -0.5),
        'gla_ba': nrm(ks[14], (DEPTH, 2, H_B * DK_B), 0.1),
        'gla_norm_g': 1.0 + nrm(ks[15], (DEPTH, DV_B), 0.02),
        'diff_lam': nrm(ks[16], (DEPTH, 4, DH_C), 0.1),
        'diff_subln_g': 1.0 + nrm(ks[17], (DEPTH, 2 * DH_C), 0.02),
        'w_branch': nrm(ks[18], (DEPTH, N_BRANCH, BRANCH_W, D_MODEL), BETA * BRANCH_W ** -0.5),
        'w_out': nrm(ks[19], (DEPTH, D_MODEL, D_MODEL), BETA * D_MODEL ** -0.5),
        'ln_g': 1.0 + nrm(ks[20], (DEPTH, 2, D_MODEL), 0.02),
        'ln_b': nrm(ks[21], (DEPTH, 2, D_MODEL), 0.02),
        'ffn_w_up': nrm(ks[22], (DEPTH, D_MODEL, 2 * D_FF), BETA * D_MODEL ** -0.5),
        'ffn_conv_w': nrm(ks[23], (DEPTH, CONV_W, D_FF), CONV_W ** -0.5),
        'ffn_conv_b': nrm(ks[24], (DEPTH, D_FF), 0.02),
        'ffn_w_down': nrm(ks[25], (DEPTH, D_FF, D_MODEL), BETA * D_FF ** -0.5),
    }


def reference(x_prompt, x_sample, cache_diff_k, cache_diff_v, state_ret, state_gla, c, c_ctx,
              ada_w, ada_b, w_in, ret_decay, gla_wa1, gla_wa2, gla_ba, gla_norm_g, diff_lam,
              diff_subln_g, w_branch, w_out, ln_g, ln_b, ffn_w_up, ffn_conv_w, ffn_conv_b, ffn_w_down):
    P = {'ada_w': ada_w, 'ada_b': ada_b, 'w_in': w_in, 'ret_decay': ret_decay,
         'gla_wa1': gla_wa1, 'gla_wa2': gla_wa2, 'gla_ba': gla_ba, 'gla_norm_g': gla_norm_g,
         'diff_lam': diff_lam, 'diff_subln_g': diff_subln_g, 'w_branch': w_branch, 'w_out': w_out,
         'ln_g': ln_g, 'ln_b': ln_b, 'ffn_w_up': ffn_w_up, 'ffn_conv_w': ffn_conv_w,
         'ffn_conv_b': ffn_conv_b, 'ffn_w_down': ffn_w_down}

    h = x_prompt
    cc = c_ctx[None, None, :]
    dks, dvs, rets, glas = [], [], [], []
    for l in range(DEPTH):
        h, (dk_l, dv_l, ret_l, gla_l) = trunk_layer(h, cc, l, P, None, None)
        dks.append(dk_l)
        dvs.append(dv_l)
        rets.append(ret_l)
        glas.append(gla_l)
    y_prompt = h
    new_diff_k = jnp.stack(dks, axis=1)
    new_diff_v = jnp.stack(dvs, axis=1)
    new_state_ret = jnp.stack(rets, axis=1)
    new_state_gla = jnp.stack(glas, axis=1)

    pos = grid_positions(x_sample.shape[1])
    z = x_sample
    cs = c[:, None, :]
    for l in range(DEPTH):
        ctx = {'dk': cache_diff_k[:, l], 'dv': cache_diff_v[:, l],
               'ret': state_ret[:, l], 'gla': state_gla[:, l]}
        z, _ = trunk_layer(z, cs, l, P, pos, ctx)
    y_sample = z

    return (y_prompt, y_sample, new_diff_k, new_diff_v, new_state_ret, new_state_gla)
```

```python
import contextlib
import os
import math
import numpy as np
import concourse.bass as bass
import concourse.mybir as mybir
from concourse.bass_utils import run_bass_kernel_spmd

F32 = mybir.dt.float32
BF16 = mybir.dt.bfloat16
AF = mybir.ActivationFunctionType
ALU = mybir.AluOpType

NCORES = 8
D = 1024
L = 2
DIN = 7680
DFF = 2816
NFC = 22
ALPHA = (2 * L) ** 0.25
EPS = 1e-5
EPSA = EPS / (ALPHA * ALPHA)
WSZ = 4096
NSLOT = 3
UW = 21504


class _Op:
    __slots__ = ("eng", "fn", "deps", "slot", "cnt", "need_inc", "inc")

    def __init__(self, eng, fn, slot, inc):
        self.eng = eng
        self.fn = fn
        self.deps = set()
        self.slot = slot
        self.cnt = 0
        self.need_inc = False
        self.inc = inc


class _Slot:
    def __init__(self, name, wait_all):
        self.name = name
        self.wait_all = wait_all
        self.total = 0
        self.sem = None


DEBUG_WAITS = None
LAST_S = None


class _Rec:
    def __init__(self):
        self.call = None

    def __getattr__(self, name):
        def f(*a, **k):
            self.call = (name, a, k)
            return None
        return f


def _freeze(fn):
    r = _Rec()
    fn(r)
    call = r.call
    if call is None:
        return lambda e: None
    name, a, k = call
    return lambda e: getattr(e, name)(*a, **k)


class Sched:
    ENGS = ("pe", "act", "dve", "pool", "sp")

    def __init__(self):
        self.ops = {e: [] for e in self.ENGS}
        self.lastw = {}
        self.readers = {}
        self.slots = []

    def slot(self, name, wait_all=False):
        s = _Slot(name, wait_all)
        self.slots.append(s)
        return s

    def add(self, eng, fn, reads=(), writes=(), slot=None, inc=16):
        op = _Op(eng, _freeze(fn), slot, inc)
        deps = op.deps
        for r in reads:
            w = self.lastw.get(r)
            if w is not None:
                deps.add(w)
            if isinstance(r, tuple) and r[0] == "ps":
                for rd in self.readers.get(r, ()):
                    if rd.eng != eng:
                        deps.add(rd)
        for k in writes:
            w = self.lastw.get(k)
            if w is not None:
                deps.add(w)
            for rd in self.readers.get(k, ()):
                deps.add(rd)
        for r in reads:
            self.readers.setdefault(r, []).append(op)
        for k in writes:
            self.lastw[k] = op
            self.readers[k] = []
        if slot is not None:
            slot.total += inc
            op.cnt = slot.total
        self.ops[eng].append(op)
        return op

    def wait_for(self, eng, ops):
        op = _Op(eng, lambda e: None, None, 0)
        op.deps = set(o for o in ops if o is not None)
        self.ops[eng].append(op)
        return op

    def emit(self, nc, stack):
        for e in self.ENGS:
            for op in self.ops[e]:
                for d in op.deps:
                    if d.slot is None:
                        if d.eng == op.eng and op.slot is None and d.eng == "pe":
                            continue
                        d.need_inc = True
        esem = {}
        for e in self.ENGS:
            c = 0
            for op in self.ops[e]:
                if op.slot is None and op.need_inc:
                    c += 1
                    op.cnt = c
            if c > 0:
                esem[e] = stack.enter_context(nc.semaphore("es_" + e))
        for s in self.slots:
            if s.total > 0:
                s.sem = stack.enter_context(nc.semaphore("sl_" + s.name))
        block = stack.enter_context(nc.Block())

        def body(e):
            def run(eng):
                seen = {}
                for op in self.ops[e]:
                    waits = {}
                    for d in op.deps:
                        if d.slot is not None:
                            if d.slot is op.slot and d.slot.wait_all:
                                continue
                            key = d.slot
                            val = d.slot.total if d.slot.wait_all else d.cnt
                            sem = d.slot.sem
                        else:
                            if d.eng == e and op.slot is None and e == "pe":
                                continue
                            key = d.eng
                            val = d.cnt
                            sem = esem[d.eng]
                        if val > waits.get(key, (None, 0))[1]:
                            waits[key] = (sem, val)
                    for key in sorted(waits.keys(), key=lambda k: k if isinstance(k, str) else "~" + k.name):
                        sem, val = waits[key]
                        if seen.get(key, 0) >= val:
                            continue
                        seen[key] = val
                        eng.wait_ge(sem, val)
                        if DEBUG_WAITS is not None:
                            DEBUG_WAITS.append((e, len(DEBUG_WAITS), key if isinstance(key, str) else key.name, val))
                    inst = op.fn(eng)
                    if op.slot is not None:
                        inst.then_inc(op.slot.sem, op.inc)
                    elif op.need_inc:
                        inst.then_inc(esem[e], 1)
            return run

        block.tensor(body("pe"))
        block.scalar(body("act"))
        block.vector(body("dve"))
        block.gpsimd(body("pool"))
        block.sync(body("sp"))


PTS = np.cumsum([0, 256, 256, 512, 512, 256, 256, 512, 512, 512, 512, 512, 1024, 1024, 1024])
(O_AQ, O_AK, O_AV, O_AG, O_BQ, O_BK, O_BV, O_BR, O_CQ, O_CK, O_CV, O_MA, O_MB, O_MC) = [int(v) for v in PTS[:-1]]


def _fchunk(w, cols):
    sub = w[:, cols]
    return sub.reshape(8, 128, sub.shape[1]).transpose(1, 0, 2)


def _pack(chunks, per):
    out = []
    for i in range(0, len(chunks), per):
        blk = np.concatenate([c.reshape(128, -1) for c in chunks[i:i + per]], axis=1)
        assert blk.shape[1] <= WSZ
        if blk.shape[1] < WSZ:
            blk = np.concatenate([blk, np.zeros((128, WSZ - blk.shape[1]), np.float32)], axis=1)
        out.append(blk)
    return np.ascontiguousarray(np.stack(out, 0), dtype=np.float32)


def _host_weights(inp):
    W = {}
    ar = np.arange
    for l in range(L):
        w_in = inp["w_in"][l]
        ch = []
        for (oq, ok) in ((O_AQ, O_AK), (O_BQ, O_BK)):
            for o in (oq, ok):
                for h in range(4):
                    cols = np.concatenate([o + h * 64 + ar(64), o + h * 64 + ar(64)])
                    ch.append(_fchunk(w_in, cols))
        for o in (O_CQ, O_CK):
            for h in range(4):
                ch.append(_fchunk(w_in, o + h * 128 + ar(128)))
        W[f"wF{l}"] = _pack(ch, 4)
        ch = [_fchunk(w_in, o + ar(512)) for o in (O_AV, O_AG, O_BV, O_BR, O_CV, O_CK)]
        W[f"wT{l}"] = _pack(ch, 1)
        ch = []
        wb = inp["w_branch"][l]
        for oc in range(8):
            g = [_fchunk(w_in, o + oc * 128 + ar(128)).reshape(128, -1) for o in (O_MA, O_MB, O_MC)]
            ch.append(np.concatenate(g, axis=1))
            b = [wb[br][:, oc * 128:(oc + 1) * 128].reshape(4, 128, 128).transpose(1, 0, 2).reshape(128, -1)
                 for br in range(3)]
            ch.append(np.concatenate(b, axis=1))
        W[f"wM{l}"] = _pack(ch, 1)
        wo = inp["w_out"][l]
        W[f"wO{l}"] = _pack([_fchunk(wo, oc * 128 + ar(128)) for oc in range(8)], 4)
        wu = inp["ffn_w_up"][l]
        ch = []
        for fc in range(NFC):
            ch.append(_fchunk(wu, fc * 128 + ar(128)))
            ch.append(_fchunk(wu, DFF + fc * 128 + ar(128)))
        W[f"wU{l}"] = _pack(ch, 4)
        wd = inp["ffn_w_down"][l]
        ch = [wd[:, oc * 128:(oc + 1) * 128].reshape(NFC, 128, 128).transpose(1, 0, 2) for oc in range(8)]
        W[f"wD{l}"] = _pack(ch, 1)
        aw = inp["ada_w"][l]
        W[f"wA{l}"] = _pack([_fchunk(aw, j * 512 + ar(512)) for j in range(12)], 1)
    wa1 = np.zeros((128, L, 8, 32), np.float32)
    wa2 = np.zeros((33, L, 512), np.float32)
    for l in range(L):
        for e in range(2):
            wa1[:, l, :, e * 16:(e + 1) * 16] = inp["gla_wa1"][l, e].reshape(8, 128, 16).transpose(1, 0, 2)
            for h in range(4):
                c0 = h * 128 + e * 64
                wa2[e * 16:(e + 1) * 16, l, c0:c0 + 64] = inp["gla_wa2"][l, e][:, h * 64:(h + 1) * 64]
                wa2[32, l, c0:c0 + 64] = inp["gla_ba"][l, e][h * 64:(h + 1) * 64]
    W["wa1"] = wa1
    W["wa2"] = wa2
    sp = []
    for l in range(L):
        cols = [inp["ada_b"][l].reshape(48, 128).T,
                inp["ln_g"][l].reshape(16, 128).T,
                inp["ln_b"][l].reshape(16, 128).T,
                inp["ffn_conv_w"][l].reshape(3 * NFC, 128).T,
                inp["ffn_conv_b"][l].reshape(NFC, 128).T]
        sp.append(np.concatenate(cols, axis=1))
    W["smallp"] = np.ascontiguousarray(np.stack(sp, 1), dtype=np.float32)
    bc = []
    for l in range(L):
        row = np.concatenate([inp["gla_norm_g"][l], inp["diff_subln_g"][l],
                              inp["diff_lam"][l].reshape(-1), inp["ret_decay"][l].reshape(-1)])
        bc.append(np.broadcast_to(row[None, :], (128, row.shape[0])))
    W["bcast"] = np.ascontiguousarray(np.stack(bc, 1), dtype=np.float32)
    j = ar(128)[:, None]
    i = ar(128)[None, :]
    cs = np.zeros((128, 9, 128), np.float32)
    cs[:, 8] = 1.0
    cs[:, 0] = (j <= i)
    cs[:, 1] = (j >= i)
    cs[:, 2] = (j > i)
    cs[:, 3] = (j < i)
    cs[:, 4] = (j == i)
    m = ar(128)
    sw = np.where((m % 32) < 16, m + 16, m - 16)
    cs[sw, 5, m] = 1.0
    cs[:, 6] = (j == i) & (j < 64)
    cs[:, 7] = (j == i) & (j >= 64)
    W["consts"] = cs
    return W


def _rope_tables(t0):
    t = t0 + np.arange(1024)
    row = (t // 64).astype(np.float32)
    col = (t % 64).astype(np.float32)
    inv = (np.float32(10000.0) ** (-np.arange(16, dtype=np.float32) / np.float32(16))).astype(np.float32)
    cos = np.zeros((128, 1024), np.float32)
    sin = np.zeros((128, 1024), np.float32)
    for p in range(128):
        d = p % 64
        pos = row if d < 32 else col
        ang = (pos * inv[d % 16]).astype(np.float32)
        cos[p] = np.cos(ang)
        s = np.sin(ang)
        sin[p] = -s if (d % 32) < 16 else s
    return cos, sin


class Builder:
    def __init__(self, nc, st, do_sample=True):
        self.nc = nc
        self.st = st
        self.S = Sched()
        self.do_sample = do_sample
        self.wslot_i = 0
        self.out_ops = []
        self.out_seen = 0

    def dram_in(self, name, shape, dt=F32):
        return self.nc.dram_tensor(name, list(shape), dt, kind="ExternalInput").ap()

    def dram_out(self, name, shape, dt=F32):
        return self.nc.dram_tensor(name, list(shape), dt, kind="ExternalOutput").ap()

    def sb(self, name, shape, dt):
        return self.st.enter_context(self.nc.sbuf_tensor(name, list(shape), dt))

    def wload(self, src_blk, nelem=WSZ):
        s = self.wslot_i % NSLOT
        self.wslot_i += 1
        dst = self.wring[:, s, 0:nelem]
        key = ("w", s)
        self.S.add("pool", lambda e: e.dma_start(out=dst, in_=src_blk[:, 0:nelem]), writes=[key], slot=self.wsl[s])
        return self.wring[:, s, :], key

    def xgroup(self, name):
        self.xs_i += 1
        return self.S.slot("x%d_%s" % (self.xs_i, name), wait_all=True)

    def gather(self, src, dst, rk, wk):
        sl = self.xgroup("cc")
        groups = [[0, 1, 2, 3], [4, 5, 6, 7]] if not os.environ.get("KG4") else [[0, 1, 2, 3]]
        self.S.add("pool", lambda e: e.collective_compute("AllGather", ALU.bypass, replica_groups=groups,
                                                          ins=[src], outs=[dst]),
                   reads=[rk], writes=[wk], slot=sl, inc=1)

    def declare(self):
        nc = self.nc
        di = self.dram_in
        self.d = d = {}
        d["xp"] = di("xp", [512, D])
        d["xs"] = di("xs", [1024, D])
        d["cond"] = di("cond", [128, 8, 2])
        for l in range(L):
            d[f"wF{l}"] = di(f"wF{l}", [6, 128, WSZ])
            d[f"wT{l}"] = di(f"wT{l}", [6, 128, WSZ])
            d[f"wM{l}"] = di(f"wM{l}", [16, 128, WSZ])
            d[f"wO{l}"] = di(f"wO{l}", [2, 128, WSZ])
            d[f"wU{l}"] = di(f"wU{l}", [11, 128, WSZ])
            d[f"wD{l}"] = di(f"wD{l}", [8, 128, WSZ])
            d[f"wA{l}"] = di(f"wA{l}", [12, 128, WSZ])
        d["wa1"] = di("wa1", [128, L, 8, 32])
        d["wa2"] = di("wa2", [33, L, 512])
        d["smallp"] = di("smallp", [128, L, 168])
        d["bcast"] = di("bcast", [128, L, 520])
        d["consts"] = di("consts", [128, 9, 128])
        d["ropec"] = di("ropec", [128, 1024])
        d["ropes"] = di("ropes", [128, 1024])
        d["meta"] = di("meta", [128, 16])
        d["sret"] = di("sret", [L, 128, 4, 128])
        d["sgla"] = di("sgla", [L, 128, 4, 128])
        d["ckc"] = di("ckc", [L, 2, 128, 512])
        d["cvc"] = di("cvc", [L, 2, 128, 4, 128])
        dint = lambda name, shape, dt: nc.dram_tensor(name, list(shape), dt, addr_space="Local", kind="Internal").ap()
        self.x_ = x_ = {}
        x_["kx_in"] = dint("kx_in", [128, 4096], BF16)
        x_["kx_out"] = dint("kx_out", [512, 4096], BF16)
        x_["vx_in"] = dint("vx_in", [128, 4096], BF16)
        x_["vx_out"] = dint("vx_out", [512, 4096], BF16)
        x_["st_in"] = dint("st_in", [128, 516], F32)
        x_["st_out"] = dint("st_out", [512, 516], F32)
        x_["hx_in"] = dint("hx_in", [128, 16], BF16)
        x_["hx_out"] = dint("hx_out", [512, 16], BF16)
        do = self.dram_out
        self.o = o = {}
        o["yp"] = do("yp", [512, D])
        o["ys"] = do("ys", [1024, D])
        o["ndk"] = do("ndk", [2, L, 8, 256, 64])
        o["ndv"] = do("ndv", [2, L, 4, 256, 128])
        o["nsr"] = do("nsr", [2, L, 2, 4, 64, 128])
        o["nsg"] = do("nsg", [2, L, 2, 4, 64, 128])


        sb = self.sb
        self.xT = sb("xT", [128, 8, 1024], F32)
        self.hT = sb("hT", [128, 8, 1024], BF16)
        self.wring = sb("wring", [128, NSLOT, WSZ], BF16)
        self.wsl = [self.S.slot(f"w{s}") for s in range(NSLOT)]
        self.cst = sb("cst", [128, 9, 128], F32)
        self.cstb = sb("cstb", [128, 9, 128], BF16)
        self.onesb = sb("onesb", [128, 128], BF16)
        self.smallp = sb("smallp_sb", [128, L, 168], F32)
        self.bcast = sb("bcast_sb", [128, L, 520], F32)
        self.modT = sb("modT", [128, L, 48, 2], F32)
        self.condf = sb("condf", [128, 8, 2], F32)
        self.condb = sb("condb", [128, 8, 2], BF16)
        self.modtok = sb("modtok", [2, 512], F32)
        self.wa1 = sb("wa1_sb", [128, L, 8, 32], BF16)
        self.wa2 = sb("wa2_sb", [33, L, 512], BF16)
        self.yT = sb("yT", [128, 3, 4, 1024], BF16)
        if self.do_sample:
            self.ropec = sb("ropec_sb", [128, 1024], F32)
            self.ropes = sb("ropes_sb", [128, 1024], F32)
        self.lam = sb("lam", [128, 8], F32)
        self.retsp = sb("retsp", [128, L, 8], F32)
        self.subg = sb("subg", [128, 128], F32)
        self.ps = self.st.enter_context(nc.psum_tensor("ps", [128, 8, 512], F32))
        self.s_in = self.S.slot("init", wait_all=True)
        self.s_inp = self.S.slot("initp", wait_all=True)
        self.s_io = [self.S.slot(f"io{i}") for i in range(2)]
        self.s_st = [self.S.slot(f"st{i}") for i in range(2)]
        self.s_out = self.S.slot("out_misc")
        self.s_cc = self.S.slot("cc")
        self.s_xs = [self.S.slot(f"xs{i}") for i in range(4)]
        self.xs_i = 0
        self.meta = sb("meta_sb", [128, 16], F32)
        U = sb("U", [128, UW], F32)
        self.U = U

        def vb(off, shape):
            n = int(np.prod(shape)) // 2
            a = U[:, off:off + n].bitcast(BF16)
            return a if len(shape) == 1 else a.rearrange(
                "p (" + " ".join("abcd"[:len(shape)]) + ") -> p " + " ".join("abcd"[:len(shape)]),
                **{k: v for k, v in zip("abcd", shape)})

        def vf(off, shape):
            n = int(np.prod(shape))
            a = U[:, off:off + n]
            return a if len(shape) == 1 else a.rearrange(
                "p (" + " ".join("abcd"[:len(shape)]) + ") -> p " + " ".join("abcd"[:len(shape)]),
                **{k: v for k, v in zip("abcd", shape)})

        self.QT = vb(0, [4, 1024])
        self.otokAB = vb(0, [8, 512])
        self.dbf = vb(0, [4, 8, 128])
        self.stage = vf(0, [2, 1024])
        self.KT = vb(2048, [4, 1024])
        self.sstk = vb(2048, [4, 8, 128])
        self.vtok = vb(4096, [8, 512])
        self.gtok = vb(6144, [8, 512])
        self.otokC = vb(6144, [8, 512])
        self.qd = vb(8192, [4, 8, 128])
        self.Pm = vb(10240, [4, 8, 128])
        self.ropex = vb(10240, [512])
        self.ropet = vf(10496, [512])
        self.ropeu = vf(11008, [512])
        self.cstage = vf(10240, [2, 512])
        self.KH = vb(8192, [4096])
        self.VH = vb(12288, [32, 130])
        self.vctx = vb(10240, [2, 4, 130])
        self.kctx = vb(10760, [4, 256])
        self.kstg = vf(11272, [512])
        self.delta = vf(12288, [4, 8, 128])
        self.vaug = vb(12288, [8, 4, 130])
        self.ET = vb(12288 + 2080, [2, 512])
        o_ = 12288 + 4160
        self.s32 = vf(o_, [2, 4, 128]); o_ += 1024
        self.EqT = vf(o_, [4, 128]); o_ += 512
        self.EkT = vf(o_, [4, 128]); o_ += 512
        self.Ekd = vf(o_, [4, 128]); o_ += 512
        self.dcol = vf(o_, [4, 8]); o_ += 32
        self.latok = vf(o_, [512]); o_ += 512
        self.uT = vb(o_, [1024]); o_ += 512
        self.stat = vf(o_, [8, 4, 2]); o_ += 64
        self.rstd = vf(o_, [8, 4]); o_ += 32
        self.st6 = vf(o_, [6]); o_ += 8
        self.ksb = vb(o_, [2, 128]); o_ += 128
        self.kdtok = vb(o_, [2, 128]); o_ += 128
        self.ytok = vb(o_, [2, 512]); o_ += 512
        self.tsm = vf(o_, [2, 128]); o_ += 256
        self.sm8 = vf(o_, [16]); o_ += 16
        self.ggt = vb(o_, [512]); o_ += 256
        assert o_ <= UW, o_
        self.Sin = self.EqT
        self.Srun = self.EkT
        self.GL = self.Ekd
        self.sinb = self.uT[:, 0:512].rearrange("p (h v) -> p h v", h=4)
        self.cp = self.latok[:, 0:32].rearrange("p (h t) -> p h t", h=4)
        self.Dg = self.latok[:, 32:48].rearrange("p (s h) -> p s h", s=4)
        self.hxs = self.latok[:, 64:72].bitcast(BF16)
        self.hG = self.latok[:, 72:104].bitcast(BF16).rearrange("p (r c) -> p r c", r=4)
        self.hhf = self.latok[:, 104:120].rearrange("p (k c) -> p k c", k=8)
        self.hh = self.latok[:, 120:128].bitcast(BF16).rearrange("p (k c) -> p k c", k=8)
        self.mergedT = vb(0, [8, 1024])
        self.dT = vf(4096, [8, 512])
        self.sqT = vb(8192, [8, 512])
        self.tmpA = vf(10240, [512])
        self.tmpB = vf(10752, [512])
        self.sig = vf(11264, [3, 512])
        self.tm3 = vf(12800, [3, 512])
        self.gT = vb(0, [NFC, 1024])
        self.aT = vf(11264, [2, 1028])
        self.cT = vf(13320, [2, 512])
        self.glT = vf(14344, [2, 512])
        self.dT2 = vf(15368, [8, 512])
        self.tmpA2 = vf(19464, [512])
        self.tmpB2 = vf(19976, [512])
        self.sqT2 = vb(11264, [8, 512])
        assert 20488 <= UW
        self.bank_i = 0
        self.bank_pool = list(range(8))

    def bank(self):
        b = self.bank_pool[self.bank_i % len(self.bank_pool)]
        self.bank_i += 1
        return b

    def barrier(self):
        S = self.S
        last = {}
        for e in S.ENGS:
            last[e] = None
            for op in reversed(S.ops[e]):
                if op.inc != 0:
                    last[e] = op
                    break
        dmas = [op for op in self.out_ops[self.out_seen:]]
        self.out_seen = len(self.out_ops)
        for e in os.environ.get("KBE", "pe,act,dve,pool,sp").split(","):
            deps = [last[o] for o in S.ENGS if (o != e or e != "pe") and last[o] is not None and last[o].slot is None]
            S.wait_for(e, deps + dmas)
        if os.environ.get("KNOCLR"):
            return
        for k in list(S.lastw.keys()):
            if S.lastw[k].slot is None:
                del S.lastw[k]
        for k in list(S.readers.keys()):
            S.readers[k] = [r for r in S.readers[k] if r.slot is not None]

    def psb(self, b):
        return self.ps[:, b, :].bitcast(BF16)

    def init_phase(self):
        S, d = self.S, self.d
        si = self.s_in
        S.add("sp", lambda e: e.dma_start(out=self.cst[:], in_=d["consts"]), writes=["cst"], slot=si)
        S.add("pool", lambda e: e.dma_start(out=self.cstb[:], in_=d["consts"]), writes=["cstb"], slot=self.s_inp)
        S.add("sp", lambda e: e.dma_start(out=self.smallp[:], in_=d["smallp"]), writes=["smallp"], slot=si)
        S.add("sp", lambda e: e.dma_start(out=self.bcast[:], in_=d["bcast"]), writes=["bcast"], slot=si)
        S.add("sp", lambda e: e.dma_start(out=self.condf[:], in_=d["cond"]), writes=["condf"], slot=si)
        S.add("pool", lambda e: e.dma_start(out=self.wa1[:], in_=d["wa1"]), writes=["wa1"], slot=self.s_inp)
        S.add("pool", lambda e: e.dma_start(out=self.wa2[:], in_=d["wa2"]), writes=["wa2"], slot=self.s_inp)
        if self.do_sample:
            S.add("sp", lambda e: e.dma_start(out=self.meta[:], in_=d["meta"]), writes=["meta"], slot=si)
            S.add("sp", lambda e: e.dma_start(out=self.ropec[:], in_=d["ropec"]), writes=["ropec"], slot=si)
            S.add("sp", lambda e: e.dma_start(out=self.ropes[:], in_=d["ropes"]), writes=["ropes"], slot=si)
        S.add("dve", lambda e: e.memset(self.onesb[:], 1.0 / 1024.0), writes=["onesb"])
        S.add("act", lambda e: e.activation(self.condb[:], self.condf[:], AF.Silu), reads=["condf"], writes=["condb"])
        for l in range(L):
            for j in range(12):
                wv, wk = self.wload(d[f"wA{l}"][j])
                b = self.bank()
                for kc in range(8):
                    S.add("pe", lambda e, kc=kc, b=b, wv=wv: e.matmul(
                        self.ps[0:2, b, :], self.condb[:, kc, :], wv[:, kc * 512:(kc + 1) * 512],
                        start=(kc == 0), stop=(kc == 7)), reads=[wk, "condb"], writes=[("ps", b)])
                S.add("act", lambda e, b=b: e.activation(self.modtok[:], self.ps[0:2, b, :], AF.Identity),
                      reads=[("ps", b)], writes=["modtok"])
                b2 = self.bank()
                for q in range(4):
                    S.add("pe", lambda e, q=q, b2=b2: e.transpose(
                        self.ps[:, b2, q * 2:(q + 1) * 2], self.modtok[0:2, q * 128:(q + 1) * 128],
                        self.cst[0:2, 4, 0:2]), reads=["modtok", "cst"], writes=[("ps", b2)])
                S.add("dve", lambda e, l=l, j=j, b2=b2: e.tensor_tensor(
                    self.modT[:, l, j * 4:(j + 1) * 4, :],
                    self.ps[:, b2, 0:8].rearrange("p (a b) -> p a b", a=4),
                    self.smallp[:, l, j * 4:(j + 1) * 4].rearrange("p (a o) -> p a o", o=1).broadcast_to([128, 4, 2]),
                    ALU.add), reads=[("ps", b2), "smallp"], writes=["modT"])
            S.add("dve", lambda e, l=l: e.tensor_scalar(self.modT[:, l, 8:16, :], self.modT[:, l, 8:16, :], 1.0, None, ALU.add),
                  reads=["modT"], writes=["modT"])
            S.add("dve", lambda e, l=l: e.tensor_scalar(self.modT[:, l, 32:40, :], self.modT[:, l, 32:40, :], 1.0, None, ALU.add),
                  reads=["modT"], writes=["modT"])
            S.add("dve", lambda e, l=l: e.tensor_scalar(self.modT[:, l, 16:24, :], self.modT[:, l, 16:24, :], 0.5 / ALPHA, None, ALU.mult),
                  reads=["modT"], writes=["modT"])
            S.add("dve", lambda e, l=l: e.tensor_scalar(self.modT[:, l, 40:48, :], self.modT[:, l, 40:48, :], 1.0 / ALPHA, None, ALU.mult),
                  reads=["modT"], writes=["modT"])
            lam_init = 0.8 - 0.6 * math.exp(-0.3 * l)
            lp = self.bcast[:, l, 256:512].rearrange("p (a b c) -> p a b c", a=2, b=2)
            S.add("dve", lambda e, lp=lp: e.tensor_tensor(self.tsm[:, 0, :].rearrange("p (a c) -> p a c", a=2),
                                                          lp[:, :, 0, :], lp[:, :, 1, :], ALU.mult),
                  reads=["bcast"], writes=["tsm"])
            S.add("dve", lambda e: e.reduce_sum(self.sm8[:, 0:2], self.tsm[:, 0, :].rearrange("p (a c) -> p a c", a=2),
                                                axis=mybir.AxisListType.X), reads=["tsm"], writes=["sm8"])
            S.add("act", lambda e: e.activation(self.sm8[:, 2:4], self.sm8[:, 0:2], AF.Exp), reads=["sm8"], writes=["sm8b"])
            S.add("dve", lambda e: e.tensor_tensor(self.sm8[:, 4:5], self.sm8[:, 3:4], self.sm8[:, 2:3], ALU.subtract),
                  reads=["sm8b"], writes=["sm8c"])
            S.add("dve", lambda e, l=l, li=lam_init: e.tensor_scalar(self.lam[:, l:l + 1], self.sm8[:, 4:5], -li, None, ALU.add),
                  reads=["sm8c"], writes=["lam"])
            S.add("act", lambda e, l=l: e.activation(self.sm8[:, 8:16], self.bcast[:, l, 512:520], AF.Exp, scale=-1.0),
                  reads=["bcast"], writes=["sm8d"])
            S.add("act", lambda e, l=l: e.activation(self.retsp[:, l, :], self.sm8[:, 8:16], AF.Ln, bias=1.0),
                  reads=["sm8d"], writes=["retsp"])

    def load_x(self, P):
        S = self.S
        src = self.d[P["x"]]
        for t in range(P["NT"]):
            sl = t % 2
            S.add("sp", lambda e, t=t, sl=sl: e.dma_start(out=self.stage[:, sl, :], in_=src[t * 128:(t + 1) * 128, :]),
                  writes=[("stage", sl)], slot=self.s_io[sl])
            for half in range(2):
                b = self.bank()
                for q in range(4):
                    kc = half * 4 + q
                    S.add("pe", lambda e, b=b, q=q, kc=kc, sl=sl: e.transpose(
                        self.ps[:, b, q * 128:(q + 1) * 128], self.stage[:, sl, kc * 128:(kc + 1) * 128], self.cst[:, 4, :]),
                        reads=[("stage", sl), "cst"], writes=[("ps", b)])
                S.add("act" if half == 0 else "dve",
                      (lambda e, b=b, half=half, t=t: e.activation(
                          self.xT[:, half * 4:half * 4 + 4, t * 128:(t + 1) * 128],
                          self.ps[:, b, :].rearrange("p (a c) -> p a c", a=4), AF.Identity)) if half == 0 else
                      (lambda e, b=b, half=half, t=t: e.tensor_copy(
                          self.xT[:, half * 4:half * 4 + 4, t * 128:(t + 1) * 128],
                          self.ps[:, b, :].rearrange("p (a c) -> p a c", a=4))),
                      reads=[("ps", b)], writes=[("xT", t // 4)])

    def modulate(self, P, l, which):
        S = self.S
        c = P["cond"]
        sh0 = 0 if which == 0 else 24
        sc0 = 8 if which == 0 else 32
        for g in range(P["NG"]):
            for kc in range(8):
                S.add("dve", lambda e, g=g, kc=kc: e.tensor_scalar(
                    self.hT[:, kc, g * 512:(g + 1) * 512], self.xT[:, kc, g * 512:(g + 1) * 512],
                    self.modT[:, l, sc0 + kc, c:c + 1], self.modT[:, l, sh0 + kc, c:c + 1], ALU.mult, ALU.add),
                    reads=[("xT", g), "modT"], writes=[("hT", g)])

    def projF(self, P, wv, wk, ci, evac):
        S = self.S
        for g in range(P["NG"]):
            b = self.bank()
            for kc in range(8):
                S.add("pe", lambda e, b=b, kc=kc, g=g: e.matmul(
                    self.ps[:, b, :], wv[:, ci * 1024 + kc * 128: ci * 1024 + (kc + 1) * 128],
                    self.hT[:, kc, g * 512:(g + 1) * 512], start=(kc == 0), stop=(kc == 7)),
                    reads=[wk, ("hT", g)], writes=[("ps", b)])
            evac(g, b)

    def projT(self, P, wv, wk, evac):
        S = self.S
        for t in range(P["NT"]):
            b = self.bank()
            for kc in range(8):
                S.add("pe", lambda e, b=b, kc=kc, t=t: e.matmul(
                    self.ps[:, b, :], self.hT[:, kc, t * 128:(t + 1) * 128], wv[:, kc * 512:(kc + 1) * 512],
                    start=(kc == 0), stop=(kc == 7)),
                    reads=[wk, ("hT", t // 4)], writes=[("ps", b)])
            evac(t, b)

    def evac_rope(self, P, dst, key, h, g, b, scale):
        S = self.S
        gc = slice(g * 512, (g + 1) * 512)
        if not P["rope"]:
            S.add("act", lambda e: e.activation(dst[:, h, gc], self.ps[:, b, :], AF.Copy, scale=scale),
                  reads=[("ps", b)], writes=[(key, h)])
            return
        S.add("act", lambda e: e.activation(self.ropex[:, :], self.ps[:, b, :], AF.Copy, scale=scale),
              reads=[("ps", b)], writes=["ropex"])
        b2 = self.bank()
        S.add("pe", lambda e: e.matmul(self.ps[:, b2, :], self.cstb[:, 5, :], self.ropex[:, :], start=True, stop=True),
              reads=["ropex", "cstb"], writes=[("ps", b2)])
        S.add("dve", lambda e: e.tensor_tensor(self.ropet[:, :], self.ps[:, b2, :], self.ropes[:, gc], ALU.mult),
              reads=[("ps", b2), "ropes"], writes=["ropet"])
        S.add("pool", lambda e: e.tensor_tensor(self.ropeu[:, :], self.ropex[:, :], self.ropec[:, gc], ALU.mult),
              reads=["ropex", "ropec"], writes=["ropeu"])
        S.add("dve", lambda e: e.tensor_tensor(dst[:, h, gc], self.ropet[:, :], self.ropeu[:, :], ALU.add),
              reads=["ropet", "ropeu"], writes=[(key, h)])

    def decay_tiles(self, scale, t):
        S = self.S
        b = self.bank()
        for h in range(4):
            for dd, ci in ((0, 2), (1, 3)):
                c0 = h * 128 + dd * 64
                S.add("pe", lambda e, c0=c0, ci=ci: e.matmul(self.ps[:, b, c0:c0 + 64], self.cst[:, ci, :], self.latok[:, c0:c0 + 64],
                                                            start=True, stop=True),
                      reads=["latok", "cst"], writes=[("ps", b)])
        S.add("act", lambda e: e.activation(self.Ekd[:, :, :], self.ps[:, b, :].rearrange("p (h c) -> p h c", h=4), AF.Exp, scale=scale),
              reads=[("ps", b)], writes=["Ekd"])
        bx, by = self.bank(), self.bank()
        for h in range(4):
            S.add("pe", lambda e, h=h: e.matmul(self.ps[:, bx, h * 128:(h + 1) * 128], self.latok[:, h * 128:(h + 1) * 128],
                                               self.cst[:, 0, :], start=True, stop=True),
                  reads=["latok", "cst"], writes=[("ps", bx)])
            S.add("pe", lambda e, h=h: e.matmul(self.ps[:, by, h * 128:(h + 1) * 128], self.latok[:, h * 128:(h + 1) * 128],
                                               self.cst[:, 1, :], start=True, stop=True),
                  reads=["latok", "cst"], writes=[("ps", by)])
        for (dst, sc, key) in ((self.EqT, scale, "EqT"), (self.EkT, -scale, "EkT")):
            S.add("act", lambda e, dst=dst, sc=sc: e.activation(
                dst[0:64, :, :], self.ps[0:64, bx, :].rearrange("p (h c) -> p h c", h=4), AF.Exp, scale=sc),
                reads=[("ps", bx)], writes=[key])
            S.add("act", lambda e, dst=dst, sc=sc: e.activation(
                dst[64:128, :, :], self.ps[64:128, by, :].rearrange("p (h c) -> p h c", h=4), AF.Exp, scale=sc),
                reads=[("ps", by)], writes=[key])
        if t is not None:
            S.add("dve", lambda e: e.tensor_copy(self.dcol[0:64, :, t:t + 1], self.EqT[0:64, :, 127:128]),
                  reads=["EqT"], writes=["dcol"])
            S.add("dve", lambda e: e.tensor_copy(self.dcol[64:128, :, t:t + 1], self.EqT[64:128, :, 0:1]),
                  reads=["EqT"], writes=["dcol"])

    def compose_states(self, P, l, mix):
        S, x_ = self.S, self.x_
        dk = [("delta", h, t) for h in range(4) for t in range(8)]
        S.add("dve", lambda e: e.tensor_copy(self.cp[0:64, :, 0:1], self.dcol[0:64, :, 0:1]), reads=["dcol"], writes=["cp"])
        S.add("dve", lambda e: e.tensor_copy(self.cp[64:128, :, 7:8], self.dcol[64:128, :, 7:8]), reads=["dcol"], writes=["cp"])
        for t in range(1, 8):
            S.add("dve", lambda e, t=t: e.tensor_tensor(self.cp[0:64, :, t:t + 1], self.cp[0:64, :, t - 1:t],
                                                        self.dcol[0:64, :, t:t + 1], ALU.mult), reads=["cp", "dcol"], writes=["cp"])
            tb = 7 - t
            S.add("dve", lambda e, tb=tb: e.tensor_tensor(self.cp[64:128, :, tb:tb + 1], self.cp[64:128, :, tb + 1:tb + 2],
                                                          self.dcol[64:128, :, tb:tb + 1], ALU.mult), reads=["cp", "dcol"], writes=["cp"])
        g1 = self.xgroup("stw")
        stL = x_["st_in"][:, 0:512].rearrange("p (h v) -> p h v", h=4)
        S.add("sp", lambda e: e.dma_start(out=stL[0:64], in_=self.delta[0:64, :, 7, :]), reads=dk, writes=["st_in"], slot=g1)
        S.add("sp", lambda e: e.dma_start(out=stL[64:128], in_=self.delta[64:128, :, 0, :]), reads=dk, writes=["st_in"], slot=g1)
        S.add("sp", lambda e: e.dma_start(out=x_["st_in"][0:64, 512:516].rearrange("p (h o) -> p h o", o=1), in_=self.cp[0:64, :, 7:8], allow_slow_non_contiguous=True), reads=["cp"], writes=["st_in"], slot=g1)
        S.add("sp", lambda e: e.dma_start(out=x_["st_in"][64:128, 512:516].rearrange("p (h o) -> p h o", o=1), in_=self.cp[64:128, :, 0:1], allow_slow_non_contiguous=True), reads=["cp"], writes=["st_in"], slot=g1)
        self.gather(x_["st_in"], x_["st_out"], "st_in", "st_out")
        g2 = self.xgroup("stl")
        src0 = self.d["sret" if mix == 0 else "sgla"][l]
        S.add("sp", lambda e: e.dma_start(out=self.Srun[:, :, :], in_=src0), writes=["Srun"], slot=g2)
        for s_ in range(4):
            S.add("sp", lambda e, s_=s_: e.dma_start(out=self.Dg[0:64, s_, :], in_=x_["st_out"][s_ * 128:s_ * 128 + 64, 512:516]),
                  reads=["st_out"], writes=["Dg"], slot=g2)
            S.add("sp", lambda e, s_=s_: e.dma_start(out=self.Dg[64:128, s_, :],
                                                     in_=x_["st_out"][(3 - s_) * 128 + 64:(3 - s_) * 128 + 128, 512:516]),
                  reads=["st_out"], writes=["Dg"], slot=g2)
        S.add("dve", lambda e: e.memset(self.Sin[:, :, :], 0.0), writes=["Sin"])
        for s_ in range(4):
            g3 = self.xgroup("gl")
            S.add("sp", lambda e, s_=s_: e.dma_start(
                out=self.GL[0:64, :, :], in_=x_["st_out"][s_ * 128:s_ * 128 + 64, 0:512].rearrange("p (h v) -> p h v", h=4)),
                reads=["st_out"], writes=["GL"], slot=g3)
            S.add("sp", lambda e, s_=s_: e.dma_start(
                out=self.GL[64:128, :, :],
                in_=x_["st_out"][(3 - s_) * 128 + 64:(3 - s_) * 128 + 128, 0:512].rearrange("p (h v) -> p h v", h=4)),
                reads=["st_out"], writes=["GL"], slot=g3)
            S.add("dve", lambda e, s_=s_: e.scalar_tensor_tensor(
                self.Sin[:, :, :].rearrange("p h v -> p (h v)"), self.Srun[:, :, :].rearrange("p h v -> p (h v)"),
                self.meta[:, s_:s_ + 1], self.Sin[:, :, :].rearrange("p h v -> p (h v)"), ALU.mult, ALU.add),
                reads=["Srun", "meta", "Sin"], writes=["Sin"])
            for h in range(4):
                S.add("dve", lambda e, s_=s_, h=h: e.scalar_tensor_tensor(
                    self.Srun[:, h, :], self.Srun[:, h, :], self.Dg[:, s_, h:h + 1], self.GL[:, h, :], ALU.mult, ALU.add),
                    reads=["Srun", "Dg", "GL"], writes=["Srun"])
        for t in range(8):
            for h in range(4):
                S.add("dve", lambda e, t=t, h=h: e.scalar_tensor_tensor(
                    self.delta[:, h, t, :], self.Sin[:, h, :], self.cp[:, h, t:t + 1], self.delta[:, h, t, :], ALU.mult, ALU.add),
                    reads=["Sin", "cp", ("delta", h, t)], writes=[("delta", h, t)])
        S.add("dve", lambda e: e.tensor_copy(self.sinb[:, :, :], self.Sin[:, :, :]), reads=["Sin"], writes=["sinb"])

    def mixer_AB(self, P, l, mix):
        S, d = self.S, self.d
        NT, NG = P["NT"], P["NG"]
        qscale = 1.0 if mix == 0 else 0.125
        kscale = 0.125 if mix == 0 else 1.0
        ropeP = P if mix == 0 else dict(P, rope=False)
        wv, wk = self.wload(d[f"wT{l}"][2 * mix])
        self.projT(P, wv, wk, lambda t, b: S.add(
            "act", lambda e: e.activation(self.vtok[:, t, :], self.ps[:, b, :], AF.Identity), reads=[("ps", b)], writes=[("vtok", t)]))
        wv, wk = self.wload(d[f"wT{l}"][2 * mix + 1])
        self.projT(P, wv, wk, lambda t, b: S.add(
            "act", lambda e: e.activation(self.gtok[:, t, :], self.ps[:, b, :], AF.Silu),
            reads=[("ps", b)], writes=[("gtok", t)]))
        ksub = int(os.environ.get("KSUB", "99"))
        if ksub < 1:
            return
        if mix == 1:
            S.add("dve", lambda e: e.memset(self.uT[0:33, 0:P["T"]], 1.0), writes=["uT"])
            for g in range(NG):
                b = self.bank()
                for kc in range(8):
                    S.add("pe", lambda e, b=b, kc=kc, g=g: e.matmul(
                        self.ps[0:32, b, :], self.wa1[:, l, kc, :], self.hT[:, kc, g * 512:(g + 1) * 512],
                        start=(kc == 0), stop=(kc == 7)), reads=["wa1", ("hT", g)], writes=[("ps", b)])
                S.add("act", lambda e, b=b, g=g: e.activation(self.uT[0:32, g * 512:(g + 1) * 512], self.ps[0:32, b, :], AF.Identity),
                      reads=[("ps", b)], writes=["uT"])
        wv, wk = self.wload(d[f"wF{l}"][2 * mix])
        for h in range(4):
            self.projF(P, wv, wk, h, lambda g, b, h=h: self.evac_rope(ropeP, self.QT, "QT", h, g, b, qscale))
        wv, wk = self.wload(d[f"wF{l}"][2 * mix + 1])
        for h in range(4):
            self.projF(P, wv, wk, h, lambda g, b, h=h: self.evac_rope(ropeP, self.KT, "KT", h, g, b, kscale))
        if ropeP["rope"]:
            self.barrier()
        if ksub < 2:
            return
        if mix == 0:
            S.add("dve", lambda e: e.tensor_copy(
                self.latok[:, :].rearrange("p (h d k) -> p h d k", h=4, d=2),
                self.retsp[:, l, :].rearrange("p (d h o) -> p h d o", d=2, o=1).broadcast_to([128, 4, 2, 64])),
                reads=["retsp"], writes=["latok"])
            self.decay_tiles(-1.0, None)
            for t in range(NT):
                S.add("dve", lambda e, t=t: e.tensor_copy(self.dcol[0:64, :, t:t + 1], self.EqT[0:64, :, 127:128]),
                      reads=["EqT"], writes=["dcol"])
                S.add("dve", lambda e, t=t: e.tensor_copy(self.dcol[64:128, :, t:t + 1], self.EqT[64:128, :, 0:1]),
                      reads=["EqT"], writes=["dcol"])
        if ksub < 3:
            return
        it = 0
        for t in range(NT):
            tc_ = slice(t * 128, (t + 1) * 128)
            if mix == 1:
                b = self.bank()
                S.add("pe", lambda e, b=b, tc_=tc_: e.matmul(self.ps[:, b, :], self.uT[0:33, tc_], self.wa2[0:33, l, :],
                                                            start=True, stop=True),
                      reads=["uT", "wa2"], writes=[("ps", b)])
                S.add("act", lambda e, b=b: e.activation(self.latok[:, :], self.ps[:, b, :], AF.Exp, scale=-1.0),
                      reads=[("ps", b)], writes=["latok"])
                S.add("act", lambda e: e.activation(self.latok[:, :], self.latok[:, :], AF.Ln, bias=1.0),
                      reads=["latok"], writes=["latok"])
                self.decay_tiles(-1.0 / 16.0, t)
            for h in range(4):
                i2 = it % 2
                it += 1
                S.add("dve", lambda e, h=h, t=t, tc_=tc_: e.tensor_tensor(
                    self.qd[:, h, t, :], self.QT[:, h, tc_], self.EqT[:, h, :], ALU.mult),
                    reads=[("QT", h), "EqT"], writes=[("qd", h, t)])
                S.add("dve", lambda e, h=h, tc_=tc_, i2=i2: e.tensor_tensor(
                    self.ksb[:, i2, :], self.KT[:, h, tc_], self.EkT[:, h, :], ALU.mult),
                    reads=[("KT", h), "EkT"], writes=[("ksb", i2)])
                bt = self.bank()
                S.add("pe", lambda e, h=h, tc_=tc_, bt=bt: e.transpose(
                    self.psb(bt)[:, 0:128], self.KT[:, h, tc_], self.cstb[:, 4, :]),
                    reads=[("KT", h), "cstb"], writes=[("ps", bt)])
                S.add("dve", lambda e, h=h, bt=bt, i2=i2: e.tensor_tensor(
                    self.kdtok[:, i2, :], self.psb(bt)[:, 0:128], self.Ekd[:, h, :], ALU.mult),
                    reads=[("ps", bt), "Ekd"], writes=[("kdtok", i2)])
                bd = self.bank()
                S.add("pe", lambda e, h=h, t=t, bd=bd, i2=i2: e.matmul(
                    self.ps[:, bd, 0:128], self.kdtok[:, i2, :], self.vtok[:, t, h * 128:(h + 1) * 128], start=True, stop=True),
                    reads=[("kdtok", i2), ("vtok", t)], writes=[("ps", bd)])
                S.add("act", lambda e, h=h, t=t, bd=bd: e.activation(self.delta[:, h, t, 0:128], self.ps[:, bd, 0:128], AF.Identity),
                      reads=[("ps", bd)], writes=[("delta", h, t)])
                b1, b2 = self.bank(), self.bank()
                S.add("pe", lambda e, h=h, t=t, b1=b1, i2=i2: e.matmul(
                    self.ps[:, b1, 0:128], self.ksb[0:64, i2, :], self.qd[0:64, h, t, :], start=True, stop=True),
                    reads=[("ksb", i2), ("qd", h, t)], writes=[("ps", b1)])
                S.add("pe", lambda e, h=h, t=t, b2=b2, i2=i2: e.matmul(
                    self.ps[:, b2, 0:128], self.ksb[64:128, i2, :], self.qd[64:128, h, t, :], start=True, stop=True),
                    reads=[("ksb", i2), ("qd", h, t)], writes=[("ps", b2)])
                S.add("dve", lambda e, b1=b1, i2=i2: e.tensor_tensor(
                    self.tsm[:, i2, :], self.ps[:, b1, 0:128], self.cst[:, 0, :], ALU.mult),
                    reads=[("ps", b1), "cst"], writes=[("tsm", i2)])
                S.add("dve", lambda e, b2=b2: e.tensor_tensor(
                    self.ps[:, b2, 128:256], self.ps[:, b2, 0:128], self.cst[:, 1, :], ALU.mult),
                    reads=[("ps", b2), "cst"], writes=[("ps", b2)])
                S.add("dve", lambda e, h=h, t=t, b2=b2, i2=i2: e.tensor_tensor(
                    self.Pm[:, h, t, :], self.ps[:, b2, 128:256], self.tsm[:, i2, :], ALU.add),
                    reads=[("ps", b2), ("tsm", i2)], writes=[("Pm", h, t)])
        if ksub < 4:
            return
        self.barrier()
        out_name = "nsr" if mix == 0 else "nsg"
        for si, (t0, n) in enumerate(P["seqs"]):
            for c_ in range(1, n):
                tf = t0 + c_
                tb = t0 + n - 1 - c_
                for h in range(4 if os.environ.get("KX", "0") != "5" else 0):
                    S.add("dve", lambda e, h=h, tf=tf: e.scalar_tensor_tensor(
                        self.delta[0:64, h, tf, 0:128], self.delta[0:64, h, tf - 1, 0:128], self.dcol[0:64, h, tf:tf + 1],
                        self.delta[0:64, h, tf, 0:128], ALU.mult, ALU.add),
                        reads=[("delta", h, tf - 1), "dcol", ("delta", h, tf)], writes=[("delta", h, tf)])
                    S.add("dve", lambda e, h=h, tb=tb: e.scalar_tensor_tensor(
                        self.delta[64:128, h, tb, 0:128], self.delta[64:128, h, tb + 1, 0:128], self.dcol[64:128, h, tb:tb + 1],
                        self.delta[64:128, h, tb, 0:128], ALU.mult, ALU.add),
                        reads=[("delta", h, tb + 1), "dcol", ("delta", h, tb)], writes=[("delta", h, tb)])
            if P.get("sample"):
                self.compose_states(P, l, mix)
            if P["states_out"] and not os.environ.get("KNOST"):
                for dr, tt in ((0, t0 + n - 1), (1, t0)):
                    op = S.add("sp", lambda e, si=si, dr=dr, tt=tt: e.dma_start(
                        out=self.o[out_name][si, l, dr].rearrange("h k v -> k h v"),
                        in_=self.delta[dr * 64:(dr + 1) * 64, :, tt, 0:128]),
                        reads=[("delta", h_, tt) for h_ in range(4)], slot=self.s_st[si % 2])
                    self.out_ops.append(op)
            if os.environ.get("KX") == "7":
                ba = self.bank()
                S.add("pe", lambda e, ba=ba: e.matmul(self.ps[:, ba, 0:128], self.kdtok[:, 0, :], self.vtok[:, 0, 0:128],
                                                      start=True, stop=True),
                      reads=[("kdtok", 0), ("vtok", 0)], writes=[("ps", ba)])
                continue
            if os.environ.get("KX") == "8":
                ba = self.bank()
                S.add("pe", lambda e, ba=ba: e.matmul(self.ps[:, ba, 0:128], self.cstb[:, 6, :], self.vtok[:, 0, 0:128],
                                                      start=True, stop=True),
                      reads=["cstb", ("vtok", 0)], writes=[("ps", ba)])
                continue
            for c_ in range(n):
                t = t0 + c_
                for h in range(4):
                    kv = os.environ.get("KV", "0")
                    if kv == "1":
                        S.add("dve", lambda e, t=t, h=h: e.tensor_tensor(self.dbf[:, h, t, :], self.cst[:, 0, :], self.cst[:, 8, :], ALU.mult),
                              reads=["cst"], writes=[("QT", h)])
                    elif kv == "3":
                        S.add("dve", lambda e, t=t, h=h: e.tensor_tensor(self.dbf[:, h, t, :], self.cst[:, 0, :], self.cst[:, 8, :], ALU.mult),
                              reads=["cst"], writes=[("tsm", 0)])
                    elif kv == "4":
                        S.add("dve", lambda e, t=t, h=h: e.tensor_tensor(self.tsm[:, 0, :], self.cst[:, 0, :], self.cst[:, 8, :], ALU.mult),
                              reads=["cst"], writes=[("QT", h)])
                    elif kv == "2":
                        S.add("dve", lambda e, t=t, h=h: e.tensor_tensor(self.tsm[:, 0, :], self.delta[:, h, t, :], self.cst[:, 8, :], ALU.mult),
                              reads=[("delta", h, t), "cst"], writes=[("tsm", 0)])
                    else:
                        S.add("dve", lambda e, t=t, h=h: e.tensor_tensor(self.dbf[:, h, t, :], self.delta[:, h, t, :], self.cst[:, 8, :], ALU.mult),
                              reads=[("delta", h, t), "cst"], writes=[("dbf", h)])
            if os.environ.get("KX") == "9":
                continue
            if os.environ.get("KX") == "10":
                ba = self.bank()
                S.add("pe", lambda e, ba=ba: e.matmul(self.ps[:, ba, 0:128], self.cstb[:, 6, :], self.dbf[:, 0, t0, :],
                                                      start=True, stop=True),
                      reads=["cstb", ("QT", 0)], writes=[("ps", ba)])
                continue
            for c_ in range(n):
                t = t0 + c_
                ba = self.bank()
                for h in range(4 if os.environ.get("KX", "0") != "4" else 0):
                    hasf, hasb = c_ > 0, c_ < n - 1
                    if P.get("sample") and not hasf:
                        S.add("pe", lambda e, h=h, ba=ba: e.matmul(
                            self.ps[:, ba, h * 128:(h + 1) * 128], self.cstb[:, 6, :], self.sinb[:, h, :],
                            start=True, stop=False), reads=["sinb", "cstb"], writes=[("ps", ba)])
                        hasf = True
                    if P.get("sample") and not hasb:
                        S.add("pe", lambda e, h=h, t=t, ba=ba: e.matmul(
                            self.ps[:, ba, h * 128:(h + 1) * 128], self.cstb[:, 6, :], self.dbf[:, h, t - 1, :],
                            start=True, stop=False), reads=[("dbf", h), "cstb"], writes=[("ps", ba)])
                        S.add("pe", lambda e, h=h, ba=ba: e.matmul(
                            self.ps[:, ba, h * 128:(h + 1) * 128], self.cstb[:, 7, :], self.sinb[:, h, :],
                            start=False, stop=True), reads=["sinb", "cstb"], writes=[("ps", ba)])
                        continue
                    if P.get("sample") and c_ == 0:
                        S.add("pe", lambda e, h=h, t=t, ba=ba: e.matmul(
                            self.ps[:, ba, h * 128:(h + 1) * 128], self.cstb[:, 7, :], self.dbf[:, h, t + 1, :],
                            start=False, stop=True), reads=[("dbf", h), "cstb"], writes=[("ps", ba)])
                        continue
                    if hasf:
                        S.add("pe", lambda e, h=h, t=t, ba=ba, hasb=hasb: e.matmul(
                            self.ps[:, ba, h * 128:(h + 1) * 128], self.cstb[:, 6, :], self.dbf[:, h, t - 1, :],
                            start=True, stop=not hasb), reads=[("dbf", h), "cstb"], writes=[("ps", ba)])
                    if hasb:
                        S.add("pe", lambda e, h=h, t=t, ba=ba, hasf=hasf: e.matmul(
                            self.ps[:, ba, h * 128:(h + 1) * 128], self.cstb[:, 7, :], self.dbf[:, h, t + 1, :],
                            start=not hasf, stop=True), reads=[("dbf", h), "cstb"], writes=[("ps", ba)])
                kx = os.environ.get("KX", "0")
                if kx == "0":
                    S.add("act", lambda e, t=t, ba=ba: e.activation(
                        self.sstk[:, :, t, :], self.ps[:, ba, :].rearrange("p (h v) -> p h v", h=4), AF.Identity),
                        reads=[("ps", ba)], writes=[("KT", h_) for h_ in range(4)])
                elif kx == "2":
                    S.add("act", lambda e, t=t, ba=ba: e.activation(
                        self.Pm[:, :, t, :], self.ps[:, ba, :].rearrange("p (h v) -> p h v", h=4), AF.Identity),
                        reads=[("ps", ba)], writes=[("Pm", h_, t) for h_ in range(4)])
                elif kx == "3":
                    S.add("dve", lambda e, t=t, ba=ba: e.tensor_copy(
                        self.sstk[:, :, t, :], self.ps[:, ba, :].rearrange("p (h v) -> p h v", h=4)),
                        reads=[("ps", ba)], writes=[("KT", h_) for h_ in range(4)])
        if ksub < 5 or os.environ.get("KNOP2"):
            return
        if os.environ.get("KPAD"):
            S.add("dve", lambda e: e.memset(self.tsm[:, 0, :], 0.0), writes=[("tsm", 0)])
        if not os.environ.get("KNOBAR"):
            self.barrier()
        for t in range(int(os.environ.get("KP2T", NT))):
            b = self.bank()
            for h in range(4):
                S.add("pe", lambda e, h=h, t=t, b=b: e.matmul(
                    self.ps[:, b, h * 128:(h + 1) * 128], self.qd[:, h, t, :], self.sstk[:, h, t, :],
                    start=True, stop=False), reads=[("qd", h, t), ("KT", h), ("Pm", h, t), ("vtok", t)], writes=[("ps", b)])
                S.add("pe", lambda e, h=h, t=t, b=b: e.matmul(
                    self.ps[:, b, h * 128:(h + 1) * 128], self.Pm[:, h, t, :], self.vtok[:, t, h * 128:(h + 1) * 128],
                    start=False, stop=True), reads=[("Pm", h, t), ("vtok", t)], writes=[("ps", b)])
            kp2 = int(os.environ.get("KP2", "3"))
            if kp2 & 1:
              S.add("act", lambda e, t=t, b=b: e.activation(self.otokAB[:, t, :], self.ps[:, b, :], AF.Identity),
                  reads=[("ps", b)], writes=[("QT", t // 2) if not os.environ.get("KOT") else ("otokAB", t)])
            for h in range(4 if (kp2 & 2) else 0):
                if mix == 0:
                    S.add("dve", lambda e, h=h, b=b: e.bn_stats(self.st6[:, :], self.ps[:, b, h * 128:(h + 1) * 128]),
                          reads=[("ps", b)], writes=["st6"])
                    S.add("dve", lambda e, h=h, t=t: e.bn_aggr(self.stat[:, t, h, :], self.st6[:, :]),
                          reads=["st6"], writes=["stat"])
                else:
                    S.add("act", lambda e, h=h, t=t, b=b: e.activation(
                        self.tsm[:, 0, :], self.ps[:, b, h * 128:(h + 1) * 128], AF.Square,
                        accum_out=self.stat[:, t, h, 1:2]), reads=[("ps", b)], writes=["stat", ("tsm", 0)])
        if ksub < 6:
            return
        self.norm_out(P, l, mix, self.otokAB, lambda t: ("QT", t // 2))

    def norm_out(self, P, l, br, otok, okey):
        S = self.S
        NT = P["NT"]
        if br == 0:
            sc, bias = 1.0, EPS
        else:
            sc, bias = 1.0 / 128.0, EPS
        S.add("act", lambda e: e.activation(self.rstd[:, 0:NT, :], self.stat[:, 0:NT, :, 1], AF.Sqrt, bias=bias, scale=sc),
              reads=["stat"], writes=["rstd"])
        S.add("dve", lambda e: e.reciprocal(self.rstd[:, 0:NT, :], self.rstd[:, 0:NT, :]), reads=["rstd"], writes=["rstd"])
        for t in range(NT):
            i2 = t % 2
            if br == 1:
                S.add("pool", lambda e, t=t: e.tensor_tensor(
                    self.ggt[:].rearrange("p (h v) -> p h v", h=4), self.gtok[:, t, :].rearrange("p (h v) -> p h v", h=4),
                    self.bcast[:, l, 0:128].rearrange("p (o v) -> p o v", o=1).broadcast_to([128, 4, 128]), ALU.mult),
                    reads=[("gtok", t), "bcast"], writes=["ggt"])
            for h in range(4):
                hs = slice(h * 128, (h + 1) * 128)
                if br == 0:
                    S.add("dve", lambda e, t=t, hs=hs, h=h, i2=i2: e.scalar_tensor_tensor(
                        self.tsm[:, 1, :], otok[:, t, hs], self.stat[:, t, h, 0:1], self.gtok[:, t, hs],
                        ALU.subtract, ALU.mult), reads=[okey(t), "stat", ("gtok", t)], writes=[("tsm", 1)])
                    S.add("dve", lambda e, t=t, hs=hs, h=h, i2=i2: e.tensor_scalar(
                        self.ytok[:, i2, hs], self.tsm[:, 1, :], self.rstd[:, t, h:h + 1], None, ALU.mult),
                        reads=[("tsm", 1), "rstd"], writes=[("ytok", i2)])
                elif br == 1:
                    S.add("dve", lambda e, t=t, hs=hs, h=h, i2=i2: e.scalar_tensor_tensor(
                        self.ytok[:, i2, hs], otok[:, t, hs], self.rstd[:, t, h:h + 1], self.ggt[:, hs],
                        ALU.mult, ALU.mult), reads=[okey(t), "rstd", "ggt"], writes=[("ytok", i2)])
                else:
                    S.add("dve", lambda e, t=t, hs=hs, h=h, i2=i2: e.scalar_tensor_tensor(
                        self.ytok[:, i2, hs], otok[:, t, hs], self.rstd[:, t, h:h + 1], self.subg[:, :],
                        ALU.mult, ALU.mult), reads=[okey(t), "rstd", "subg"], writes=[("ytok", i2)])
            bt = self.bank()
            for h in range(4):
                S.add("pe", lambda e, h=h, bt=bt, i2=i2: e.transpose(
                    self.psb(bt)[:, h * 128:(h + 1) * 128], self.ytok[:, i2, h * 128:(h + 1) * 128], self.cstb[:, 4, :]),
                    reads=[("ytok", i2), "cstb"], writes=[("ps", bt)])
            S.add("act", lambda e, t=t, bt=bt: e.activation(
                self.yT[:, br, :, t * 128:(t + 1) * 128], self.psb(bt)[:, 0:512].rearrange("p (h c) -> p h c", h=4), AF.Identity),
                reads=[("ps", bt)], writes=[("yT", br, t // 4)])

    def mixer_C(self, P, l):
        S, d = self.S, self.d
        NT = P["NT"]
        lam_init = 0.8 - 0.6 * math.exp(-0.3 * l)
        S.add("dve", lambda e: e.tensor_scalar(self.subg[:, :], self.bcast[:, l, 128:256], 1.0 - lam_init, None, ALU.mult),
              reads=["bcast"], writes=["subg"])
        S.add("dve", lambda e: e.memset(self.vaug[:, :, :, 128:130], 1.0), writes=["vaug1"])
        wv, wk = self.wload(d[f"wT{l}"][4])

        def ev_v(t, b):
            S.add("act", lambda e: e.activation(self.vaug[:, t, :, 0:128], self.ps[:, b, :].rearrange("p (h v) -> p h v", h=4), AF.Identity),
                  reads=[("ps", b)], writes=[("vaug", t)])
            if P["kv_out"]:
                sl = t % 2
                S.add("dve", lambda e: e.tensor_copy(self.cstage[:, sl, :], self.ps[:, b, :]),
                      reads=[("ps", b)], writes=[("cstage", sl)])
                si, tt = t // 2, t % 2
                op = S.add("sp", lambda e: e.dma_start(
                    out=self.o["ndv"][si, l, :, tt * 128:(tt + 1) * 128, :].rearrange("h t v -> t h v"),
                    in_=self.cstage[:, sl, :].rearrange("p (h v) -> p h v", h=4)),
                    reads=[("cstage", sl)], slot=self.s_st[sl])
                self.out_ops.append(op)
        self.projT(P, wv, wk, ev_v)
        if P["kv_out"]:
            wv, wk = self.wload(d[f"wT{l}"][5])

            def ev_k(t, b):
                sl = t % 2
                S.add("dve", lambda e: e.tensor_copy(self.cstage[:, sl, :], self.ps[:, b, :]),
                      reads=[("ps", b)], writes=[("cstage", sl)])
                si, tt = t // 2, t % 2
                op = S.add("sp", lambda e: e.dma_start(
                    out=self.o["ndk"][si, l, :, tt * 128:(tt + 1) * 128, :].rearrange("g t k -> t g k"),
                    in_=self.cstage[:, sl, :].rearrange("p (g k) -> p g k", g=8)),
                    reads=[("cstage", sl)], slot=self.s_st[sl])
                self.out_ops.append(op)
            self.projT(P, wv, wk, ev_k)
        wv, wk = self.wload(d[f"wF{l}"][4])
        for h in range(4):
            self.projF(P, wv, wk, h, lambda g, b, h=h: self.evac_rope(P, self.QT, "QT", h, g, b, 1.0))
        wv, wk = self.wload(d[f"wF{l}"][5])
        for h in range(4):
            self.projF(P, wv, wk, h, lambda g, b, h=h: self.evac_rope(P, self.KT, "KT", h, g, b, 1.0))
        if P["rope"]:
            self.barrier()
        self.barrier()
        if P.get("sample"):
            self.attn_sample(P, l)
            self.barrier()
            self.norm_out(P, l, 2, self.otokC, lambda t: ("otokC", t))
            return
        self.bank_pool = [4, 5, 6, 7]
        for (q0, nq, keyt) in P["attn"]:
            qc = slice(q0 * 128, q0 * 128 + 256)
            for hc in range(4):
                for ji, kt in enumerate(keyt):
                    kc_ = slice(kt * 128, (kt + 1) * 128)
                    bs = [self.bank(), self.bank()]
                    for sub in range(2):
                        lo, hi = sub * 64, sub * 64 + 64
                        S.add("pe", lambda e, sub=sub, lo=lo, hi=hi, bs=bs, kc_=kc_, hc=hc: e.matmul(
                            self.ps[:, bs[sub], 0:256], self.KT[lo:hi, hc, kc_], self.QT[lo:hi, hc, qc],
                            start=True, stop=True), reads=[("KT", hc), ("QT", hc)], writes=[("ps", bs[sub])])
                    i2 = ji % 2
                    for sub in range(2):
                        S.add("act", lambda e, sub=sub, bs=bs, i2=i2: e.activation(
                            self.ET[:, i2, sub * 256:(sub + 1) * 256], self.ps[:, bs[sub], 0:256], AF.Exp, scale=0.125),
                            reads=[("ps", bs[sub])], writes=[("ET", i2, sub)])
                    for qt in range(2):
                        for sub in range(2):
                            acc = qt * 2 + sub
                            S.add("pe", lambda e, qt=qt, sub=sub, acc=acc, i2=i2, kt=kt, hc=hc, ji=ji: e.matmul(
                                self.ps[:, acc, 0:130], self.ET[:, i2, sub * 256 + qt * 128: sub * 256 + (qt + 1) * 128],
                                self.vaug[:, kt, hc, :], start=(ji == 0), stop=(ji == len(keyt) - 1)),
                                reads=[("ET", i2, sub), ("vaug", kt), "vaug1"], writes=[("ps", acc)])
                for qt in range(2):
                    t = q0 + qt
                    a1, a2 = qt * 2, qt * 2 + 1
                    S.add("dve", lambda e, a1=a1: e.reciprocal(self.sm8[:, 0:1], self.ps[:, a1, 128:129]),
                          reads=[("ps", a1)], writes=["sm8"])
                    S.add("dve", lambda e, a2=a2: e.reciprocal(self.sm8[:, 1:2], self.ps[:, a2, 128:129]),
                          reads=[("ps", a2)], writes=["sm8"])
                    S.add("dve", lambda e: e.tensor_tensor(self.sm8[:, 1:2], self.sm8[:, 1:2], self.lam[:, l:l + 1], ALU.mult),
                          reads=["sm8", "lam"], writes=["sm8"])
                    S.add("dve", lambda e, a2=a2: e.tensor_scalar(self.tsm[:, 0, :], self.ps[:, a2, 0:128], self.sm8[:, 1:2], None, ALU.mult),
                          reads=[("ps", a2), "sm8"], writes=[("tsm", 0)])
                    S.add("dve", lambda e, a1=a1, t=t, hc=hc: e.scalar_tensor_tensor(
                        self.otokC[:, t, hc * 128:(hc + 1) * 128], self.ps[:, a1, 0:128], self.sm8[:, 0:1], self.tsm[:, 0, :],
                        ALU.mult, ALU.add), reads=[("ps", a1), "sm8", ("tsm", 0)], writes=[("otokC", t)])
                    S.add("act", lambda e, t=t, hc=hc: e.activation(
                        self.tsm[:, 1, :], self.otokC[:, t, hc * 128:(hc + 1) * 128], AF.Square,
                        accum_out=self.stat[:, t, hc, 1:2]), reads=[("otokC", t)], writes=["stat", ("tsm", 1)])
        self.bank_pool = list(range(8))
        self.barrier()
        self.norm_out(P, l, 2, self.otokC, lambda t: ("otokC", t))

    def attn_sample(self, P, l):
        S, d, x_ = self.S, self.d, self.x_
        g1 = self.xgroup("kvw")
        S.add("sp", lambda e: e.dma_start(out=x_["kx_in"].rearrange("p (h t) -> p h t", h=4), in_=self.KT[:, :, :]),
              reads=[("KT", h) for h in range(4)], writes=["kx_in"], slot=g1)
        S.add("sp", lambda e: e.dma_start(out=x_["vx_in"].rearrange("p (t h v) -> p t h v", t=8, h=4), in_=self.vaug[:, :, :, 0:128]),
              reads=[("vaug", t) for t in range(8)], writes=["vx_in"], slot=g1)
        self.gather(x_["kx_in"], x_["kx_out"], "kx_in", "kx_out")
        self.gather(x_["vx_in"], x_["vx_out"], "vx_in", "vx_out")
        for kt in range(2):
            gk = self.xgroup("ck")
            S.add("sp", lambda e, kt=kt: e.dma_start(out=self.kstg[:, :], in_=d["ckc"][l, kt]), writes=["kstg"], slot=gk)
            b = self.bank()
            for hc in range(4):
                S.add("pe", lambda e, hc=hc, b=b: e.transpose(self.ps[:, b, hc * 128:(hc + 1) * 128],
                                                              self.kstg[:, hc * 128:(hc + 1) * 128], self.cst[:, 4, :]),
                      reads=["kstg", "cst"], writes=[("ps", b)])
            S.add("act", lambda e, kt=kt, b=b: e.activation(self.kctx[:, :, kt * 128:(kt + 1) * 128],
                                                            self.ps[:, b, :].rearrange("p (h c) -> p h c", h=4), AF.Identity),
                  reads=[("ps", b)], writes=["kctx"])
        gv = self.xgroup("cv")
        S.add("dve", lambda e: e.memset(self.vctx[:, :, :, 128:130], 1.0), writes=["vctx1"])
        for kt in range(2):
            S.add("pool", lambda e, kt=kt: e.dma_start(out=self.vctx[:, kt, :, 0:128], in_=d["cvc"][l, kt]),
                  writes=["vctx"], slot=gv)
        self.barrier()
        S.add("dve", lambda e: e.memset(self.VH[:, :, 128:130], 1.0), reads=["vx_in"], writes=["VH1"])
        self.bank_pool = [4, 5, 6, 7]
        for hc in range(4):
            gh = self.xgroup("kvh")
            S.add("sp", lambda e, hc=hc: e.dma_start(
                out=self.KH[:, :].rearrange("p (r c) -> p r c", r=4),
                in_=x_["kx_out"].rearrange("(r p) c -> p r c", p=128)[:, :, hc * 1024:(hc + 1) * 1024]),
                reads=["kx_out"], writes=["KH"], slot=gh)
            for r in range(4):
                S.add("sp", lambda e, hc=hc, r=r: e.dma_start(
                    out=self.VH[:, r * 8:(r + 1) * 8, 0:128],
                    in_=x_["vx_out"][r * 128:(r + 1) * 128, :].rearrange("p (t h v) -> p t h v", t=8, h=4)[:, :, hc, :]),
                    reads=["vx_out"], writes=["VH"], slot=gh)
            keyt = [("c", 0), ("c", 1)] + [("g", i) for i in range(32)]
            for qb in range(4):
                q0 = qb * 2
                qc = slice(q0 * 128, q0 * 128 + 256)
                for ji, (kind, kt) in enumerate(keyt):
                    kc_ = slice(kt * 128, (kt + 1) * 128)
                    bs = [self.bank(), self.bank()]
                    for sub in range(2):
                        lo, hi = sub * 64, sub * 64 + 64
                        ksrc = self.kctx[lo:hi, hc, kc_] if kind == "c" else self.KH[lo:hi, kc_]
                        S.add("pe", lambda e, sub=sub, lo=lo, hi=hi, bs=bs, ksrc=ksrc: e.matmul(
                            self.ps[:, bs[sub], 0:256], ksrc, self.QT[lo:hi, hc, qc], start=True, stop=True),
                            reads=["kctx" if kind == "c" else "KH", ("QT", hc)], writes=[("ps", bs[sub])])
                    i2 = ji % 2
                    for sub in range(2):
                        S.add("act", lambda e, sub=sub, bs=bs, i2=i2: e.activation(
                            self.ET[:, i2, sub * 256:(sub + 1) * 256], self.ps[:, bs[sub], 0:256], AF.Exp, scale=0.125),
                            reads=[("ps", bs[sub])], writes=[("ET", i2, sub)])
                    vsrc = self.vctx[:, kt, hc, :] if kind == "c" else self.VH[:, kt, :]
                    vkeys = ["vctx", "vctx1"] if kind == "c" else ["VH", "VH1"]
                    for qt in range(2):
                        for sub in range(2):
                            acc = qt * 2 + sub
                            S.add("pe", lambda e, qt=qt, sub=sub, acc=acc, i2=i2, vsrc=vsrc, ji=ji: e.matmul(
                                self.ps[:, acc, 0:130], self.ET[:, i2, sub * 256 + qt * 128: sub * 256 + (qt + 1) * 128],
                                vsrc, start=(ji == 0), stop=(ji == len(keyt) - 1)),
                                reads=[("ET", i2, sub)] + vkeys, writes=[("ps", acc)])
                self.attn_finalize(l, hc, q0)
        self.bank_pool = list(range(8))

    def attn_finalize(self, l, hc, q0):
        S = self.S
        for qt in range(2):
            t = q0 + qt
            a1, a2 = qt * 2, qt * 2 + 1
            S.add("dve", lambda e, a1=a1: e.reciprocal(self.sm8[:, 0:1], self.ps[:, a1, 128:129]),
                  reads=[("ps", a1)], writes=["sm8"])
            S.add("dve", lambda e, a2=a2: e.reciprocal(self.sm8[:, 1:2], self.ps[:, a2, 128:129]),
                  reads=[("ps", a2)], writes=["sm8"])
            S.add("dve", lambda e: e.tensor_tensor(self.sm8[:, 1:2], self.sm8[:, 1:2], self.lam[:, l:l + 1], ALU.mult),
                  reads=["sm8", "lam"], writes=["sm8"])
            S.add("dve", lambda e, a2=a2: e.tensor_scalar(self.tsm[:, 0, :], self.ps[:, a2, 0:128], self.sm8[:, 1:2], None, ALU.mult),
                  reads=[("ps", a2), "sm8"], writes=[("tsm", 0)])
            S.add("dve", lambda e, a1=a1, t=t, hc=hc: e.scalar_tensor_tensor(
                self.otokC[:, t, hc * 128:(hc + 1) * 128], self.ps[:, a1, 0:128], self.sm8[:, 0:1], self.tsm[:, 0, :],
                ALU.mult, ALU.add), reads=[("ps", a1), "sm8", ("tsm", 0)], writes=[("otokC", t)])
            S.add("act", lambda e, t=t, hc=hc: e.activation(
                self.tsm[:, 1, :], self.otokC[:, t, hc * 128:(hc + 1) * 128], AF.Square,
                accum_out=self.stat[:, t, hc, 1:2]), reads=[("otokC", t)], writes=["stat", ("tsm", 1)])

    def layer_norm(self, P, l, which, g, dT, sqT, tmpA, tmpB):
        S = self.S
        gc = slice(g * 512, (g + 1) * 512)
        xk = ("xT", g)
        for kc in range(8):
            S.add("act", lambda e, kc=kc: e.activation(sqT[:, kc, :], self.xT[:, kc, gc], AF.Identity), reads=[xk], writes=[("sqT", kc)])
        bm = self.bank()
        for kc in range(8):
            S.add("pe", lambda e, kc=kc: e.matmul(self.ps[:, bm, :], self.onesb[:, :], sqT[:, kc, :],
                                                  start=(kc == 0), stop=(kc == 7)),
                  reads=[("sqT", kc), "onesb"], writes=[("ps", bm)])
        for kc in range(8):
            S.add("dve", lambda e, kc=kc: e.tensor_tensor(dT[:, kc, :], self.xT[:, kc, gc], self.ps[:, bm, :], ALU.subtract),
                  reads=[xk, ("ps", bm)], writes=[("dT", kc)])
            S.add("act", lambda e, kc=kc: e.activation(sqT[:, kc, :], dT[:, kc, :], AF.Square),
                  reads=[("dT", kc)], writes=[("sqT", kc)])
        bv = self.bank()
        for kc in range(8):
            S.add("pe", lambda e, kc=kc: e.matmul(self.ps[:, bv, :], self.onesb[:, :], sqT[:, kc, :],
                                                  start=(kc == 0), stop=(kc == 7)),
                  reads=[("sqT", kc), "onesb"], writes=[("ps", bv)])
        S.add("act", lambda e: e.activation(tmpA[:], self.ps[:, bv, :], AF.Sqrt, bias=EPSA), reads=[("ps", bv)], writes=["tmpA"])
        S.add("dve", lambda e: e.reciprocal(tmpB[:], tmpA[:]), reads=["tmpA"], writes=["tmpB"])
        go = 48 + which * 8
        bo = 64 + which * 8
        for kc in range(8):
            S.add("dve", lambda e, kc=kc: e.tensor_tensor(dT[:, kc, :], dT[:, kc, :], tmpB[:], ALU.mult),
                  reads=[("dT", kc), "tmpB"], writes=[("dT", kc)])
            S.add("act", lambda e, kc=kc: e.activation(self.xT[:, kc, gc], dT[:, kc, :], AF.Identity,
                                                       scale=self.smallp[:, l, go + kc:go + kc + 1],
                                                       bias=self.smallp[:, l, bo + kc:bo + kc + 1]),
                  reads=[("dT", kc), "smallp"], writes=[xk])

    def merge_phase(self, P, l):
        S, d = self.S, self.d
        c = P["cond"]
        for oc in range(8):
            wg, kg = self.wload(d[f"wM{l}"][2 * oc], 3072)
            wb, kb = self.wload(d[f"wM{l}"][2 * oc + 1], 1536)
            for g in range(P["NG"]):
                gc = slice(g * 512, (g + 1) * 512)
                for br in range(3):
                    b = self.bank()
                    for kc in range(8):
                        S.add("pe", lambda e, b=b, kc=kc, br=br: e.matmul(
                            self.ps[:, b, :], wg[:, br * 1024 + kc * 128: br * 1024 + (kc + 1) * 128], self.hT[:, kc, gc],
                            start=(kc == 0), stop=(kc == 7)), reads=[kg, ("hT", g)], writes=[("ps", b)])
                    S.add("act", lambda e, b=b, br=br: e.activation(self.sig[:, br, :], self.ps[:, b, :], AF.Tanh, scale=0.5),
                          reads=[("ps", b)], writes=[("sig", br)])
                for br in range(3):
                    b = self.bank()
                    for kc in range(4):
                        S.add("pe", lambda e, b=b, kc=kc, br=br: e.matmul(
                            self.ps[:, b, :], wb[:, br * 512 + kc * 128: br * 512 + (kc + 1) * 128], self.yT[:, br, kc, gc],
                            start=(kc == 0), stop=(kc == 3)), reads=[kb, ("yT", br, g)], writes=[("ps", b)])
                    S.add("dve", lambda e, b=b, br=br: e.scalar_tensor_tensor(
                        self.tm3[:, br, :], self.sig[:, br, :], 1.0, self.ps[:, b, :], ALU.add, ALU.mult),
                        reads=[("sig", br), ("ps", b)], writes=[("tm3", br)])
                S.add("pool", lambda e: e.tensor_tensor(self.tm3[:, 0, :], self.tm3[:, 0, :], self.tm3[:, 1, :], ALU.add),
                      reads=[("tm3", 0), ("tm3", 1)], writes=[("tm3", 0)])
                S.add("dve", lambda e, oc=oc, gc=gc: e.tensor_tensor(self.mergedT[:, oc, gc], self.tm3[:, 0, :], self.tm3[:, 2, :], ALU.add),
                      reads=[("tm3", 0), ("tm3", 2)], writes=[("mergedT", g)])
        wo = [self.wload(d[f"wO{l}"][i]) for i in range(2)]
        for g in range(P["NG"]):
            gc = slice(g * 512, (g + 1) * 512)
            for oc in range(8):
                wv, wk = wo[oc // 4]
                ci = oc % 4
                b = self.bank()
                for kc in range(8):
                    S.add("pe", lambda e, b=b, kc=kc, wv=wv, ci=ci: e.matmul(
                        self.ps[:, b, :], wv[:, ci * 1024 + kc * 128: ci * 1024 + (kc + 1) * 128], self.mergedT[:, kc, gc],
                        start=(kc == 0), stop=(kc == 7)), reads=[wk, ("mergedT", g)], writes=[("ps", b)])
                S.add("dve", lambda e, b=b, oc=oc: e.scalar_tensor_tensor(
                    self.xT[:, oc, gc], self.ps[:, b, :], self.modT[:, l, 16 + oc, c:c + 1], self.xT[:, oc, gc],
                    ALU.mult, ALU.add), reads=[("ps", b), "modT", ("xT", g)], writes=[("xT", g)])
            self.layer_norm(P, l, 0, g, self.dT, self.sqT, self.tmpA, self.tmpB)

    def ffn(self, P, l):
        S, d = self.S, self.d
        c = P["cond"]
        T = P["T"]
        S.add("dve", lambda e: e.memset(self.aT[:, :, :], 0.0), writes=[("aT", 0), ("aT", 1)])
        wv = wk = None
        sample = bool(P.get("sample"))
        if sample:
            x_ = self.x_
            S.add("dve", lambda e: e.tensor_copy(self.hxs[:, 0:8].rearrange("p (k o) -> p k o", o=1), self.hT[:, :, 0:1]),
                  reads=[("hT", 0)], writes=["hxs"])
            S.add("dve", lambda e: e.tensor_copy(self.hxs[:, 8:16].rearrange("p (k o) -> p k o", o=1), self.hT[:, :, 1023:1024]),
                  reads=[("hT", 1)], writes=["hxs"])
            g1 = self.xgroup("hxw")
            S.add("sp", lambda e: e.dma_start(out=x_["hx_in"], in_=self.hxs[:, :]), reads=["hxs"], writes=["hx_in"], slot=g1)
            self.gather(x_["hx_in"], x_["hx_out"], "hx_in", "hx_out")
            g2 = self.xgroup("hxl")
            S.add("sp", lambda e: e.dma_start(out=self.hG[:, :, :], in_=x_["hx_out"].rearrange("(r p) c -> p r c", p=128)),
                  reads=["hx_out"], writes=["hG"], slot=g2)
            S.add("dve", lambda e: e.memset(self.hhf[:, :, :], 0.0), writes=["hhf"])
            for j in range(4):
                S.add("dve", lambda e, j=j: e.scalar_tensor_tensor(
                    self.hhf[:, :, 0], self.hG[:, j, 8:16], self.meta[:, 6 + j:7 + j], self.hhf[:, :, 0], ALU.mult, ALU.add),
                    reads=["hG", "meta", "hhf"], writes=["hhf"])
                S.add("dve", lambda e, j=j: e.scalar_tensor_tensor(
                    self.hhf[:, :, 1], self.hG[:, j, 0:8], self.meta[:, 10 + j:11 + j], self.hhf[:, :, 1], ALU.mult, ALU.add),
                    reads=["hG", "meta", "hhf"], writes=["hhf"])
            S.add("dve", lambda e: e.tensor_copy(self.hh[:, :, :], self.hhf[:, :, :]), reads=["hhf"], writes=["hh"])
        for fc in range(NFC):
            if fc % 2 == 0:
                wv, wk = self.wload(d[f"wU{l}"][fc // 2])
            ca, cb = (fc % 2) * 2, (fc % 2) * 2 + 1
            a2 = fc % 2
            bbs = {}
            for g in range(P["NG"]):
                gc = slice(g * 512, (g + 1) * 512)
                ba = self.bank()
                for kc in range(8):
                    S.add("pe", lambda e, b=ba, kc=kc, ci=ca, wv=wv, gc=gc: e.matmul(
                        self.ps[:, b, :], wv[:, ci * 1024 + kc * 128: ci * 1024 + (kc + 1) * 128], self.hT[:, kc, gc],
                        start=(kc == 0), stop=(kc == 7)), reads=[wk, ("hT", g)], writes=[("ps", ba)])
                for (s0, n, col0) in P["conv_rows"][g]:
                    S.add("act", lambda e, s0=s0, n=n, col0=col0, ba=ba: e.activation(
                        self.aT[:, a2, col0:col0 + n], self.ps[:, ba, s0:s0 + n], AF.Identity),
                        reads=[("ps", ba)], writes=[("aT", a2)])
            if sample:
                bh = self.bank()
                for kc in range(8):
                    S.add("pe", lambda e, kc=kc, wv=wv, bh=bh: e.matmul(
                        self.ps[:, bh, 0:2], wv[:, ca * 1024 + kc * 128: ca * 1024 + (kc + 1) * 128], self.hh[:, kc, :],
                        start=(kc == 0), stop=(kc == 7)), reads=[wk, "hh"], writes=[("ps", bh)])
                S.add("dve", lambda e, bh=bh: e.tensor_scalar(self.aT[:, a2, 0:1], self.ps[:, bh, 0:1], self.meta[:, 4:5], None, ALU.mult),
                      reads=[("ps", bh), "meta"], writes=[("aT", a2)])
                S.add("dve", lambda e, bh=bh: e.tensor_scalar(self.aT[:, a2, 1025:1026], self.ps[:, bh, 1:2], self.meta[:, 5:6], None, ALU.mult),
                      reads=[("ps", bh), "meta"], writes=[("aT", a2)])
            for g in range(P["NG"]):
                gc = slice(g * 512, (g + 1) * 512)
                bb = self.bank()
                for kc in range(8):
                    S.add("pe", lambda e, b=bb, kc=kc, ci=cb, wv=wv, gc=gc: e.matmul(
                        self.ps[:, b, :], wv[:, ci * 1024 + kc * 128: ci * 1024 + (kc + 1) * 128], self.hT[:, kc, gc],
                        start=(kc == 0), stop=(kc == 7)), reads=[wk, ("hT", g)], writes=[("ps", bb)])
                cw = 80
                for (s0, n, col0) in P["conv_rows"][g]:
                    S.add("dve", lambda e, s0=s0, n=n, col0=col0: e.tensor_scalar(
                        self.cT[:, a2, s0:s0 + n], self.aT[:, a2, col0 - 1:col0 - 1 + n],
                        self.smallp[:, l, cw + fc:cw + fc + 1], self.smallp[:, l, cw + 66 + fc:cw + 66 + fc + 1],
                        ALU.mult, ALU.add), reads=[("aT", a2), "smallp"], writes=[("cT", a2)])
                    for k in (1, 2):
                        S.add("dve", lambda e, s0=s0, n=n, col0=col0, k=k: e.scalar_tensor_tensor(
                            self.cT[:, a2, s0:s0 + n], self.aT[:, a2, col0 - 1 + k:col0 - 1 + k + n],
                            self.smallp[:, l, cw + k * NFC + fc:cw + k * NFC + fc + 1], self.cT[:, a2, s0:s0 + n],
                            ALU.mult, ALU.add), reads=[("aT", a2), "smallp", ("cT", a2)], writes=[("cT", a2)])
                S.add("act", lambda e: e.activation(self.glT[:, a2, :], self.cT[:, a2, :], AF.Gelu_apprx_tanh),
                      reads=[("cT", a2)], writes=[("glT", a2)])
                S.add("dve", lambda e, bb=bb, gc=gc: e.tensor_tensor(self.gT[:, fc, gc], self.glT[:, a2, :], self.ps[:, bb, :], ALU.mult),
                      reads=[("glT", a2), ("ps", bb)], writes=[("gT", g)])
        for oc in range(8):
            wv, wk = self.wload(d[f"wD{l}"][oc], NFC * 128)
            for g in range(P["NG"]):
                gc = slice(g * 512, (g + 1) * 512)
                b = self.bank()
                for fc in range(NFC):
                    S.add("pe", lambda e, b=b, fc=fc, wv=wv: e.matmul(
                        self.ps[:, b, :], wv[:, fc * 128:(fc + 1) * 128], self.gT[:, fc, gc],
                        start=(fc == 0), stop=(fc == NFC - 1)), reads=[wk, ("gT", g)], writes=[("ps", b)])
                S.add("dve", lambda e, b=b, oc=oc, gc=gc: e.scalar_tensor_tensor(
                    self.xT[:, oc, gc], self.ps[:, b, :], self.modT[:, l, 40 + oc, c:c + 1], self.xT[:, oc, gc],
                    ALU.mult, ALU.add), reads=[("ps", b), "modT", ("xT", g)], writes=[("xT", g)])
        for g in range(P["NG"]):
            self.layer_norm(P, l, 1, g, self.dT2, self.sqT2, self.tmpA2, self.tmpB2)

    def store_x(self, P):
        S = self.S
        dst = self.o[P["y"]]
        for t in range(P["NT"]):
            sl = t % 2
            for half in range(2):
                b = self.bank()
                for q in range(4):
                    kc = half * 4 + q
                    S.add("pe", lambda e, b=b, q=q, kc=kc, t=t: e.transpose(
                        self.ps[:, b, q * 128:(q + 1) * 128], self.xT[:, kc, t * 128:(t + 1) * 128], self.cst[:, 4, :]),
                        reads=[("xT", t // 4), "cst"], writes=[("ps", b)])
                S.add("act" if half == 0 else "dve",
                      (lambda e, b=b, half=half, sl=sl: e.activation(self.stage[:, sl, half * 512:(half + 1) * 512], self.ps[:, b, :], AF.Identity))
                      if half == 0 else
                      (lambda e, b=b, half=half, sl=sl: e.tensor_copy(self.stage[:, sl, half * 512:(half + 1) * 512], self.ps[:, b, :])),
                      reads=[("ps", b)], writes=[("stage", sl)])
            op = S.add("sp", lambda e, t=t, sl=sl: e.dma_start(out=dst[t * 128:(t + 1) * 128, :], in_=self.stage[:, sl, :]),
                       reads=[("stage", sl)], slot=self.s_io[sl])
            self.out_ops.append(op)

    def run_pass(self, P):
        stop = int(os.environ.get("KSTOP", "99"))
        if stop < 1:
            return
        self.load_x(P)
        self.barrier()
        for l in range(L):
            if stop < 2 + 10 * l:
                break
            self.modulate(P, l, 0)
            self.mixer_AB(P, l, 0)
            self.barrier()
            if stop < 3 + 10 * l:
                break
            self.mixer_AB(P, l, 1)
            self.barrier()
            if stop < 4 + 10 * l:
                break
            self.mixer_C(P, l)
            self.barrier()
            if stop < 5 + 10 * l:
                break
            self.merge_phase(P, l)
            self.modulate(P, l, 1)
            self.barrier()
            if stop < 6 + 10 * l:
                break
            self.ffn(P, l)
            self.barrier()
        self.store_x(P)
        self.barrier()

    def build(self):
        self.declare()
        self.init_phase()
        PP = dict(T=512, NT=4, NG=1, cond=0, x="xp", y="yp", rope=False, states_out=True, kv_out=True,
                  seqs=[(0, 2), (2, 2)],
                  attn=[(0, 2, [0, 1]), (2, 2, [2, 3])],
                  conv_rows=[[(0, 256, 1), (256, 256, 515)]])
        if not os.environ.get("KNOPROMPT"):
            self.run_pass(PP)
        if self.do_sample:
            PS = dict(T=1024, NT=8, NG=2, cond=1, x="xs", y="ys", rope=True, states_out=False, kv_out=False,
                      seqs=[(0, 8)], sample=True, attn=[],
                      conv_rows=[[(0, 512, 1)], [(0, 512, 513)]])
            self.run_pass(PS)
        self.S.wait_for("sp", self.out_ops)
        global LAST_S
        LAST_S = self.S
        self.S.emit(self.nc, self.st)


_CACHE = {}


def _build_nc(do_sample=True):
    key = ("nc", do_sample)
    if key not in _CACHE:
        nc = bass.Bass("TRN2", target_bir_lowering=False)
        with contextlib.ExitStack() as st:
            b = Builder(nc, st, do_sample=do_sample)
            b.build()
        _CACHE[key] = nc
    return _CACHE[key]


def kernel(**inp):
    inp = {k: np.asarray(v) for k, v in inp.items()}
    W = _host_weights(inp)
    nc = _build_nc(not os.environ.get("KNOSAMPLE"))
    in_maps = []
    ncores = int(os.environ.get("KCORES", str(NCORES)))
    for c in range(ncores):
        b, r = c // 4, c % 4
        m = dict(W)
        m["xp"] = np.ascontiguousarray(inp["x_prompt"][2 * c:2 * c + 2].reshape(512, D))
        m["xs"] = np.ascontiguousarray(inp["x_sample"][b, r * 1024:(r + 1) * 1024])
        cond = np.stack([inp["c_ctx"], inp["c"][b]], axis=-1)
        m["cond"] = np.ascontiguousarray(cond.reshape(8, 128, 2).transpose(1, 0, 2), dtype=np.float32)
        rc, rs = _rope_tables(r * 1024)
        m["ropec"], m["ropes"] = rc, rs
        meta = np.zeros((128, 16), np.float32)
        for s_ in range(4):
            meta[0:64, s_] = 1.0 if r == s_ else 0.0
            meta[64:128, s_] = 1.0 if r == 3 - s_ else 0.0
            meta[:, 6 + s_] = 1.0 if s_ == r - 1 else 0.0
            meta[:, 10 + s_] = 1.0 if s_ == r + 1 else 0.0
        meta[:, 4] = 1.0 if r > 0 else 0.0
        meta[:, 5] = 1.0 if r < 3 else 0.0
        m["meta"] = meta
        for nm, src in (("sret", inp["state_ret"]), ("sgla", inp["state_gla"])):
            a = src[b]
            m[nm] = np.ascontiguousarray(a.transpose(0, 1, 3, 2, 4).reshape(L, 128, 4, 128), dtype=np.float32)
        ck = inp["cache_diff_k"][b]
        m["ckc"] = np.ascontiguousarray(ck.reshape(L, 8, 2, 128, 64).transpose(0, 2, 3, 1, 4).reshape(L, 2, 128, 512), dtype=np.float32)
        cv = inp["cache_diff_v"][b]
        m["cvc"] = np.ascontiguousarray(cv.reshape(L, 4, 2, 128, 128).transpose(0, 2, 3, 1, 4), dtype=np.float32)
        in_maps.append(m)
    res = run_bass_kernel_spmd(nc, in_maps, core_ids=list(range(ncores)))
    R = list(res.results)
    while len(R) < NCORES:
        R.append(R[0])
    y_prompt = np.concatenate([R[c]["yp"].reshape(2, 256, D) for c in range(NCORES)], axis=0)
    y_sample = np.stack([np.concatenate([R[b * 4 + r]["ys"] for r in range(4)], axis=0) for b in range(2)], axis=0)
    ndk = np.concatenate([R[c]["ndk"] for c in range(NCORES)], axis=0)
    ndv = np.concatenate([R[c]["ndv"] for c in range(NCORES)], axis=0)
    nsr = np.concatenate([R[c]["nsr"] for c in range(NCORES)], axis=0)
    nsg = np.concatenate([R[c]["nsg"] for c in range(NCORES)], axis=0)
    f = np.float32
    return (y_prompt.astype(f), y_sample.astype(f), ndk.astype(f), ndv.astype(f), nsr.astype(f), nsg.astype(f))
```

```python
import contextlib
import os
import math
import numpy as np
import concourse.bass as bass
import concourse.mybir as mybir
from concourse.bass_utils import run_bass_kernel_spmd

F32 = mybir.dt.float32
BF16 = mybir.dt.bfloat16
AF = mybir.ActivationFunctionType
ALU = mybir.AluOpType

NCORES = 8
D = 1024
L = 2
DIN = 7680
DFF = 2816
NFC = 22
ALPHA = (2 * L) ** 0.25
EPS = 1e-5
EPSA = EPS / (ALPHA * ALPHA)
WSZ = 4096
NSLOT = 3
UW = 21504


class _Op:
    __slots__ = ("eng", "fn", "deps", "slot", "cnt", "need_inc", "inc")

    def __init__(self, eng, fn, slot, inc):
        self.eng = eng
        self.fn = fn
        self.deps = set()
        self.slot = slot
        self.cnt = 0
        self.need_inc = False
        self.inc = inc


class _Slot:
    def __init__(self, name, wait_all):
        self.name = name
        self.wait_all = wait_all
        self.total = 0
        self.sem = None


DEBUG_WAITS = None
LAST_S = None


class _Rec:
    def __init__(self):
        self.call = None

    def __getattr__(self, name):
        def f(*a, **k):
            self.call = (name, a, k)
            return None
        return f


def _freeze(fn):
    r = _Rec()
    fn(r)
    call = r.call
    if call is None:
        return lambda e: None
    name, a, k = call
    return lambda e: getattr(e, name)(*a, **k)


class Sched:
    ENGS = ("pe", "act", "dve", "pool", "sp")

    def __init__(self):
        self.ops = {e: [] for e in self.ENGS}
        self.lastw = {}
        self.readers = {}
        self.slots = []

    def slot(self, name, wait_all=False):
        s = _Slot(name, wait_all)
        self.slots.append(s)
        return s

    def add(self, eng, fn, reads=(), writes=(), slot=None, inc=16):
        op = _Op(eng, _freeze(fn), slot, inc)
        deps = op.deps
        for r in reads:
            w = self.lastw.get(r)
            if w is not None:
                deps.add(w)
            if isinstance(r, tuple) and r[0] == "ps":
                for rd in self.readers.get(r, ()):
                    if rd.eng != eng:
                        deps.add(rd)
        for k in writes:
            w = self.lastw.get(k)
            if w is not None:
                deps.add(w)
            for rd in self.readers.get(k, ()):
                deps.add(rd)
        for r in reads:
            self.readers.setdefault(r, []).append(op)
        for k in writes:
            self.lastw[k] = op
            self.readers[k] = []
        if slot is not None:
            slot.total += inc
            op.cnt = slot.total
        self.ops[eng].append(op)
        return op

    def wait_for(self, eng, ops):
        op = _Op(eng, lambda e: None, None, 0)
        op.deps = set(o for o in ops if o is not None)
        self.ops[eng].append(op)
        return op

    def emit(self, nc, stack):
        for e in self.ENGS:
            for op in self.ops[e]:
                for d in op.deps:
                    if d.slot is None:
                        if d.eng == op.eng and op.slot is None and d.eng == "pe":
                            continue
                        d.need_inc = True
        esem = {}
        for e in self.ENGS:
            c = 0
            for op in self.ops[e]:
                if op.slot is None and op.need_inc:
                    c += 1
                    op.cnt = c
            if c > 0:
                esem[e] = stack.enter_context(nc.semaphore("es_" + e))
        for s in self.slots:
            if s.total > 0:
                s.sem = stack.enter_context(nc.semaphore("sl_" + s.name))
        block = stack.enter_context(nc.Block())

        def body(e):
            def run(eng):
                seen = {}
                for op in self.ops[e]:
                    waits = {}
                    for d in op.deps:
                        if d.slot is not None:
                            if d.slot is op.slot and d.slot.wait_all:
                                continue
                            key = d.slot
                            val = d.slot.total if d.slot.wait_all else d.cnt
                            sem = d.slot.sem
                        else:
                            if d.eng == e and op.slot is None and e == "pe":
                                continue
                            key = d.eng
                            val = d.cnt
                            sem = esem[d.eng]
                        if val > waits.get(key, (None, 0))[1]:
                            waits[key] = (sem, val)
                    for key in sorted(waits.keys(), key=lambda k: k if isinstance(k, str) else "~" + k.name):
                        sem, val = waits[key]
                        if seen.get(key, 0) >= val:
                            continue
                        seen[key] = val
                        eng.wait_ge(sem, val)
                        if DEBUG_WAITS is not None:
                            DEBUG_WAITS.append((e, len(DEBUG_WAITS), key if isinstance(key, str) else key.name, val))
                    inst = op.fn(eng)
                    if op.slot is not None:
                        inst.then_inc(op.slot.sem, op.inc)
                    elif op.need_inc:
                        inst.then_inc(esem[e], 1)
            return run

        block.tensor(body("pe"))
        block.scalar(body("act"))
        block.vector(body("dve"))
        block.gpsimd(body("pool"))
        block.sync(body("sp"))


PTS = np.cumsum([0, 256, 256, 512, 512, 256, 256, 512, 512, 512, 512, 512, 1024, 1024, 1024])
(O_AQ, O_AK, O_AV, O_AG, O_BQ, O_BK, O_BV, O_BR, O_CQ, O_CK, O_CV, O_MA, O_MB, O_MC) = [int(v) for v in PTS[:-1]]


def _fchunk(w, cols):
    sub = w[:, cols]
    return sub.reshape(8, 128, sub.shape[1]).transpose(1, 0, 2)


def _pack(chunks, per):
    out = []
    for i in range(0, len(chunks), per):
        blk = np.concatenate([c.reshape(128, -1) for c in chunks[i:i + per]], axis=1)
        assert blk.shape[1] <= WSZ
        if blk.shape[1] < WSZ:
            blk = np.concatenate([blk, np.zeros((128, WSZ - blk.shape[1]), np.float32)], axis=1)
        out.append(blk)
    return np.ascontiguousarray(np.stack(out, 0), dtype=np.float32)


def _host_weights(inp):
    W = {}
    ar = np.arange
    for l in range(L):
        w_in = inp["w_in"][l]
        ch = []
        for (oq, ok) in ((O_AQ, O_AK), (O_BQ, O_BK)):
            for o in (oq, ok):
                for h in range(4):
                    cols = np.concatenate([o + h * 64 + ar(64), o + h * 64 + ar(64)])
                    ch.append(_fchunk(w_in, cols))
        for o in (O_CQ, O_CK):
            for h in range(4):
                ch.append(_fchunk(w_in, o + h * 128 + ar(128)))
        W[f"wF{l}"] = _pack(ch, 4)
        ch = [_fchunk(w_in, o + ar(512)) for o in (O_AV, O_AG, O_BV, O_BR, O_CV, O_CK)]
        W[f"wT{l}"] = _pack(ch, 1)
        ch = []
        wb = inp["w_branch"][l]
        for oc in range(8):
            g = [_fchunk(w_in, o + oc * 128 + ar(128)).reshape(128, -1) for o in (O_MA, O_MB, O_MC)]
            ch.append(np.concatenate(g, axis=1))
            b = [wb[br][:, oc * 128:(oc + 1) * 128].reshape(4, 128, 128).transpose(1, 0, 2).reshape(128, -1)
                 for br in range(3)]
            ch.append(np.concatenate(b, axis=1))
        W[f"wM{l}"] = _pack(ch, 1)
        wo = inp["w_out"][l]
        W[f"wO{l}"] = _pack([_fchunk(wo, oc * 128 + ar(128)) for oc in range(8)], 4)
        wu = inp["ffn_w_up"][l]
        ch = []
        for fc in range(NFC):
            ch.append(_fchunk(wu, fc * 128 + ar(128)))
            ch.append(_fchunk(wu, DFF + fc * 128 + ar(128)))
        W[f"wU{l}"] = _pack(ch, 4)
        wd = inp["ffn_w_down"][l]
        ch = [wd[:, oc * 128:(oc + 1) * 128].reshape(NFC, 128, 128).transpose(1, 0, 2) for oc in range(8)]
        W[f"wD{l}"] = _pack(ch, 1)
        aw = inp["ada_w"][l]
        W[f"wA{l}"] = _pack([_fchunk(aw, j * 512 + ar(512)) for j in range(12)], 1)
    wa1 = np.zeros((128, L, 8, 32), np.float32)
    wa2 = np.zeros((33, L, 512), np.float32)
    for l in range(L):
        for e in range(2):
            wa1[:, l, :, e * 16:(e + 1) * 16] = inp["gla_wa1"][l, e].reshape(8, 128, 16).transpose(1, 0, 2)
            for h in range(4):
                c0 = h * 128 + e * 64
                wa2[e * 16:(e + 1) * 16, l, c0:c0 + 64] = inp["gla_wa2"][l, e][:, h * 64:(h + 1) * 64]
                wa2[32, l, c0:c0 + 64] = inp["gla_ba"][l, e][h * 64:(h + 1) * 64]
    W["wa1"] = wa1
    W["wa2"] = wa2
    sp = []
    for l in range(L):
        cols = [inp["ada_b"][l].reshape(48, 128).T,
                inp["ln_g"][l].reshape(16, 128).T,
                inp["ln_b"][l].reshape(16, 128).T,
                inp["ffn_conv_w"][l].reshape(3 * NFC, 128).T,
                inp["ffn_conv_b"][l].reshape(NFC, 128).T]
        sp.append(np.concatenate(cols, axis=1))
    W["smallp"] = np.ascontiguousarray(np.stack(sp, 1), dtype=np.float32)
    bc = []
    for l in range(L):
        row = np.concatenate([inp["gla_norm_g"][l], inp["diff_subln_g"][l],
                              inp["diff_lam"][l].reshape(-1), inp["ret_decay"][l].reshape(-1)])
        bc.append(np.broadcast_to(row[None, :], (128, row.shape[0])))
    W["bcast"] = np.ascontiguousarray(np.stack(bc, 1), dtype=np.float32)
    j = ar(128)[:, None]
    i = ar(128)[None, :]
    cs = np.zeros((128, 9, 128), np.float32)
    cs[:, 8] = 1.0
    cs[:, 0] = (j <= i)
    cs[:, 1] = (j >= i)
    cs[:, 2] = (j > i)
    cs[:, 3] = (j < i)
    cs[:, 4] = (j == i)
    m = ar(128)
    sw = np.where((m % 32) < 16, m + 16, m - 16)
    cs[sw, 5, m] = 1.0
    cs[:, 6] = (j == i) & (j < 64)
    cs[:, 7] = (j == i) & (j >= 64)
    W["consts"] = cs
    return W


def _rope_tables(t0):
    t = t0 + np.arange(1024)
    row = (t // 64).astype(np.float32)
    col = (t % 64).astype(np.float32)
    inv = (np.float32(10000.0) ** (-np.arange(16, dtype=np.float32) / np.float32(16))).astype(np.float32)
    cos = np.zeros((128, 1024), np.float32)
    sin = np.zeros((128, 1024), np.float32)
    for p in range(128):
        d = p % 64
        pos = row if d < 32 else col
        ang = (pos * inv[d % 16]).astype(np.float32)
        cos[p] = np.cos(ang)
        s = np.sin(ang)
        sin[p] = -s if (d % 32) < 16 else s
    return cos, sin


class Builder:
    def __init__(self, nc, st, do_sample=True):
        self.nc = nc
        self.st = st
        self.S = Sched()
        self.do_sample = do_sample
        self.wslot_i = 0
        self.out_ops = []
        self.out_seen = 0

    def dram_in(self, name, shape, dt=F32):
        return self.nc.dram_tensor(name, list(shape), dt, kind="ExternalInput").ap()

    def dram_out(self, name, shape, dt=F32):
        return self.nc.dram_tensor(name, list(shape), dt, kind="ExternalOutput").ap()

    def sb(self, name, shape, dt):
        return self.st.enter_context(self.nc.sbuf_tensor(name, list(shape), dt))

    def wload(self, src_blk, nelem=WSZ):
        s = self.wslot_i % NSLOT
        self.wslot_i += 1
        dst = self.wring[:, s, 0:nelem]
        key = ("w", s)
        self.S.add("pool", lambda e: e.dma_start(out=dst, in_=src_blk[:, 0:nelem]), writes=[key], slot=self.wsl[s])
        return self.wring[:, s, :], key

    def xgroup(self, name):
        self.xs_i += 1
        return self.S.slot("x%d_%s" % (self.xs_i, name), wait_all=True)

    def gather(self, src, dst, rk, wk):
        sl = self.xgroup("cc")
        groups = [[0, 1, 2, 3], [4, 5, 6, 7]] if not os.environ.get("KG4") else [[0, 1, 2, 3]]
        self.S.add("pool", lambda e: e.collective_compute("AllGather", ALU.bypass, replica_groups=groups,
                                                          ins=[src], outs=[dst]),
                   reads=[rk], writes=[wk], slot=sl, inc=1)

    def declare(self):
        nc = self.nc
        di = self.dram_in
        self.d = d = {}
        d["xp"] = di("xp", [512, D])
        d["xs"] = di("xs", [1024, D])
        d["cond"] = di("cond", [128, 8, 2])
        for l in range(L):
            d[f"wF{l}"] = di(f"wF{l}", [6, 128, WSZ])
            d[f"wT{l}"] = di(f"wT{l}", [6, 128, WSZ])
            d[f"wM{l}"] = di(f"wM{l}", [16, 128, WSZ])
            d[f"wO{l}"] = di(f"wO{l}", [2, 128, WSZ])
            d[f"wU{l}"] = di(f"wU{l}", [11, 128, WSZ])
            d[f"wD{l}"] = di(f"wD{l}", [8, 128, WSZ])
            d[f"wA{l}"] = di(f"wA{l}", [12, 128, WSZ])
        d["wa1"] = di("wa1", [128, L, 8, 32])
        d["wa2"] = di("wa2", [33, L, 512])
        d["smallp"] = di("smallp", [128, L, 168])
        d["bcast"] = di("bcast", [128, L, 520])
        d["consts"] = di("consts", [128, 9, 128])
        d["ropec"] = di("ropec", [128, 1024])
        d["ropes"] = di("ropes", [128, 1024])
        d["meta"] = di("meta", [128, 16])
        d["sret"] = di("sret", [L, 128, 4, 128])
        d["sgla"] = di("sgla", [L, 128, 4, 128])
        d["ckc"] = di("ckc", [L, 2, 128, 512])
        d["cvc"] = di("cvc", [L, 2, 128, 4, 128])
        dint = lambda name, shape, dt: nc.dram_tensor(name, list(shape), dt, addr_space="Local", kind="Internal").ap()
        self.x_ = x_ = {}
        x_["kx_in"] = dint("kx_in", [128, 4096], BF16)
        x_["kx_out"] = dint("kx_out", [512, 4096], BF16)
        x_["vx_in"] = dint("vx_in", [128, 4096], BF16)
        x_["vx_out"] = dint("vx_out", [512, 4096], BF16)
        x_["st_in"] = dint("st_in", [128, 516], F32)
        x_["st_out"] = dint("st_out", [512, 516], F32)
        x_["hx_in"] = dint("hx_in", [128, 16], BF16)
        x_["hx_out"] = dint("hx_out", [512, 16], BF16)
        do = self.dram_out
        self.o = o = {}
        o["yp"] = do("yp", [512, D])
        o["ys"] = do("ys", [1024, D])
        o["ndk"] = do("ndk", [2, L, 8, 256, 64])
        o["ndv"] = do("ndv", [2, L, 4, 256, 128])
        o["nsr"] = do("nsr", [2, L, 2, 4, 64, 128])
        o["nsg"] = do("nsg", [2, L, 2, 4, 64, 128])


        sb = self.sb
        self.xT = sb("xT", [128, 8, 1024], F32)
        self.hT = sb("hT", [128, 8, 1024], BF16)
        self.wring = sb("wring", [128, NSLOT, WSZ], BF16)
        self.wsl = [self.S.slot(f"w{s}") for s in range(NSLOT)]
        self.cst = sb("cst", [128, 9, 128], F32)
        self.cstb = sb("cstb", [128, 9, 128], BF16)
        self.onesb = sb("onesb", [128, 128], BF16)
        self.smallp = sb("smallp_sb", [128, L, 168], F32)
        self.bcast = sb("bcast_sb", [128, L, 520], F32)
        self.modT = sb("modT", [128, L, 48, 2], F32)
        self.condf = sb("condf", [128, 8, 2], F32)
        self.condb = sb("condb", [128, 8, 2], BF16)
        self.modtok = sb("modtok", [2, 512], F32)
        self.wa1 = sb("wa1_sb", [128, L, 8, 32], BF16)
        self.wa2 = sb("wa2_sb", [33, L, 512], BF16)
        self.yT = sb("yT", [128, 3, 4, 1024], BF16)
        if self.do_sample:
            self.ropec = sb("ropec_sb", [128, 1024], F32)
            self.ropes = sb("ropes_sb", [128, 1024], F32)
        self.lam = sb("lam", [128, 8], F32)
        self.retsp = sb("retsp", [128, L, 8], F32)
        self.subg = sb("subg", [128, 128], F32)
        self.ps = self.st.enter_context(nc.psum_tensor("ps", [128, 8, 512], F32))
        self.s_in = self.S.slot("init", wait_all=True)
        self.s_inp = self.S.slot("initp", wait_all=True)
        self.s_io = [self.S.slot(f"io{i}") for i in range(2)]
        self.s_st = [self.S.slot(f"st{i}") for i in range(2)]
        self.s_out = self.S.slot("out_misc")
        self.s_cc = self.S.slot("cc")
        self.s_xs = [self.S.slot(f"xs{i}") for i in range(4)]
        self.xs_i = 0
        self.meta = sb("meta_sb", [128, 16], F32)
        U = sb("U", [128, UW], F32)
        self.U = U

        def vb(off, shape):
            n = int(np.prod(shape)) // 2
            a = U[:, off:off + n].bitcast(BF16)
            return a if len(shape) == 1 else a.rearrange(
                "p (" + " ".join("abcd"[:len(shape)]) + ") -> p " + " ".join("abcd"[:len(shape)]),
                **{k: v for k, v in zip("abcd", shape)})

        def vf(off, shape):
            n = int(np.prod(shape))
            a = U[:, off:off + n]
            return a if len(shape) == 1 else a.rearrange(
                "p (" + " ".join("abcd"[:len(shape)]) + ") -> p " + " ".join("abcd"[:len(shape)]),
                **{k: v for k, v in zip("abcd", shape)})

        self.QT = vb(0, [4, 1024])
        self.otokAB = vb(0, [8, 512])
        self.dbf = vb(0, [4, 8, 128])
        self.stage = vf(0, [2, 1024])
        self.KT = vb(2048, [4, 1024])
        self.sstk = vb(2048, [4, 8, 128])
        self.vtok = vb(4096, [8, 512])
        self.gtok = vb(6144, [8, 512])
        self.otokC = vb(6144, [8, 512])
        self.qd = vb(8192, [4, 8, 128])
        self.Pm = vb(10240, [4, 8, 128])
        self.ropex = vb(10240, [512])
        self.ropet = vf(10496, [512])
        self.ropeu = vf(11008, [512])
        self.cstage = vf(10240, [2, 512])
        self.KH = vb(8192, [4096])
        self.VH = vb(12288, [32, 130])
        self.vctx = vb(10240, [2, 4, 130])
        self.kctx = vb(10760, [4, 256])
        self.kstg = vf(11272, [512])
        self.delta = vf(12288, [4, 8, 128])
        self.vaug = vb(12288, [8, 4, 130])
        self.ET = vb(12288 + 2080, [2, 512])
        o_ = 12288 + 4160
        self.s32 = vf(o_, [2, 4, 128]); o_ += 1024
        self.EqT = vf(o_, [4, 128]); o_ += 512
        self.EkT = vf(o_, [4, 128]); o_ += 512
        self.Ekd = vf(o_, [4, 128]); o_ += 512
        self.dcol = vf(o_, [4, 8]); o_ += 32
        self.latok = vf(o_, [512]); o_ += 512
        self.uT = vb(o_, [1024]); o_ += 512
        self.stat = vf(o_, [8, 4, 2]); o_ += 64
        self.rstd = vf(o_, [8, 4]); o_ += 32
        self.st6 = vf(o_, [6]); o_ += 8
        self.ksb = vb(o_, [2, 128]); o_ += 128
        self.kdtok = vb(o_, [2, 128]); o_ += 128
        self.ytok = vb(o_, [2, 512]); o_ += 512
        self.tsm = vf(o_, [2, 128]); o_ += 256
        self.sm8 = vf(o_, [16]); o_ += 16
        self.ggt = vb(o_, [512]); o_ += 256
        assert o_ <= UW, o_
        self.Sin = self.EqT
        self.Srun = self.EkT
        self.GL = self.Ekd
        self.sinb = self.uT[:, 0:512].rearrange("p (h v) -> p h v", h=4)
        self.cp = self.latok[:, 0:32].rearrange("p (h t) -> p h t", h=4)
        self.Dg = self.latok[:, 32:48].rearrange("p (s h) -> p s h", s=4)
        self.hxs = self.latok[:, 64:72].bitcast(BF16)
        self.hG = self.latok[:, 72:104].bitcast(BF16).rearrange("p (r c) -> p r c", r=4)
        self.hhf = self.latok[:, 104:120].rearrange("p (k c) -> p k c", k=8)
        self.hh = self.latok[:, 120:128].bitcast(BF16).rearrange("p (k c) -> p k c", k=8)
        self.mergedT = vb(0, [8, 1024])
        self.dT = vf(4096, [8, 512])
        self.sqT = vb(8192, [8, 512])
        self.tmpA = vf(10240, [512])
        self.tmpB = vf(10752, [512])
        self.sig = vf(11264, [3, 512])
        self.tm3 = vf(12800, [3, 512])
        self.gT = vb(0, [NFC, 1024])
        self.aT = vf(11264, [2, 1028])
        self.cT = vf(13320, [2, 512])
        self.glT = vf(14344, [2, 512])
        self.dT2 = vf(15368, [8, 512])
        self.tmpA2 = vf(19464, [512])
        self.tmpB2 = vf(19976, [512])
        self.sqT2 = vb(11264, [8, 512])
        assert 20488 <= UW
        self.bank_i = 0
        self.bank_pool = list(range(8))

    def bank(self):
        b = self.bank_pool[self.bank_i % len(self.bank_pool)]
        self.bank_i += 1
        return b

    def barrier(self, pool=False):
        S = self.S
        last = {}
        for e in S.ENGS:
            last[e] = None
            for op in reversed(S.ops[e]):
                if op.inc != 0:
                    last[e] = op
                    break
        dmas = [op for op in self.out_ops[self.out_seen:]]
        self.out_seen = len(self.out_ops)
        for e in (("pe", "act", "dve", "pool", "sp") if (pool or os.environ.get("KPOOLBAR")) else ("pe", "act", "dve", "sp")):
            deps = [last[o] for o in S.ENGS if (o != e or e != "pe") and last[o] is not None and last[o].slot is None]
            S.wait_for(e, deps + dmas)
        if os.environ.get("KNOCLR"):
            return
        keepw = lambda k: isinstance(k, tuple) and k[0] == "w"
        for k in list(S.lastw.keys()):
            if S.lastw[k].slot is None and not keepw(k):
                del S.lastw[k]
        for k in list(S.readers.keys()):
            if not keepw(k):
                S.readers[k] = [r for r in S.readers[k] if r.slot is not None]

    def psb(self, b):
        return self.ps[:, b, :].bitcast(BF16)

    def init_phase(self):
        S, d = self.S, self.d
        si = self.s_in
        S.add("sp", lambda e: e.dma_start(out=self.cst[:], in_=d["consts"]), writes=["cst"], slot=si)
        S.add("pool", lambda e: e.dma_start(out=self.cstb[:], in_=d["consts"]), writes=["cstb"], slot=self.s_inp)
        S.add("sp", lambda e: e.dma_start(out=self.smallp[:], in_=d["smallp"]), writes=["smallp"], slot=si)
        S.add("sp", lambda e: e.dma_start(out=self.bcast[:], in_=d["bcast"]), writes=["bcast"], slot=si)
        S.add("sp", lambda e: e.dma_start(out=self.condf[:], in_=d["cond"]), writes=["condf"], slot=si)
        S.add("pool", lambda e: e.dma_start(out=self.wa1[:], in_=d["wa1"]), writes=["wa1"], slot=self.s_inp)
        S.add("pool", lambda e: e.dma_start(out=self.wa2[:], in_=d["wa2"]), writes=["wa2"], slot=self.s_inp)
        if self.do_sample:
            S.add("sp", lambda e: e.dma_start(out=self.meta[:], in_=d["meta"]), writes=["meta"], slot=si)
            S.add("sp", lambda e: e.dma_start(out=self.ropec[:], in_=d["ropec"]), writes=["ropec"], slot=si)
            S.add("sp", lambda e: e.dma_start(out=self.ropes[:], in_=d["ropes"]), writes=["ropes"], slot=si)
        S.add("dve", lambda e: e.memset(self.onesb[:], 1.0 / 1024.0), writes=["onesb"])
        S.add("act", lambda e: e.activation(self.condb[:], self.condf[:], AF.Silu), reads=["condf"], writes=["condb"])
        for l in range(L):
            for j in range(12):
                wv, wk = self.wload(d[f"wA{l}"][j])
                b = self.bank()
                for kc in range(8):
                    S.add("pe", lambda e, kc=kc, b=b, wv=wv: e.matmul(
                        self.ps[0:2, b, :], self.condb[:, kc, :], wv[:, kc * 512:(kc + 1) * 512],
                        start=(kc == 0), stop=(kc == 7)), reads=[wk, "condb"], writes=[("ps", b)])
                S.add("act", lambda e, b=b: e.activation(self.modtok[:], self.ps[0:2, b, :], AF.Identity),
                      reads=[("ps", b)], writes=["modtok"])
                b2 = self.bank()
                for q in range(4):
                    S.add("pe", lambda e, q=q, b2=b2: e.transpose(
                        self.ps[:, b2, q * 2:(q + 1) * 2], self.modtok[0:2, q * 128:(q + 1) * 128],
                        self.cst[0:2, 4, 0:2]), reads=["modtok", "cst"], writes=[("ps", b2)])
                S.add("dve", lambda e, l=l, j=j, b2=b2: e.tensor_tensor(
                    self.modT[:, l, j * 4:(j + 1) * 4, :],
                    self.ps[:, b2, 0:8].rearrange("p (a b) -> p a b", a=4),
                    self.smallp[:, l, j * 4:(j + 1) * 4].rearrange("p (a o) -> p a o", o=1).broadcast_to([128, 4, 2]),
                    ALU.add), reads=[("ps", b2), "smallp"], writes=["modT"])
            S.add("dve", lambda e, l=l: e.tensor_scalar(self.modT[:, l, 8:16, :], self.modT[:, l, 8:16, :], 1.0, None, ALU.add),
                  reads=["modT"], writes=["modT"])
            S.add("dve", lambda e, l=l: e.tensor_scalar(self.modT[:, l, 32:40, :], self.modT[:, l, 32:40, :], 1.0, None, ALU.add),
                  reads=["modT"], writes=["modT"])
            S.add("dve", lambda e, l=l: e.tensor_scalar(self.modT[:, l, 16:24, :], self.modT[:, l, 16:24, :], 0.5 / ALPHA, None, ALU.mult),
                  reads=["modT"], writes=["modT"])
            S.add("dve", lambda e, l=l: e.tensor_scalar(self.modT[:, l, 40:48, :], self.modT[:, l, 40:48, :], 1.0 / ALPHA, None, ALU.mult),
                  reads=["modT"], writes=["modT"])
            lam_init = 0.8 - 0.6 * math.exp(-0.3 * l)
            lp = self.bcast[:, l, 256:512].rearrange("p (a b c) -> p a b c", a=2, b=2)
            S.add("dve", lambda e, lp=lp: e.tensor_tensor(self.tsm[:, 0, :].rearrange("p (a c) -> p a c", a=2),
                                                          lp[:, :, 0, :], lp[:, :, 1, :], ALU.mult),
                  reads=["bcast"], writes=["tsm"])
            S.add("dve", lambda e: e.reduce_sum(self.sm8[:, 0:2], self.tsm[:, 0, :].rearrange("p (a c) -> p a c", a=2),
                                                axis=mybir.AxisListType.X), reads=["tsm"], writes=["sm8"])
            S.add("act", lambda e: e.activation(self.sm8[:, 2:4], self.sm8[:, 0:2], AF.Exp), reads=["sm8"], writes=["sm8b"])
            S.add("dve", lambda e: e.tensor_tensor(self.sm8[:, 4:5], self.sm8[:, 3:4], self.sm8[:, 2:3], ALU.subtract),
                  reads=["sm8b"], writes=["sm8c"])
            S.add("dve", lambda e, l=l, li=lam_init: e.tensor_scalar(self.lam[:, l:l + 1], self.sm8[:, 4:5], -li, None, ALU.add),
                  reads=["sm8c"], writes=["lam"])
            S.add("act", lambda e, l=l: e.activation(self.sm8[:, 8:16], self.bcast[:, l, 512:520], AF.Exp, scale=-1.0),
                  reads=["bcast"], writes=["sm8d"])
            S.add("act", lambda e, l=l: e.activation(self.retsp[:, l, :], self.sm8[:, 8:16], AF.Ln, bias=1.0),
                  reads=["sm8d"], writes=["retsp"])

    def load_x(self, P):
        S = self.S
        src = self.d[P["x"]]
        for t in range(P["NT"]):
            sl = t % 2
            S.add("sp", lambda e, t=t, sl=sl: e.dma_start(out=self.stage[:, sl, :], in_=src[t * 128:(t + 1) * 128, :]),
                  writes=[("stage", sl)], slot=self.s_io[sl])
            for half in range(2):
                b = self.bank()
                for q in range(4):
                    kc = half * 4 + q
                    S.add("pe", lambda e, b=b, q=q, kc=kc, sl=sl: e.transpose(
                        self.ps[:, b, q * 128:(q + 1) * 128], self.stage[:, sl, kc * 128:(kc + 1) * 128], self.cst[:, 4, :]),
                        reads=[("stage", sl), "cst"], writes=[("ps", b)])
                S.add("act" if half == 0 else "dve",
                      (lambda e, b=b, half=half, t=t: e.activation(
                          self.xT[:, half * 4:half * 4 + 4, t * 128:(t + 1) * 128],
                          self.ps[:, b, :].rearrange("p (a c) -> p a c", a=4), AF.Identity)) if half == 0 else
                      (lambda e, b=b, half=half, t=t: e.tensor_copy(
                          self.xT[:, half * 4:half * 4 + 4, t * 128:(t + 1) * 128],
                          self.ps[:, b, :].rearrange("p (a c) -> p a c", a=4))),
                      reads=[("ps", b)], writes=[("xT", t // 4)])

    def modulate(self, P, l, which):
        S = self.S
        c = P["cond"]
        sh0 = 0 if which == 0 else 24
        sc0 = 8 if which == 0 else 32
        for g in range(P["NG"]):
            for kc in range(8):
                S.add("dve", lambda e, g=g, kc=kc: e.tensor_scalar(
                    self.hT[:, kc, g * 512:(g + 1) * 512], self.xT[:, kc, g * 512:(g + 1) * 512],
                    self.modT[:, l, sc0 + kc, c:c + 1], self.modT[:, l, sh0 + kc, c:c + 1], ALU.mult, ALU.add),
                    reads=[("xT", g), "modT"], writes=[("hT", g)])

    def projF(self, P, wv, wk, ci, evac):
        S = self.S
        for g in range(P["NG"]):
            b = self.bank()
            for kc in range(8):
                S.add("pe", lambda e, b=b, kc=kc, g=g: e.matmul(
                    self.ps[:, b, :], wv[:, ci * 1024 + kc * 128: ci * 1024 + (kc + 1) * 128],
                    self.hT[:, kc, g * 512:(g + 1) * 512], start=(kc == 0), stop=(kc == 7)),
                    reads=[wk, ("hT", g)], writes=[("ps", b)])
            evac(g, b)

    def projT(self, P, wv, wk, evac):
        S = self.S
        for t in range(P["NT"]):
            b = self.bank()
            for kc in range(8):
                S.add("pe", lambda e, b=b, kc=kc, t=t: e.matmul(
                    self.ps[:, b, :], self.hT[:, kc, t * 128:(t + 1) * 128], wv[:, kc * 512:(kc + 1) * 512],
                    start=(kc == 0), stop=(kc == 7)),
                    reads=[wk, ("hT", t // 4)], writes=[("ps", b)])
            evac(t, b)

    def evac_rope(self, P, dst, key, h, g, b, scale):
        S = self.S
        gc = slice(g * 512, (g + 1) * 512)
        if not P["rope"]:
            S.add("act", lambda e: e.activation(dst[:, h, gc], self.ps[:, b, :], AF.Copy, scale=scale),
                  reads=[("ps", b)], writes=[(key, h)])
            return
        S.add("act", lambda e: e.activation(self.ropex[:, :], self.ps[:, b, :], AF.Copy, scale=scale),
              reads=[("ps", b)], writes=["ropex"])
        b2 = self.bank()
        S.add("pe", lambda e: e.matmul(self.ps[:, b2, :], self.cstb[:, 5, :], self.ropex[:, :], start=True, stop=True),
              reads=["ropex", "cstb"], writes=[("ps", b2)])
        S.add("dve", lambda e: e.tensor_tensor(self.ropet[:, :], self.ps[:, b2, :], self.ropes[:, gc], ALU.mult),
              reads=[("ps", b2), "ropes"], writes=["ropet"])
        S.add("dve", lambda e: e.tensor_tensor(self.ropeu[:, :], self.ropex[:, :], self.ropec[:, gc], ALU.mult),
              reads=["ropex", "ropec"], writes=["ropeu"])
        S.add("dve", lambda e: e.tensor_tensor(dst[:, h, gc], self.ropet[:, :], self.ropeu[:, :], ALU.add),
              reads=["ropet", "ropeu"], writes=[(key, h)])

    def decay_tiles(self, scale, t):
        S = self.S
        b = self.bank()
        for h in range(4):
            for dd, ci in ((0, 2), (1, 3)):
                c0 = h * 128 + dd * 64
                S.add("pe", lambda e, c0=c0, ci=ci: e.matmul(self.ps[:, b, c0:c0 + 64], self.cst[:, ci, :], self.latok[:, c0:c0 + 64],
                                                            start=True, stop=True),
                      reads=["latok", "cst"], writes=[("ps", b)])
        S.add("act", lambda e: e.activation(self.Ekd[:, :, :], self.ps[:, b, :].rearrange("p (h c) -> p h c", h=4), AF.Exp, scale=scale),
              reads=[("ps", b)], writes=["Ekd"])
        bx, by = self.bank(), self.bank()
        for h in range(4):
            S.add("pe", lambda e, h=h: e.matmul(self.ps[:, bx, h * 128:(h + 1) * 128], self.latok[:, h * 128:(h + 1) * 128],
                                               self.cst[:, 0, :], start=True, stop=True),
                  reads=["latok", "cst"], writes=[("ps", bx)])
            S.add("pe", lambda e, h=h: e.matmul(self.ps[:, by, h * 128:(h + 1) * 128], self.latok[:, h * 128:(h + 1) * 128],
                                               self.cst[:, 1, :], start=True, stop=True),
                  reads=["latok", "cst"], writes=[("ps", by)])
        for (dst, sc, key) in ((self.EqT, scale, "EqT"), (self.EkT, -scale, "EkT")):
            S.add("act", lambda e, dst=dst, sc=sc: e.activation(
                dst[0:64, :, :], self.ps[0:64, bx, :].rearrange("p (h c) -> p h c", h=4), AF.Exp, scale=sc),
                reads=[("ps", bx)], writes=[key])
            S.add("act", lambda e, dst=dst, sc=sc: e.activation(
                dst[64:128, :, :], self.ps[64:128, by, :].rearrange("p (h c) -> p h c", h=4), AF.Exp, scale=sc),
                reads=[("ps", by)], writes=[key])
        if t is not None:
            S.add("dve", lambda e: e.tensor_copy(self.dcol[0:64, :, t:t + 1], self.EqT[0:64, :, 127:128]),
                  reads=["EqT"], writes=["dcol"])
            S.add("dve", lambda e: e.tensor_copy(self.dcol[64:128, :, t:t + 1], self.EqT[64:128, :, 0:1]),
                  reads=["EqT"], writes=["dcol"])

    def compose_states(self, P, l, mix):
        S, x_ = self.S, self.x_
        dk = [("delta", h, t) for h in range(4) for t in range(8)]
        S.add("dve", lambda e: e.tensor_copy(self.cp[0:64, :, 0:1], self.dcol[0:64, :, 0:1]), reads=["dcol"], writes=["cp"])
        S.add("dve", lambda e: e.tensor_copy(self.cp[64:128, :, 7:8], self.dcol[64:128, :, 7:8]), reads=["dcol"], writes=["cp"])
        for t in range(1, 8):
            S.add("dve", lambda e, t=t: e.tensor_tensor(self.cp[0:64, :, t:t + 1], self.cp[0:64, :, t - 1:t],
                                                        self.dcol[0:64, :, t:t + 1], ALU.mult), reads=["cp", "dcol"], writes=["cp"])
            tb = 7 - t
            S.add("dve", lambda e, tb=tb: e.tensor_tensor(self.cp[64:128, :, tb:tb + 1], self.cp[64:128, :, tb + 1:tb + 2],
                                                          self.dcol[64:128, :, tb:tb + 1], ALU.mult), reads=["cp", "dcol"], writes=["cp"])
        g1 = self.xgroup("stw")
        stL = x_["st_in"][:, 0:512].rearrange("p (h v) -> p h v", h=4)
        S.add("sp", lambda e: e.dma_start(out=stL[0:64], in_=self.delta[0:64, :, 7, :]), reads=dk, writes=["st_in"], slot=g1)
        S.add("sp", lambda e: e.dma_start(out=stL[64:128], in_=self.delta[64:128, :, 0, :]), reads=dk, writes=["st_in"], slot=g1)
        S.add("sp", lambda e: e.dma_start(out=x_["st_in"][0:64, 512:516].rearrange("p (h o) -> p h o", o=1), in_=self.cp[0:64, :, 7:8], allow_slow_non_contiguous=True), reads=["cp"], writes=["st_in"], slot=g1)
        S.add("sp", lambda e: e.dma_start(out=x_["st_in"][64:128, 512:516].rearrange("p (h o) -> p h o", o=1), in_=self.cp[64:128, :, 0:1], allow_slow_non_contiguous=True), reads=["cp"], writes=["st_in"], slot=g1)
        self.gather(x_["st_in"], x_["st_out"], "st_in", "st_out")
        g2 = self.xgroup("stl")
        src0 = self.d["sret" if mix == 0 else "sgla"][l]
        S.add("sp", lambda e: e.dma_start(out=self.Srun[:, :, :], in_=src0), writes=["Srun"], slot=g2)
        for s_ in range(4):
            S.add("sp", lambda e, s_=s_: e.dma_start(out=self.Dg[0:64, s_, :], in_=x_["st_out"][s_ * 128:s_ * 128 + 64, 512:516]),
                  reads=["st_out"], writes=["Dg"], slot=g2)
            S.add("sp", lambda e, s_=s_: e.dma_start(out=self.Dg[64:128, s_, :],
                                                     in_=x_["st_out"][(3 - s_) * 128 + 64:(3 - s_) * 128 + 128, 512:516]),
                  reads=["st_out"], writes=["Dg"], slot=g2)
        S.add("dve", lambda e: e.memset(self.Sin[:, :, :], 0.0), writes=["Sin"])
        for s_ in range(4):
            g3 = self.xgroup("gl")
            S.add("sp", lambda e, s_=s_: e.dma_start(
                out=self.GL[0:64, :, :], in_=x_["st_out"][s_ * 128:s_ * 128 + 64, 0:512].rearrange("p (h v) -> p h v", h=4)),
                reads=["st_out"], writes=["GL"], slot=g3)
            S.add("sp", lambda e, s_=s_: e.dma_start(
                out=self.GL[64:128, :, :],
                in_=x_["st_out"][(3 - s_) * 128 + 64:(3 - s_) * 128 + 128, 0:512].rearrange("p (h v) -> p h v", h=4)),
                reads=["st_out"], writes=["GL"], slot=g3)
            S.add("dve", lambda e, s_=s_: e.scalar_tensor_tensor(
                self.Sin[:, :, :].rearrange("p h v -> p (h v)"), self.Srun[:, :, :].rearrange("p h v -> p (h v)"),
                self.meta[:, s_:s_ + 1], self.Sin[:, :, :].rearrange("p h v -> p (h v)"), ALU.mult, ALU.add),
                reads=["Srun", "meta", "Sin"], writes=["Sin"])
            for h in range(4):
                S.add("dve", lambda e, s_=s_, h=h: e.scalar_tensor_tensor(
                    self.Srun[:, h, :], self.Srun[:, h, :], self.Dg[:, s_, h:h + 1], self.GL[:, h, :], ALU.mult, ALU.add),
                    reads=["Srun", "Dg", "GL"], writes=["Srun"])
        for t in range(8):
            for h in range(4):
                S.add("dve", lambda e, t=t, h=h: e.scalar_tensor_tensor(
                    self.delta[:, h, t, :], self.Sin[:, h, :], self.cp[:, h, t:t + 1], self.delta[:, h, t, :], ALU.mult, ALU.add),
                    reads=["Sin", "cp", ("delta", h, t)], writes=[("delta", h, t)])
        S.add("dve", lambda e: e.tensor_copy(self.sinb[:, :, :], self.Sin[:, :, :]), reads=["Sin"], writes=["sinb"])

    def mixer_AB(self, P, l, mix):
        S, d = self.S, self.d
        NT, NG = P["NT"], P["NG"]
        qscale = 1.0 if mix == 0 else 0.125
        kscale = 0.125 if mix == 0 else 1.0
        ropeP = P if mix == 0 else dict(P, rope=False)
        wv, wk = self.wload(d[f"wT{l}"][2 * mix])
        self.projT(P, wv, wk, lambda t, b: S.add(
            "act", lambda e: e.activation(self.vtok[:, t, :], self.ps[:, b, :], AF.Identity), reads=[("ps", b)], writes=[("vtok", t)]))
        wv, wk = self.wload(d[f"wT{l}"][2 * mix + 1])
        self.projT(P, wv, wk, lambda t, b: S.add(
            "act", lambda e: e.activation(self.gtok[:, t, :], self.ps[:, b, :], AF.Silu),
            reads=[("ps", b)], writes=[("gtok", t)]))
        ksub = int(os.environ.get("KSUB", "99"))
        if ksub < 1:
            return
        if mix == 1:
            S.add("dve", lambda e: e.memset(self.uT[0:33, 0:P["T"]], 1.0), writes=["uT"])
            for g in range(NG):
                b = self.bank()
                for kc in range(8):
                    S.add("pe", lambda e, b=b, kc=kc, g=g: e.matmul(
                        self.ps[0:32, b, :], self.wa1[:, l, kc, :], self.hT[:, kc, g * 512:(g + 1) * 512],
                        start=(kc == 0), stop=(kc == 7)), reads=["wa1", ("hT", g)], writes=[("ps", b)])
                S.add("act", lambda e, b=b, g=g: e.activation(self.uT[0:32, g * 512:(g + 1) * 512], self.ps[0:32, b, :], AF.Identity),
                      reads=[("ps", b)], writes=["uT"])
        wv, wk = self.wload(d[f"wF{l}"][2 * mix])
        for h in range(4):
            self.projF(P, wv, wk, h, lambda g, b, h=h: self.evac_rope(ropeP, self.QT, "QT", h, g, b, qscale))
        wv, wk = self.wload(d[f"wF{l}"][2 * mix + 1])
        for h in range(4):
            self.projF(P, wv, wk, h, lambda g, b, h=h: self.evac_rope(ropeP, self.KT, "KT", h, g, b, kscale))
        if ropeP["rope"]:
            self.barrier()
        if ksub < 2:
            return
        if mix == 0:
            S.add("dve", lambda e: e.tensor_copy(
                self.latok[:, :].rearrange("p (h d k) -> p h d k", h=4, d=2),
                self.retsp[:, l, :].rearrange("p (d h o) -> p h d o", d=2, o=1).broadcast_to([128, 4, 2, 64])),
                reads=["retsp"], writes=["latok"])
            self.decay_tiles(-1.0, None)
            for t in range(NT):
                S.add("dve", lambda e, t=t: e.tensor_copy(self.dcol[0:64, :, t:t + 1], self.EqT[0:64, :, 127:128]),
                      reads=["EqT"], writes=["dcol"])
                S.add("dve", lambda e, t=t: e.tensor_copy(self.dcol[64:128, :, t:t + 1], self.EqT[64:128, :, 0:1]),
                      reads=["EqT"], writes=["dcol"])
        if ksub < 3:
            return
        it = 0
        for t in range(NT):
            tc_ = slice(t * 128, (t + 1) * 128)
            if mix == 1:
                b = self.bank()
                S.add("pe", lambda e, b=b, tc_=tc_: e.matmul(self.ps[:, b, :], self.uT[0:33, tc_], self.wa2[0:33, l, :],
                                                            start=True, stop=True),
                      reads=["uT", "wa2"], writes=[("ps", b)])
                S.add("act", lambda e, b=b: e.activation(self.latok[:, :], self.ps[:, b, :], AF.Exp, scale=-1.0),
                      reads=[("ps", b)], writes=["latok"])
                S.add("act", lambda e: e.activation(self.latok[:, :], self.latok[:, :], AF.Ln, bias=1.0),
                      reads=["latok"], writes=["latok"])
                self.decay_tiles(-1.0 / 16.0, t)
            for h in range(4):
                i2 = it % 2
                it += 1
                S.add("dve", lambda e, h=h, t=t, tc_=tc_: e.tensor_tensor(
                    self.qd[:, h, t, :], self.QT[:, h, tc_], self.EqT[:, h, :], ALU.mult),
                    reads=[("QT", h), "EqT"], writes=[("qd", h, t)])
                S.add("dve", lambda e, h=h, tc_=tc_, i2=i2: e.tensor_tensor(
                    self.ksb[:, i2, :], self.KT[:, h, tc_], self.EkT[:, h, :], ALU.mult),
                    reads=[("KT", h), "EkT"], writes=[("ksb", i2)])
                bt = self.bank()
                S.add("pe", lambda e, h=h, tc_=tc_, bt=bt: e.transpose(
                    self.psb(bt)[:, 0:128], self.KT[:, h, tc_], self.cstb[:, 4, :]),
                    reads=[("KT", h), "cstb"], writes=[("ps", bt)])
                S.add("dve", lambda e, h=h, bt=bt, i2=i2: e.tensor_tensor(
                    self.kdtok[:, i2, :], self.psb(bt)[:, 0:128], self.Ekd[:, h, :], ALU.mult),
                    reads=[("ps", bt), "Ekd"], writes=[("kdtok", i2)])
                bd = self.bank()
                S.add("pe", lambda e, h=h, t=t, bd=bd, i2=i2: e.matmul(
                    self.ps[:, bd, 0:128], self.kdtok[:, i2, :], self.vtok[:, t, h * 128:(h + 1) * 128], start=True, stop=True),
                    reads=[("kdtok", i2), ("vtok", t)], writes=[("ps", bd)])
                S.add("act", lambda e, h=h, t=t, bd=bd: e.activation(self.delta[:, h, t, 0:128], self.ps[:, bd, 0:128], AF.Identity),
                      reads=[("ps", bd)], writes=[("delta", h, t)])
                b1, b2 = self.bank(), self.bank()
                S.add("pe", lambda e, h=h, t=t, b1=b1, i2=i2: e.matmul(
                    self.ps[:, b1, 0:128], self.ksb[0:64, i2, :], self.qd[0:64, h, t, :], start=True, stop=True),
                    reads=[("ksb", i2), ("qd", h, t)], writes=[("ps", b1)])
                S.add("pe", lambda e, h=h, t=t, b2=b2, i2=i2: e.matmul(
                    self.ps[:, b2, 0:128], self.ksb[64:128, i2, :], self.qd[64:128, h, t, :], start=True, stop=True),
                    reads=[("ksb", i2), ("qd", h, t)], writes=[("ps", b2)])
                S.add("dve", lambda e, b1=b1, i2=i2: e.tensor_tensor(
                    self.tsm[:, i2, :], self.ps[:, b1, 0:128], self.cst[:, 0, :], ALU.mult),
                    reads=[("ps", b1), "cst"], writes=[("tsm", i2)])
                S.add("dve", lambda e, b2=b2: e.tensor_tensor(
                    self.ps[:, b2, 128:256], self.ps[:, b2, 0:128], self.cst[:, 1, :], ALU.mult),
                    reads=[("ps", b2), "cst"], writes=[("ps", b2)])
                S.add("dve", lambda e, h=h, t=t, b2=b2, i2=i2: e.tensor_tensor(
                    self.Pm[:, h, t, :], self.ps[:, b2, 128:256], self.tsm[:, i2, :], ALU.add),
                    reads=[("ps", b2), ("tsm", i2)], writes=[("Pm", h, t)])
        if ksub < 4:
            return
        self.barrier()
        out_name = "nsr" if mix == 0 else "nsg"
        for si, (t0, n) in enumerate(P["seqs"]):
            for c_ in range(1, n):
                tf = t0 + c_
                tb = t0 + n - 1 - c_
                for h in range(4 if os.environ.get("KX", "0") != "5" else 0):
                    S.add("dve", lambda e, h=h, tf=tf: e.scalar_tensor_tensor(
                        self.delta[0:64, h, tf, 0:128], self.delta[0:64, h, tf - 1, 0:128], self.dcol[0:64, h, tf:tf + 1],
                        self.delta[0:64, h, tf, 0:128], ALU.mult, ALU.add),
                        reads=[("delta", h, tf - 1), "dcol", ("delta", h, tf)], writes=[("delta", h, tf)])
                    S.add("dve", lambda e, h=h, tb=tb: e.scalar_tensor_tensor(
                        self.delta[64:128, h, tb, 0:128], self.delta[64:128, h, tb + 1, 0:128], self.dcol[64:128, h, tb:tb + 1],
                        self.delta[64:128, h, tb, 0:128], ALU.mult, ALU.add),
                        reads=[("delta", h, tb + 1), "dcol", ("delta", h, tb)], writes=[("delta", h, tb)])
            if P.get("sample"):
                self.compose_states(P, l, mix)
            if P["states_out"] and not os.environ.get("KNOST"):
                for dr, tt in ((0, t0 + n - 1), (1, t0)):
                    op = S.add("sp", lambda e, si=si, dr=dr, tt=tt: e.dma_start(
                        out=self.o[out_name][si, l, dr].rearrange("h k v -> k h v"),
                        in_=self.delta[dr * 64:(dr + 1) * 64, :, tt, 0:128]),
                        reads=[("delta", h_, tt) for h_ in range(4)], slot=self.s_st[si % 2])
                    self.out_ops.append(op)
            if os.environ.get("KX") == "7":
                ba = self.bank()
                S.add("pe", lambda e, ba=ba: e.matmul(self.ps[:, ba, 0:128], self.kdtok[:, 0, :], self.vtok[:, 0, 0:128],
                                                      start=True, stop=True),
                      reads=[("kdtok", 0), ("vtok", 0)], writes=[("ps", ba)])
                continue
            if os.environ.get("KX") == "8":
                ba = self.bank()
                S.add("pe", lambda e, ba=ba: e.matmul(self.ps[:, ba, 0:128], self.cstb[:, 6, :], self.vtok[:, 0, 0:128],
                                                      start=True, stop=True),
                      reads=["cstb", ("vtok", 0)], writes=[("ps", ba)])
                continue
            for c_ in range(n):
                t = t0 + c_
                for h in range(4):
                    kv = os.environ.get("KV", "0")
                    if kv == "1":
                        S.add("dve", lambda e, t=t, h=h: e.tensor_tensor(self.dbf[:, h, t, :], self.cst[:, 0, :], self.cst[:, 8, :], ALU.mult),
                              reads=["cst"], writes=[("QT", h)])
                    elif kv == "3":
                        S.add("dve", lambda e, t=t, h=h: e.tensor_tensor(self.dbf[:, h, t, :], self.cst[:, 0, :], self.cst[:, 8, :], ALU.mult),
                              reads=["cst"], writes=[("tsm", 0)])
                    elif kv == "4":
                        S.add("dve", lambda e, t=t, h=h: e.tensor_tensor(self.tsm[:, 0, :], self.cst[:, 0, :], self.cst[:, 8, :], ALU.mult),
                              reads=["cst"], writes=[("QT", h)])
                    elif kv == "2":
                        S.add("dve", lambda e, t=t, h=h: e.tensor_tensor(self.tsm[:, 0, :], self.delta[:, h, t, :], self.cst[:, 8, :], ALU.mult),
                              reads=[("delta", h, t), "cst"], writes=[("tsm", 0)])
                    else:
                        S.add("dve", lambda e, t=t, h=h: e.tensor_tensor(self.dbf[:, h, t, :], self.delta[:, h, t, :], self.cst[:, 8, :], ALU.mult),
                              reads=[("delta", h, t), "cst"], writes=[("dbf", h)])
            if os.environ.get("KX") == "9":
                continue
            if os.environ.get("KX") == "10":
                ba = self.bank()
                S.add("pe", lambda e, ba=ba: e.matmul(self.ps[:, ba, 0:128], self.cstb[:, 6, :], self.dbf[:, 0, t0, :],
                                                      start=True, stop=True),
                      reads=["cstb", ("QT", 0)], writes=[("ps", ba)])
                continue
            for c_ in range(n):
                t = t0 + c_
                ba = self.bank()
                for h in range(4 if os.environ.get("KX", "0") != "4" else 0):
                    hasf, hasb = c_ > 0, c_ < n - 1
                    if P.get("sample") and not hasf:
                        S.add("pe", lambda e, h=h, ba=ba: e.matmul(
                            self.ps[:, ba, h * 128:(h + 1) * 128], self.cstb[:, 6, :], self.sinb[:, h, :],
                            start=True, stop=False), reads=["sinb", "cstb"], writes=[("ps", ba)])
                        hasf = True
                    if P.get("sample") and not hasb:
                        S.add("pe", lambda e, h=h, t=t, ba=ba: e.matmul(
                            self.ps[:, ba, h * 128:(h + 1) * 128], self.cstb[:, 6, :], self.dbf[:, h, t - 1, :],
                            start=True, stop=False), reads=[("dbf", h), "cstb"], writes=[("ps", ba)])
                        S.add("pe", lambda e, h=h, ba=ba: e.matmul(
                            self.ps[:, ba, h * 128:(h + 1) * 128], self.cstb[:, 7, :], self.sinb[:, h, :],
                            start=False, stop=True), reads=["sinb", "cstb"], writes=[("ps", ba)])
                        continue
                    if P.get("sample") and c_ == 0:
                        S.add("pe", lambda e, h=h, t=t, ba=ba: e.matmul(
                            self.ps[:, ba, h * 128:(h + 1) * 128], self.cstb[:, 7, :], self.dbf[:, h, t + 1, :],
                            start=False, stop=True), reads=[("dbf", h), "cstb"], writes=[("ps", ba)])
                        continue
                    if hasf:
                        S.add("pe", lambda e, h=h, t=t, ba=ba, hasb=hasb: e.matmul(
                            self.ps[:, ba, h * 128:(h + 1) * 128], self.cstb[:, 6, :], self.dbf[:, h, t - 1, :],
                            start=True, stop=not hasb), reads=[("dbf", h), "cstb"], writes=[("ps", ba)])
                    if hasb:
                        S.add("pe", lambda e, h=h, t=t, ba=ba, hasf=hasf: e.matmul(
                            self.ps[:, ba, h * 128:(h + 1) * 128], self.cstb[:, 7, :], self.dbf[:, h, t + 1, :],
                            start=not hasf, stop=True), reads=[("dbf", h), "cstb"], writes=[("ps", ba)])
                kx = os.environ.get("KX", "0")
                if kx == "0":
                    S.add("act", lambda e, t=t, ba=ba: e.activation(
                        self.sstk[:, :, t, :], self.ps[:, ba, :].rearrange("p (h v) -> p h v", h=4), AF.Identity),
                        reads=[("ps", ba)], writes=[("KT", h_) for h_ in range(4)])
                elif kx == "2":
                    S.add("act", lambda e, t=t, ba=ba: e.activation(
                        self.Pm[:, :, t, :], self.ps[:, ba, :].rearrange("p (h v) -> p h v", h=4), AF.Identity),
                        reads=[("ps", ba)], writes=[("Pm", h_, t) for h_ in range(4)])
                elif kx == "3":
                    S.add("dve", lambda e, t=t, ba=ba: e.tensor_copy(
                        self.sstk[:, :, t, :], self.ps[:, ba, :].rearrange("p (h v) -> p h v", h=4)),
                        reads=[("ps", ba)], writes=[("KT", h_) for h_ in range(4)])
        if ksub < 5 or os.environ.get("KNOP2"):
            return
        if os.environ.get("KPAD"):
            S.add("dve", lambda e: e.memset(self.tsm[:, 0, :], 0.0), writes=[("tsm", 0)])
        if not os.environ.get("KNOBAR"):
            self.barrier()
        for t in range(int(os.environ.get("KP2T", NT))):
            b = self.bank()
            for h in range(4):
                S.add("pe", lambda e, h=h, t=t, b=b: e.matmul(
                    self.ps[:, b, h * 128:(h + 1) * 128], self.qd[:, h, t, :], self.sstk[:, h, t, :],
                    start=True, stop=False), reads=[("qd", h, t), ("KT", h), ("Pm", h, t), ("vtok", t)], writes=[("ps", b)])
                S.add("pe", lambda e, h=h, t=t, b=b: e.matmul(
                    self.ps[:, b, h * 128:(h + 1) * 128], self.Pm[:, h, t, :], self.vtok[:, t, h * 128:(h + 1) * 128],
                    start=False, stop=True), reads=[("Pm", h, t), ("vtok", t)], writes=[("ps", b)])
            kp2 = int(os.environ.get("KP2", "3"))
            if kp2 & 1:
              S.add("act", lambda e, t=t, b=b: e.activation(self.otokAB[:, t, :], self.ps[:, b, :], AF.Identity),
                  reads=[("ps", b)], writes=[("QT", t // 2) if not os.environ.get("KOT") else ("otokAB", t)])
            for h in range(4 if (kp2 & 2) else 0):
                if mix == 0:
                    S.add("dve", lambda e, h=h, b=b: e.bn_stats(self.st6[:, :], self.ps[:, b, h * 128:(h + 1) * 128]),
                          reads=[("ps", b)], writes=["st6"])
                    S.add("dve", lambda e, h=h, t=t: e.bn_aggr(self.stat[:, t, h, :], self.st6[:, :]),
                          reads=["st6"], writes=["stat"])
                else:
                    S.add("act", lambda e, h=h, t=t, b=b: e.activation(
                        self.tsm[:, 0, :], self.ps[:, b, h * 128:(h + 1) * 128], AF.Square,
                        accum_out=self.stat[:, t, h, 1:2]), reads=[("ps", b)], writes=["stat", ("tsm", 0)])
        if ksub < 6:
            return
        self.norm_out(P, l, mix, self.otokAB, lambda t: ("QT", t // 2))

    def norm_out(self, P, l, br, otok, okey):
        S = self.S
        NT = P["NT"]
        if br == 0:
            sc, bias = 1.0, EPS
        else:
            sc, bias = 1.0 / 128.0, EPS
        S.add("act", lambda e: e.activation(self.rstd[:, 0:NT, :], self.stat[:, 0:NT, :, 1], AF.Sqrt, bias=bias, scale=sc),
              reads=["stat"], writes=["rstd"])
        S.add("dve", lambda e: e.reciprocal(self.rstd[:, 0:NT, :], self.rstd[:, 0:NT, :]), reads=["rstd"], writes=["rstd"])
        for t in range(NT):
            i2 = t % 2
            if br == 1:
                S.add("dve", lambda e, t=t: e.tensor_tensor(
                    self.ggt[:].rearrange("p (h v) -> p h v", h=4), self.gtok[:, t, :].rearrange("p (h v) -> p h v", h=4),
                    self.bcast[:, l, 0:128].rearrange("p (o v) -> p o v", o=1).broadcast_to([128, 4, 128]), ALU.mult),
                    reads=[("gtok", t), "bcast"], writes=["ggt"])
            for h in range(4):
                hs = slice(h * 128, (h + 1) * 128)
                if br == 0:
                    S.add("dve", lambda e, t=t, hs=hs, h=h, i2=i2: e.scalar_tensor_tensor(
                        self.tsm[:, 1, :], otok[:, t, hs], self.stat[:, t, h, 0:1], self.gtok[:, t, hs],
                        ALU.subtract, ALU.mult), reads=[okey(t), "stat", ("gtok", t)], writes=[("tsm", 1)])
                    S.add("dve", lambda e, t=t, hs=hs, h=h, i2=i2: e.tensor_scalar(
                        self.ytok[:, i2, hs], self.tsm[:, 1, :], self.rstd[:, t, h:h + 1], None, ALU.mult),
                        reads=[("tsm", 1), "rstd"], writes=[("ytok", i2)])
                elif br == 1:
                    S.add("dve", lambda e, t=t, hs=hs, h=h, i2=i2: e.scalar_tensor_tensor(
                        self.ytok[:, i2, hs], otok[:, t, hs], self.rstd[:, t, h:h + 1], self.ggt[:, hs],
                        ALU.mult, ALU.mult), reads=[okey(t), "rstd", "ggt"], writes=[("ytok", i2)])
                else:
                    S.add("dve", lambda e, t=t, hs=hs, h=h, i2=i2: e.scalar_tensor_tensor(
                        self.ytok[:, i2, hs], otok[:, t, hs], self.rstd[:, t, h:h + 1], self.subg[:, :],
                        ALU.mult, ALU.mult), reads=[okey(t), "rstd", "subg"], writes=[("ytok", i2)])
            bt = self.bank()
            for h in range(4):
                S.add("pe", lambda e, h=h, bt=bt, i2=i2: e.transpose(
                    self.psb(bt)[:, h * 128:(h + 1) * 128], self.ytok[:, i2, h * 128:(h + 1) * 128], self.cstb[:, 4, :]),
                    reads=[("ytok", i2), "cstb"], writes=[("ps", bt)])
            S.add("act", lambda e, t=t, bt=bt: e.activation(
                self.yT[:, br, :, t * 128:(t + 1) * 128], self.psb(bt)[:, 0:512].rearrange("p (h c) -> p h c", h=4), AF.Identity),
                reads=[("ps", bt)], writes=[("yT", br, t // 4)])

    def mixer_C(self, P, l):
        S, d = self.S, self.d
        NT = P["NT"]
        lam_init = 0.8 - 0.6 * math.exp(-0.3 * l)
        S.add("dve", lambda e: e.tensor_scalar(self.subg[:, :], self.bcast[:, l, 128:256], 1.0 - lam_init, None, ALU.mult),
              reads=["bcast"], writes=["subg"])
        S.add("dve", lambda e: e.memset(self.vaug[:, :, :, 128:130], 1.0), writes=["vaug1"])
        wv, wk = self.wload(d[f"wT{l}"][4])

        def ev_v(t, b):
            S.add("act", lambda e: e.activation(self.vaug[:, t, :, 0:128], self.ps[:, b, :].rearrange("p (h v) -> p h v", h=4), AF.Identity),
                  reads=[("ps", b)], writes=[("vaug", t)])
            if P["kv_out"]:
                sl = t % 2
                S.add("dve", lambda e: e.tensor_copy(self.cstage[:, sl, :], self.ps[:, b, :]),
                      reads=[("ps", b)], writes=[("cstage", sl)])
                si, tt = t // 2, t % 2
                op = S.add("sp", lambda e: e.dma_start(
                    out=self.o["ndv"][si, l, :, tt * 128:(tt + 1) * 128, :].rearrange("h t v -> t h v"),
                    in_=self.cstage[:, sl, :].rearrange("p (h v) -> p h v", h=4)),
                    reads=[("cstage", sl)], slot=self.s_st[sl])
                self.out_ops.append(op)
        self.projT(P, wv, wk, ev_v)
        if P["kv_out"]:
            wv, wk = self.wload(d[f"wT{l}"][5])

            def ev_k(t, b):
                sl = t % 2
                S.add("dve", lambda e: e.tensor_copy(self.cstage[:, sl, :], self.ps[:, b, :]),
                      reads=[("ps", b)], writes=[("cstage", sl)])
                si, tt = t // 2, t % 2
                op = S.add("sp", lambda e: e.dma_start(
                    out=self.o["ndk"][si, l, :, tt * 128:(tt + 1) * 128, :].rearrange("g t k -> t g k"),
                    in_=self.cstage[:, sl, :].rearrange("p (g k) -> p g k", g=8)),
                    reads=[("cstage", sl)], slot=self.s_st[sl])
                self.out_ops.append(op)
            self.projT(P, wv, wk, ev_k)
        wv, wk = self.wload(d[f"wF{l}"][4])
        for h in range(4):
            self.projF(P, wv, wk, h, lambda g, b, h=h: self.evac_rope(P, self.QT, "QT", h, g, b, 1.0))
        wv, wk = self.wload(d[f"wF{l}"][5])
        for h in range(4):
            self.projF(P, wv, wk, h, lambda g, b, h=h: self.evac_rope(P, self.KT, "KT", h, g, b, 1.0))
        if P["rope"]:
            self.barrier()
        self.barrier(pool=bool(P.get("sample")))
        if P.get("sample"):
            self.attn_sample(P, l)
            self.barrier()
            self.norm_out(P, l, 2, self.otokC, lambda t: ("otokC", t))
            return
        self.bank_pool = [4, 5, 6, 7]
        for (q0, nq, keyt) in P["attn"]:
            qc = slice(q0 * 128, q0 * 128 + 256)
            for hc in range(4):
                for ji, kt in enumerate(keyt):
                    kc_ = slice(kt * 128, (kt + 1) * 128)
                    bs = [self.bank(), self.bank()]
                    for sub in range(2):
                        lo, hi = sub * 64, sub * 64 + 64
                        S.add("pe", lambda e, sub=sub, lo=lo, hi=hi, bs=bs, kc_=kc_, hc=hc: e.matmul(
                            self.ps[:, bs[sub], 0:256], self.KT[lo:hi, hc, kc_], self.QT[lo:hi, hc, qc],
                            start=True, stop=True), reads=[("KT", hc), ("QT", hc)], writes=[("ps", bs[sub])])
                    i2 = ji % 2
                    for sub in range(2):
                        S.add("act", lambda e, sub=sub, bs=bs, i2=i2: e.activation(
                            self.ET[:, i2, sub * 256:(sub + 1) * 256], self.ps[:, bs[sub], 0:256], AF.Exp, scale=0.125),
                            reads=[("ps", bs[sub])], writes=[("ET", i2, sub)])
                    for qt in range(2):
                        for sub in range(2):
                            acc = qt * 2 + sub
                            S.add("pe", lambda e, qt=qt, sub=sub, acc=acc, i2=i2, kt=kt, hc=hc, ji=ji: e.matmul(
                                self.ps[:, acc, 0:130], self.ET[:, i2, sub * 256 + qt * 128: sub * 256 + (qt + 1) * 128],
                                self.vaug[:, kt, hc, :], start=(ji == 0), stop=(ji == len(keyt) - 1)),
                                reads=[("ET", i2, sub), ("vaug", kt), "vaug1"], writes=[("ps", acc)])
                for qt in range(2):
                    t = q0 + qt
                    a1, a2 = qt * 2, qt * 2 + 1
                    S.add("dve", lambda e, a1=a1: e.reciprocal(self.sm8[:, 0:1], self.ps[:, a1, 128:129]),
                          reads=[("ps", a1)], writes=["sm8"])
                    S.add("dve", lambda e, a2=a2: e.reciprocal(self.sm8[:, 1:2], self.ps[:, a2, 128:129]),
                          reads=[("ps", a2)], writes=["sm8"])
                    S.add("dve", lambda e: e.tensor_tensor(self.sm8[:, 1:2], self.sm8[:, 1:2], self.lam[:, l:l + 1], ALU.mult),
                          reads=["sm8", "lam"], writes=["sm8"])
                    S.add("dve", lambda e, a2=a2: e.tensor_scalar(self.tsm[:, 0, :], self.ps[:, a2, 0:128], self.sm8[:, 1:2], None, ALU.mult),
                          reads=[("ps", a2), "sm8"], writes=[("tsm", 0)])
                    S.add("dve", lambda e, a1=a1, t=t, hc=hc: e.scalar_tensor_tensor(
                        self.otokC[:, t, hc * 128:(hc + 1) * 128], self.ps[:, a1, 0:128], self.sm8[:, 0:1], self.tsm[:, 0, :],
                        ALU.mult, ALU.add), reads=[("ps", a1), "sm8", ("tsm", 0)], writes=[("otokC", t)])
                    S.add("act", lambda e, t=t, hc=hc: e.activation(
                        self.tsm[:, 1, :], self.otokC[:, t, hc * 128:(hc + 1) * 128], AF.Square,
                        accum_out=self.stat[:, t, hc, 1:2]), reads=[("otokC", t)], writes=["stat", ("tsm", 1)])
        self.bank_pool = list(range(8))
        self.barrier()
        self.norm_out(P, l, 2, self.otokC, lambda t: ("otokC", t))

    def attn_sample(self, P, l):
        S, d, x_ = self.S, self.d, self.x_
        g1 = self.xgroup("kvw")
        S.add("sp", lambda e: e.dma_start(out=x_["kx_in"].rearrange("p (h t) -> p h t", h=4), in_=self.KT[:, :, :]),
              reads=[("KT", h) for h in range(4)], writes=["kx_in"], slot=g1)
        S.add("sp", lambda e: e.dma_start(out=x_["vx_in"].rearrange("p (t h v) -> p t h v", t=8, h=4), in_=self.vaug[:, :, :, 0:128]),
              reads=[("vaug", t) for t in range(8)], writes=["vx_in"], slot=g1)
        self.gather(x_["kx_in"], x_["kx_out"], "kx_in", "kx_out")
        self.gather(x_["vx_in"], x_["vx_out"], "vx_in", "vx_out")
        for kt in range(2):
            gk = self.xgroup("ck")
            S.add("sp", lambda e, kt=kt: e.dma_start(out=self.kstg[:, :], in_=d["ckc"][l, kt]), writes=["kstg"], slot=gk)
            b = self.bank()
            for hc in range(4):
                S.add("pe", lambda e, hc=hc, b=b: e.transpose(self.ps[:, b, hc * 128:(hc + 1) * 128],
                                                              self.kstg[:, hc * 128:(hc + 1) * 128], self.cst[:, 4, :]),
                      reads=["kstg", "cst"], writes=[("ps", b)])
            S.add("act", lambda e, kt=kt, b=b: e.activation(self.kctx[:, :, kt * 128:(kt + 1) * 128],
                                                            self.ps[:, b, :].rearrange("p (h c) -> p h c", h=4), AF.Identity),
                  reads=[("ps", b)], writes=["kctx"])
        gv = self.xgroup("cv")
        S.add("dve", lambda e: e.memset(self.vctx[:, :, :, 128:130], 1.0), writes=["vctx1"])
        for kt in range(2):
            S.add("pool", lambda e, kt=kt: e.dma_start(out=self.vctx[:, kt, :, 0:128], in_=d["cvc"][l, kt]),
                  writes=["vctx"], slot=gv)
        self.barrier()
        S.add("dve", lambda e: e.memset(self.VH[:, :, 128:130], 1.0), reads=["vx_in"], writes=["VH1"])
        self.bank_pool = [4, 5, 6, 7]
        for hc in range(4):
            gh = self.xgroup("kvh")
            S.add("sp", lambda e, hc=hc: e.dma_start(
                out=self.KH[:, :].rearrange("p (r c) -> p r c", r=4),
                in_=x_["kx_out"].rearrange("(r p) c -> p r c", p=128)[:, :, hc * 1024:(hc + 1) * 1024]),
                reads=["kx_out"], writes=["KH"], slot=gh)
            for r in range(4):
                S.add("sp", lambda e, hc=hc, r=r: e.dma_start(
                    out=self.VH[:, r * 8:(r + 1) * 8, 0:128],
                    in_=x_["vx_out"][r * 128:(r + 1) * 128, :].rearrange("p (t h v) -> p t h v", t=8, h=4)[:, :, hc, :]),
                    reads=["vx_out"], writes=["VH"], slot=gh)
            keyt = [("c", 0), ("c", 1)] + [("g", i) for i in range(32)]
            for qb in range(4):
                q0 = qb * 2
                qc = slice(q0 * 128, q0 * 128 + 256)
                for ji, (kind, kt) in enumerate(keyt):
                    kc_ = slice(kt * 128, (kt + 1) * 128)
                    bs = [self.bank(), self.bank()]
                    for sub in range(2):
                        lo, hi = sub * 64, sub * 64 + 64
                        ksrc = self.kctx[lo:hi, hc, kc_] if kind == "c" else self.KH[lo:hi, kc_]
                        S.add("pe", lambda e, sub=sub, lo=lo, hi=hi, bs=bs, ksrc=ksrc: e.matmul(
                            self.ps[:, bs[sub], 0:256], ksrc, self.QT[lo:hi, hc, qc], start=True, stop=True),
                            reads=["kctx" if kind == "c" else "KH", ("QT", hc)], writes=[("ps", bs[sub])])
                    i2 = ji % 2
                    for sub in range(2):
                        S.add("act", lambda e, sub=sub, bs=bs, i2=i2: e.activation(
                            self.ET[:, i2, sub * 256:(sub + 1) * 256], self.ps[:, bs[sub], 0:256], AF.Exp, scale=0.125),
                            reads=[("ps", bs[sub])], writes=[("ET", i2, sub)])
                    vsrc = self.vctx[:, kt, hc, :] if kind == "c" else self.VH[:, kt, :]
                    vkeys = ["vctx", "vctx1"] if kind == "c" else ["VH", "VH1"]
                    for qt in range(2):
                        for sub in range(2):
                            acc = qt * 2 + sub
                            S.add("pe", lambda e, qt=qt, sub=sub, acc=acc, i2=i2, vsrc=vsrc, ji=ji: e.matmul(
                                self.ps[:, acc, 0:130], self.ET[:, i2, sub * 256 + qt * 128: sub * 256 + (qt + 1) * 128],
                                vsrc, start=(ji == 0), stop=(ji == len(keyt) - 1)),
                                reads=[("ET", i2, sub)] + vkeys, writes=[("ps", acc)])
                self.attn_finalize(l, hc, q0)
        self.bank_pool = list(range(8))

    def attn_finalize(self, l, hc, q0):
        S = self.S
        for qt in range(2):
            t = q0 + qt
            a1, a2 = qt * 2, qt * 2 + 1
            S.add("dve", lambda e, a1=a1: e.reciprocal(self.sm8[:, 0:1], self.ps[:, a1, 128:129]),
                  reads=[("ps", a1)], writes=["sm8"])
            S.add("dve", lambda e, a2=a2: e.reciprocal(self.sm8[:, 1:2], self.ps[:, a2, 128:129]),
                  reads=[("ps", a2)], writes=["sm8"])
            S.add("dve", lambda e: e.tensor_tensor(self.sm8[:, 1:2], self.sm8[:, 1:2], self.lam[:, l:l + 1], ALU.mult),
                  reads=["sm8", "lam"], writes=["sm8"])
            S.add("dve", lambda e, a2=a2: e.tensor_scalar(self.tsm[:, 0, :], self.ps[:, a2, 0:128], self.sm8[:, 1:2], None, ALU.mult),
                  reads=[("ps", a2), "sm8"], writes=[("tsm", 0)])
            S.add("dve", lambda e, a1=a1, t=t, hc=hc: e.scalar_tensor_tensor(
                self.otokC[:, t, hc * 128:(hc + 1) * 128], self.ps[:, a1, 0:128], self.sm8[:, 0:1], self.tsm[:, 0, :],
                ALU.mult, ALU.add), reads=[("ps", a1), "sm8", ("tsm", 0)], writes=[("otokC", t)])
            S.add("act", lambda e, t=t, hc=hc: e.activation(
                self.tsm[:, 1, :], self.otokC[:, t, hc * 128:(hc + 1) * 128], AF.Square,
                accum_out=self.stat[:, t, hc, 1:2]), reads=[("otokC", t)], writes=["stat", ("tsm", 1)])

    def layer_norm(self, P, l, which, g, dT, sqT, tmpA, tmpB):
        S = self.S
        gc = slice(g * 512, (g + 1) * 512)
        xk = ("xT", g)
        for kc in range(8):
            S.add("act", lambda e, kc=kc: e.activation(sqT[:, kc, :], self.xT[:, kc, gc], AF.Identity), reads=[xk], writes=[("sqT", kc)])
        bm = self.bank()
        for kc in range(8):
            S.add("pe", lambda e, kc=kc: e.matmul(self.ps[:, bm, :], self.onesb[:, :], sqT[:, kc, :],
                                                  start=(kc == 0), stop=(kc == 7)),
                  reads=[("sqT", kc), "onesb"], writes=[("ps", bm)])
        for kc in range(8):
            S.add("dve", lambda e, kc=kc: e.tensor_tensor(dT[:, kc, :], self.xT[:, kc, gc], self.ps[:, bm, :], ALU.subtract),
                  reads=[xk, ("ps", bm)], writes=[("dT", kc)])
            S.add("act", lambda e, kc=kc: e.activation(sqT[:, kc, :], dT[:, kc, :], AF.Square),
                  reads=[("dT", kc)], writes=[("sqT", kc)])
        bv = self.bank()
        for kc in range(8):
            S.add("pe", lambda e, kc=kc: e.matmul(self.ps[:, bv, :], self.onesb[:, :], sqT[:, kc, :],
                                                  start=(kc == 0), stop=(kc == 7)),
                  reads=[("sqT", kc), "onesb"], writes=[("ps", bv)])
        S.add("act", lambda e: e.activation(tmpA[:], self.ps[:, bv, :], AF.Sqrt, bias=EPSA), reads=[("ps", bv)], writes=["tmpA"])
        S.add("dve", lambda e: e.reciprocal(tmpB[:], tmpA[:]), reads=["tmpA"], writes=["tmpB"])
        go = 48 + which * 8
        bo = 64 + which * 8
        for kc in range(8):
            S.add("dve", lambda e, kc=kc: e.tensor_tensor(dT[:, kc, :], dT[:, kc, :], tmpB[:], ALU.mult),
                  reads=[("dT", kc), "tmpB"], writes=[("dT", kc)])
            S.add("act", lambda e, kc=kc: e.activation(self.xT[:, kc, gc], dT[:, kc, :], AF.Identity,
                                                       scale=self.smallp[:, l, go + kc:go + kc + 1],
                                                       bias=self.smallp[:, l, bo + kc:bo + kc + 1]),
                  reads=[("dT", kc), "smallp"], writes=[xk])

    def merge_phase(self, P, l):
        S, d = self.S, self.d
        c = P["cond"]
        for oc in range(8):
            wg, kg = self.wload(d[f"wM{l}"][2 * oc], 3072)
            wb, kb = self.wload(d[f"wM{l}"][2 * oc + 1], 1536)
            for g in range(P["NG"]):
                gc = slice(g * 512, (g + 1) * 512)
                for br in range(3):
                    b = self.bank()
                    for kc in range(8):
                        S.add("pe", lambda e, b=b, kc=kc, br=br: e.matmul(
                            self.ps[:, b, :], wg[:, br * 1024 + kc * 128: br * 1024 + (kc + 1) * 128], self.hT[:, kc, gc],
                            start=(kc == 0), stop=(kc == 7)), reads=[kg, ("hT", g)], writes=[("ps", b)])
                    S.add("act", lambda e, b=b, br=br: e.activation(self.sig[:, br, :], self.ps[:, b, :], AF.Tanh, scale=0.5),
                          reads=[("ps", b)], writes=[("sig", br)])
                for br in range(3):
                    b = self.bank()
                    for kc in range(4):
                        S.add("pe", lambda e, b=b, kc=kc, br=br: e.matmul(
                            self.ps[:, b, :], wb[:, br * 512 + kc * 128: br * 512 + (kc + 1) * 128], self.yT[:, br, kc, gc],
                            start=(kc == 0), stop=(kc == 3)), reads=[kb, ("yT", br, g)], writes=[("ps", b)])
                    S.add("dve", lambda e, b=b, br=br: e.scalar_tensor_tensor(
                        self.tm3[:, br, :], self.sig[:, br, :], 1.0, self.ps[:, b, :], ALU.add, ALU.mult),
                        reads=[("sig", br), ("ps", b)], writes=[("tm3", br)])
                S.add("dve", lambda e: e.tensor_tensor(self.tm3[:, 0, :], self.tm3[:, 0, :], self.tm3[:, 1, :], ALU.add),
                      reads=[("tm3", 0), ("tm3", 1)], writes=[("tm3", 0)])
                S.add("dve", lambda e, oc=oc, gc=gc: e.tensor_tensor(self.mergedT[:, oc, gc], self.tm3[:, 0, :], self.tm3[:, 2, :], ALU.add),
                      reads=[("tm3", 0), ("tm3", 2)], writes=[("mergedT", g)])
        wo = [self.wload(d[f"wO{l}"][i]) for i in range(2)]
        for g in range(P["NG"]):
            gc = slice(g * 512, (g + 1) * 512)
            for oc in range(8):
                wv, wk = wo[oc // 4]
                ci = oc % 4
                b = self.bank()
                for kc in range(8):
                    S.add("pe", lambda e, b=b, kc=kc, wv=wv, ci=ci: e.matmul(
                        self.ps[:, b, :], wv[:, ci * 1024 + kc * 128: ci * 1024 + (kc + 1) * 128], self.mergedT[:, kc, gc],
                        start=(kc == 0), stop=(kc == 7)), reads=[wk, ("mergedT", g)], writes=[("ps", b)])
                S.add("dve", lambda e, b=b, oc=oc: e.scalar_tensor_tensor(
                    self.xT[:, oc, gc], self.ps[:, b, :], self.modT[:, l, 16 + oc, c:c + 1], self.xT[:, oc, gc],
                    ALU.mult, ALU.add), reads=[("ps", b), "modT", ("xT", g)], writes=[("xT", g)])
            self.layer_norm(P, l, 0, g, self.dT, self.sqT, self.tmpA, self.tmpB)

    def ffn(self, P, l):
        S, d = self.S, self.d
        c = P["cond"]
        T = P["T"]
        S.add("dve", lambda e: e.memset(self.aT[:, :, :], 0.0), writes=[("aT", 0), ("aT", 1)])
        wv = wk = None
        sample = bool(P.get("sample"))
        if sample:
            x_ = self.x_
            S.add("dve", lambda e: e.tensor_copy(self.hxs[:, 0:8].rearrange("p (k o) -> p k o", o=1), self.hT[:, :, 0:1]),
                  reads=[("hT", 0)], writes=["hxs"])
            S.add("dve", lambda e: e.tensor_copy(self.hxs[:, 8:16].rearrange("p (k o) -> p k o", o=1), self.hT[:, :, 1023:1024]),
                  reads=[("hT", 1)], writes=["hxs"])
            g1 = self.xgroup("hxw")
            S.add("sp", lambda e: e.dma_start(out=x_["hx_in"], in_=self.hxs[:, :]), reads=["hxs"], writes=["hx_in"], slot=g1)
            self.gather(x_["hx_in"], x_["hx_out"], "hx_in", "hx_out")
            g2 = self.xgroup("hxl")
            S.add("sp", lambda e: e.dma_start(out=self.hG[:, :, :], in_=x_["hx_out"].rearrange("(r p) c -> p r c", p=128)),
                  reads=["hx_out"], writes=["hG"], slot=g2)
            S.add("dve", lambda e: e.memset(self.hhf[:, :, :], 0.0), writes=["hhf"])
            for j in range(4):
                S.add("dve", lambda e, j=j: e.scalar_tensor_tensor(
                    self.hhf[:, :, 0], self.hG[:, j, 8:16], self.meta[:, 6 + j:7 + j], self.hhf[:, :, 0], ALU.mult, ALU.add),
                    reads=["hG", "meta", "hhf"], writes=["hhf"])
                S.add("dve", lambda e, j=j: e.scalar_tensor_tensor(
                    self.hhf[:, :, 1], self.hG[:, j, 0:8], self.meta[:, 10 + j:11 + j], self.hhf[:, :, 1], ALU.mult, ALU.add),
                    reads=["hG", "meta", "hhf"], writes=["hhf"])
            S.add("dve", lambda e: e.tensor_copy(self.hh[:, :, :], self.hhf[:, :, :]), reads=["hhf"], writes=["hh"])
        for fc in range(NFC):
            if fc % 2 == 0:
                wv, wk = self.wload(d[f"wU{l}"][fc // 2])
            ca, cb = (fc % 2) * 2, (fc % 2) * 2 + 1
            a2 = fc % 2
            bbs = {}
            for g in range(P["NG"]):
                gc = slice(g * 512, (g + 1) * 512)
                ba = self.bank()
                for kc in range(8):
                    S.add("pe", lambda e, b=ba, kc=kc, ci=ca, wv=wv, gc=gc: e.matmul(
                        self.ps[:, b, :], wv[:, ci * 1024 + kc * 128: ci * 1024 + (kc + 1) * 128], self.hT[:, kc, gc],
                        start=(kc == 0), stop=(kc == 7)), reads=[wk, ("hT", g)], writes=[("ps", ba)])
                for (s0, n, col0) in P["conv_rows"][g]:
                    S.add("act", lambda e, s0=s0, n=n, col0=col0, ba=ba: e.activation(
                        self.aT[:, a2, col0:col0 + n], self.ps[:, ba, s0:s0 + n], AF.Identity),
                        reads=[("ps", ba)], writes=[("aT", a2)])
            if sample:
                bh = self.bank()
                for kc in range(8):
                    S.add("pe", lambda e, kc=kc, wv=wv, bh=bh: e.matmul(
                        self.ps[:, bh, 0:2], wv[:, ca * 1024 + kc * 128: ca * 1024 + (kc + 1) * 128], self.hh[:, kc, :],
                        start=(kc == 0), stop=(kc == 7)), reads=[wk, "hh"], writes=[("ps", bh)])
                S.add("dve", lambda e, bh=bh: e.tensor_scalar(self.aT[:, a2, 0:1], self.ps[:, bh, 0:1], self.meta[:, 4:5], None, ALU.mult),
                      reads=[("ps", bh), "meta"], writes=[("aT", a2)])
                S.add("dve", lambda e, bh=bh: e.tensor_scalar(self.aT[:, a2, 1025:1026], self.ps[:, bh, 1:2], self.meta[:, 5:6], None, ALU.mult),
                      reads=[("ps", bh), "meta"], writes=[("aT", a2)])
            for g in range(P["NG"]):
                gc = slice(g * 512, (g + 1) * 512)
                bb = self.bank()
                for kc in range(8):
                    S.add("pe", lambda e, b=bb, kc=kc, ci=cb, wv=wv, gc=gc: e.matmul(
                        self.ps[:, b, :], wv[:, ci * 1024 + kc * 128: ci * 1024 + (kc + 1) * 128], self.hT[:, kc, gc],
                        start=(kc == 0), stop=(kc == 7)), reads=[wk, ("hT", g)], writes=[("ps", bb)])
                cw = 80
                for (s0, n, col0) in P["conv_rows"][g]:
                    S.add("dve", lambda e, s0=s0, n=n, col0=col0: e.tensor_scalar(
                        self.cT[:, a2, s0:s0 + n], self.aT[:, a2, col0 - 1:col0 - 1 + n],
                        self.smallp[:, l, cw + fc:cw + fc + 1], self.smallp[:, l, cw + 66 + fc:cw + 66 + fc + 1],
                        ALU.mult, ALU.add), reads=[("aT", a2), "smallp"], writes=[("cT", a2)])
                    for k in (1, 2):
                        S.add("dve", lambda e, s0=s0, n=n, col0=col0, k=k: e.scalar_tensor_tensor(
                            self.cT[:, a2, s0:s0 + n], self.aT[:, a2, col0 - 1 + k:col0 - 1 + k + n],
                            self.smallp[:, l, cw + k * NFC + fc:cw + k * NFC + fc + 1], self.cT[:, a2, s0:s0 + n],
                            ALU.mult, ALU.add), reads=[("aT", a2), "smallp", ("cT", a2)], writes=[("cT", a2)])
                S.add("act", lambda e: e.activation(self.glT[:, a2, :], self.cT[:, a2, :], AF.Gelu_apprx_tanh),
                      reads=[("cT", a2)], writes=[("glT", a2)])
                S.add("dve", lambda e, bb=bb, gc=gc: e.tensor_tensor(self.gT[:, fc, gc], self.glT[:, a2, :], self.ps[:, bb, :], ALU.mult),
                      reads=[("glT", a2), ("ps", bb)], writes=[("gT", g)])
        for oc in range(8):
            wv, wk = self.wload(d[f"wD{l}"][oc], NFC * 128)
            for g in range(P["NG"]):
                gc = slice(g * 512, (g + 1) * 512)
                b = self.bank()
                for fc in range(NFC):
                    S.add("pe", lambda e, b=b, fc=fc, wv=wv: e.matmul(
                        self.ps[:, b, :], wv[:, fc * 128:(fc + 1) * 128], self.gT[:, fc, gc],
                        start=(fc == 0), stop=(fc == NFC - 1)), reads=[wk, ("gT", g)], writes=[("ps", b)])
                S.add("dve", lambda e, b=b, oc=oc, gc=gc: e.scalar_tensor_tensor(
                    self.xT[:, oc, gc], self.ps[:, b, :], self.modT[:, l, 40 + oc, c:c + 1], self.xT[:, oc, gc],
                    ALU.mult, ALU.add), reads=[("ps", b), "modT", ("xT", g)], writes=[("xT", g)])
        for g in range(P["NG"]):
            self.layer_norm(P, l, 1, g, self.dT2, self.sqT2, self.tmpA2, self.tmpB2)

    def store_x(self, P):
        S = self.S
        dst = self.o[P["y"]]
        for t in range(P["NT"]):
            sl = t % 2
            for half in range(2):
                b = self.bank()
                for q in range(4):
                    kc = half * 4 + q
                    S.add("pe", lambda e, b=b, q=q, kc=kc, t=t: e.transpose(
                        self.ps[:, b, q * 128:(q + 1) * 128], self.xT[:, kc, t * 128:(t + 1) * 128], self.cst[:, 4, :]),
                        reads=[("xT", t // 4), "cst"], writes=[("ps", b)])
                S.add("act" if half == 0 else "dve",
                      (lambda e, b=b, half=half, sl=sl: e.activation(self.stage[:, sl, half * 512:(half + 1) * 512], self.ps[:, b, :], AF.Identity))
                      if half == 0 else
                      (lambda e, b=b, half=half, sl=sl: e.tensor_copy(self.stage[:, sl, half * 512:(half + 1) * 512], self.ps[:, b, :])),
                      reads=[("ps", b)], writes=[("stage", sl)])
            op = S.add("sp", lambda e, t=t, sl=sl: e.dma_start(out=dst[t * 128:(t + 1) * 128, :], in_=self.stage[:, sl, :]),
                       reads=[("stage", sl)], slot=self.s_io[sl])
            self.out_ops.append(op)

    def run_pass(self, P):
        stop = int(os.environ.get("KSTOP", "99"))
        if stop < 1:
            return
        self.load_x(P)
        self.barrier()
        for l in range(L):
            if stop < 2 + 10 * l:
                break
            self.modulate(P, l, 0)
            self.mixer_AB(P, l, 0)
            self.barrier()
            if stop < 3 + 10 * l:
                break
            self.mixer_AB(P, l, 1)
            self.barrier()
            if stop < 4 + 10 * l:
                break
            self.mixer_C(P, l)
            self.barrier()
            if stop < 5 + 10 * l:
                break
            self.merge_phase(P, l)
            self.modulate(P, l, 1)
            self.barrier()
            if stop < 6 + 10 * l:
                break
            self.ffn(P, l)
            self.barrier()
        self.store_x(P)
        self.barrier()

    def build(self):
        self.declare()
        self.init_phase()
        PP = dict(T=512, NT=4, NG=1, cond=0, x="xp", y="yp", rope=False, states_out=True, kv_out=True,
                  seqs=[(0, 2), (2, 2)],
                  attn=[(0, 2, [0, 1]), (2, 2, [2, 3])],
                  conv_rows=[[(0, 256, 1), (256, 256, 515)]])
        if not os.environ.get("KNOPROMPT"):
            self.run_pass(PP)
        if self.do_sample:
            PS = dict(T=1024, NT=8, NG=2, cond=1, x="xs", y="ys", rope=True, states_out=False, kv_out=False,
                      seqs=[(0, 8)], sample=True, attn=[],
                      conv_rows=[[(0, 512, 1)], [(0, 512, 513)]])
            self.run_pass(PS)
        self.S.wait_for("sp", self.out_ops)
        global LAST_S
        LAST_S = self.S
        self.S.emit(self.nc, self.st)


_CACHE = {}


def _build_nc(do_sample=True):
    key = ("nc", do_sample)
    if key not in _CACHE:
        nc = bass.Bass("TRN2", target_bir_lowering=False)
        with contextlib.ExitStack() as st:
            b = Builder(nc, st, do_sample=do_sample)
            b.build()
        _CACHE[key] = nc
    return _CACHE[key]


def kernel(**inp):
    inp = {k: np.asarray(v) for k, v in inp.items()}
    W = _host_weights(inp)
    nc = _build_nc(not os.environ.get("KNOSAMPLE"))
    in_maps = []
    ncores = int(os.environ.get("KCORES", str(NCORES)))
    for c in range(ncores):
        b, r = c // 4, c % 4
        m = dict(W)
        m["xp"] = np.ascontiguousarray(inp["x_prompt"][2 * c:2 * c + 2].reshape(512, D))
        m["xs"] = np.ascontiguousarray(inp["x_sample"][b, r * 1024:(r + 1) * 1024])
        cond = np.stack([inp["c_ctx"], inp["c"][b]], axis=-1)
        m["cond"] = np.ascontiguousarray(cond.reshape(8, 128, 2).transpose(1, 0, 2), dtype=np.float32)
        rc, rs = _rope_tables(r * 1024)
        m["ropec"], m["ropes"] = rc, rs
        meta = np.zeros((128, 16), np.float32)
        for s_ in range(4):
            meta[0:64, s_] = 1.0 if r == s_ else 0.0
            meta[64:128, s_] = 1.0 if r == 3 - s_ else 0.0
            meta[:, 6 + s_] = 1.0 if s_ == r - 1 else 0.0
            meta[:, 10 + s_] = 1.0 if s_ == r + 1 else 0.0
        meta[:, 4] = 1.0 if r > 0 else 0.0
        meta[:, 5] = 1.0 if r < 3 else 0.0
        m["meta"] = meta
        for nm, src in (("sret", inp["state_ret"]), ("sgla", inp["state_gla"])):
            a = src[b]
            m[nm] = np.ascontiguousarray(a.transpose(0, 1, 3, 2, 4).reshape(L, 128, 4, 128), dtype=np.float32)
        ck = inp["cache_diff_k"][b]
        m["ckc"] = np.ascontiguousarray(ck.reshape(L, 8, 2, 128, 64).transpose(0, 2, 3, 1, 4).reshape(L, 2, 128, 512), dtype=np.float32)
        cv = inp["cache_diff_v"][b]
        m["cvc"] = np.ascontiguousarray(cv.reshape(L, 4, 2, 128, 128).transpose(0, 2, 3, 1, 4), dtype=np.float32)
        in_maps.append(m)
    res = run_bass_kernel_spmd(nc, in_maps, core_ids=list(range(ncores)))
    R = list(res.results)
    while len(R) < NCORES:
        R.append(R[0])
    y_prompt = np.concatenate([R[c]["yp"].reshape(2, 256, D) for c in range(NCORES)], axis=0)
    y_sample = np.stack([np.concatenate([R[b * 4 + r]["ys"] for r in range(4)], axis=0) for b in range(2)], axis=0)
    ndk = np.concatenate([R[c]["ndk"] for c in range(NCORES)], axis=0)
    ndv = np.concatenate([R[c]["ndv"] for c in range(NCORES)], axis=0)
    nsr = np.concatenate([R[c]["nsr"] for c in range(NCORES)], axis=0)
    nsg = np.concatenate([R[c]["nsg"] for c in range(NCORES)], axis=0)
    f = np.float32
    return (y_prompt.astype(f), y_sample.astype(f), ndk.astype(f), ndv.astype(f), nsr.astype(f), nsg.astype(f))
```

```python
import contextlib
import os
import math
import numpy as np
import concourse.bass as bass
import concourse.mybir as mybir
from concourse.bass_utils import run_bass_kernel_spmd

F32 = mybir.dt.float32
BF16 = mybir.dt.bfloat16
AF = mybir.ActivationFunctionType
ALU = mybir.AluOpType

NCORES = 8
D = 1024
L = 2
DIN = 7680
DFF = 2816
NFC = 22
ALPHA = (2 * L) ** 0.25
EPS = 1e-5
EPSA = EPS / (ALPHA * ALPHA)
WSZ = 4096
NSLOT = 3
UW = 21504


class _Op:
    __slots__ = ("eng", "fn", "deps", "slot", "cnt", "need_inc", "inc")

    def __init__(self, eng, fn, slot, inc):
        self.eng = eng
        self.fn = fn
        self.deps = set()
        self.slot = slot
        self.cnt = 0
        self.need_inc = False
        self.inc = inc


class _Slot:
    def __init__(self, name, wait_all):
        self.name = name
        self.wait_all = wait_all
        self.total = 0
        self.sem = None


DEBUG_WAITS = None
LAST_S = None


class _Rec:
    def __init__(self):
        self.call = None

    def __getattr__(self, name):
        def f(*a, **k):
            self.call = (name, a, k)
            return None
        return f


def _freeze(fn):
    r = _Rec()
    fn(r)
    call = r.call
    if call is None:
        return lambda e: None
    name, a, k = call
    return lambda e: getattr(e, name)(*a, **k)


class Sched:
    ENGS = ("pe", "act", "dve", "pool", "sp")

    def __init__(self):
        self.ops = {e: [] for e in self.ENGS}
        self.lastw = {}
        self.readers = {}
        self.slots = []

    def slot(self, name, wait_all=False):
        s = _Slot(name, wait_all)
        self.slots.append(s)
        return s

    def add(self, eng, fn, reads=(), writes=(), slot=None, inc=16):
        op = _Op(eng, _freeze(fn), slot, inc)
        deps = op.deps
        for r in reads:
            w = self.lastw.get(r)
            if w is not None:
                deps.add(w)
            if isinstance(r, tuple) and r[0] == "ps":
                for rd in self.readers.get(r, ()):
                    if rd.eng != eng:
                        deps.add(rd)
        for k in writes:
            w = self.lastw.get(k)
            if w is not None:
                deps.add(w)
            for rd in self.readers.get(k, ()):
                deps.add(rd)
        for r in reads:
            self.readers.setdefault(r, []).append(op)
        for k in writes:
            self.lastw[k] = op
            self.readers[k] = []
        if slot is not None:
            slot.total += inc
            op.cnt = slot.total
        self.ops[eng].append(op)
        return op

    def wait_for(self, eng, ops):
        op = _Op(eng, lambda e: None, None, 0)
        op.deps = set(o for o in ops if o is not None)
        self.ops[eng].append(op)
        return op

    def emit(self, nc, stack):
        for e in self.ENGS:
            for op in self.ops[e]:
                for d in op.deps:
                    if d.slot is None:
                        if d.eng == op.eng and op.slot is None and d.eng == "pe":
                            continue
                        d.need_inc = True
        esem = {}
        for e in self.ENGS:
            c = 0
            for op in self.ops[e]:
                if op.slot is None and op.need_inc:
                    c += 1
                    op.cnt = c
            if c > 0:
                esem[e] = stack.enter_context(nc.semaphore("es_" + e))
        for s in self.slots:
            if s.total > 0:
                s.sem = stack.enter_context(nc.semaphore("sl_" + s.name))
        block = stack.enter_context(nc.Block())

        def body(e):
            def run(eng):
                seen = {}
                for op in self.ops[e]:
                    waits = {}
                    for d in op.deps:
                        if d.slot is not None:
                            if d.slot is op.slot and d.slot.wait_all:
                                continue
                            key = d.slot
                            val = d.slot.total if d.slot.wait_all else d.cnt
                            sem = d.slot.sem
                        else:
                            if d.eng == e and op.slot is None and e == "pe":
                                continue
                            key = d.eng
                            val = d.cnt
                            sem = esem[d.eng]
                        if val > waits.get(key, (None, 0))[1]:
                            waits[key] = (sem, val)
                    for key in sorted(waits.keys(), key=lambda k: k if isinstance(k, str) else "~" + k.name):
                        sem, val = waits[key]
                        if seen.get(key, 0) >= val:
                            continue
                        seen[key] = val
                        eng.wait_ge(sem, val)
                        if DEBUG_WAITS is not None:
                            DEBUG_WAITS.append((e, len(DEBUG_WAITS), key if isinstance(key, str) else key.name, val))
                    inst = op.fn(eng)
                    if op.slot is not None:
                        inst.then_inc(op.slot.sem, op.inc)
                    elif op.need_inc:
                        inst.then_inc(esem[e], 1)
            return run

        block.tensor(body("pe"))
        block.scalar(body("act"))
        block.vector(body("dve"))
        block.gpsimd(body("pool"))
        block.sync(body("sp"))


PTS = np.cumsum([0, 256, 256, 512, 512, 256, 256, 512, 512, 512, 512, 512, 1024, 1024, 1024])
(O_AQ, O_AK, O_AV, O_AG, O_BQ, O_BK, O_BV, O_BR, O_CQ, O_CK, O_CV, O_MA, O_MB, O_MC) = [int(v) for v in PTS[:-1]]


def _fchunk(w, cols):
    sub = w[:, cols]
    return sub.reshape(8, 128, sub.shape[1]).transpose(1, 0, 2)


def _pack(chunks, per):
    out = []
    for i in range(0, len(chunks), per):
        blk = np.concatenate([c.reshape(128, -1) for c in chunks[i:i + per]], axis=1)
        assert blk.shape[1] <= WSZ
        if blk.shape[1] < WSZ:
            blk = np.concatenate([blk, np.zeros((128, WSZ - blk.shape[1]), np.float32)], axis=1)
        out.append(blk)
    return np.ascontiguousarray(np.stack(out, 0), dtype=np.float32)


def _host_weights(inp):
    W = {}
    ar = np.arange
    for l in range(L):
        w_in = inp["w_in"][l]
        ch = []
        for (oq, ok) in ((O_AQ, O_AK), (O_BQ, O_BK)):
            for o in (oq, ok):
                for h in range(4):
                    cols = np.concatenate([o + h * 64 + ar(64), o + h * 64 + ar(64)])
                    ch.append(_fchunk(w_in, cols))
        for o in (O_CQ, O_CK):
            for h in range(4):
                ch.append(_fchunk(w_in, o + h * 128 + ar(128)))
        W[f"wF{l}"] = _pack(ch, 4)
        ch = [_fchunk(w_in, o + ar(512)) for o in (O_AV, O_AG, O_BV, O_BR, O_CV, O_CK)]
        W[f"wT{l}"] = _pack(ch, 1)
        ch = []
        wb = inp["w_branch"][l]
        for oc in range(8):
            g = [_fchunk(w_in, o + oc * 128 + ar(128)).reshape(128, -1) for o in (O_MA, O_MB, O_MC)]
            ch.append(np.concatenate(g, axis=1))
            b = [wb[br][:, oc * 128:(oc + 1) * 128].reshape(4, 128, 128).transpose(1, 0, 2).reshape(128, -1)
                 for br in range(3)]
            ch.append(np.concatenate(b, axis=1))
        W[f"wM{l}"] = _pack(ch, 1)
        wo = inp["w_out"][l]
        W[f"wO{l}"] = _pack([_fchunk(wo, oc * 128 + ar(128)) for oc in range(8)], 4)
        wu = inp["ffn_w_up"][l]
        ch = []
        for fc in range(NFC):
            ch.append(_fchunk(wu, fc * 128 + ar(128)))
            ch.append(_fchunk(wu, DFF + fc * 128 + ar(128)))
        W[f"wU{l}"] = _pack(ch, 4)
        wd = inp["ffn_w_down"][l]
        ch = [wd[:, oc * 128:(oc + 1) * 128].reshape(NFC, 128, 128).transpose(1, 0, 2) for oc in range(8)]
        W[f"wD{l}"] = _pack(ch, 1)
        aw = inp["ada_w"][l]
        W[f"wA{l}"] = _pack([_fchunk(aw, j * 512 + ar(512)) for j in range(12)], 1)
    wa1 = np.zeros((128, L, 8, 32), np.float32)
    wa2 = np.zeros((33, L, 512), np.float32)
    for l in range(L):
        for e in range(2):
            wa1[:, l, :, e * 16:(e + 1) * 16] = inp["gla_wa1"][l, e].reshape(8, 128, 16).transpose(1, 0, 2)
            for h in range(4):
                c0 = h * 128 + e * 64
                wa2[e * 16:(e + 1) * 16, l, c0:c0 + 64] = inp["gla_wa2"][l, e][:, h * 64:(h + 1) * 64]
                wa2[32, l, c0:c0 + 64] = inp["gla_ba"][l, e][h * 64:(h + 1) * 64]
    W["wa1"] = wa1
    W["wa2"] = wa2
    sp = []
    for l in range(L):
        cols = [inp["ada_b"][l].reshape(48, 128).T,
                inp["ln_g"][l].reshape(16, 128).T,
                inp["ln_b"][l].reshape(16, 128).T,
                inp["ffn_conv_w"][l].reshape(3 * NFC, 128).T,
                inp["ffn_conv_b"][l].reshape(NFC, 128).T]
        sp.append(np.concatenate(cols, axis=1))
    W["smallp"] = np.ascontiguousarray(np.stack(sp, 1), dtype=np.float32)
    bc = []
    for l in range(L):
        row = np.concatenate([inp["gla_norm_g"][l], inp["diff_subln_g"][l],
                              inp["diff_lam"][l].reshape(-1), inp["ret_decay"][l].reshape(-1)])
        bc.append(np.broadcast_to(row[None, :], (128, row.shape[0])))
    W["bcast"] = np.ascontiguousarray(np.stack(bc, 1), dtype=np.float32)
    j = ar(128)[:, None]
    i = ar(128)[None, :]
    cs = np.zeros((128, 9, 128), np.float32)
    cs[:, 8] = 1.0
    cs[:, 0] = (j <= i)
    cs[:, 1] = (j >= i)
    cs[:, 2] = (j > i)
    cs[:, 3] = (j < i)
    cs[:, 4] = (j == i)
    m = ar(128)
    sw = np.where((m % 32) < 16, m + 16, m - 16)
    cs[sw, 5, m] = 1.0
    cs[:, 6] = (j == i) & (j < 64)
    cs[:, 7] = (j == i) & (j >= 64)
    W["consts"] = cs
    return W


def _rope_tables(t0):
    t = t0 + np.arange(1024)
    row = (t // 64).astype(np.float32)
    col = (t % 64).astype(np.float32)
    inv = (np.float32(10000.0) ** (-np.arange(16, dtype=np.float32) / np.float32(16))).astype(np.float32)
    cos = np.zeros((128, 1024), np.float32)
    sin = np.zeros((128, 1024), np.float32)
    for p in range(128):
        d = p % 64
        pos = row if d < 32 else col
        ang = (pos * inv[d % 16]).astype(np.float32)
        cos[p] = np.cos(ang)
        s = np.sin(ang)
        sin[p] = -s if (d % 32) < 16 else s
    return cos, sin


class Builder:
    def __init__(self, nc, st, do_sample=True):
        self.nc = nc
        self.st = st
        self.S = Sched()
        self.do_sample = do_sample
        self.wslot_i = 0
        self.out_ops = []
        self.out_seen = 0

    def dram_in(self, name, shape, dt=F32):
        return self.nc.dram_tensor(name, list(shape), dt, kind="ExternalInput").ap()

    def dram_out(self, name, shape, dt=F32):
        return self.nc.dram_tensor(name, list(shape), dt, kind="ExternalOutput").ap()

    def sb(self, name, shape, dt):
        return self.st.enter_context(self.nc.sbuf_tensor(name, list(shape), dt))

    def wload(self, src_blk, nelem=WSZ):
        s = self.wslot_i % NSLOT
        self.wslot_i += 1
        dst = self.wring[:, s, 0:nelem]
        key = ("w", s)
        self.S.add("pool", lambda e: e.dma_start(out=dst, in_=src_blk[:, 0:nelem]), writes=[key], slot=self.wsl[s])
        return self.wring[:, s, :], key

    def xgroup(self, name):
        self.xs_i += 1
        return self.S.slot("x%d_%s" % (self.xs_i, name), wait_all=True)

    def gather(self, src, dst, rk, wk):
        sl = self.xgroup("cc")
        groups = [[0, 1, 2, 3], [4, 5, 6, 7]] if not os.environ.get("KG4") else [[0, 1, 2, 3]]
        self.S.add("pool", lambda e: e.collective_compute("AllGather", ALU.bypass, replica_groups=groups,
                                                          ins=[src], outs=[dst]),
                   reads=[rk], writes=[wk], slot=sl, inc=1)

    def declare(self):
        nc = self.nc
        di = self.dram_in
        self.d = d = {}
        d["xp"] = di("xp", [512, D])
        d["xs"] = di("xs", [1024, D])
        d["cond"] = di("cond", [128, 8, 2])
        for l in range(L):
            d[f"wF{l}"] = di(f"wF{l}", [6, 128, WSZ])
            d[f"wT{l}"] = di(f"wT{l}", [6, 128, WSZ])
            d[f"wM{l}"] = di(f"wM{l}", [16, 128, WSZ])
            d[f"wO{l}"] = di(f"wO{l}", [2, 128, WSZ])
            d[f"wU{l}"] = di(f"wU{l}", [11, 128, WSZ])
            d[f"wD{l}"] = di(f"wD{l}", [8, 128, WSZ])
            d[f"wA{l}"] = di(f"wA{l}", [12, 128, WSZ])
        d["wa1"] = di("wa1", [128, L, 8, 32])
        d["wa2"] = di("wa2", [33, L, 512])
        d["smallp"] = di("smallp", [128, L, 168])
        d["bcast"] = di("bcast", [128, L, 520])
        d["consts"] = di("consts", [128, 9, 128])
        d["ropec"] = di("ropec", [128, 1024])
        d["ropes"] = di("ropes", [128, 1024])
        d["meta"] = di("meta", [128, 16])
        d["sret"] = di("sret", [L, 128, 4, 128])
        d["sgla"] = di("sgla", [L, 128, 4, 128])
        d["ckc"] = di("ckc", [L, 2, 128, 512])
        d["cvc"] = di("cvc", [L, 2, 128, 4, 128])
        dint = lambda name, shape, dt: nc.dram_tensor(name, list(shape), dt, addr_space="Local", kind="Internal").ap()
        self.x_ = x_ = {}
        x_["kx_in"] = dint("kx_in", [128, 4096], BF16)
        x_["kx_out"] = dint("kx_out", [512, 4096], BF16)
        x_["vx_in"] = dint("vx_in", [128, 4096], BF16)
        x_["vx_out"] = dint("vx_out", [512, 4096], BF16)
        x_["st_in"] = dint("st_in", [128, 516], F32)
        x_["st_out"] = dint("st_out", [512, 516], F32)
        x_["hx_in"] = dint("hx_in", [128, 16], BF16)
        x_["hx_out"] = dint("hx_out", [512, 16], BF16)
        do = self.dram_out
        self.o = o = {}
        o["yp"] = do("yp", [512, D])
        o["ys"] = do("ys", [1024, D])
        o["ndk"] = do("ndk", [2, L, 8, 256, 64])
        o["ndv"] = do("ndv", [2, L, 4, 256, 128])
        o["nsr"] = do("nsr", [2, L, 2, 4, 64, 128])
        o["nsg"] = do("nsg", [2, L, 2, 4, 64, 128])


        sb = self.sb
        self.xT = sb("xT", [128, 8, 1024], F32)
        self.hT = sb("hT", [128, 8, 1024], BF16)
        self.wring = sb("wring", [128, NSLOT, WSZ], BF16)
        self.wsl = [self.S.slot(f"w{s}") for s in range(NSLOT)]
        self.cst = sb("cst", [128, 9, 128], F32)
        self.cstb = sb("cstb", [128, 9, 128], BF16)
        self.onesb = sb("onesb", [128, 128], BF16)
        self.smallp = sb("smallp_sb", [128, L, 168], F32)
        self.bcast = sb("bcast_sb", [128, L, 520], F32)
        self.modT = sb("modT", [128, L, 48, 2], F32)
        self.condf = sb("condf", [128, 8, 2], F32)
        self.condb = sb("condb", [128, 8, 2], BF16)
        self.modtok = sb("modtok", [2, 512], F32)
        self.wa1 = sb("wa1_sb", [128, L, 8, 32], BF16)
        self.wa2 = sb("wa2_sb", [33, L, 512], BF16)
        self.yT = sb("yT", [128, 3, 4, 1024], BF16)
        if self.do_sample:
            self.ropec = sb("ropec_sb", [128, 1024], F32)
            self.ropes = sb("ropes_sb", [128, 1024], F32)
        self.lam = sb("lam", [128, 8], F32)
        self.retsp = sb("retsp", [128, L, 8], F32)
        self.subg = sb("subg", [128, 128], F32)
        self.ps = self.st.enter_context(nc.psum_tensor("ps", [128, 8, 512], F32))
        self.s_in = self.S.slot("init", wait_all=True)
        self.s_inp = self.S.slot("initp", wait_all=True)
        self.s_io = [self.S.slot(f"io{i}") for i in range(2)]
        self.s_st = [self.S.slot(f"st{i}") for i in range(2)]
        self.s_out = self.S.slot("out_misc")
        self.s_cc = self.S.slot("cc")
        self.s_xs = [self.S.slot(f"xs{i}") for i in range(4)]
        self.xs_i = 0
        self.meta = sb("meta_sb", [128, 16], F32)
        U = sb("U", [128, UW], F32)
        self.U = U

        def vb(off, shape):
            n = int(np.prod(shape)) // 2
            a = U[:, off:off + n].bitcast(BF16)
            return a if len(shape) == 1 else a.rearrange(
                "p (" + " ".join("abcd"[:len(shape)]) + ") -> p " + " ".join("abcd"[:len(shape)]),
                **{k: v for k, v in zip("abcd", shape)})

        def vf(off, shape):
            n = int(np.prod(shape))
            a = U[:, off:off + n]
            return a if len(shape) == 1 else a.rearrange(
                "p (" + " ".join("abcd"[:len(shape)]) + ") -> p " + " ".join("abcd"[:len(shape)]),
                **{k: v for k, v in zip("abcd", shape)})

        self.QT = vb(0, [4, 1024])
        self.otokAB = vb(0, [8, 512])
        self.dbf = vb(0, [4, 8, 128])
        self.stage = vf(0, [2, 1024])
        self.KT = vb(2048, [4, 1024])
        self.sstk = vb(2048, [4, 8, 128])
        self.vtok = vb(4096, [8, 512])
        self.gtok = vb(6144, [8, 512])
        self.otokC = vb(6144, [8, 512])
        self.qd = vb(8192, [4, 8, 128])
        self.Pm = vb(10240, [4, 8, 128])
        self.ropex = vb(10240, [512])
        self.ropet = vf(10496, [512])
        self.ropeu = vf(11008, [512])
        self.cstage = vf(10240, [2, 512])
        self.KH = vb(8192, [4096])
        self.VH = vb(12288, [32, 130])
        self.vctx = vb(10240, [2, 4, 130])
        self.kctx = vb(10760, [4, 256])
        self.kstg = vf(11272, [512])
        self.delta = vf(12288, [4, 8, 128])
        self.vaug = vb(12288, [8, 4, 130])
        self.ET = vb(12288 + 2080, [2, 512])
        o_ = 12288 + 4160
        self.s32 = vf(o_, [2, 4, 128]); o_ += 1024
        self.EqT = vf(o_, [4, 128]); o_ += 512
        self.EkT = vf(o_, [4, 128]); o_ += 512
        self.Ekd = vf(o_, [4, 128]); o_ += 512
        self.dcol = vf(o_, [4, 8]); o_ += 32
        self.latok = vf(o_, [512]); o_ += 512
        self.uT = vb(o_, [1024]); o_ += 512
        self.stat = vf(o_, [8, 4, 2]); o_ += 64
        self.rstd = vf(o_, [8, 4]); o_ += 32
        self.st6 = vf(o_, [6]); o_ += 8
        self.ksb = vb(o_, [2, 128]); o_ += 128
        self.kdtok = vb(o_, [2, 128]); o_ += 128
        self.ytok = vb(o_, [2, 512]); o_ += 512
        self.tsm = vf(o_, [2, 128]); o_ += 256
        self.sm8 = vf(o_, [16]); o_ += 16
        self.ggt = vb(o_, [512]); o_ += 256
        assert o_ <= UW, o_
        self.Sin = self.EqT
        self.Srun = self.EkT
        self.GL = self.Ekd
        self.sinb = self.uT[:, 0:512].rearrange("p (h v) -> p h v", h=4)
        self.cp = self.latok[:, 0:32].rearrange("p (h t) -> p h t", h=4)
        self.Dg = self.latok[:, 32:48].rearrange("p (s h) -> p s h", s=4)
        self.hxs = self.latok[:, 64:72].bitcast(BF16)
        self.hG = self.latok[:, 72:104].bitcast(BF16).rearrange("p (r c) -> p r c", r=4)
        self.hhf = self.latok[:, 104:120].rearrange("p (k c) -> p k c", k=8)
        self.hh = self.latok[:, 120:128].bitcast(BF16).rearrange("p (k c) -> p k c", k=8)
        self.mergedT = vb(0, [8, 1024])
        self.dT = vf(4096, [8, 512])
        self.sqT = vb(8192, [8, 512])
        self.tmpA = vf(10240, [512])
        self.tmpB = vf(10752, [512])
        self.sig = vf(11264, [3, 512])
        self.tm3 = vf(12800, [3, 512])
        self.gT = vb(0, [NFC, 1024])
        self.aT = vf(11264, [2, 1028])
        self.cT = vf(13320, [2, 512])
        self.glT = vf(14344, [2, 512])
        self.dT2 = vf(15368, [8, 512])
        self.tmpA2 = vf(19464, [512])
        self.tmpB2 = vf(19976, [512])
        self.sqT2 = vb(11264, [8, 512])
        assert 20488 <= UW
        self.bank_i = 0
        self.bank_pool = list(range(8))

    def bank(self):
        b = self.bank_pool[self.bank_i % len(self.bank_pool)]
        self.bank_i += 1
        return b

    def barrier(self, pool=False):
        S = self.S
        last = {}
        for e in S.ENGS:
            last[e] = None
            for op in reversed(S.ops[e]):
                if op.inc != 0:
                    last[e] = op
                    break
        dmas = [op for op in self.out_ops[self.out_seen:]]
        self.out_seen = len(self.out_ops)
        for e in (("pe", "act", "dve", "pool", "sp") if (pool or os.environ.get("KPOOLBAR")) else ("pe", "act", "dve", "sp")):
            deps = [last[o] for o in S.ENGS if (o != e or e != "pe") and last[o] is not None and last[o].slot is None]
            S.wait_for(e, deps + dmas)
        if os.environ.get("KNOCLR"):
            return
        keepw = lambda k: isinstance(k, tuple) and k[0] == "w"
        for k in list(S.lastw.keys()):
            if S.lastw[k].slot is None and not keepw(k):
                del S.lastw[k]
        for k in list(S.readers.keys()):
            if not keepw(k):
                S.readers[k] = [r for r in S.readers[k] if r.slot is not None]

    def psb(self, b):
        return self.ps[:, b, :].bitcast(BF16)

    def init_phase(self):
        S, d = self.S, self.d
        si = self.s_in
        S.add("sp", lambda e: e.dma_start(out=self.cst[:], in_=d["consts"]), writes=["cst"], slot=si)
        S.add("pool", lambda e: e.dma_start(out=self.cstb[:], in_=d["consts"]), writes=["cstb"], slot=self.s_inp)
        S.add("sp", lambda e: e.dma_start(out=self.smallp[:], in_=d["smallp"]), writes=["smallp"], slot=si)
        S.add("sp", lambda e: e.dma_start(out=self.bcast[:], in_=d["bcast"]), writes=["bcast"], slot=si)
        S.add("sp", lambda e: e.dma_start(out=self.condf[:], in_=d["cond"]), writes=["condf"], slot=si)
        S.add("pool", lambda e: e.dma_start(out=self.wa1[:], in_=d["wa1"]), writes=["wa1"], slot=self.s_inp)
        S.add("pool", lambda e: e.dma_start(out=self.wa2[:], in_=d["wa2"]), writes=["wa2"], slot=self.s_inp)
        if self.do_sample:
            S.add("sp", lambda e: e.dma_start(out=self.meta[:], in_=d["meta"]), writes=["meta"], slot=si)
            S.add("sp", lambda e: e.dma_start(out=self.ropec[:], in_=d["ropec"]), writes=["ropec"], slot=si)
            S.add("sp", lambda e: e.dma_start(out=self.ropes[:], in_=d["ropes"]), writes=["ropes"], slot=si)
        S.add("dve", lambda e: e.memset(self.onesb[:], 1.0 / 1024.0), writes=["onesb"])
        S.add("act", lambda e: e.activation(self.condb[:], self.condf[:], AF.Silu), reads=["condf"], writes=["condb"])
        self.ada_blocks(0, range(0, 4))
        self.ada_pending = [(0, range(4, 12)), (1, range(0, 12))]
        for l in range(L):
            lam_init = 0.8 - 0.6 * math.exp(-0.3 * l)
            lp = self.bcast[:, l, 256:512].rearrange("p (a b c) -> p a b c", a=2, b=2)
            S.add("dve", lambda e, lp=lp: e.tensor_tensor(self.tsm[:, 0, :].rearrange("p (a c) -> p a c", a=2),
                                                          lp[:, :, 0, :], lp[:, :, 1, :], ALU.mult),
                  reads=["bcast"], writes=["tsm"])
            S.add("dve", lambda e: e.reduce_sum(self.sm8[:, 0:2], self.tsm[:, 0, :].rearrange("p (a c) -> p a c", a=2),
                                                axis=mybir.AxisListType.X), reads=["tsm"], writes=["sm8"])
            S.add("act", lambda e: e.activation(self.sm8[:, 2:4], self.sm8[:, 0:2], AF.Exp), reads=["sm8"], writes=["sm8b"])
            S.add("dve", lambda e: e.tensor_tensor(self.sm8[:, 4:5], self.sm8[:, 3:4], self.sm8[:, 2:3], ALU.subtract),
                  reads=["sm8b"], writes=["sm8c"])
            S.add("dve", lambda e, l=l, li=lam_init: e.tensor_scalar(self.lam[:, l:l + 1], self.sm8[:, 4:5], -li, None, ALU.add),
                  reads=["sm8c"], writes=["lam"])
            S.add("act", lambda e, l=l: e.activation(self.sm8[:, 8:16], self.bcast[:, l, 512:520], AF.Exp, scale=-1.0),
                  reads=["bcast"], writes=["sm8d"])
            S.add("act", lambda e, l=l: e.activation(self.retsp[:, l, :], self.sm8[:, 8:16], AF.Ln, bias=1.0),
                  reads=["sm8d"], writes=["retsp"])

    def ada_blocks(self, l, js):
        S, d = self.S, self.d
        for j in js:
            wv, wk = self.wload(d[f"wA{l}"][j])
            b = self.bank()
            for kc in range(8):
                S.add("pe", lambda e, kc=kc, b=b, wv=wv: e.matmul(
                    self.ps[0:2, b, :], self.condb[:, kc, :], wv[:, kc * 512:(kc + 1) * 512],
                    start=(kc == 0), stop=(kc == 7)), reads=[wk, "condb"], writes=[("ps", b)])
            S.add("act", lambda e, b=b: e.activation(self.modtok[:], self.ps[0:2, b, :], AF.Identity),
                  reads=[("ps", b)], writes=["modtok"])
            b2 = self.bank()
            for q in range(4):
                S.add("pe", lambda e, q=q, b2=b2: e.transpose(
                    self.ps[:, b2, q * 2:(q + 1) * 2], self.modtok[0:2, q * 128:(q + 1) * 128],
                    self.cst[0:2, 4, 0:2]), reads=["modtok", "cst"], writes=[("ps", b2)])
            sec = j // 2
            S.add("dve", lambda e, l=l, j=j, b2=b2: e.tensor_tensor(
                self.modT[:, l, j * 4:(j + 1) * 4, :],
                self.ps[:, b2, 0:8].rearrange("p (a b) -> p a b", a=4),
                self.smallp[:, l, j * 4:(j + 1) * 4].rearrange("p (a o) -> p a o", o=1).broadcast_to([128, 4, 2]),
                ALU.add), reads=[("ps", b2), "smallp"], writes=[("modT", l, sec)])
            if j % 2 == 1:
                lo = sec * 8
                if sec in (1, 4):
                    S.add("dve", lambda e, l=l, lo=lo: e.tensor_scalar(self.modT[:, l, lo:lo + 8, :], self.modT[:, l, lo:lo + 8, :],
                                                                      1.0, None, ALU.add),
                          reads=[("modT", l, sec)], writes=[("modT", l, sec)])
                elif sec in (2, 5):
                    f = 0.5 / ALPHA if sec == 2 else 1.0 / ALPHA
                    S.add("dve", lambda e, l=l, lo=lo, f=f: e.tensor_scalar(self.modT[:, l, lo:lo + 8, :], self.modT[:, l, lo:lo + 8, :],
                                                                           f, None, ALU.mult),
                          reads=[("modT", l, sec)], writes=[("modT", l, sec)])

    def ada_deferred(self):
        for (l, js) in self.ada_pending:
            self.ada_blocks(l, js)
        self.ada_pending = []

    def load_x(self, P):
        S = self.S
        src = self.d[P["x"]]
        for t in range(P["NT"]):
            sl = t % 2
            S.add("sp", lambda e, t=t, sl=sl: e.dma_start(out=self.stage[:, sl, :], in_=src[t * 128:(t + 1) * 128, :]),
                  writes=[("stage", sl)], slot=self.s_io[sl])
            for half in range(2):
                b = self.bank()
                for q in range(4):
                    kc = half * 4 + q
                    S.add("pe", lambda e, b=b, q=q, kc=kc, sl=sl: e.transpose(
                        self.ps[:, b, q * 128:(q + 1) * 128], self.stage[:, sl, kc * 128:(kc + 1) * 128], self.cst[:, 4, :]),
                        reads=[("stage", sl), "cst"], writes=[("ps", b)])
                S.add("act" if half == 0 else "dve",
                      (lambda e, b=b, half=half, t=t: e.activation(
                          self.xT[:, half * 4:half * 4 + 4, t * 128:(t + 1) * 128],
                          self.ps[:, b, :].rearrange("p (a c) -> p a c", a=4), AF.Identity)) if half == 0 else
                      (lambda e, b=b, half=half, t=t: e.tensor_copy(
                          self.xT[:, half * 4:half * 4 + 4, t * 128:(t + 1) * 128],
                          self.ps[:, b, :].rearrange("p (a c) -> p a c", a=4))),
                      reads=[("ps", b)], writes=[("xT", t // 4)])

    def modulate(self, P, l, which):
        S = self.S
        c = P["cond"]
        sh0 = 0 if which == 0 else 24
        sc0 = 8 if which == 0 else 32
        for g in range(P["NG"]):
            for kc in range(8):
                S.add("dve", lambda e, g=g, kc=kc: e.tensor_scalar(
                    self.hT[:, kc, g * 512:(g + 1) * 512], self.xT[:, kc, g * 512:(g + 1) * 512],
                    self.modT[:, l, sc0 + kc, c:c + 1], self.modT[:, l, sh0 + kc, c:c + 1], ALU.mult, ALU.add),
                    reads=[("xT", g), ("modT", l, sh0 // 8), ("modT", l, sc0 // 8)], writes=[("hT", g)])

    def projF(self, P, wv, wk, ci, evac):
        S = self.S
        for g in range(P["NG"]):
            b = self.bank()
            for kc in range(8):
                S.add("pe", lambda e, b=b, kc=kc, g=g: e.matmul(
                    self.ps[:, b, :], wv[:, ci * 1024 + kc * 128: ci * 1024 + (kc + 1) * 128],
                    self.hT[:, kc, g * 512:(g + 1) * 512], start=(kc == 0), stop=(kc == 7)),
                    reads=[wk, ("hT", g)], writes=[("ps", b)])
            evac(g, b)

    def projT(self, P, wv, wk, evac):
        S = self.S
        for t in range(P["NT"]):
            b = self.bank()
            for kc in range(8):
                S.add("pe", lambda e, b=b, kc=kc, t=t: e.matmul(
                    self.ps[:, b, :], self.hT[:, kc, t * 128:(t + 1) * 128], wv[:, kc * 512:(kc + 1) * 512],
                    start=(kc == 0), stop=(kc == 7)),
                    reads=[wk, ("hT", t // 4)], writes=[("ps", b)])
            evac(t, b)

    def evac_rope(self, P, dst, key, h, g, b, scale):
        S = self.S
        gc = slice(g * 512, (g + 1) * 512)
        if not P["rope"]:
            S.add("act", lambda e: e.activation(dst[:, h, gc], self.ps[:, b, :], AF.Copy, scale=scale),
                  reads=[("ps", b)], writes=[(key, h)])
            return
        S.add("act", lambda e: e.activation(self.ropex[:, :], self.ps[:, b, :], AF.Copy, scale=scale),
              reads=[("ps", b)], writes=["ropex"])
        b2 = self.bank()
        S.add("pe", lambda e: e.matmul(self.ps[:, b2, :], self.cstb[:, 5, :], self.ropex[:, :], start=True, stop=True),
              reads=["ropex", "cstb"], writes=[("ps", b2)])
        S.add("dve", lambda e: e.tensor_tensor(self.ropet[:, :], self.ps[:, b2, :], self.ropes[:, gc], ALU.mult),
              reads=[("ps", b2), "ropes"], writes=["ropet"])
        S.add("dve", lambda e: e.tensor_tensor(self.ropeu[:, :], self.ropex[:, :], self.ropec[:, gc], ALU.mult),
              reads=["ropex", "ropec"], writes=["ropeu"])
        S.add("dve", lambda e: e.tensor_tensor(dst[:, h, gc], self.ropet[:, :], self.ropeu[:, :], ALU.add),
              reads=["ropet", "ropeu"], writes=[(key, h)])

    def decay_tiles(self, scale, t):
        S = self.S
        b = self.bank()
        for h in range(4):
            for dd, ci in ((0, 2), (1, 3)):
                c0 = h * 128 + dd * 64
                S.add("pe", lambda e, c0=c0, ci=ci: e.matmul(self.ps[:, b, c0:c0 + 64], self.cst[:, ci, :], self.latok[:, c0:c0 + 64],
                                                            start=True, stop=True),
                      reads=["latok", "cst"], writes=[("ps", b)])
        S.add("act", lambda e: e.activation(self.Ekd[:, :, :], self.ps[:, b, :].rearrange("p (h c) -> p h c", h=4), AF.Exp, scale=scale),
              reads=[("ps", b)], writes=["Ekd"])
        bx, by = self.bank(), self.bank()
        for h in range(4):
            S.add("pe", lambda e, h=h: e.matmul(self.ps[:, bx, h * 128:(h + 1) * 128], self.latok[:, h * 128:(h + 1) * 128],
                                               self.cst[:, 0, :], start=True, stop=True),
                  reads=["latok", "cst"], writes=[("ps", bx)])
            S.add("pe", lambda e, h=h: e.matmul(self.ps[:, by, h * 128:(h + 1) * 128], self.latok[:, h * 128:(h + 1) * 128],
                                               self.cst[:, 1, :], start=True, stop=True),
                  reads=["latok", "cst"], writes=[("ps", by)])
        for (dst, sc, key) in ((self.EqT, scale, "EqT"), (self.EkT, -scale, "EkT")):
            S.add("act", lambda e, dst=dst, sc=sc: e.activation(
                dst[0:64, :, :], self.ps[0:64, bx, :].rearrange("p (h c) -> p h c", h=4), AF.Exp, scale=sc),
                reads=[("ps", bx)], writes=[key])
            S.add("act", lambda e, dst=dst, sc=sc: e.activation(
                dst[64:128, :, :], self.ps[64:128, by, :].rearrange("p (h c) -> p h c", h=4), AF.Exp, scale=sc),
                reads=[("ps", by)], writes=[key])
        if t is not None:
            S.add("dve", lambda e: e.tensor_copy(self.dcol[0:64, :, t:t + 1], self.EqT[0:64, :, 127:128]),
                  reads=["EqT"], writes=["dcol"])
            S.add("dve", lambda e: e.tensor_copy(self.dcol[64:128, :, t:t + 1], self.EqT[64:128, :, 0:1]),
                  reads=["EqT"], writes=["dcol"])

    def compose_states(self, P, l, mix):
        S, x_ = self.S, self.x_
        dk = [("delta", h, t) for h in range(4) for t in range(8)]
        S.add("dve", lambda e: e.tensor_copy(self.cp[0:64, :, 0:1], self.dcol[0:64, :, 0:1]), reads=["dcol"], writes=["cp"])
        S.add("dve", lambda e: e.tensor_copy(self.cp[64:128, :, 7:8], self.dcol[64:128, :, 7:8]), reads=["dcol"], writes=["cp"])
        for t in range(1, 8):
            S.add("dve", lambda e, t=t: e.tensor_tensor(self.cp[0:64, :, t:t + 1], self.cp[0:64, :, t - 1:t],
                                                        self.dcol[0:64, :, t:t + 1], ALU.mult), reads=["cp", "dcol"], writes=["cp"])
            tb = 7 - t
            S.add("dve", lambda e, tb=tb: e.tensor_tensor(self.cp[64:128, :, tb:tb + 1], self.cp[64:128, :, tb + 1:tb + 2],
                                                          self.dcol[64:128, :, tb:tb + 1], ALU.mult), reads=["cp", "dcol"], writes=["cp"])
        g1 = self.xgroup("stw")
        stL = x_["st_in"][:, 0:512].rearrange("p (h v) -> p h v", h=4)
        S.add("sp", lambda e: e.dma_start(out=stL[0:64], in_=self.delta[0:64, :, 7, :]), reads=dk, writes=["st_in"], slot=g1)
        S.add("sp", lambda e: e.dma_start(out=stL[64:128], in_=self.delta[64:128, :, 0, :]), reads=dk, writes=["st_in"], slot=g1)
        S.add("sp", lambda e: e.dma_start(out=x_["st_in"][0:64, 512:516].rearrange("p (h o) -> p h o", o=1), in_=self.cp[0:64, :, 7:8], allow_slow_non_contiguous=True), reads=["cp"], writes=["st_in"], slot=g1)
        S.add("sp", lambda e: e.dma_start(out=x_["st_in"][64:128, 512:516].rearrange("p (h o) -> p h o", o=1), in_=self.cp[64:128, :, 0:1], allow_slow_non_contiguous=True), reads=["cp"], writes=["st_in"], slot=g1)
        self.gather(x_["st_in"], x_["st_out"], "st_in", "st_out")
        g2 = self.xgroup("stl")
        src0 = self.d["sret" if mix == 0 else "sgla"][l]
        S.add("sp", lambda e: e.dma_start(out=self.Srun[:, :, :], in_=src0), writes=["Srun"], slot=g2)
        for s_ in range(4):
            S.add("sp", lambda e, s_=s_: e.dma_start(out=self.Dg[0:64, s_, :], in_=x_["st_out"][s_ * 128:s_ * 128 + 64, 512:516]),
                  reads=["st_out"], writes=["Dg"], slot=g2)
            S.add("sp", lambda e, s_=s_: e.dma_start(out=self.Dg[64:128, s_, :],
                                                     in_=x_["st_out"][(3 - s_) * 128 + 64:(3 - s_) * 128 + 128, 512:516]),
                  reads=["st_out"], writes=["Dg"], slot=g2)
        S.add("dve", lambda e: e.memset(self.Sin[:, :, :], 0.0), writes=["Sin"])
        for s_ in range(4):
            g3 = self.xgroup("gl")
            S.add("sp", lambda e, s_=s_: e.dma_start(
                out=self.GL[0:64, :, :], in_=x_["st_out"][s_ * 128:s_ * 128 + 64, 0:512].rearrange("p (h v) -> p h v", h=4)),
                reads=["st_out"], writes=["GL"], slot=g3)
            S.add("sp", lambda e, s_=s_: e.dma_start(
                out=self.GL[64:128, :, :],
                in_=x_["st_out"][(3 - s_) * 128 + 64:(3 - s_) * 128 + 128, 0:512].rearrange("p (h v) -> p h v", h=4)),
                reads=["st_out"], writes=["GL"], slot=g3)
            S.add("dve", lambda e, s_=s_: e.scalar_tensor_tensor(
                self.Sin[:, :, :].rearrange("p h v -> p (h v)"), self.Srun[:, :, :].rearrange("p h v -> p (h v)"),
                self.meta[:, s_:s_ + 1], self.Sin[:, :, :].rearrange("p h v -> p (h v)"), ALU.mult, ALU.add),
                reads=["Srun", "meta", "Sin"], writes=["Sin"])
            for h in range(4):
                S.add("dve", lambda e, s_=s_, h=h: e.scalar_tensor_tensor(
                    self.Srun[:, h, :], self.Srun[:, h, :], self.Dg[:, s_, h:h + 1], self.GL[:, h, :], ALU.mult, ALU.add),
                    reads=["Srun", "Dg", "GL"], writes=["Srun"])
        for t in range(8):
            for h in range(4):
                S.add("dve", lambda e, t=t, h=h: e.scalar_tensor_tensor(
                    self.delta[:, h, t, :], self.Sin[:, h, :], self.cp[:, h, t:t + 1], self.delta[:, h, t, :], ALU.mult, ALU.add),
                    reads=["Sin", "cp", ("delta", h, t)], writes=[("delta", h, t)])
        S.add("dve", lambda e: e.tensor_copy(self.sinb[:, :, :], self.Sin[:, :, :]), reads=["Sin"], writes=["sinb"])

    def mixer_AB(self, P, l, mix):
        S, d = self.S, self.d
        NT, NG = P["NT"], P["NG"]
        qscale = 1.0 if mix == 0 else 0.125
        kscale = 0.125 if mix == 0 else 1.0
        ropeP = P if mix == 0 else dict(P, rope=False)
        wv, wk = self.wload(d[f"wT{l}"][2 * mix])
        self.projT(P, wv, wk, lambda t, b: S.add(
            "act", lambda e: e.activation(self.vtok[:, t, :], self.ps[:, b, :], AF.Identity), reads=[("ps", b)], writes=[("vtok", t)]))
        wv, wk = self.wload(d[f"wT{l}"][2 * mix + 1])
        self.projT(P, wv, wk, lambda t, b: S.add(
            "act", lambda e: e.activation(self.gtok[:, t, :], self.ps[:, b, :], AF.Silu),
            reads=[("ps", b)], writes=[("gtok", t)]))
        ksub = int(os.environ.get("KSUB", "99"))
        if ksub < 1:
            return
        if mix == 1:
            S.add("dve", lambda e: e.memset(self.uT[0:33, 0:P["T"]], 1.0), writes=["uT"])
            for g in range(NG):
                b = self.bank()
                for kc in range(8):
                    S.add("pe", lambda e, b=b, kc=kc, g=g: e.matmul(
                        self.ps[0:32, b, :], self.wa1[:, l, kc, :], self.hT[:, kc, g * 512:(g + 1) * 512],
                        start=(kc == 0), stop=(kc == 7)), reads=["wa1", ("hT", g)], writes=[("ps", b)])
                S.add("act", lambda e, b=b, g=g: e.activation(self.uT[0:32, g * 512:(g + 1) * 512], self.ps[0:32, b, :], AF.Identity),
                      reads=[("ps", b)], writes=["uT"])
        wv, wk = self.wload(d[f"wF{l}"][2 * mix])
        for h in range(4):
            self.projF(P, wv, wk, h, lambda g, b, h=h: self.evac_rope(ropeP, self.QT, "QT", h, g, b, qscale))
        wv, wk = self.wload(d[f"wF{l}"][2 * mix + 1])
        for h in range(4):
            self.projF(P, wv, wk, h, lambda g, b, h=h: self.evac_rope(ropeP, self.KT, "KT", h, g, b, kscale))
        if ropeP["rope"]:
            self.barrier()
        if ksub < 2:
            return
        if mix == 0:
            S.add("dve", lambda e: e.tensor_copy(
                self.latok[:, :].rearrange("p (h d k) -> p h d k", h=4, d=2),
                self.retsp[:, l, :].rearrange("p (d h o) -> p h d o", d=2, o=1).broadcast_to([128, 4, 2, 64])),
                reads=["retsp"], writes=["latok"])
            self.decay_tiles(-1.0, None)
            for t in range(NT):
                S.add("dve", lambda e, t=t: e.tensor_copy(self.dcol[0:64, :, t:t + 1], self.EqT[0:64, :, 127:128]),
                      reads=["EqT"], writes=["dcol"])
                S.add("dve", lambda e, t=t: e.tensor_copy(self.dcol[64:128, :, t:t + 1], self.EqT[64:128, :, 0:1]),
                      reads=["EqT"], writes=["dcol"])
        if ksub < 3:
            return
        it = 0
        for t in range(NT):
            tc_ = slice(t * 128, (t + 1) * 128)
            if mix == 1:
                b = self.bank()
                S.add("pe", lambda e, b=b, tc_=tc_: e.matmul(self.ps[:, b, :], self.uT[0:33, tc_], self.wa2[0:33, l, :],
                                                            start=True, stop=True),
                      reads=["uT", "wa2"], writes=[("ps", b)])
                S.add("act", lambda e, b=b: e.activation(self.latok[:, :], self.ps[:, b, :], AF.Exp, scale=-1.0),
                      reads=[("ps", b)], writes=["latok"])
                S.add("act", lambda e: e.activation(self.latok[:, :], self.latok[:, :], AF.Ln, bias=1.0),
                      reads=["latok"], writes=["latok"])
                self.decay_tiles(-1.0 / 16.0, t)
            for h in range(4):
                i2 = it % 2
                it += 1
                S.add("dve", lambda e, h=h, t=t, tc_=tc_: e.tensor_tensor(
                    self.qd[:, h, t, :], self.QT[:, h, tc_], self.EqT[:, h, :], ALU.mult),
                    reads=[("QT", h), "EqT"], writes=[("qd", h, t)])
                S.add("dve", lambda e, h=h, tc_=tc_, i2=i2: e.tensor_tensor(
                    self.ksb[:, i2, :], self.KT[:, h, tc_], self.EkT[:, h, :], ALU.mult),
                    reads=[("KT", h), "EkT"], writes=[("ksb", i2)])
                bt = self.bank()
                S.add("pe", lambda e, h=h, tc_=tc_, bt=bt: e.transpose(
                    self.psb(bt)[:, 0:128], self.KT[:, h, tc_], self.cstb[:, 4, :]),
                    reads=[("KT", h), "cstb"], writes=[("ps", bt)])
                S.add("dve", lambda e, h=h, bt=bt, i2=i2: e.tensor_tensor(
                    self.kdtok[:, i2, :], self.psb(bt)[:, 0:128], self.Ekd[:, h, :], ALU.mult),
                    reads=[("ps", bt), "Ekd"], writes=[("kdtok", i2)])
                bd = self.bank()
                S.add("pe", lambda e, h=h, t=t, bd=bd, i2=i2: e.matmul(
                    self.ps[:, bd, 0:128], self.kdtok[:, i2, :], self.vtok[:, t, h * 128:(h + 1) * 128], start=True, stop=True),
                    reads=[("kdtok", i2), ("vtok", t)], writes=[("ps", bd)])
                S.add("act", lambda e, h=h, t=t, bd=bd: e.activation(self.delta[:, h, t, 0:128], self.ps[:, bd, 0:128], AF.Identity),
                      reads=[("ps", bd)], writes=[("delta", h, t)])
                b1, b2 = self.bank(), self.bank()
                S.add("pe", lambda e, h=h, t=t, b1=b1, i2=i2: e.matmul(
                    self.ps[:, b1, 0:128], self.ksb[0:64, i2, :], self.qd[0:64, h, t, :], start=True, stop=True),
                    reads=[("ksb", i2), ("qd", h, t)], writes=[("ps", b1)])
                S.add("pe", lambda e, h=h, t=t, b2=b2, i2=i2: e.matmul(
                    self.ps[:, b2, 0:128], self.ksb[64:128, i2, :], self.qd[64:128, h, t, :], start=True, stop=True),
                    reads=[("ksb", i2), ("qd", h, t)], writes=[("ps", b2)])
                S.add("dve", lambda e, b1=b1, i2=i2: e.tensor_tensor(
                    self.tsm[:, i2, :], self.ps[:, b1, 0:128], self.cst[:, 0, :], ALU.mult),
                    reads=[("ps", b1), "cst"], writes=[("tsm", i2)])
                S.add("dve", lambda e, b2=b2: e.tensor_tensor(
                    self.ps[:, b2, 128:256], self.ps[:, b2, 0:128], self.cst[:, 1, :], ALU.mult),
                    reads=[("ps", b2), "cst"], writes=[("ps", b2)])
                S.add("dve", lambda e, h=h, t=t, b2=b2, i2=i2: e.tensor_tensor(
                    self.Pm[:, h, t, :], self.ps[:, b2, 128:256], self.tsm[:, i2, :], ALU.add),
                    reads=[("ps", b2), ("tsm", i2)], writes=[("Pm", h, t)])
        if ksub < 4:
            return
        self.barrier()
        out_name = "nsr" if mix == 0 else "nsg"
        for si, (t0, n) in enumerate(P["seqs"]):
            for c_ in range(1, n):
                tf = t0 + c_
                tb = t0 + n - 1 - c_
                for h in range(4 if os.environ.get("KX", "0") != "5" else 0):
                    S.add("dve", lambda e, h=h, tf=tf: e.scalar_tensor_tensor(
                        self.delta[0:64, h, tf, 0:128], self.delta[0:64, h, tf - 1, 0:128], self.dcol[0:64, h, tf:tf + 1],
                        self.delta[0:64, h, tf, 0:128], ALU.mult, ALU.add),
                        reads=[("delta", h, tf - 1), "dcol", ("delta", h, tf)], writes=[("delta", h, tf)])
                    S.add("dve", lambda e, h=h, tb=tb: e.scalar_tensor_tensor(
                        self.delta[64:128, h, tb, 0:128], self.delta[64:128, h, tb + 1, 0:128], self.dcol[64:128, h, tb:tb + 1],
                        self.delta[64:128, h, tb, 0:128], ALU.mult, ALU.add),
                        reads=[("delta", h, tb + 1), "dcol", ("delta", h, tb)], writes=[("delta", h, tb)])
            if P.get("sample"):
                self.compose_states(P, l, mix)
            if P["states_out"] and not os.environ.get("KNOST"):
                for dr, tt in ((0, t0 + n - 1), (1, t0)):
                    op = S.add("sp", lambda e, si=si, dr=dr, tt=tt: e.dma_start(
                        out=self.o[out_name][si, l, dr].rearrange("h k v -> k h v"),
                        in_=self.delta[dr * 64:(dr + 1) * 64, :, tt, 0:128]),
                        reads=[("delta", h_, tt) for h_ in range(4)], slot=self.s_st[si % 2])
                    self.out_ops.append(op)
            if os.environ.get("KX") == "7":
                ba = self.bank()
                S.add("pe", lambda e, ba=ba: e.matmul(self.ps[:, ba, 0:128], self.kdtok[:, 0, :], self.vtok[:, 0, 0:128],
                                                      start=True, stop=True),
                      reads=[("kdtok", 0), ("vtok", 0)], writes=[("ps", ba)])
                continue
            if os.environ.get("KX") == "8":
                ba = self.bank()
                S.add("pe", lambda e, ba=ba: e.matmul(self.ps[:, ba, 0:128], self.cstb[:, 6, :], self.vtok[:, 0, 0:128],
                                                      start=True, stop=True),
                      reads=["cstb", ("vtok", 0)], writes=[("ps", ba)])
                continue
            for c_ in range(n):
                t = t0 + c_
                for h in range(4):
                    kv = os.environ.get("KV", "0")
                    if kv == "1":
                        S.add("dve", lambda e, t=t, h=h: e.tensor_tensor(self.dbf[:, h, t, :], self.cst[:, 0, :], self.cst[:, 8, :], ALU.mult),
                              reads=["cst"], writes=[("QT", h)])
                    elif kv == "3":
                        S.add("dve", lambda e, t=t, h=h: e.tensor_tensor(self.dbf[:, h, t, :], self.cst[:, 0, :], self.cst[:, 8, :], ALU.mult),
                              reads=["cst"], writes=[("tsm", 0)])
                    elif kv == "4":
                        S.add("dve", lambda e, t=t, h=h: e.tensor_tensor(self.tsm[:, 0, :], self.cst[:, 0, :], self.cst[:, 8, :], ALU.mult),
                              reads=["cst"], writes=[("QT", h)])
                    elif kv == "2":
                        S.add("dve", lambda e, t=t, h=h: e.tensor_tensor(self.tsm[:, 0, :], self.delta[:, h, t, :], self.cst[:, 8, :], ALU.mult),
                              reads=[("delta", h, t), "cst"], writes=[("tsm", 0)])
                    else:
                        S.add("dve", lambda e, t=t, h=h: e.tensor_tensor(self.dbf[:, h, t, :], self.delta[:, h, t, :], self.cst[:, 8, :], ALU.mult),
                              reads=[("delta", h, t), "cst"], writes=[("dbf", h)])
            if os.environ.get("KX") == "9":
                continue
            if os.environ.get("KX") == "10":
                ba = self.bank()
                S.add("pe", lambda e, ba=ba: e.matmul(self.ps[:, ba, 0:128], self.cstb[:, 6, :], self.dbf[:, 0, t0, :],
                                                      start=True, stop=True),
                      reads=["cstb", ("QT", 0)], writes=[("ps", ba)])
                continue
            for c_ in range(n):
                t = t0 + c_
                ba = self.bank()
                for h in range(4 if os.environ.get("KX", "0") != "4" else 0):
                    hasf, hasb = c_ > 0, c_ < n - 1
                    if P.get("sample") and not hasf:
                        S.add("pe", lambda e, h=h, ba=ba: e.matmul(
                            self.ps[:, ba, h * 128:(h + 1) * 128], self.cstb[:, 6, :], self.sinb[:, h, :],
                            start=True, stop=False), reads=["sinb", "cstb"], writes=[("ps", ba)])
                        hasf = True
                    if P.get("sample") and not hasb:
                        S.add("pe", lambda e, h=h, t=t, ba=ba: e.matmul(
                            self.ps[:, ba, h * 128:(h + 1) * 128], self.cstb[:, 6, :], self.dbf[:, h, t - 1, :],
                            start=True, stop=False), reads=[("dbf", h), "cstb"], writes=[("ps", ba)])
                        S.add("pe", lambda e, h=h, ba=ba: e.matmul(
                            self.ps[:, ba, h * 128:(h + 1) * 128], self.cstb[:, 7, :], self.sinb[:, h, :],
                            start=False, stop=True), reads=["sinb", "cstb"], writes=[("ps", ba)])
                        continue
                    if P.get("sample") and c_ == 0:
                        S.add("pe", lambda e, h=h, t=t, ba=ba: e.matmul(
                            self.ps[:, ba, h * 128:(h + 1) * 128], self.cstb[:, 7, :], self.dbf[:, h, t + 1, :],
                            start=False, stop=True), reads=[("dbf", h), "cstb"], writes=[("ps", ba)])
                        continue
                    if hasf:
                        S.add("pe", lambda e, h=h, t=t, ba=ba, hasb=hasb: e.matmul(
                            self.ps[:, ba, h * 128:(h + 1) * 128], self.cstb[:, 6, :], self.dbf[:, h, t - 1, :],
                            start=True, stop=not hasb), reads=[("dbf", h), "cstb"], writes=[("ps", ba)])
                    if hasb:
                        S.add("pe", lambda e, h=h, t=t, ba=ba, hasf=hasf: e.matmul(
                            self.ps[:, ba, h * 128:(h + 1) * 128], self.cstb[:, 7, :], self.dbf[:, h, t + 1, :],
                            start=not hasf, stop=True), reads=[("dbf", h), "cstb"], writes=[("ps", ba)])
                kx = os.environ.get("KX", "0")
                if kx == "0":
                    S.add("act", lambda e, t=t, ba=ba: e.activation(
                        self.sstk[:, :, t, :], self.ps[:, ba, :].rearrange("p (h v) -> p h v", h=4), AF.Identity),
                        reads=[("ps", ba)], writes=[("KT", h_) for h_ in range(4)])
                elif kx == "2":
                    S.add("act", lambda e, t=t, ba=ba: e.activation(
                        self.Pm[:, :, t, :], self.ps[:, ba, :].rearrange("p (h v) -> p h v", h=4), AF.Identity),
                        reads=[("ps", ba)], writes=[("Pm", h_, t) for h_ in range(4)])
                elif kx == "3":
                    S.add("dve", lambda e, t=t, ba=ba: e.tensor_copy(
                        self.sstk[:, :, t, :], self.ps[:, ba, :].rearrange("p (h v) -> p h v", h=4)),
                        reads=[("ps", ba)], writes=[("KT", h_) for h_ in range(4)])
        if ksub < 5 or os.environ.get("KNOP2"):
            return
        if os.environ.get("KPAD"):
            S.add("dve", lambda e: e.memset(self.tsm[:, 0, :], 0.0), writes=[("tsm", 0)])
        if not os.environ.get("KNOBAR"):
            self.barrier()
        for t in range(int(os.environ.get("KP2T", NT))):
            b = self.bank()
            for h in range(4):
                S.add("pe", lambda e, h=h, t=t, b=b: e.matmul(
                    self.ps[:, b, h * 128:(h + 1) * 128], self.qd[:, h, t, :], self.sstk[:, h, t, :],
                    start=True, stop=False), reads=[("qd", h, t), ("KT", h), ("Pm", h, t), ("vtok", t)], writes=[("ps", b)])
                S.add("pe", lambda e, h=h, t=t, b=b: e.matmul(
                    self.ps[:, b, h * 128:(h + 1) * 128], self.Pm[:, h, t, :], self.vtok[:, t, h * 128:(h + 1) * 128],
                    start=False, stop=True), reads=[("Pm", h, t), ("vtok", t)], writes=[("ps", b)])
            kp2 = int(os.environ.get("KP2", "3"))
            if kp2 & 1:
              S.add("act", lambda e, t=t, b=b: e.activation(self.otokAB[:, t, :], self.ps[:, b, :], AF.Identity),
                  reads=[("ps", b)], writes=[("QT", t // 2) if not os.environ.get("KOT") else ("otokAB", t)])
            for h in range(4 if (kp2 & 2) else 0):
                if mix == 0:
                    S.add("dve", lambda e, h=h, b=b: e.bn_stats(self.st6[:, :], self.ps[:, b, h * 128:(h + 1) * 128]),
                          reads=[("ps", b)], writes=["st6"])
                    S.add("dve", lambda e, h=h, t=t: e.bn_aggr(self.stat[:, t, h, :], self.st6[:, :]),
                          reads=["st6"], writes=["stat"])
                else:
                    S.add("act", lambda e, h=h, t=t, b=b: e.activation(
                        self.tsm[:, 0, :], self.ps[:, b, h * 128:(h + 1) * 128], AF.Square,
                        accum_out=self.stat[:, t, h, 1:2]), reads=[("ps", b)], writes=["stat", ("tsm", 0)])
        if ksub < 6:
            return
        self.norm_out(P, l, mix, self.otokAB, lambda t: ("QT", t // 2))

    def norm_out(self, P, l, br, otok, okey):
        S = self.S
        NT = P["NT"]
        if br == 0:
            sc, bias = 1.0, EPS
        else:
            sc, bias = 1.0 / 128.0, EPS
        S.add("act", lambda e: e.activation(self.rstd[:, 0:NT, :], self.stat[:, 0:NT, :, 1], AF.Sqrt, bias=bias, scale=sc),
              reads=["stat"], writes=["rstd"])
        S.add("dve", lambda e: e.reciprocal(self.rstd[:, 0:NT, :], self.rstd[:, 0:NT, :]), reads=["rstd"], writes=["rstd"])
        for t in range(NT):
            i2 = t % 2
            if br == 1:
                S.add("dve", lambda e, t=t: e.tensor_tensor(
                    self.ggt[:].rearrange("p (h v) -> p h v", h=4), self.gtok[:, t, :].rearrange("p (h v) -> p h v", h=4),
                    self.bcast[:, l, 0:128].rearrange("p (o v) -> p o v", o=1).broadcast_to([128, 4, 128]), ALU.mult),
                    reads=[("gtok", t), "bcast"], writes=["ggt"])
            for h in range(4):
                hs = slice(h * 128, (h + 1) * 128)
                if br == 0:
                    S.add("dve", lambda e, t=t, hs=hs, h=h, i2=i2: e.scalar_tensor_tensor(
                        self.tsm[:, 1, :], otok[:, t, hs], self.stat[:, t, h, 0:1], self.gtok[:, t, hs],
                        ALU.subtract, ALU.mult), reads=[okey(t), "stat", ("gtok", t)], writes=[("tsm", 1)])
                    S.add("dve", lambda e, t=t, hs=hs, h=h, i2=i2: e.tensor_scalar(
                        self.ytok[:, i2, hs], self.tsm[:, 1, :], self.rstd[:, t, h:h + 1], None, ALU.mult),
                        reads=[("tsm", 1), "rstd"], writes=[("ytok", i2)])
                elif br == 1:
                    S.add("dve", lambda e, t=t, hs=hs, h=h, i2=i2: e.scalar_tensor_tensor(
                        self.ytok[:, i2, hs], otok[:, t, hs], self.rstd[:, t, h:h + 1], self.ggt[:, hs],
                        ALU.mult, ALU.mult), reads=[okey(t), "rstd", "ggt"], writes=[("ytok", i2)])
                else:
                    S.add("dve", lambda e, t=t, hs=hs, h=h, i2=i2: e.scalar_tensor_tensor(
                        self.ytok[:, i2, hs], otok[:, t, hs], self.rstd[:, t, h:h + 1], self.subg[:, :],
                        ALU.mult, ALU.mult), reads=[okey(t), "rstd", "subg"], writes=[("ytok", i2)])
            bt = self.bank()
            for h in range(4):
                S.add("pe", lambda e, h=h, bt=bt, i2=i2: e.transpose(
                    self.psb(bt)[:, h * 128:(h + 1) * 128], self.ytok[:, i2, h * 128:(h + 1) * 128], self.cstb[:, 4, :]),
                    reads=[("ytok", i2), "cstb"], writes=[("ps", bt)])
            S.add("act", lambda e, t=t, bt=bt: e.activation(
                self.yT[:, br, :, t * 128:(t + 1) * 128], self.psb(bt)[:, 0:512].rearrange("p (h c) -> p h c", h=4), AF.Identity),
                reads=[("ps", bt)], writes=[("yT", br, t // 4)])

    def mixer_C(self, P, l):
        S, d = self.S, self.d
        NT = P["NT"]
        lam_init = 0.8 - 0.6 * math.exp(-0.3 * l)
        S.add("dve", lambda e: e.tensor_scalar(self.subg[:, :], self.bcast[:, l, 128:256], 1.0 - lam_init, None, ALU.mult),
              reads=["bcast"], writes=["subg"])
        S.add("dve", lambda e: e.memset(self.vaug[:, :, :, 128:130], 1.0), writes=["vaug1"])
        wv, wk = self.wload(d[f"wT{l}"][4])

        def ev_v(t, b):
            S.add("act", lambda e: e.activation(self.vaug[:, t, :, 0:128], self.ps[:, b, :].rearrange("p (h v) -> p h v", h=4), AF.Identity),
                  reads=[("ps", b)], writes=[("vaug", t)])
            if P["kv_out"]:
                sl = t % 2
                S.add("dve", lambda e: e.tensor_copy(self.cstage[:, sl, :], self.ps[:, b, :]),
                      reads=[("ps", b)], writes=[("cstage", sl)])
                si, tt = t // 2, t % 2
                op = S.add("sp", lambda e: e.dma_start(
                    out=self.o["ndv"][si, l, :, tt * 128:(tt + 1) * 128, :].rearrange("h t v -> t h v"),
                    in_=self.cstage[:, sl, :].rearrange("p (h v) -> p h v", h=4)),
                    reads=[("cstage", sl)], slot=self.s_st[sl])
                self.out_ops.append(op)
        self.projT(P, wv, wk, ev_v)
        if P["kv_out"]:
            wv, wk = self.wload(d[f"wT{l}"][5])

            def ev_k(t, b):
                sl = t % 2
                S.add("dve", lambda e: e.tensor_copy(self.cstage[:, sl, :], self.ps[:, b, :]),
                      reads=[("ps", b)], writes=[("cstage", sl)])
                si, tt = t // 2, t % 2
                op = S.add("sp", lambda e: e.dma_start(
                    out=self.o["ndk"][si, l, :, tt * 128:(tt + 1) * 128, :].rearrange("g t k -> t g k"),
                    in_=self.cstage[:, sl, :].rearrange("p (g k) -> p g k", g=8)),
                    reads=[("cstage", sl)], slot=self.s_st[sl])
                self.out_ops.append(op)
            self.projT(P, wv, wk, ev_k)
        wv, wk = self.wload(d[f"wF{l}"][4])
        for h in range(4):
            self.projF(P, wv, wk, h, lambda g, b, h=h: self.evac_rope(P, self.QT, "QT", h, g, b, 1.0))
        wv, wk = self.wload(d[f"wF{l}"][5])
        for h in range(4):
            self.projF(P, wv, wk, h, lambda g, b, h=h: self.evac_rope(P, self.KT, "KT", h, g, b, 1.0))
        self.barrier(pool=bool(P.get("sample")))
        if P.get("sample"):
            self.attn_sample(P, l)
            self.barrier()
            self.norm_out(P, l, 2, self.otokC, lambda t: ("otokC", t))
            return
        self.bank_pool = [4, 5, 6, 7]
        for (q0, nq, keyt) in P["attn"]:
            qc = slice(q0 * 128, q0 * 128 + 256)
            for hc in range(4):
                for ji, kt in enumerate(keyt):
                    kc_ = slice(kt * 128, (kt + 1) * 128)
                    bs = [self.bank(), self.bank()]
                    for sub in range(2):
                        lo, hi = sub * 64, sub * 64 + 64
                        S.add("pe", lambda e, sub=sub, lo=lo, hi=hi, bs=bs, kc_=kc_, hc=hc: e.matmul(
                            self.ps[:, bs[sub], 0:256], self.KT[lo:hi, hc, kc_], self.QT[lo:hi, hc, qc],
                            start=True, stop=True), reads=[("KT", hc), ("QT", hc)], writes=[("ps", bs[sub])])
                    i2 = ji % 2
                    for sub in range(2):
                        S.add("act", lambda e, sub=sub, bs=bs, i2=i2: e.activation(
                            self.ET[:, i2, sub * 256:(sub + 1) * 256], self.ps[:, bs[sub], 0:256], AF.Exp, scale=0.125),
                            reads=[("ps", bs[sub])], writes=[("ET", i2, sub)])
                    for qt in range(2):
                        for sub in range(2):
                            acc = qt * 2 + sub
                            S.add("pe", lambda e, qt=qt, sub=sub, acc=acc, i2=i2, kt=kt, hc=hc, ji=ji: e.matmul(
                                self.ps[:, acc, 0:130], self.ET[:, i2, sub * 256 + qt * 128: sub * 256 + (qt + 1) * 128],
                                self.vaug[:, kt, hc, :], start=(ji == 0), stop=(ji == len(keyt) - 1)),
                                reads=[("ET", i2, sub), ("vaug", kt), "vaug1"], writes=[("ps", acc)])
                for qt in range(2):
                    t = q0 + qt
                    a1, a2 = qt * 2, qt * 2 + 1
                    S.add("dve", lambda e, a1=a1: e.reciprocal(self.sm8[:, 0:1], self.ps[:, a1, 128:129]),
                          reads=[("ps", a1)], writes=["sm8"])
                    S.add("dve", lambda e, a2=a2: e.reciprocal(self.sm8[:, 1:2], self.ps[:, a2, 128:129]),
                          reads=[("ps", a2)], writes=["sm8"])
                    S.add("dve", lambda e: e.tensor_tensor(self.sm8[:, 1:2], self.sm8[:, 1:2], self.lam[:, l:l + 1], ALU.mult),
                          reads=["sm8", "lam"], writes=["sm8"])
                    S.add("dve", lambda e, a2=a2: e.tensor_scalar(self.tsm[:, 0, :], self.ps[:, a2, 0:128], self.sm8[:, 1:2], None, ALU.mult),
                          reads=[("ps", a2), "sm8"], writes=[("tsm", 0)])
                    S.add("dve", lambda e, a1=a1, t=t, hc=hc: e.scalar_tensor_tensor(
                        self.otokC[:, t, hc * 128:(hc + 1) * 128], self.ps[:, a1, 0:128], self.sm8[:, 0:1], self.tsm[:, 0, :],
                        ALU.mult, ALU.add), reads=[("ps", a1), "sm8", ("tsm", 0)], writes=[("otokC", t)])
                    S.add("act", lambda e, t=t, hc=hc: e.activation(
                        self.tsm[:, 1, :], self.otokC[:, t, hc * 128:(hc + 1) * 128], AF.Square,
                        accum_out=self.stat[:, t, hc, 1:2]), reads=[("otokC", t)], writes=["stat", ("tsm", 1)])
        self.bank_pool = list(range(8))
        self.barrier()
        self.norm_out(P, l, 2, self.otokC, lambda t: ("otokC", t))

    def attn_sample(self, P, l):
        S, d, x_ = self.S, self.d, self.x_
        g1 = self.xgroup("kvw")
        S.add("sp", lambda e: e.dma_start(out=x_["kx_in"].rearrange("p (h t) -> p h t", h=4), in_=self.KT[:, :, :]),
              reads=[("KT", h) for h in range(4)], writes=["kx_in"], slot=g1)
        S.add("sp", lambda e: e.dma_start(out=x_["vx_in"].rearrange("p (t h v) -> p t h v", t=8, h=4), in_=self.vaug[:, :, :, 0:128]),
              reads=[("vaug", t) for t in range(8)], writes=["vx_in"], slot=g1)
        self.gather(x_["kx_in"], x_["kx_out"], "kx_in", "kx_out")
        self.gather(x_["vx_in"], x_["vx_out"], "vx_in", "vx_out")
        for kt in range(2):
            gk = self.xgroup("ck")
            S.add("sp", lambda e, kt=kt: e.dma_start(out=self.kstg[:, :], in_=d["ckc"][l, kt]), writes=["kstg"], slot=gk)
            b = self.bank()
            for hc in range(4):
                S.add("pe", lambda e, hc=hc, b=b: e.transpose(self.ps[:, b, hc * 128:(hc + 1) * 128],
                                                              self.kstg[:, hc * 128:(hc + 1) * 128], self.cst[:, 4, :]),
                      reads=["kstg", "cst"], writes=[("ps", b)])
            S.add("act", lambda e, kt=kt, b=b: e.activation(self.kctx[:, :, kt * 128:(kt + 1) * 128],
                                                            self.ps[:, b, :].rearrange("p (h c) -> p h c", h=4), AF.Identity),
                  reads=[("ps", b)], writes=["kctx"])
        gv = self.xgroup("cv")
        S.add("dve", lambda e: e.memset(self.vctx[:, :, :, 128:130], 1.0), writes=["vctx1"])
        for kt in range(2):
            S.add("pool", lambda e, kt=kt: e.dma_start(out=self.vctx[:, kt, :, 0:128], in_=d["cvc"][l, kt]),
                  writes=["vctx"], slot=gv)
        self.barrier()
        S.add("dve", lambda e: e.memset(self.VH[:, :, 128:130], 1.0), reads=["vx_in"], writes=["VH1"])
        self.bank_pool = [4, 5, 6, 7]
        for hc in range(4):
            gh = self.xgroup("kvh")
            S.add("sp", lambda e, hc=hc: e.dma_start(
                out=self.KH[:, :].rearrange("p (r c) -> p r c", r=4),
                in_=x_["kx_out"].rearrange("(r p) c -> p r c", p=128)[:, :, hc * 1024:(hc + 1) * 1024]),
                reads=["kx_out"], writes=["KH"], slot=gh)
            for r in range(4):
                S.add("sp", lambda e, hc=hc, r=r: e.dma_start(
                    out=self.VH[:, r * 8:(r + 1) * 8, 0:128],
                    in_=x_["vx_out"][r * 128:(r + 1) * 128, :].rearrange("p (t h v) -> p t h v", t=8, h=4)[:, :, hc, :]),
                    reads=["vx_out"], writes=["VH"], slot=gh)
            keyt = [("c", 0), ("c", 1)] + [("g", i) for i in range(32)]
            for qb in range(4):
                q0 = qb * 2
                qc = slice(q0 * 128, q0 * 128 + 256)
                for ji, (kind, kt) in enumerate(keyt):
                    kc_ = slice(kt * 128, (kt + 1) * 128)
                    bs = [self.bank(), self.bank()]
                    for sub in range(2):
                        lo, hi = sub * 64, sub * 64 + 64
                        ksrc = self.kctx[lo:hi, hc, kc_] if kind == "c" else self.KH[lo:hi, kc_]
                        S.add("pe", lambda e, sub=sub, lo=lo, hi=hi, bs=bs, ksrc=ksrc: e.matmul(
                            self.ps[:, bs[sub], 0:256], ksrc, self.QT[lo:hi, hc, qc], start=True, stop=True),
                            reads=["kctx" if kind == "c" else "KH", ("QT", hc)], writes=[("ps", bs[sub])])
                    i2 = ji % 2
                    for sub in range(2):
                        S.add("act", lambda e, sub=sub, bs=bs, i2=i2: e.activation(
                            self.ET[:, i2, sub * 256:(sub + 1) * 256], self.ps[:, bs[sub], 0:256], AF.Exp, scale=0.125),
                            reads=[("ps", bs[sub])], writes=[("ET", i2, sub)])
                    vsrc = self.vctx[:, kt, hc, :] if kind == "c" else self.VH[:, kt, :]
                    vkeys = ["vctx", "vctx1"] if kind == "c" else ["VH", "VH1"]
                    for qt in range(2):
                        for sub in range(2):
                            acc = qt * 2 + sub
                            S.add("pe", lambda e, qt=qt, sub=sub, acc=acc, i2=i2, vsrc=vsrc, ji=ji: e.matmul(
                                self.ps[:, acc, 0:130], self.ET[:, i2, sub * 256 + qt * 128: sub * 256 + (qt + 1) * 128],
                                vsrc, start=(ji == 0), stop=(ji == len(keyt) - 1)),
                                reads=[("ET", i2, sub)] + vkeys, writes=[("ps", acc)])
                self.attn_finalize(l, hc, q0)
        self.bank_pool = list(range(8))

    def attn_finalize(self, l, hc, q0):
        S = self.S
        for qt in range(2):
            t = q0 + qt
            a1, a2 = qt * 2, qt * 2 + 1
            S.add("dve", lambda e, a1=a1: e.reciprocal(self.sm8[:, 0:1], self.ps[:, a1, 128:129]),
                  reads=[("ps", a1)], writes=["sm8"])
            S.add("dve", lambda e, a2=a2: e.reciprocal(self.sm8[:, 1:2], self.ps[:, a2, 128:129]),
                  reads=[("ps", a2)], writes=["sm8"])
            S.add("dve", lambda e: e.tensor_tensor(self.sm8[:, 1:2], self.sm8[:, 1:2], self.lam[:, l:l + 1], ALU.mult),
                  reads=["sm8", "lam"], writes=["sm8"])
            S.add("dve", lambda e, a2=a2: e.tensor_scalar(self.tsm[:, 0, :], self.ps[:, a2, 0:128], self.sm8[:, 1:2], None, ALU.mult),
                  reads=[("ps", a2), "sm8"], writes=[("tsm", 0)])
            S.add("dve", lambda e, a1=a1, t=t, hc=hc: e.scalar_tensor_tensor(
                self.otokC[:, t, hc * 128:(hc + 1) * 128], self.ps[:, a1, 0:128], self.sm8[:, 0:1], self.tsm[:, 0, :],
                ALU.mult, ALU.add), reads=[("ps", a1), "sm8", ("tsm", 0)], writes=[("otokC", t)])
            S.add("act", lambda e, t=t, hc=hc: e.activation(
                self.tsm[:, 1, :], self.otokC[:, t, hc * 128:(hc + 1) * 128], AF.Square,
                accum_out=self.stat[:, t, hc, 1:2]), reads=[("otokC", t)], writes=["stat", ("tsm", 1)])

    def layer_norm(self, P, l, which, g, dT, sqT, tmpA, tmpB):
        S = self.S
        gc = slice(g * 512, (g + 1) * 512)
        xk = ("xT", g)
        for kc in range(8):
            S.add("act", lambda e, kc=kc: e.activation(sqT[:, kc, :], self.xT[:, kc, gc], AF.Identity), reads=[xk], writes=[("sqT", kc)])
        bm = self.bank()
        for kc in range(8):
            S.add("pe", lambda e, kc=kc: e.matmul(self.ps[:, bm, :], self.onesb[:, :], sqT[:, kc, :],
                                                  start=(kc == 0), stop=(kc == 7)),
                  reads=[("sqT", kc), "onesb"], writes=[("ps", bm)])
        for kc in range(8):
            S.add("dve", lambda e, kc=kc: e.tensor_tensor(dT[:, kc, :], self.xT[:, kc, gc], self.ps[:, bm, :], ALU.subtract),
                  reads=[xk, ("ps", bm)], writes=[("dT", kc)])
            S.add("act", lambda e, kc=kc: e.activation(sqT[:, kc, :], dT[:, kc, :], AF.Square),
                  reads=[("dT", kc)], writes=[("sqT", kc)])
        bv = self.bank()
        for kc in range(8):
            S.add("pe", lambda e, kc=kc: e.matmul(self.ps[:, bv, :], self.onesb[:, :], sqT[:, kc, :],
                                                  start=(kc == 0), stop=(kc == 7)),
                  reads=[("sqT", kc), "onesb"], writes=[("ps", bv)])
        S.add("act", lambda e: e.activation(tmpA[:], self.ps[:, bv, :], AF.Sqrt, bias=EPSA), reads=[("ps", bv)], writes=["tmpA"])
        S.add("dve", lambda e: e.reciprocal(tmpB[:], tmpA[:]), reads=["tmpA"], writes=["tmpB"])
        go = 48 + which * 8
        bo = 64 + which * 8
        for kc in range(8):
            S.add("dve", lambda e, kc=kc: e.tensor_tensor(dT[:, kc, :], dT[:, kc, :], tmpB[:], ALU.mult),
                  reads=[("dT", kc), "tmpB"], writes=[("dT", kc)])
            S.add("act", lambda e, kc=kc: e.activation(self.xT[:, kc, gc], dT[:, kc, :], AF.Identity,
                                                       scale=self.smallp[:, l, go + kc:go + kc + 1],
                                                       bias=self.smallp[:, l, bo + kc:bo + kc + 1]),
                  reads=[("dT", kc), "smallp"], writes=[xk])

    def merge_phase(self, P, l):
        S, d = self.S, self.d
        c = P["cond"]
        for oc in range(8):
            wg, kg = self.wload(d[f"wM{l}"][2 * oc], 3072)
            wb, kb = self.wload(d[f"wM{l}"][2 * oc + 1], 1536)
            for g in range(P["NG"]):
                gc = slice(g * 512, (g + 1) * 512)
                for br in range(3):
                    b = self.bank()
                    for kc in range(8):
                        S.add("pe", lambda e, b=b, kc=kc, br=br: e.matmul(
                            self.ps[:, b, :], wg[:, br * 1024 + kc * 128: br * 1024 + (kc + 1) * 128], self.hT[:, kc, gc],
                            start=(kc == 0), stop=(kc == 7)), reads=[kg, ("hT", g)], writes=[("ps", b)])
                    S.add("act", lambda e, b=b, br=br: e.activation(self.sig[:, br, :], self.ps[:, b, :], AF.Tanh, scale=0.5),
                          reads=[("ps", b)], writes=[("sig", br)])
                for br in range(3):
                    b = self.bank()
                    for kc in range(4):
                        S.add("pe", lambda e, b=b, kc=kc, br=br: e.matmul(
                            self.ps[:, b, :], wb[:, br * 512 + kc * 128: br * 512 + (kc + 1) * 128], self.yT[:, br, kc, gc],
                            start=(kc == 0), stop=(kc == 3)), reads=[kb, ("yT", br, g)], writes=[("ps", b)])
                    S.add("dve", lambda e, b=b, br=br: e.scalar_tensor_tensor(
                        self.tm3[:, br, :], self.sig[:, br, :], 1.0, self.ps[:, b, :], ALU.add, ALU.mult),
                        reads=[("sig", br), ("ps", b)], writes=[("tm3", br)])
                S.add("dve", lambda e: e.tensor_tensor(self.tm3[:, 0, :], self.tm3[:, 0, :], self.tm3[:, 1, :], ALU.add),
                      reads=[("tm3", 0), ("tm3", 1)], writes=[("tm3", 0)])
                S.add("dve", lambda e, oc=oc, gc=gc: e.tensor_tensor(self.mergedT[:, oc, gc], self.tm3[:, 0, :], self.tm3[:, 2, :], ALU.add),
                      reads=[("tm3", 0), ("tm3", 2)], writes=[("mergedT", g)])
        wo = [self.wload(d[f"wO{l}"][i]) for i in range(2)]
        for g in range(P["NG"]):
            gc = slice(g * 512, (g + 1) * 512)
            for oc in range(8):
                wv, wk = wo[oc // 4]
                ci = oc % 4
                b = self.bank()
                for kc in range(8):
                    S.add("pe", lambda e, b=b, kc=kc, wv=wv, ci=ci: e.matmul(
                        self.ps[:, b, :], wv[:, ci * 1024 + kc * 128: ci * 1024 + (kc + 1) * 128], self.mergedT[:, kc, gc],
                        start=(kc == 0), stop=(kc == 7)), reads=[wk, ("mergedT", g)], writes=[("ps", b)])
                S.add("dve", lambda e, b=b, oc=oc: e.scalar_tensor_tensor(
                    self.xT[:, oc, gc], self.ps[:, b, :], self.modT[:, l, 16 + oc, c:c + 1], self.xT[:, oc, gc],
                    ALU.mult, ALU.add), reads=[("ps", b), ("modT", l, 2), ("xT", g)], writes=[("xT", g)])
            self.layer_norm(P, l, 0, g, self.dT, self.sqT, self.tmpA, self.tmpB)

    def ffn(self, P, l):
        S, d = self.S, self.d
        c = P["cond"]
        T = P["T"]
        S.add("dve", lambda e: e.memset(self.aT[:, :, :], 0.0), writes=[("aT", 0), ("aT", 1)])
        wv = wk = None
        sample = bool(P.get("sample"))
        if sample:
            x_ = self.x_
            S.add("dve", lambda e: e.tensor_copy(self.hxs[:, 0:8].rearrange("p (k o) -> p k o", o=1), self.hT[:, :, 0:1]),
                  reads=[("hT", 0)], writes=["hxs"])
            S.add("dve", lambda e: e.tensor_copy(self.hxs[:, 8:16].rearrange("p (k o) -> p k o", o=1), self.hT[:, :, 1023:1024]),
                  reads=[("hT", 1)], writes=["hxs"])
            g1 = self.xgroup("hxw")
            S.add("sp", lambda e: e.dma_start(out=x_["hx_in"], in_=self.hxs[:, :]), reads=["hxs"], writes=["hx_in"], slot=g1)
            self.gather(x_["hx_in"], x_["hx_out"], "hx_in", "hx_out")
            g2 = self.xgroup("hxl")
            S.add("sp", lambda e: e.dma_start(out=self.hG[:, :, :], in_=x_["hx_out"].rearrange("(r p) c -> p r c", p=128)),
                  reads=["hx_out"], writes=["hG"], slot=g2)
            S.add("dve", lambda e: e.memset(self.hhf[:, :, :], 0.0), writes=["hhf"])
            for j in range(4):
                S.add("dve", lambda e, j=j: e.scalar_tensor_tensor(
                    self.hhf[:, :, 0], self.hG[:, j, 8:16], self.meta[:, 6 + j:7 + j], self.hhf[:, :, 0], ALU.mult, ALU.add),
                    reads=["hG", "meta", "hhf"], writes=["hhf"])
                S.add("dve", lambda e, j=j: e.scalar_tensor_tensor(
                    self.hhf[:, :, 1], self.hG[:, j, 0:8], self.meta[:, 10 + j:11 + j], self.hhf[:, :, 1], ALU.mult, ALU.add),
                    reads=["hG", "meta", "hhf"], writes=["hhf"])
            S.add("dve", lambda e: e.tensor_copy(self.hh[:, :, :], self.hhf[:, :, :]), reads=["hhf"], writes=["hh"])
        for fc in range(NFC):
            if fc % 2 == 0:
                wv, wk = self.wload(d[f"wU{l}"][fc // 2])
            ca, cb = (fc % 2) * 2, (fc % 2) * 2 + 1
            a2 = fc % 2
            bbs = {}
            for g in range(P["NG"]):
                gc = slice(g * 512, (g + 1) * 512)
                ba = self.bank()
                for kc in range(8):
                    S.add("pe", lambda e, b=ba, kc=kc, ci=ca, wv=wv, gc=gc: e.matmul(
                        self.ps[:, b, :], wv[:, ci * 1024 + kc * 128: ci * 1024 + (kc + 1) * 128], self.hT[:, kc, gc],
                        start=(kc == 0), stop=(kc == 7)), reads=[wk, ("hT", g)], writes=[("ps", ba)])
                for (s0, n, col0) in P["conv_rows"][g]:
                    S.add("act", lambda e, s0=s0, n=n, col0=col0, ba=ba: e.activation(
                        self.aT[:, a2, col0:col0 + n], self.ps[:, ba, s0:s0 + n], AF.Identity),
                        reads=[("ps", ba)], writes=[("aT", a2)])
            if sample:
                bh = self.bank()
                for kc in range(8):
                    S.add("pe", lambda e, kc=kc, wv=wv, bh=bh: e.matmul(
                        self.ps[:, bh, 0:2], wv[:, ca * 1024 + kc * 128: ca * 1024 + (kc + 1) * 128], self.hh[:, kc, :],
                        start=(kc == 0), stop=(kc == 7)), reads=[wk, "hh"], writes=[("ps", bh)])
                S.add("dve", lambda e, bh=bh: e.tensor_scalar(self.aT[:, a2, 0:1], self.ps[:, bh, 0:1], self.meta[:, 4:5], None, ALU.mult),
                      reads=[("ps", bh), "meta"], writes=[("aT", a2)])
                S.add("dve", lambda e, bh=bh: e.tensor_scalar(self.aT[:, a2, 1025:1026], self.ps[:, bh, 1:2], self.meta[:, 5:6], None, ALU.mult),
                      reads=[("ps", bh), "meta"], writes=[("aT", a2)])
            for g in range(P["NG"]):
                gc = slice(g * 512, (g + 1) * 512)
                bb = self.bank()
                for kc in range(8):
                    S.add("pe", lambda e, b=bb, kc=kc, ci=cb, wv=wv, gc=gc: e.matmul(
                        self.ps[:, b, :], wv[:, ci * 1024 + kc * 128: ci * 1024 + (kc + 1) * 128], self.hT[:, kc, gc],
                        start=(kc == 0), stop=(kc == 7)), reads=[wk, ("hT", g)], writes=[("ps", bb)])
                cw = 80
                for (s0, n, col0) in P["conv_rows"][g]:
                    S.add("dve", lambda e, s0=s0, n=n, col0=col0: e.tensor_scalar(
                        self.cT[:, a2, s0:s0 + n], self.aT[:, a2, col0 - 1:col0 - 1 + n],
                        self.smallp[:, l, cw + fc:cw + fc + 1], self.smallp[:, l, cw + 66 + fc:cw + 66 + fc + 1],
                        ALU.mult, ALU.add), reads=[("aT", a2), "smallp"], writes=[("cT", a2)])
                    for k in (1, 2):
                        S.add("dve", lambda e, s0=s0, n=n, col0=col0, k=k: e.scalar_tensor_tensor(
                            self.cT[:, a2, s0:s0 + n], self.aT[:, a2, col0 - 1 + k:col0 - 1 + k + n],
                            self.smallp[:, l, cw + k * NFC + fc:cw + k * NFC + fc + 1], self.cT[:, a2, s0:s0 + n],
                            ALU.mult, ALU.add), reads=[("aT", a2), "smallp", ("cT", a2)], writes=[("cT", a2)])
                S.add("act", lambda e: e.activation(self.glT[:, a2, :], self.cT[:, a2, :], AF.Gelu_apprx_tanh),
                      reads=[("cT", a2)], writes=[("glT", a2)])
                S.add("dve", lambda e, bb=bb, gc=gc: e.tensor_tensor(self.gT[:, fc, gc], self.glT[:, a2, :], self.ps[:, bb, :], ALU.mult),
                      reads=[("glT", a2), ("ps", bb)], writes=[("gT", g)])
        for oc in range(8):
            wv, wk = self.wload(d[f"wD{l}"][oc], NFC * 128)
            for g in range(P["NG"]):
                gc = slice(g * 512, (g + 1) * 512)
                b = self.bank()
                for fc in range(NFC):
                    S.add("pe", lambda e, b=b, fc=fc, wv=wv: e.matmul(
                        self.ps[:, b, :], wv[:, fc * 128:(fc + 1) * 128], self.gT[:, fc, gc],
                        start=(fc == 0), stop=(fc == NFC - 1)), reads=[wk, ("gT", g)], writes=[("ps", b)])
                S.add("dve", lambda e, b=b, oc=oc, gc=gc: e.scalar_tensor_tensor(
                    self.xT[:, oc, gc], self.ps[:, b, :], self.modT[:, l, 40 + oc, c:c + 1], self.xT[:, oc, gc],
                    ALU.mult, ALU.add), reads=[("ps", b), ("modT", l, 5), ("xT", g)], writes=[("xT", g)])
        for g in range(P["NG"]):
            self.layer_norm(P, l, 1, g, self.dT2, self.sqT2, self.tmpA2, self.tmpB2)

    def store_x(self, P):
        S = self.S
        dst = self.o[P["y"]]
        for t in range(P["NT"]):
            sl = t % 2
            for half in range(2):
                b = self.bank()
                for q in range(4):
                    kc = half * 4 + q
                    S.add("pe", lambda e, b=b, q=q, kc=kc, t=t: e.transpose(
                        self.ps[:, b, q * 128:(q + 1) * 128], self.xT[:, kc, t * 128:(t + 1) * 128], self.cst[:, 4, :]),
                        reads=[("xT", t // 4), "cst"], writes=[("ps", b)])
                S.add("act" if half == 0 else "dve",
                      (lambda e, b=b, half=half, sl=sl: e.activation(self.stage[:, sl, half * 512:(half + 1) * 512], self.ps[:, b, :], AF.Identity))
                      if half == 0 else
                      (lambda e, b=b, half=half, sl=sl: e.tensor_copy(self.stage[:, sl, half * 512:(half + 1) * 512], self.ps[:, b, :])),
                      reads=[("ps", b)], writes=[("stage", sl)])
            op = S.add("sp", lambda e, t=t, sl=sl: e.dma_start(out=dst[t * 128:(t + 1) * 128, :], in_=self.stage[:, sl, :]),
                       reads=[("stage", sl)], slot=self.s_io[sl])
            self.out_ops.append(op)

    def run_pass(self, P):
        stop = int(os.environ.get("KSTOP", "99"))
        if stop < 1:
            return
        self.load_x(P)
        self.barrier()
        for l in range(L):
            if stop < 2 + 10 * l:
                break
            self.modulate(P, l, 0)
            self.mixer_AB(P, l, 0)
            self.ada_deferred()
            self.barrier()
            if stop < 3 + 10 * l:
                break
            self.mixer_AB(P, l, 1)
            self.barrier()
            if stop < 4 + 10 * l:
                break
            self.mixer_C(P, l)
            self.barrier()
            if stop < 5 + 10 * l:
                break
            self.merge_phase(P, l)
            self.modulate(P, l, 1)
            self.barrier()
            if stop < 6 + 10 * l:
                break
            self.ffn(P, l)
            self.barrier()
        self.store_x(P)
        self.barrier()

    def build(self):
        self.declare()
        self.init_phase()
        PP = dict(T=512, NT=4, NG=1, cond=0, x="xp", y="yp", rope=False, states_out=True, kv_out=True,
                  seqs=[(0, 2), (2, 2)],
                  attn=[(0, 2, [0, 1]), (2, 2, [2, 3])],
                  conv_rows=[[(0, 256, 1), (256, 256, 515)]])
        if not os.environ.get("KNOPROMPT"):
            self.run_pass(PP)
        if self.do_sample:
            PS = dict(T=1024, NT=8, NG=2, cond=1, x="xs", y="ys", rope=True, states_out=False, kv_out=False,
                      seqs=[(0, 8)], sample=True, attn=[],
                      conv_rows=[[(0, 512, 1)], [(0, 512, 513)]])
            self.run_pass(PS)
        self.S.wait_for("sp", self.out_ops)
        global LAST_S
        LAST_S = self.S
        self.S.emit(self.nc, self.st)


_CACHE = {}


def _build_nc(do_sample=True):
    key = ("nc", do_sample)
    if key not in _CACHE:
        nc = bass.Bass("TRN2", target_bir_lowering=False)
        with contextlib.ExitStack() as st:
            b = Builder(nc, st, do_sample=do_sample)
            b.build()
        _CACHE[key] = nc
    return _CACHE[key]


def kernel(**inp):
    inp = {k: np.asarray(v) for k, v in inp.items()}
    W = _host_weights(inp)
    nc = _build_nc(not os.environ.get("KNOSAMPLE"))
    in_maps = []
    ncores = int(os.environ.get("KCORES", str(NCORES)))
    for c in range(ncores):
        b, r = c // 4, c % 4
        m = dict(W)
        m["xp"] = np.ascontiguousarray(inp["x_prompt"][2 * c:2 * c + 2].reshape(512, D))
        m["xs"] = np.ascontiguousarray(inp["x_sample"][b, r * 1024:(r + 1) * 1024])
        cond = np.stack([inp["c_ctx"], inp["c"][b]], axis=-1)
        m["cond"] = np.ascontiguousarray(cond.reshape(8, 128, 2).transpose(1, 0, 2), dtype=np.float32)
        rc, rs = _rope_tables(r * 1024)
        m["ropec"], m["ropes"] = rc, rs
        meta = np.zeros((128, 16), np.float32)
        for s_ in range(4):
            meta[0:64, s_] = 1.0 if r == s_ else 0.0
            meta[64:128, s_] = 1.0 if r == 3 - s_ else 0.0
            meta[:, 6 + s_] = 1.0 if s_ == r - 1 else 0.0
            meta[:, 10 + s_] = 1.0 if s_ == r + 1 else 0.0
        meta[:, 4] = 1.0 if r > 0 else 0.0
        meta[:, 5] = 1.0 if r < 3 else 0.0
        m["meta"] = meta
        for nm, src in (("sret", inp["state_ret"]), ("sgla", inp["state_gla"])):
            a = src[b]
            m[nm] = np.ascontiguousarray(a.transpose(0, 1, 3, 2, 4).reshape(L, 128, 4, 128), dtype=np.float32)
        ck = inp["cache_diff_k"][b]
        m["ckc"] = np.ascontiguousarray(ck.reshape(L, 8, 2, 128, 64).transpose(0, 2, 3, 1, 4).reshape(L, 2, 128, 512), dtype=np.float32)
        cv = inp["cache_diff_v"][b]
        m["cvc"] = np.ascontiguousarray(cv.reshape(L, 4, 2, 128, 128).transpose(0, 2, 3, 1, 4), dtype=np.float32)
        in_maps.append(m)
    res = run_bass_kernel_spmd(nc, in_maps, core_ids=list(range(ncores)))
    R = list(res.results)
    while len(R) < NCORES:
        R.append(R[0])
    y_prompt = np.concatenate([R[c]["yp"].reshape(2, 256, D) for c in range(NCORES)], axis=0)
    y_sample = np.stack([np.concatenate([R[b * 4 + r]["ys"] for r in range(4)], axis=0) for b in range(2)], axis=0)
    ndk = np.concatenate([R[c]["ndk"] for c in range(NCORES)], axis=0)
    ndv = np.concatenate([R[c]["ndv"] for c in range(NCORES)], axis=0)
    nsr = np.concatenate([R[c]["nsr"] for c in range(NCORES)], axis=0)
    nsg = np.concatenate([R[c]["nsg"] for c in range(NCORES)], axis=0)
    f = np.float32
    return (y_prompt.astype(f), y_sample.astype(f), ndk.astype(f), ndv.astype(f), nsr.astype(f), nsg.astype(f))
```

```python
import contextlib
import os
import math
import numpy as np
import concourse.bass as bass
import concourse.mybir as mybir
from concourse.bass_utils import run_bass_kernel_spmd

F32 = mybir.dt.float32
BF16 = mybir.dt.bfloat16
AF = mybir.ActivationFunctionType
ALU = mybir.AluOpType

NCORES = 8
D = 1024
L = 2
DIN = 7680
DFF = 2816
NFC = 22
ALPHA = (2 * L) ** 0.25
EPS = 1e-5
EPSA = EPS / (ALPHA * ALPHA)
WSZ = 4096
NSLOT = 3
UW = 21504


class _Op:
    __slots__ = ("eng", "fn", "deps", "slot", "cnt", "need_inc", "inc")

    def __init__(self, eng, fn, slot, inc):
        self.eng = eng
        self.fn = fn
        self.deps = set()
        self.slot = slot
        self.cnt = 0
        self.need_inc = False
        self.inc = inc


class _Slot:
    def __init__(self, name, wait_all):
        self.name = name
        self.wait_all = wait_all
        self.total = 0
        self.sem = None


DEBUG_WAITS = None
LAST_S = None


class _Rec:
    def __init__(self):
        self.call = None

    def __getattr__(self, name):
        def f(*a, **k):
            self.call = (name, a, k)
            return None
        return f


def _freeze(fn):
    r = _Rec()
    fn(r)
    call = r.call
    if call is None:
        return lambda e: None
    name, a, k = call
    return lambda e: getattr(e, name)(*a, **k)


class Sched:
    ENGS = ("pe", "act", "dve", "pool", "sp")

    def __init__(self):
        self.ops = {e: [] for e in self.ENGS}
        self.lastw = {}
        self.readers = {}
        self.slots = []

    def slot(self, name, wait_all=False):
        s = _Slot(name, wait_all)
        self.slots.append(s)
        return s

    def add(self, eng, fn, reads=(), writes=(), slot=None, inc=16):
        op = _Op(eng, _freeze(fn), slot, inc)
        deps = op.deps
        for r in reads:
            w = self.lastw.get(r)
            if w is not None:
                deps.add(w)
            if isinstance(r, tuple) and r[0] == "ps":
                for rd in self.readers.get(r, {}).values():
                    if rd.eng != eng:
                        deps.add(rd)
        for k in writes:
            w = self.lastw.get(k)
            if w is not None:
                deps.add(w)
            for rd in self.readers.get(k, {}).values():
                deps.add(rd)
        rk = eng if slot is None else ("dma", id(op))
        for r in reads:
            self.readers.setdefault(r, {})[rk] = op
        for k in writes:
            self.lastw[k] = op
            self.readers[k] = {}
        if slot is not None:
            slot.total += inc
            op.cnt = slot.total
        self.ops[eng].append(op)
        return op

    def wait_for(self, eng, ops):
        op = _Op(eng, lambda e: None, None, 0)
        op.deps = set(o for o in ops if o is not None)
        self.ops[eng].append(op)
        return op

    def emit(self, nc, stack):
        for e in self.ENGS:
            for op in self.ops[e]:
                for d in op.deps:
                    if d.slot is None:
                        if d.eng == op.eng and op.slot is None and d.eng == "pe":
                            continue
                        d.need_inc = True
        esem = {}
        for e in self.ENGS:
            c = 0
            for op in self.ops[e]:
                if op.slot is None and op.need_inc:
                    c += 1
                    op.cnt = c
            if c > 0:
                esem[e] = stack.enter_context(nc.semaphore("es_" + e))
        for s in self.slots:
            if s.total > 0:
                s.sem = stack.enter_context(nc.semaphore("sl_" + s.name))
        block = stack.enter_context(nc.Block())

        def body(e):
            def run(eng):
                seen = {}
                for op in self.ops[e]:
                    waits = {}
                    for d in op.deps:
                        if d.slot is not None:
                            if d.slot is op.slot and d.slot.wait_all:
                                continue
                            key = d.slot
                            val = d.slot.total if d.slot.wait_all else d.cnt
                            sem = d.slot.sem
                        else:
                            if d.eng == e and op.slot is None and e == "pe":
                                continue
                            key = d.eng
                            val = d.cnt
                            sem = esem[d.eng]
                        if val > waits.get(key, (None, 0))[1]:
                            waits[key] = (sem, val)
                    for key in sorted(waits.keys(), key=lambda k: k if isinstance(k, str) else "~" + k.name):
                        sem, val = waits[key]
                        if seen.get(key, 0) >= val:
                            continue
                        seen[key] = val
                        eng.wait_ge(sem, val)
                        if DEBUG_WAITS is not None:
                            DEBUG_WAITS.append((e, len(DEBUG_WAITS), key if isinstance(key, str) else key.name, val))
                    inst = op.fn(eng)
                    if op.slot is not None:
                        inst.then_inc(op.slot.sem, op.inc)
                    elif op.need_inc:
                        inst.then_inc(esem[e], 1)
            return run

        block.tensor(body("pe"))
        block.scalar(body("act"))
        block.vector(body("dve"))
        block.gpsimd(body("pool"))
        block.sync(body("sp"))


PTS = np.cumsum([0, 256, 256, 512, 512, 256, 256, 512, 512, 512, 512, 512, 1024, 1024, 1024])
(O_AQ, O_AK, O_AV, O_AG, O_BQ, O_BK, O_BV, O_BR, O_CQ, O_CK, O_CV, O_MA, O_MB, O_MC) = [int(v) for v in PTS[:-1]]


def _fchunk(w, cols):
    sub = w[:, cols]
    return sub.reshape(8, 128, sub.shape[1]).transpose(1, 0, 2)


def _pack(chunks, per):
    out = []
    for i in range(0, len(chunks), per):
        blk = np.concatenate([c.reshape(128, -1) for c in chunks[i:i + per]], axis=1)
        assert blk.shape[1] <= WSZ
        if blk.shape[1] < WSZ:
            blk = np.concatenate([blk, np.zeros((128, WSZ - blk.shape[1]), np.float32)], axis=1)
        out.append(blk)
    return np.ascontiguousarray(np.stack(out, 0), dtype=np.float32)


def _host_weights(inp):
    W = {}
    ar = np.arange
    for l in range(L):
        w_in = inp["w_in"][l]
        ch = []
        for (oq, ok) in ((O_AQ, O_AK), (O_BQ, O_BK)):
            for o in (oq, ok):
                for h in range(4):
                    cols = np.concatenate([o + h * 64 + ar(64), o + h * 64 + ar(64)])
                    ch.append(_fchunk(w_in, cols))
        for o in (O_CQ, O_CK):
            for h in range(4):
                ch.append(_fchunk(w_in, o + h * 128 + ar(128)))
        W[f"wF{l}"] = _pack(ch, 4)
        ch = [_fchunk(w_in, o + ar(512)) for o in (O_AV, O_AG, O_BV, O_BR, O_CV, O_CK)]
        W[f"wT{l}"] = _pack(ch, 1)
        ch = []
        wb = inp["w_branch"][l]
        for oc in range(8):
            g = [_fchunk(w_in, o + oc * 128 + ar(128)).reshape(128, -1) for o in (O_MA, O_MB, O_MC)]
            ch.append(np.concatenate(g, axis=1))
            b = [wb[br][:, oc * 128:(oc + 1) * 128].reshape(4, 128, 128).transpose(1, 0, 2).reshape(128, -1)
                 for br in range(3)]
            ch.append(np.concatenate(b, axis=1))
        W[f"wM{l}"] = _pack(ch, 1)
        wo = inp["w_out"][l]
        W[f"wO{l}"] = _pack([_fchunk(wo, oc * 128 + ar(128)) for oc in range(8)], 4)
        wu = inp["ffn_w_up"][l]
        ch = []
        for fc in range(NFC):
            ch.append(_fchunk(wu, fc * 128 + ar(128)))
            ch.append(_fchunk(wu, DFF + fc * 128 + ar(128)))
        W[f"wU{l}"] = _pack(ch, 4)
        wd = inp["ffn_w_down"][l]
        ch = [wd[:, oc * 128:(oc + 1) * 128].reshape(NFC, 128, 128).transpose(1, 0, 2) for oc in range(8)]
        W[f"wD{l}"] = _pack(ch, 1)
        aw = inp["ada_w"][l]
        W[f"wAfull{l}"] = _pack([_fchunk(aw, j * 512 + ar(512)) for j in range(12)], 1)
    wa1 = np.zeros((128, L, 8, 32), np.float32)
    wa2 = np.zeros((33, L, 512), np.float32)
    for l in range(L):
        for e in range(2):
            wa1[:, l, :, e * 16:(e + 1) * 16] = inp["gla_wa1"][l, e].reshape(8, 128, 16).transpose(1, 0, 2)
            for h in range(4):
                c0 = h * 128 + e * 64
                wa2[e * 16:(e + 1) * 16, l, c0:c0 + 64] = inp["gla_wa2"][l, e][:, h * 64:(h + 1) * 64]
                wa2[32, l, c0:c0 + 64] = inp["gla_ba"][l, e][h * 64:(h + 1) * 64]
    W["wa1"] = wa1
    W["wa2"] = wa2
    sp = []
    for l in range(L):
        cols = [inp["ada_b"][l].reshape(48, 128).T,
                inp["ln_g"][l].reshape(16, 128).T,
                inp["ln_b"][l].reshape(16, 128).T,
                inp["ffn_conv_w"][l].reshape(3 * NFC, 128).T,
                inp["ffn_conv_b"][l].reshape(NFC, 128).T]
        sp.append(np.concatenate(cols, axis=1))
    W["smallp"] = np.ascontiguousarray(np.stack(sp, 1), dtype=np.float32)
    bc = []
    for l in range(L):
        row = np.concatenate([inp["gla_norm_g"][l], inp["diff_subln_g"][l],
                              inp["diff_lam"][l].reshape(-1), inp["ret_decay"][l].reshape(-1)])
        bc.append(np.broadcast_to(row[None, :], (128, row.shape[0])))
    W["bcast"] = np.ascontiguousarray(np.stack(bc, 1), dtype=np.float32)
    j = ar(128)[:, None]
    i = ar(128)[None, :]
    cs = np.zeros((128, 9, 128), np.float32)
    cs[:, 8] = 1.0
    cs[:, 0] = (j <= i)
    cs[:, 1] = (j >= i)
    cs[:, 2] = (j > i)
    cs[:, 3] = (j < i)
    cs[:, 4] = (j == i)
    m = ar(128)
    sw = np.where((m % 32) < 16, m + 16, m - 16)
    cs[sw, 5, m] = 1.0
    cs[:, 6] = (j == i) & (j < 64)
    cs[:, 7] = (j == i) & (j >= 64)
    W["consts"] = cs
    return W


def _rope_tables(t0):
    t = t0 + np.arange(1024)
    row = (t // 64).astype(np.float32)
    col = (t % 64).astype(np.float32)
    inv = (np.float32(10000.0) ** (-np.arange(16, dtype=np.float32) / np.float32(16))).astype(np.float32)
    cos = np.zeros((128, 1024), np.float32)
    sin = np.zeros((128, 1024), np.float32)
    for p in range(128):
        d = p % 64
        pos = row if d < 32 else col
        ang = (pos * inv[d % 16]).astype(np.float32)
        cos[p] = np.cos(ang)
        s = np.sin(ang)
        sin[p] = -s if (d % 32) < 16 else s
    return cos, sin


class Builder:
    def __init__(self, nc, st, do_sample=True):
        self.nc = nc
        self.st = st
        self.S = Sched()
        self.do_sample = do_sample
        self.wslot_i = 0
        self.out_ops = []
        self.out_seen = 0

    def dram_in(self, name, shape, dt=F32):
        return self.nc.dram_tensor(name, list(shape), dt, kind="ExternalInput").ap()

    def dram_out(self, name, shape, dt=F32):
        return self.nc.dram_tensor(name, list(shape), dt, kind="ExternalOutput").ap()

    def sb(self, name, shape, dt):
        return self.st.enter_context(self.nc.sbuf_tensor(name, list(shape), dt))

    def wload(self, src_blk, nelem=WSZ):
        s = self.wslot_i % NSLOT
        self.wslot_i += 1
        dst = self.wring[:, s, 0:nelem]
        key = ("w", s)
        self.S.add("pool", lambda e: e.dma_start(out=dst, in_=src_blk[:, 0:nelem]), writes=[key], slot=self.wsl[s])
        return self.wring[:, s, :], key

    def xgroup(self, name):
        self.xs_i += 1
        return self.S.slot("x%d_%s" % (self.xs_i, name), wait_all=True)

    def gather(self, src, dst, rk, wk):
        sl = self.xgroup("cc")
        groups = [[0, 1, 2, 3], [4, 5, 6, 7]] if not os.environ.get("KG4") else [[0, 1, 2, 3]]
        self.S.add("pool", lambda e: e.collective_compute("AllGather", ALU.bypass, replica_groups=groups,
                                                          ins=[src], outs=[dst]),
                   reads=[rk], writes=[wk], slot=sl, inc=1)

    def declare(self):
        nc = self.nc
        di = self.dram_in
        self.d = d = {}
        d["xp"] = di("xp", [512, D])
        d["xs"] = di("xs", [1024, D])
        d["cond"] = di("cond", [128, 8, 2])
        for l in range(L):
            d[f"wF{l}"] = di(f"wF{l}", [6, 128, WSZ])
            d[f"wT{l}"] = di(f"wT{l}", [6, 128, WSZ])
            d[f"wM{l}"] = di(f"wM{l}", [16, 128, WSZ])
            d[f"wO{l}"] = di(f"wO{l}", [2, 128, WSZ])
            d[f"wU{l}"] = di(f"wU{l}", [11, 128, WSZ])
            d[f"wD{l}"] = di(f"wD{l}", [8, 128, WSZ])
            d[f"wAq{l}"] = di(f"wAq{l}", [3, 128, WSZ])
        d["wa1"] = di("wa1", [128, L, 8, 32])
        d["wa2"] = di("wa2", [33, L, 512])
        d["smallp"] = di("smallp", [128, L, 168])
        d["bcast"] = di("bcast", [128, L, 520])
        d["consts"] = di("consts", [128, 9, 128])
        d["ropec"] = di("ropec", [128, 1024])
        d["ropes"] = di("ropes", [128, 1024])
        d["meta"] = di("meta", [128, 16])
        d["sret"] = di("sret", [L, 128, 4, 128])
        d["sgla"] = di("sgla", [L, 128, 4, 128])
        d["ckc"] = di("ckc", [L, 2, 128, 512])
        d["cvc"] = di("cvc", [L, 2, 128, 4, 128])
        dint = lambda name, shape, dt: nc.dram_tensor(name, list(shape), dt, addr_space="Local", kind="Internal").ap()
        self.x_ = x_ = {}
        x_["kx_in"] = dint("kx_in", [128, 4096], BF16)
        x_["kx_out"] = dint("kx_out", [512, 4096], BF16)
        x_["vx_in"] = dint("vx_in", [128, 4096], BF16)
        x_["vx_out"] = dint("vx_out", [512, 4096], BF16)
        x_["st_in"] = dint("st_in", [128, 516], F32)
        x_["st_out"] = dint("st_out", [512, 516], F32)
        x_["md_in"] = dint("md_in", [12, 512], F32)
        x_["md_out"] = dint("md_out", [48, 512], F32)
        x_["hx_in"] = dint("hx_in", [128, 16], BF16)
        x_["hx_out"] = dint("hx_out", [512, 16], BF16)
        do = self.dram_out
        self.o = o = {}
        o["yp"] = do("yp", [512, D])
        o["ys"] = do("ys", [1024, D])
        o["ndk"] = do("ndk", [2, L, 8, 256, 64])
        o["ndv"] = do("ndv", [2, L, 4, 256, 128])
        o["nsr"] = do("nsr", [2, L, 2, 4, 64, 128])
        o["nsg"] = do("nsg", [2, L, 2, 4, 64, 128])


        sb = self.sb
        self.xT = sb("xT", [128, 8, 1024], F32)
        self.hT = sb("hT", [128, 8, 1024], BF16)
        self.wring = sb("wring", [128, NSLOT, WSZ], BF16)
        self.wsl = [self.S.slot(f"w{s}") for s in range(NSLOT)]
        self.cst = sb("cst", [128, 9, 128], F32)
        self.cstb = sb("cstb", [128, 9, 128], BF16)
        self.onesb = sb("onesb", [128, 128], BF16)
        self.smallp = sb("smallp_sb", [128, L, 168], F32)
        self.bcast = sb("bcast_sb", [128, L, 520], F32)
        self.modT = sb("modT", [128, L, 48, 2], F32)
        self.condf = sb("condf", [128, 8, 2], F32)
        self.condb = sb("condb", [128, 8, 2], BF16)
        self.modtok = sb("modtok", [2, 512], F32)
        self.wa1 = sb("wa1_sb", [128, L, 8, 32], BF16)
        self.wa2 = sb("wa2_sb", [33, L, 512], BF16)
        self.yT = sb("yT", [128, 3, 4, 1024], BF16)
        if self.do_sample:
            self.ropec = sb("ropec_sb", [128, 1024], F32)
            self.ropes = sb("ropes_sb", [128, 1024], F32)
        self.lam = sb("lam", [128, 8], F32)
        self.retsp = sb("retsp", [128, L, 8], F32)
        self.subg = sb("subg", [128, 128], F32)
        self.ps = self.st.enter_context(nc.psum_tensor("ps", [128, 8, 512], F32))
        self.s_in = self.S.slot("init", wait_all=True)
        self.s_inp = self.S.slot("initp", wait_all=True)
        self.s_io = [self.S.slot(f"io{i}") for i in range(2)]
        self.s_st = [self.S.slot(f"st{i}") for i in range(2)]
        self.s_out = self.S.slot("out_misc")
        self.s_cc = self.S.slot("cc")
        self.s_xs = [self.S.slot(f"xs{i}") for i in range(4)]
        self.xs_i = 0
        self.meta = sb("meta_sb", [128, 16], F32)
        U = sb("U", [128, UW], F32)
        self.U = U

        def vb(off, shape):
            n = int(np.prod(shape)) // 2
            a = U[:, off:off + n].bitcast(BF16)
            return a if len(shape) == 1 else a.rearrange(
                "p (" + " ".join("abcd"[:len(shape)]) + ") -> p " + " ".join("abcd"[:len(shape)]),
                **{k: v for k, v in zip("abcd", shape)})

        def vf(off, shape):
            n = int(np.prod(shape))
            a = U[:, off:off + n]
            return a if len(shape) == 1 else a.rearrange(
                "p (" + " ".join("abcd"[:len(shape)]) + ") -> p " + " ".join("abcd"[:len(shape)]),
                **{k: v for k, v in zip("abcd", shape)})

        self.QT = vb(0, [4, 1024])
        self.otokAB = vb(0, [8, 512])
        self.dbf = vb(0, [4, 8, 128])
        self.stage = vf(0, [2, 1024])
        self.KT = vb(2048, [4, 1024])
        self.sstk = vb(2048, [4, 8, 128])
        self.vtok = vb(4096, [8, 512])
        self.gtok = vb(6144, [8, 512])
        self.otokC = vb(6144, [8, 512])
        self.qd = vb(8192, [4, 8, 128])
        self.Pm = vb(10240, [4, 8, 128])
        self.ropex = vb(10240, [512])
        self.ropet = vf(10496, [512])
        self.ropeu = vf(11008, [512])
        self.cstage = vf(10240, [2, 512])
        self.KH = vb(8192, [4096])
        self.VH = vb(12288, [32, 130])
        self.vctx = vb(10240, [2, 4, 130])
        self.kctx = vb(10760, [4, 256])
        self.kstg = vf(11272, [512])
        self.delta = vf(12288, [4, 8, 128])
        self.vaug = vb(12288, [8, 4, 130])
        self.ET = vb(12288 + 2080, [2, 512])
        o_ = 12288 + 4160
        self.s32 = vf(o_, [2, 4, 128]); o_ += 1024
        self.EqT = vf(o_, [4, 128]); o_ += 512
        self.EkT = vf(o_, [4, 128]); o_ += 512
        self.Ekd = vf(o_, [4, 128]); o_ += 512
        self.dcol = vf(o_, [4, 8]); o_ += 32
        self.latok = vf(o_, [512]); o_ += 512
        self.uT = vb(o_, [1024]); o_ += 512
        self.stat = vf(o_, [8, 4, 2]); o_ += 64
        self.rstd = vf(o_, [8, 4]); o_ += 32
        self.st6 = vf(o_, [6]); o_ += 8
        self.ksb = vb(o_, [2, 128]); o_ += 128
        self.kdtok = vb(o_, [2, 128]); o_ += 128
        self.ytok = vb(o_, [2, 512]); o_ += 512
        self.tsm = vf(o_, [2, 128]); o_ += 256
        self.sm8 = vf(o_, [16]); o_ += 16
        self.ggt = vb(o_, [512]); o_ += 256
        assert o_ <= UW, o_
        self.Sin = self.EqT
        self.Srun = self.EkT
        self.GL = self.Ekd
        self.sinb = self.uT[:, 0:512].rearrange("p (h v) -> p h v", h=4)
        self.cp = self.latok[:, 0:32].rearrange("p (h t) -> p h t", h=4)
        self.Dg = self.latok[:, 32:48].rearrange("p (s h) -> p s h", s=4)
        self.hxs = self.latok[:, 64:72].bitcast(BF16)
        self.hG = self.latok[:, 72:104].bitcast(BF16).rearrange("p (r c) -> p r c", r=4)
        self.hhf = self.latok[:, 104:120].rearrange("p (k c) -> p k c", k=8)
        self.hh = self.latok[:, 120:128].bitcast(BF16).rearrange("p (k c) -> p k c", k=8)
        self.mergedT = vb(0, [8, 1024])
        self.dT = vf(4096, [8, 512])
        self.sqT = vb(8192, [8, 512])
        self.tmpA = vf(10240, [512])
        self.tmpB = vf(10752, [512])
        self.sig = vf(11264, [3, 512])
        self.tm3 = vf(12800, [3, 512])
        self.gT = vb(0, [NFC, 1024])
        self.aT = vf(11264, [2, 1028])
        self.cT = vf(13320, [2, 512])
        self.glT = vf(14344, [2, 512])
        self.dT2 = vf(15368, [8, 512])
        self.tmpA2 = vf(19464, [512])
        self.tmpB2 = vf(19976, [512])
        self.sqT2 = vb(11264, [8, 512])
        assert 20488 <= UW
        self.bank_i = 0
        self.bank_pool = list(range(8))

    def bank(self):
        b = self.bank_pool[self.bank_i % len(self.bank_pool)]
        self.bank_i += 1
        return b

    def barrier(self, pool=False):
        S = self.S
        last = {}
        for e in S.ENGS:
            last[e] = None
            for op in reversed(S.ops[e]):
                if op.inc != 0:
                    last[e] = op
                    break
        dmas = [op for op in self.out_ops[self.out_seen:]]
        self.out_seen = len(self.out_ops)
        for e in (("pe", "act", "dve", "pool", "sp") if (pool or os.environ.get("KPOOLBAR")) else ("pe", "act", "dve", "sp")):
            deps = [last[o] for o in S.ENGS if (o != e or e != "pe") and last[o] is not None and last[o].slot is None]
            S.wait_for(e, deps + dmas)
        if os.environ.get("KNOCLR"):
            return
        keepw = lambda k: isinstance(k, tuple) and k[0] == "w"
        for k in list(S.lastw.keys()):
            if S.lastw[k].slot is None and not keepw(k):
                del S.lastw[k]
        for k in list(S.readers.keys()):
            if not keepw(k):
                S.readers[k] = {a: r for a, r in S.readers[k].items() if r.slot is not None}

    def psb(self, b):
        return self.ps[:, b, :].bitcast(BF16)

    def init_phase(self):
        S, d = self.S, self.d
        si = self.s_in
        S.add("sp", lambda e: e.dma_start(out=self.cst[:], in_=d["consts"]), writes=["cst"], slot=si)
        S.add("pool", lambda e: e.dma_start(out=self.cstb[:], in_=d["consts"]), writes=["cstb"], slot=self.s_inp)
        S.add("sp", lambda e: e.dma_start(out=self.smallp[:], in_=d["smallp"]), writes=["smallp"], slot=si)
        S.add("sp", lambda e: e.dma_start(out=self.bcast[:], in_=d["bcast"]), writes=["bcast"], slot=si)
        S.add("sp", lambda e: e.dma_start(out=self.condf[:], in_=d["cond"]), writes=["condf"], slot=si)
        S.add("pool", lambda e: e.dma_start(out=self.wa1[:], in_=d["wa1"]), writes=["wa1"], slot=self.s_inp)
        S.add("pool", lambda e: e.dma_start(out=self.wa2[:], in_=d["wa2"]), writes=["wa2"], slot=self.s_inp)
        if self.do_sample:
            S.add("sp", lambda e: e.dma_start(out=self.meta[:], in_=d["meta"]), writes=["meta"], slot=si)
            S.add("sp", lambda e: e.dma_start(out=self.ropec[:], in_=d["ropec"]), writes=["ropec"], slot=si)
            S.add("sp", lambda e: e.dma_start(out=self.ropes[:], in_=d["ropes"]), writes=["ropes"], slot=si)
        S.add("dve", lambda e: e.memset(self.onesb[:], 1.0 / 1024.0), writes=["onesb"])
        S.add("act", lambda e: e.activation(self.condb[:], self.condf[:], AF.Silu), reads=["condf"], writes=["condb"])
        self.s_mdw = self.S.slot("mdw")
        self.s_mdl = self.S.slot("mdl")
        for l in range(L):
            for i in range(3):
                wv, wk = self.wload(d[f"wAq{l}"][i])
                b = self.bank()
                for kc in range(8):
                    S.add("pe", lambda e, kc=kc, b=b, wv=wv: e.matmul(
                        self.ps[0:2, b, :], self.condb[:, kc, :], wv[:, kc * 512:(kc + 1) * 512],
                        start=(kc == 0), stop=(kc == 7)), reads=[wk, "condb"], writes=[("ps", b)])
                S.add("act", lambda e, b=b: e.activation(self.modtok[:], self.ps[0:2, b, :], AF.Identity),
                      reads=[("ps", b)], writes=["modtok"])
                r0 = (l * 3 + i) * 2
                S.add("sp", lambda e, r0=r0: e.dma_start(out=self.x_["md_in"][r0:r0 + 2, :], in_=self.modtok[:]),
                      reads=["modtok"], writes=["md_in"], slot=self.s_mdw)
        self.gather(self.x_["md_in"], self.x_["md_out"], "md_in", "md_out")
        self.ada_blocks(0, range(0, 12))
        self.ada_blocks(1, range(0, 12))
        self.ada_pending = []
        for l in range(L):
            lam_init = 0.8 - 0.6 * math.exp(-0.3 * l)
            lp = self.bcast[:, l, 256:512].rearrange("p (a b c) -> p a b c", a=2, b=2)
            S.add("dve", lambda e, lp=lp: e.tensor_tensor(self.tsm[:, 0, :].rearrange("p (a c) -> p a c", a=2),
                                                          lp[:, :, 0, :], lp[:, :, 1, :], ALU.mult),
                  reads=["bcast"], writes=["tsm"])
            S.add("dve", lambda e: e.reduce_sum(self.sm8[:, 0:2], self.tsm[:, 0, :].rearrange("p (a c) -> p a c", a=2),
                                                axis=mybir.AxisListType.X), reads=["tsm"], writes=["sm8"])
            S.add("act", lambda e: e.activation(self.sm8[:, 2:4], self.sm8[:, 0:2], AF.Exp), reads=["sm8"], writes=["sm8b"])
            S.add("dve", lambda e: e.tensor_tensor(self.sm8[:, 4:5], self.sm8[:, 3:4], self.sm8[:, 2:3], ALU.subtract),
                  reads=["sm8b"], writes=["sm8c"])
            S.add("dve", lambda e, l=l, li=lam_init: e.tensor_scalar(self.lam[:, l:l + 1], self.sm8[:, 4:5], -li, None, ALU.add),
                  reads=["sm8c"], writes=["lam"])
            S.add("act", lambda e, l=l: e.activation(self.sm8[:, 8:16], self.bcast[:, l, 512:520], AF.Exp, scale=-1.0),
                  reads=["bcast"], writes=["sm8d"])
            S.add("act", lambda e, l=l: e.activation(self.retsp[:, l, :], self.sm8[:, 8:16], AF.Ln, bias=1.0),
                  reads=["sm8d"], writes=["retsp"])

    def ada_blocks(self, l, js):
        S, d = self.S, self.d
        for j in js:
            row = (j // 3) * 12 + l * 6 + (j % 3) * 2
            S.add("sp", lambda e, row=row: e.dma_start(out=self.modtok[:], in_=self.x_["md_out"][row:row + 2, :]),
                  reads=["md_out"], writes=["modtok"], slot=self.s_mdl)
            b2 = self.bank()
            for q in range(4):
                S.add("pe", lambda e, q=q, b2=b2: e.transpose(
                    self.ps[:, b2, q * 2:(q + 1) * 2], self.modtok[0:2, q * 128:(q + 1) * 128],
                    self.cst[0:2, 4, 0:2]), reads=["modtok", "cst"], writes=[("ps", b2)])
            sec = j // 2
            S.add("dve", lambda e, l=l, j=j, b2=b2: e.tensor_tensor(
                self.modT[:, l, j * 4:(j + 1) * 4, :],
                self.ps[:, b2, 0:8].rearrange("p (a b) -> p a b", a=4),
                self.smallp[:, l, j * 4:(j + 1) * 4].rearrange("p (a o) -> p a o", o=1).broadcast_to([128, 4, 2]),
                ALU.add), reads=[("ps", b2), "smallp"], writes=[("modT", l, sec)])
            if j % 2 == 1:
                lo = sec * 8
                if sec in (1, 4):
                    S.add("dve", lambda e, l=l, lo=lo: e.tensor_scalar(self.modT[:, l, lo:lo + 8, :], self.modT[:, l, lo:lo + 8, :],
                                                                      1.0, None, ALU.add),
                          reads=[("modT", l, sec)], writes=[("modT", l, sec)])
                elif sec in (2, 5):
                    f = 0.5 / ALPHA if sec == 2 else 1.0 / ALPHA
                    S.add("dve", lambda e, l=l, lo=lo, f=f: e.tensor_scalar(self.modT[:, l, lo:lo + 8, :], self.modT[:, l, lo:lo + 8, :],
                                                                           f, None, ALU.mult),
                          reads=[("modT", l, sec)], writes=[("modT", l, sec)])

    def ada_deferred(self):
        for (l, js) in self.ada_pending:
            self.ada_blocks(l, js)
        self.ada_pending = []

    def load_x(self, P):
        S = self.S
        src = self.d[P["x"]]
        for t in range(P["NT"]):
            sl = t % 2
            S.add("sp", lambda e, t=t, sl=sl: e.dma_start(out=self.stage[:, sl, :], in_=src[t * 128:(t + 1) * 128, :]),
                  writes=[("stage", sl)], slot=self.s_io[sl])
            for half in range(2):
                b = self.bank()
                for q in range(4):
                    kc = half * 4 + q
                    S.add("pe", lambda e, b=b, q=q, kc=kc, sl=sl: e.transpose(
                        self.ps[:, b, q * 128:(q + 1) * 128], self.stage[:, sl, kc * 128:(kc + 1) * 128], self.cst[:, 4, :]),
                        reads=[("stage", sl), "cst"], writes=[("ps", b)])
                S.add("act" if half == 0 else "dve",
                      (lambda e, b=b, half=half, t=t: e.activation(
                          self.xT[:, half * 4:half * 4 + 4, t * 128:(t + 1) * 128],
                          self.ps[:, b, :].rearrange("p (a c) -> p a c", a=4), AF.Identity)) if half == 0 else
                      (lambda e, b=b, half=half, t=t: e.tensor_copy(
                          self.xT[:, half * 4:half * 4 + 4, t * 128:(t + 1) * 128],
                          self.ps[:, b, :].rearrange("p (a c) -> p a c", a=4))),
                      reads=[("ps", b)], writes=[("xT", t // 4)])

    def modulate(self, P, l, which):
        S = self.S
        c = P["cond"]
        sh0 = 0 if which == 0 else 24
        sc0 = 8 if which == 0 else 32
        for g in range(P["NG"]):
            for kc in range(8):
                S.add("dve", lambda e, g=g, kc=kc: e.tensor_scalar(
                    self.hT[:, kc, g * 512:(g + 1) * 512], self.xT[:, kc, g * 512:(g + 1) * 512],
                    self.modT[:, l, sc0 + kc, c:c + 1], self.modT[:, l, sh0 + kc, c:c + 1], ALU.mult, ALU.add),
                    reads=[("xT", g), ("modT", l, sh0 // 8), ("modT", l, sc0 // 8)], writes=[("hT", g)])

    def projF(self, P, wv, wk, ci, evac):
        S = self.S
        for g in range(P["NG"]):
            b = self.bank()
            for kc in range(8):
                S.add("pe", lambda e, b=b, kc=kc, g=g: e.matmul(
                    self.ps[:, b, :], wv[:, ci * 1024 + kc * 128: ci * 1024 + (kc + 1) * 128],
                    self.hT[:, kc, g * 512:(g + 1) * 512], start=(kc == 0), stop=(kc == 7)),
                    reads=[wk, ("hT", g)], writes=[("ps", b)])
            evac(g, b)

    def projT(self, P, wv, wk, evac):
        S = self.S
        for t in range(P["NT"]):
            b = self.bank()
            for kc in range(8):
                S.add("pe", lambda e, b=b, kc=kc, t=t: e.matmul(
                    self.ps[:, b, :], self.hT[:, kc, t * 128:(t + 1) * 128], wv[:, kc * 512:(kc + 1) * 512],
                    start=(kc == 0), stop=(kc == 7)),
                    reads=[wk, ("hT", t // 4)], writes=[("ps", b)])
            evac(t, b)

    def evac_rope(self, P, dst, key, h, g, b, scale):
        S = self.S
        gc = slice(g * 512, (g + 1) * 512)
        if not P["rope"]:
            S.add("act", lambda e: e.activation(dst[:, h, gc], self.ps[:, b, :], AF.Copy, scale=scale),
                  reads=[("ps", b)], writes=[(key, h)])
            return
        S.add("act", lambda e: e.activation(self.ropex[:, :], self.ps[:, b, :], AF.Copy, scale=scale),
              reads=[("ps", b)], writes=["ropex"])
        b2 = self.bank()
        S.add("pe", lambda e: e.matmul(self.ps[:, b2, :], self.cstb[:, 5, :], self.ropex[:, :], start=True, stop=True),
              reads=["ropex", "cstb"], writes=[("ps", b2)])
        S.add("dve", lambda e: e.tensor_tensor(self.ropet[:, :], self.ps[:, b2, :], self.ropes[:, gc], ALU.mult),
              reads=[("ps", b2), "ropes"], writes=["ropet"])
        S.add("dve", lambda e: e.tensor_tensor(self.ropeu[:, :], self.ropex[:, :], self.ropec[:, gc], ALU.mult),
              reads=["ropex", "ropec"], writes=["ropeu"])
        S.add("dve", lambda e: e.tensor_tensor(dst[:, h, gc], self.ropet[:, :], self.ropeu[:, :], ALU.add),
              reads=["ropet", "ropeu"], writes=[(key, h)])

    def decay_tiles(self, scale, t):
        S = self.S
        b = self.bank()
        for h in range(4):
            for dd, ci in ((0, 2), (1, 3)):
                c0 = h * 128 + dd * 64
                S.add("pe", lambda e, c0=c0, ci=ci: e.matmul(self.ps[:, b, c0:c0 + 64], self.cst[:, ci, :], self.latok[:, c0:c0 + 64],
                                                            start=True, stop=True),
                      reads=["latok", "cst"], writes=[("ps", b)])
        S.add("act", lambda e: e.activation(self.Ekd[:, :, :], self.ps[:, b, :].rearrange("p (h c) -> p h c", h=4), AF.Exp, scale=scale),
              reads=[("ps", b)], writes=["Ekd"])
        bx, by = self.bank(), self.bank()
        for h in range(4):
            S.add("pe", lambda e, h=h: e.matmul(self.ps[:, bx, h * 128:(h + 1) * 128], self.latok[:, h * 128:(h + 1) * 128],
                                               self.cst[:, 0, :], start=True, stop=True),
                  reads=["latok", "cst"], writes=[("ps", bx)])
            S.add("pe", lambda e, h=h: e.matmul(self.ps[:, by, h * 128:(h + 1) * 128], self.latok[:, h * 128:(h + 1) * 128],
                                               self.cst[:, 1, :], start=True, stop=True),
                  reads=["latok", "cst"], writes=[("ps", by)])
        for (dst, sc, key) in ((self.EqT, scale, "EqT"), (self.EkT, -scale, "EkT")):
            S.add("act", lambda e, dst=dst, sc=sc: e.activation(
                dst[0:64, :, :], self.ps[0:64, bx, :].rearrange("p (h c) -> p h c", h=4), AF.Exp, scale=sc),
                reads=[("ps", bx)], writes=[key])
            S.add("act", lambda e, dst=dst, sc=sc: e.activation(
                dst[64:128, :, :], self.ps[64:128, by, :].rearrange("p (h c) -> p h c", h=4), AF.Exp, scale=sc),
                reads=[("ps", by)], writes=[key])
        if t is not None:
            S.add("dve", lambda e: e.tensor_copy(self.dcol[0:64, :, t:t + 1], self.EqT[0:64, :, 127:128]),
                  reads=["EqT"], writes=["dcol"])
            S.add("dve", lambda e: e.tensor_copy(self.dcol[64:128, :, t:t + 1], self.EqT[64:128, :, 0:1]),
                  reads=["EqT"], writes=["dcol"])

    def compose_states(self, P, l, mix):
        S, x_ = self.S, self.x_
        dk = [("delta", h, t) for h in range(4) for t in range(8)]
        S.add("dve", lambda e: e.tensor_copy(self.cp[0:64, :, 0:1], self.dcol[0:64, :, 0:1]), reads=["dcol"], writes=["cp"])
        S.add("dve", lambda e: e.tensor_copy(self.cp[64:128, :, 7:8], self.dcol[64:128, :, 7:8]), reads=["dcol"], writes=["cp"])
        for t in range(1, 8):
            S.add("dve", lambda e, t=t: e.tensor_tensor(self.cp[0:64, :, t:t + 1], self.cp[0:64, :, t - 1:t],
                                                        self.dcol[0:64, :, t:t + 1], ALU.mult), reads=["cp", "dcol"], writes=["cp"])
            tb = 7 - t
            S.add("dve", lambda e, tb=tb: e.tensor_tensor(self.cp[64:128, :, tb:tb + 1], self.cp[64:128, :, tb + 1:tb + 2],
                                                          self.dcol[64:128, :, tb:tb + 1], ALU.mult), reads=["cp", "dcol"], writes=["cp"])
        g1 = self.xgroup("stw")
        stL = x_["st_in"][:, 0:512].rearrange("p (h v) -> p h v", h=4)
        S.add("sp", lambda e: e.dma_start(out=stL[0:64], in_=self.delta[0:64, :, 7, :]), reads=dk, writes=["st_in"], slot=g1)
        S.add("sp", lambda e: e.dma_start(out=stL[64:128], in_=self.delta[64:128, :, 0, :]), reads=dk, writes=["st_in"], slot=g1)
        S.add("sp", lambda e: e.dma_start(out=x_["st_in"][0:64, 512:516].rearrange("p (h o) -> p h o", o=1), in_=self.cp[0:64, :, 7:8], allow_slow_non_contiguous=True), reads=["cp"], writes=["st_in"], slot=g1)
        S.add("sp", lambda e: e.dma_start(out=x_["st_in"][64:128, 512:516].rearrange("p (h o) -> p h o", o=1), in_=self.cp[64:128, :, 0:1], allow_slow_non_contiguous=True), reads=["cp"], writes=["st_in"], slot=g1)
        self.gather(x_["st_in"], x_["st_out"], "st_in", "st_out")
        g2 = self.xgroup("stl")
        src0 = self.d["sret" if mix == 0 else "sgla"][l]
        S.add("sp", lambda e: e.dma_start(out=self.Srun[:, :, :], in_=src0), writes=["Srun"], slot=g2)
        for s_ in range(4):
            S.add("sp", lambda e, s_=s_: e.dma_start(out=self.Dg[0:64, s_, :], in_=x_["st_out"][s_ * 128:s_ * 128 + 64, 512:516]),
                  reads=["st_out"], writes=["Dg"], slot=g2)
            S.add("sp", lambda e, s_=s_: e.dma_start(out=self.Dg[64:128, s_, :],
                                                     in_=x_["st_out"][(3 - s_) * 128 + 64:(3 - s_) * 128 + 128, 512:516]),
                  reads=["st_out"], writes=["Dg"], slot=g2)
        S.add("dve", lambda e: e.memset(self.Sin[:, :, :], 0.0), writes=["Sin"])
        for s_ in range(4):
            g3 = self.xgroup("gl")
            S.add("sp", lambda e, s_=s_: e.dma_start(
                out=self.GL[0:64, :, :], in_=x_["st_out"][s_ * 128:s_ * 128 + 64, 0:512].rearrange("p (h v) -> p h v", h=4)),
                reads=["st_out"], writes=["GL"], slot=g3)
            S.add("sp", lambda e, s_=s_: e.dma_start(
                out=self.GL[64:128, :, :],
                in_=x_["st_out"][(3 - s_) * 128 + 64:(3 - s_) * 128 + 128, 0:512].rearrange("p (h v) -> p h v", h=4)),
                reads=["st_out"], writes=["GL"], slot=g3)
            S.add("dve", lambda e, s_=s_: e.scalar_tensor_tensor(
                self.Sin[:, :, :].rearrange("p h v -> p (h v)"), self.Srun[:, :, :].rearrange("p h v -> p (h v)"),
                self.meta[:, s_:s_ + 1], self.Sin[:, :, :].rearrange("p h v -> p (h v)"), ALU.mult, ALU.add),
                reads=["Srun", "meta", "Sin"], writes=["Sin"])
            for h in range(4):
                S.add("dve", lambda e, s_=s_, h=h: e.scalar_tensor_tensor(
                    self.Srun[:, h, :], self.Srun[:, h, :], self.Dg[:, s_, h:h + 1], self.GL[:, h, :], ALU.mult, ALU.add),
                    reads=["Srun", "Dg", "GL"], writes=["Srun"])
        for t in range(8):
            for h in range(4):
                S.add("dve", lambda e, t=t, h=h: e.scalar_tensor_tensor(
                    self.delta[:, h, t, :], self.Sin[:, h, :], self.cp[:, h, t:t + 1], self.delta[:, h, t, :], ALU.mult, ALU.add),
                    reads=["Sin", "cp", ("delta", h, t)], writes=[("delta", h, t)])
        S.add("dve", lambda e: e.tensor_copy(self.sinb[:, :, :], self.Sin[:, :, :]), reads=["Sin"], writes=["sinb"])

    def mixer_AB(self, P, l, mix):
        S, d = self.S, self.d
        NT, NG = P["NT"], P["NG"]
        qscale = 1.0 if mix == 0 else 0.125
        kscale = 0.125 if mix == 0 else 1.0
        ropeP = P if mix == 0 else dict(P, rope=False)
        wv, wk = self.wload(d[f"wT{l}"][2 * mix])
        self.projT(P, wv, wk, lambda t, b: S.add(
            "act", lambda e: e.activation(self.vtok[:, t, :], self.ps[:, b, :], AF.Identity), reads=[("ps", b)], writes=[("vtok", t)]))
        wv, wk = self.wload(d[f"wT{l}"][2 * mix + 1])
        self.projT(P, wv, wk, lambda t, b: S.add(
            "act", lambda e: e.activation(self.gtok[:, t, :], self.ps[:, b, :], AF.Silu),
            reads=[("ps", b)], writes=[("gtok", t)]))
        ksub = int(os.environ.get("KSUB", "99"))
        if ksub < 1:
            return
        if mix == 1:
            S.add("dve", lambda e: e.memset(self.uT[0:33, 0:P["T"]], 1.0), writes=["uT"])
            for g in range(NG):
                b = self.bank()
                for kc in range(8):
                    S.add("pe", lambda e, b=b, kc=kc, g=g: e.matmul(
                        self.ps[0:32, b, :], self.wa1[:, l, kc, :], self.hT[:, kc, g * 512:(g + 1) * 512],
                        start=(kc == 0), stop=(kc == 7)), reads=["wa1", ("hT", g)], writes=[("ps", b)])
                S.add("act", lambda e, b=b, g=g: e.activation(self.uT[0:32, g * 512:(g + 1) * 512], self.ps[0:32, b, :], AF.Identity),
                      reads=[("ps", b)], writes=["uT"])
        wv, wk = self.wload(d[f"wF{l}"][2 * mix])
        for h in range(4):
            self.projF(P, wv, wk, h, lambda g, b, h=h: self.evac_rope(ropeP, self.QT, "QT", h, g, b, qscale))
        wv, wk = self.wload(d[f"wF{l}"][2 * mix + 1])
        for h in range(4):
            self.projF(P, wv, wk, h, lambda g, b, h=h: self.evac_rope(ropeP, self.KT, "KT", h, g, b, kscale))
        if ropeP["rope"]:
            self.barrier()
        if ksub < 2:
            return
        if mix == 0:
            S.add("dve", lambda e: e.tensor_copy(
                self.latok[:, :].rearrange("p (h d k) -> p h d k", h=4, d=2),
                self.retsp[:, l, :].rearrange("p (d h o) -> p h d o", d=2, o=1).broadcast_to([128, 4, 2, 64])),
                reads=["retsp"], writes=["latok"])
            self.decay_tiles(-1.0, None)
            for t in range(NT):
                S.add("dve", lambda e, t=t: e.tensor_copy(self.dcol[0:64, :, t:t + 1], self.EqT[0:64, :, 127:128]),
                      reads=["EqT"], writes=["dcol"])
                S.add("dve", lambda e, t=t: e.tensor_copy(self.dcol[64:128, :, t:t + 1], self.EqT[64:128, :, 0:1]),
                      reads=["EqT"], writes=["dcol"])
        if ksub < 3:
            return
        it = 0
        for t in range(NT):
            tc_ = slice(t * 128, (t + 1) * 128)
            if mix == 1:
                b = self.bank()
                S.add("pe", lambda e, b=b, tc_=tc_: e.matmul(self.ps[:, b, :], self.uT[0:33, tc_], self.wa2[0:33, l, :],
                                                            start=True, stop=True),
                      reads=["uT", "wa2"], writes=[("ps", b)])
                S.add("act", lambda e, b=b: e.activation(self.latok[:, :], self.ps[:, b, :], AF.Exp, scale=-1.0),
                      reads=[("ps", b)], writes=["latok"])
                S.add("act", lambda e: e.activation(self.latok[:, :], self.latok[:, :], AF.Ln, bias=1.0),
                      reads=["latok"], writes=["latok"])
                self.decay_tiles(-1.0 / 16.0, t)
            for h in range(4):
                i2 = it % 2
                it += 1
                S.add("dve", lambda e, h=h, t=t, tc_=tc_: e.tensor_tensor(
                    self.qd[:, h, t, :], self.QT[:, h, tc_], self.EqT[:, h, :], ALU.mult),
                    reads=[("QT", h), "EqT"], writes=[("qd", h, t)])
                S.add("dve", lambda e, h=h, tc_=tc_, i2=i2: e.tensor_tensor(
                    self.ksb[:, i2, :], self.KT[:, h, tc_], self.EkT[:, h, :], ALU.mult),
                    reads=[("KT", h), "EkT"], writes=[("ksb", i2)])
                bt = self.bank()
                S.add("pe", lambda e, h=h, tc_=tc_, bt=bt: e.transpose(
                    self.psb(bt)[:, 0:128], self.KT[:, h, tc_], self.cstb[:, 4, :]),
                    reads=[("KT", h), "cstb"], writes=[("ps", bt)])
                S.add("dve", lambda e, h=h, bt=bt, i2=i2: e.tensor_tensor(
                    self.kdtok[:, i2, :], self.psb(bt)[:, 0:128], self.Ekd[:, h, :], ALU.mult),
                    reads=[("ps", bt), "Ekd"], writes=[("kdtok", i2)])
                bd = self.bank()
                S.add("pe", lambda e, h=h, t=t, bd=bd, i2=i2: e.matmul(
                    self.ps[:, bd, 0:128], self.kdtok[:, i2, :], self.vtok[:, t, h * 128:(h + 1) * 128], start=True, stop=True),
                    reads=[("kdtok", i2), ("vtok", t)], writes=[("ps", bd)])
                S.add("act", lambda e, h=h, t=t, bd=bd: e.activation(self.delta[:, h, t, 0:128], self.ps[:, bd, 0:128], AF.Identity),
                      reads=[("ps", bd)], writes=[("delta", h, t)])
                b1, b2 = self.bank(), self.bank()
                S.add("pe", lambda e, h=h, t=t, b1=b1, i2=i2: e.matmul(
                    self.ps[:, b1, 0:128], self.ksb[0:64, i2, :], self.qd[0:64, h, t, :], start=True, stop=True),
                    reads=[("ksb", i2), ("qd", h, t)], writes=[("ps", b1)])
                S.add("pe", lambda e, h=h, t=t, b2=b2, i2=i2: e.matmul(
                    self.ps[:, b2, 0:128], self.ksb[64:128, i2, :], self.qd[64:128, h, t, :], start=True, stop=True),
                    reads=[("ksb", i2), ("qd", h, t)], writes=[("ps", b2)])
                S.add("dve", lambda e, b1=b1, i2=i2: e.tensor_tensor(
                    self.tsm[:, i2, :], self.ps[:, b1, 0:128], self.cst[:, 0, :], ALU.mult),
                    reads=[("ps", b1), "cst"], writes=[("tsm", i2)])
                S.add("dve", lambda e, b2=b2: e.tensor_tensor(
                    self.ps[:, b2, 128:256], self.ps[:, b2, 0:128], self.cst[:, 1, :], ALU.mult),
                    reads=[("ps", b2), "cst"], writes=[("ps", b2)])
                S.add("dve", lambda e, h=h, t=t, b2=b2, i2=i2: e.tensor_tensor(
                    self.Pm[:, h, t, :], self.ps[:, b2, 128:256], self.tsm[:, i2, :], ALU.add),
                    reads=[("ps", b2), ("tsm", i2)], writes=[("Pm", h, t)])
        if ksub < 4:
            return
        self.barrier()
        out_name = "nsr" if mix == 0 else "nsg"
        for si, (t0, n) in enumerate(P["seqs"]):
            for c_ in range(1, n):
                tf = t0 + c_
                tb = t0 + n - 1 - c_
                for h in range(4 if os.environ.get("KX", "0") != "5" else 0):
                    S.add("dve", lambda e, h=h, tf=tf: e.scalar_tensor_tensor(
                        self.delta[0:64, h, tf, 0:128], self.delta[0:64, h, tf - 1, 0:128], self.dcol[0:64, h, tf:tf + 1],
                        self.delta[0:64, h, tf, 0:128], ALU.mult, ALU.add),
                        reads=[("delta", h, tf - 1), "dcol", ("delta", h, tf)], writes=[("delta", h, tf)])
                    S.add("dve", lambda e, h=h, tb=tb: e.scalar_tensor_tensor(
                        self.delta[64:128, h, tb, 0:128], self.delta[64:128, h, tb + 1, 0:128], self.dcol[64:128, h, tb:tb + 1],
                        self.delta[64:128, h, tb, 0:128], ALU.mult, ALU.add),
                        reads=[("delta", h, tb + 1), "dcol", ("delta", h, tb)], writes=[("delta", h, tb)])
            if P.get("sample"):
                self.compose_states(P, l, mix)
            if P["states_out"] and not os.environ.get("KNOST"):
                for dr, tt in ((0, t0 + n - 1), (1, t0)):
                    op = S.add("sp", lambda e, si=si, dr=dr, tt=tt: e.dma_start(
                        out=self.o[out_name][si, l, dr].rearrange("h k v -> k h v"),
                        in_=self.delta[dr * 64:(dr + 1) * 64, :, tt, 0:128]),
                        reads=[("delta", h_, tt) for h_ in range(4)], slot=self.s_st[si % 2])
                    self.out_ops.append(op)
            if os.environ.get("KX") == "7":
                ba = self.bank()
                S.add("pe", lambda e, ba=ba: e.matmul(self.ps[:, ba, 0:128], self.kdtok[:, 0, :], self.vtok[:, 0, 0:128],
                                                      start=True, stop=True),
                      reads=[("kdtok", 0), ("vtok", 0)], writes=[("ps", ba)])
                continue
            if os.environ.get("KX") == "8":
                ba = self.bank()
                S.add("pe", lambda e, ba=ba: e.matmul(self.ps[:, ba, 0:128], self.cstb[:, 6, :], self.vtok[:, 0, 0:128],
                                                      start=True, stop=True),
                      reads=["cstb", ("vtok", 0)], writes=[("ps", ba)])
                continue
            for c_ in range(n):
                t = t0 + c_
                for h in range(4):
                    kv = os.environ.get("KV", "0")
                    if kv == "1":
                        S.add("dve", lambda e, t=t, h=h: e.tensor_tensor(self.dbf[:, h, t, :], self.cst[:, 0, :], self.cst[:, 8, :], ALU.mult),
                              reads=["cst"], writes=[("QT", h)])
                    elif kv == "3":
                        S.add("dve", lambda e, t=t, h=h: e.tensor_tensor(self.dbf[:, h, t, :], self.cst[:, 0, :], self.cst[:, 8, :], ALU.mult),
                              reads=["cst"], writes=[("tsm", 0)])
                    elif kv == "4":
                        S.add("dve", lambda e, t=t, h=h: e.tensor_tensor(self.tsm[:, 0, :], self.cst[:, 0, :], self.cst[:, 8, :], ALU.mult),
                              reads=["cst"], writes=[("QT", h)])
                    elif kv == "2":
                        S.add("dve", lambda e, t=t, h=h: e.tensor_tensor(self.tsm[:, 0, :], self.delta[:, h, t, :], self.cst[:, 8, :], ALU.mult),
                              reads=[("delta", h, t), "cst"], writes=[("tsm", 0)])
                    else:
                        S.add("dve", lambda e, t=t, h=h: e.tensor_tensor(self.dbf[:, h, t, :], self.delta[:, h, t, :], self.cst[:, 8, :], ALU.mult),
                              reads=[("delta", h, t), "cst"], writes=[("dbf", h)])
            if os.environ.get("KX") == "9":
                continue
            if os.environ.get("KX") == "10":
                ba = self.bank()
                S.add("pe", lambda e, ba=ba: e.matmul(self.ps[:, ba, 0:128], self.cstb[:, 6, :], self.dbf[:, 0, t0, :],
                                                      start=True, stop=True),
                      reads=["cstb", ("QT", 0)], writes=[("ps", ba)])
                continue
            for c_ in range(n):
                t = t0 + c_
                ba = self.bank()
                for h in range(4 if os.environ.get("KX", "0") != "4" else 0):
                    hasf, hasb = c_ > 0, c_ < n - 1
                    if P.get("sample") and not hasf:
                        S.add("pe", lambda e, h=h, ba=ba: e.matmul(
                            self.ps[:, ba, h * 128:(h + 1) * 128], self.cstb[:, 6, :], self.sinb[:, h, :],
                            start=True, stop=False), reads=["sinb", "cstb"], writes=[("ps", ba)])
                        hasf = True
                    if P.get("sample") and not hasb:
                        S.add("pe", lambda e, h=h, t=t, ba=ba: e.matmul(
                            self.ps[:, ba, h * 128:(h + 1) * 128], self.cstb[:, 6, :], self.dbf[:, h, t - 1, :],
                            start=True, stop=False), reads=[("dbf", h), "cstb"], writes=[("ps", ba)])
                        S.add("pe", lambda e, h=h, ba=ba: e.matmul(
                            self.ps[:, ba, h * 128:(h + 1) * 128], self.cstb[:, 7, :], self.sinb[:, h, :],
                            start=False, stop=True), reads=["sinb", "cstb"], writes=[("ps", ba)])
                        continue
                    if P.get("sample") and c_ == 0:
                        S.add("pe", lambda e, h=h, t=t, ba=ba: e.matmul(
                            self.ps[:, ba, h * 128:(h + 1) * 128], self.cstb[:, 7, :], self.dbf[:, h, t + 1, :],
                            start=False, stop=True), reads=[("dbf", h), "cstb"], writes=[("ps", ba)])
                        continue
                    if hasf:
                        S.add("pe", lambda e, h=h, t=t, ba=ba, hasb=hasb: e.matmul(
                            self.ps[:, ba, h * 128:(h + 1) * 128], self.cstb[:, 6, :], self.dbf[:, h, t - 1, :],
                            start=True, stop=not hasb), reads=[("dbf", h), "cstb"], writes=[("ps", ba)])
                    if hasb:
                        S.add("pe", lambda e, h=h, t=t, ba=ba, hasf=hasf: e.matmul(
                            self.ps[:, ba, h * 128:(h + 1) * 128], self.cstb[:, 7, :], self.dbf[:, h, t + 1, :],
                            start=not hasf, stop=True), reads=[("dbf", h), "cstb"], writes=[("ps", ba)])
                kx = os.environ.get("KX", "0")
                if kx == "0":
                    S.add("act", lambda e, t=t, ba=ba: e.activation(
                        self.sstk[:, :, t, :], self.ps[:, ba, :].rearrange("p (h v) -> p h v", h=4), AF.Identity),
                        reads=[("ps", ba)], writes=[("KT", h_) for h_ in range(4)])
                elif kx == "2":
                    S.add("act", lambda e, t=t, ba=ba: e.activation(
                        self.Pm[:, :, t, :], self.ps[:, ba, :].rearrange("p (h v) -> p h v", h=4), AF.Identity),
                        reads=[("ps", ba)], writes=[("Pm", h_, t) for h_ in range(4)])
                elif kx == "3":
                    S.add("dve", lambda e, t=t, ba=ba: e.tensor_copy(
                        self.sstk[:, :, t, :], self.ps[:, ba, :].rearrange("p (h v) -> p h v", h=4)),
                        reads=[("ps", ba)], writes=[("KT", h_) for h_ in range(4)])
        if ksub < 5 or os.environ.get("KNOP2"):
            return
        if os.environ.get("KPAD"):
            S.add("dve", lambda e: e.memset(self.tsm[:, 0, :], 0.0), writes=[("tsm", 0)])
        if not os.environ.get("KNOBAR"):
            self.barrier()
        for t in range(int(os.environ.get("KP2T", NT))):
            b = self.bank()
            for h in range(4):
                S.add("pe", lambda e, h=h, t=t, b=b: e.matmul(
                    self.ps[:, b, h * 128:(h + 1) * 128], self.qd[:, h, t, :], self.sstk[:, h, t, :],
                    start=True, stop=False), reads=[("qd", h, t), ("KT", h), ("Pm", h, t), ("vtok", t)], writes=[("ps", b)])
                S.add("pe", lambda e, h=h, t=t, b=b: e.matmul(
                    self.ps[:, b, h * 128:(h + 1) * 128], self.Pm[:, h, t, :], self.vtok[:, t, h * 128:(h + 1) * 128],
                    start=False, stop=True), reads=[("Pm", h, t), ("vtok", t)], writes=[("ps", b)])
            kp2 = int(os.environ.get("KP2", "3"))
            if kp2 & 1:
              S.add("act", lambda e, t=t, b=b: e.activation(self.otokAB[:, t, :], self.ps[:, b, :], AF.Identity),
                  reads=[("ps", b)], writes=[("QT", t // 2) if not os.environ.get("KOT") else ("otokAB", t)])
            for h in range(4 if (kp2 & 2) else 0):
                if mix == 0:
                    S.add("dve", lambda e, h=h, b=b: e.bn_stats(self.st6[:, :], self.ps[:, b, h * 128:(h + 1) * 128]),
                          reads=[("ps", b)], writes=["st6"])
                    S.add("dve", lambda e, h=h, t=t: e.bn_aggr(self.stat[:, t, h, :], self.st6[:, :]),
                          reads=["st6"], writes=["stat"])
                else:
                    S.add("act", lambda e, h=h, t=t, b=b: e.activation(
                        self.tsm[:, 0, :], self.ps[:, b, h * 128:(h + 1) * 128], AF.Square,
                        accum_out=self.stat[:, t, h, 1:2]), reads=[("ps", b)], writes=["stat", ("tsm", 0)])
        if ksub < 6:
            return
        self.norm_out(P, l, mix, self.otokAB, lambda t: ("QT", t // 2))

    def norm_out(self, P, l, br, otok, okey):
        S = self.S
        NT = P["NT"]
        if br == 0:
            sc, bias = 1.0, EPS
        else:
            sc, bias = 1.0 / 128.0, EPS
        S.add("act", lambda e: e.activation(self.rstd[:, 0:NT, :], self.stat[:, 0:NT, :, 1], AF.Sqrt, bias=bias, scale=sc),
              reads=["stat"], writes=["rstd"])
        S.add("dve", lambda e: e.reciprocal(self.rstd[:, 0:NT, :], self.rstd[:, 0:NT, :]), reads=["rstd"], writes=["rstd"])
        for t in range(NT):
            i2 = t % 2
            if br == 1:
                S.add("dve", lambda e, t=t: e.tensor_tensor(
                    self.ggt[:].rearrange("p (h v) -> p h v", h=4), self.gtok[:, t, :].rearrange("p (h v) -> p h v", h=4),
                    self.bcast[:, l, 0:128].rearrange("p (o v) -> p o v", o=1).broadcast_to([128, 4, 128]), ALU.mult),
                    reads=[("gtok", t), "bcast"], writes=["ggt"])
            for h in range(4):
                hs = slice(h * 128, (h + 1) * 128)
                if br == 0:
                    S.add("dve", lambda e, t=t, hs=hs, h=h, i2=i2: e.scalar_tensor_tensor(
                        self.tsm[:, 1, :], otok[:, t, hs], self.stat[:, t, h, 0:1], self.gtok[:, t, hs],
                        ALU.subtract, ALU.mult), reads=[okey(t), "stat", ("gtok", t)], writes=[("tsm", 1)])
                    S.add("dve", lambda e, t=t, hs=hs, h=h, i2=i2: e.tensor_scalar(
                        self.ytok[:, i2, hs], self.tsm[:, 1, :], self.rstd[:, t, h:h + 1], None, ALU.mult),
                        reads=[("tsm", 1), "rstd"], writes=[("ytok", i2)])
                elif br == 1:
                    S.add("dve", lambda e, t=t, hs=hs, h=h, i2=i2: e.scalar_tensor_tensor(
                        self.ytok[:, i2, hs], otok[:, t, hs], self.rstd[:, t, h:h + 1], self.ggt[:, hs],
                        ALU.mult, ALU.mult), reads=[okey(t), "rstd", "ggt"], writes=[("ytok", i2)])
                else:
                    S.add("dve", lambda e, t=t, hs=hs, h=h, i2=i2: e.scalar_tensor_tensor(
                        self.ytok[:, i2, hs], otok[:, t, hs], self.rstd[:, t, h:h + 1], self.subg[:, :],
                        ALU.mult, ALU.mult), reads=[okey(t), "rstd", "subg"], writes=[("ytok", i2)])
            bt = self.bank()
            for h in range(4):
                S.add("pe", lambda e, h=h, bt=bt, i2=i2: e.transpose(
                    self.psb(bt)[:, h * 128:(h + 1) * 128], self.ytok[:, i2, h * 128:(h + 1) * 128], self.cstb[:, 4, :]),
                    reads=[("ytok", i2), "cstb"], writes=[("ps", bt)])
            S.add("act", lambda e, t=t, bt=bt: e.activation(
                self.yT[:, br, :, t * 128:(t + 1) * 128], self.psb(bt)[:, 0:512].rearrange("p (h c) -> p h c", h=4), AF.Identity),
                reads=[("ps", bt)], writes=[("yT", br, t // 4)])

    def mixer_C(self, P, l):
        S, d = self.S, self.d
        NT = P["NT"]
        lam_init = 0.8 - 0.6 * math.exp(-0.3 * l)
        S.add("dve", lambda e: e.tensor_scalar(self.subg[:, :], self.bcast[:, l, 128:256], 1.0 - lam_init, None, ALU.mult),
              reads=["bcast"], writes=["subg"])
        S.add("dve", lambda e: e.memset(self.vaug[:, :, :, 128:130], 1.0), writes=["vaug1"])
        wv, wk = self.wload(d[f"wT{l}"][4])

        def ev_v(t, b):
            S.add("act", lambda e: e.activation(self.vaug[:, t, :, 0:128], self.ps[:, b, :].rearrange("p (h v) -> p h v", h=4), AF.Identity),
                  reads=[("ps", b)], writes=[("vaug", t)])
            if P["kv_out"]:
                sl = t % 2
                S.add("dve", lambda e: e.tensor_copy(self.cstage[:, sl, :], self.ps[:, b, :]),
                      reads=[("ps", b)], writes=[("cstage", sl)])
                si, tt = t // 2, t % 2
                op = S.add("sp", lambda e: e.dma_start(
                    out=self.o["ndv"][si, l, :, tt * 128:(tt + 1) * 128, :].rearrange("h t v -> t h v"),
                    in_=self.cstage[:, sl, :].rearrange("p (h v) -> p h v", h=4)),
                    reads=[("cstage", sl)], slot=self.s_st[sl])
                self.out_ops.append(op)
        self.projT(P, wv, wk, ev_v)
        if P["kv_out"]:
            wv, wk = self.wload(d[f"wT{l}"][5])

            def ev_k(t, b):
                sl = t % 2
                S.add("dve", lambda e: e.tensor_copy(self.cstage[:, sl, :], self.ps[:, b, :]),
                      reads=[("ps", b)], writes=[("cstage", sl)])
                si, tt = t // 2, t % 2
                op = S.add("sp", lambda e: e.dma_start(
                    out=self.o["ndk"][si, l, :, tt * 128:(tt + 1) * 128, :].rearrange("g t k -> t g k"),
                    in_=self.cstage[:, sl, :].rearrange("p (g k) -> p g k", g=8)),
                    reads=[("cstage", sl)], slot=self.s_st[sl])
                self.out_ops.append(op)
            self.projT(P, wv, wk, ev_k)
        wv, wk = self.wload(d[f"wF{l}"][4])
        for h in range(4):
            self.projF(P, wv, wk, h, lambda g, b, h=h: self.evac_rope(P, self.QT, "QT", h, g, b, 1.0))
        wv, wk = self.wload(d[f"wF{l}"][5])
        for h in range(4):
            self.projF(P, wv, wk, h, lambda g, b, h=h: self.evac_rope(P, self.KT, "KT", h, g, b, 1.0))
        self.barrier(pool=bool(P.get("sample")))
        if P.get("sample"):
            self.attn_sample(P, l)
            self.barrier()
            self.norm_out(P, l, 2, self.otokC, lambda t: ("otokC", t))
            return
        self.bank_pool = [4, 5, 6, 7]
        for (q0, nq, keyt) in P["attn"]:
            qc = slice(q0 * 128, q0 * 128 + 256)
            for hc in range(4):
                for ji, kt in enumerate(keyt):
                    kc_ = slice(kt * 128, (kt + 1) * 128)
                    bs = [self.bank(), self.bank()]
                    for sub in range(2):
                        lo, hi = sub * 64, sub * 64 + 64
                        S.add("pe", lambda e, sub=sub, lo=lo, hi=hi, bs=bs, kc_=kc_, hc=hc: e.matmul(
                            self.ps[:, bs[sub], 0:256], self.KT[lo:hi, hc, kc_], self.QT[lo:hi, hc, qc],
                            start=True, stop=True), reads=[("KT", hc), ("QT", hc)], writes=[("ps", bs[sub])])
                    i2 = ji % 2
                    for sub in range(2):
                        S.add("act", lambda e, sub=sub, bs=bs, i2=i2: e.activation(
                            self.ET[:, i2, sub * 256:(sub + 1) * 256], self.ps[:, bs[sub], 0:256], AF.Exp, scale=0.125),
                            reads=[("ps", bs[sub])], writes=[("ET", i2, sub)])
                    for qt in range(2):
                        for sub in range(2):
                            acc = qt * 2 + sub
                            S.add("pe", lambda e, qt=qt, sub=sub, acc=acc, i2=i2, kt=kt, hc=hc, ji=ji: e.matmul(
                                self.ps[:, acc, 0:130], self.ET[:, i2, sub * 256 + qt * 128: sub * 256 + (qt + 1) * 128],
                                self.vaug[:, kt, hc, :], start=(ji == 0), stop=(ji == len(keyt) - 1)),
                                reads=[("ET", i2, sub), ("vaug", kt), "vaug1"], writes=[("ps", acc)])
                for qt in range(2):
                    t = q0 + qt
                    a1, a2 = qt * 2, qt * 2 + 1
                    S.add("dve", lambda e, a1=a1: e.reciprocal(self.sm8[:, 0:1], self.ps[:, a1, 128:129]),
                          reads=[("ps", a1)], writes=["sm8"])
                    S.add("dve", lambda e, a2=a2: e.reciprocal(self.sm8[:, 1:2], self.ps[:, a2, 128:129]),
                          reads=[("ps", a2)], writes=["sm8"])
                    S.add("dve", lambda e: e.tensor_tensor(self.sm8[:, 1:2], self.sm8[:, 1:2], self.lam[:, l:l + 1], ALU.mult),
                          reads=["sm8", "lam"], writes=["sm8"])
                    S.add("dve", lambda e, a2=a2: e.tensor_scalar(self.tsm[:, 0, :], self.ps[:, a2, 0:128], self.sm8[:, 1:2], None, ALU.mult),
                          reads=[("ps", a2), "sm8"], writes=[("tsm", 0)])
                    S.add("dve", lambda e, a1=a1, t=t, hc=hc: e.scalar_tensor_tensor(
                        self.otokC[:, t, hc * 128:(hc + 1) * 128], self.ps[:, a1, 0:128], self.sm8[:, 0:1], self.tsm[:, 0, :],
                        ALU.mult, ALU.add), reads=[("ps", a1), "sm8", ("tsm", 0)], writes=[("otokC", t)])
                    S.add("act", lambda e, t=t, hc=hc: e.activation(
                        self.tsm[:, 1, :], self.otokC[:, t, hc * 128:(hc + 1) * 128], AF.Square,
                        accum_out=self.stat[:, t, hc, 1:2]), reads=[("otokC", t)], writes=["stat", ("tsm", 1)])
        self.bank_pool = list(range(8))
        self.barrier()
        self.norm_out(P, l, 2, self.otokC, lambda t: ("otokC", t))

    def attn_sample(self, P, l):
        S, d, x_ = self.S, self.d, self.x_
        g1 = self.xgroup("kvw")
        S.add("sp", lambda e: e.dma_start(out=x_["kx_in"].rearrange("p (h t) -> p h t", h=4), in_=self.KT[:, :, :]),
              reads=[("KT", h) for h in range(4)], writes=["kx_in"], slot=g1)
        S.add("sp", lambda e: e.dma_start(out=x_["vx_in"].rearrange("p (t h v) -> p t h v", t=8, h=4), in_=self.vaug[:, :, :, 0:128]),
              reads=[("vaug", t) for t in range(8)], writes=["vx_in"], slot=g1)
        self.gather(x_["kx_in"], x_["kx_out"], "kx_in", "kx_out")
        self.gather(x_["vx_in"], x_["vx_out"], "vx_in", "vx_out")
        for kt in range(2):
            gk = self.xgroup("ck")
            S.add("sp", lambda e, kt=kt: e.dma_start(out=self.kstg[:, :], in_=d["ckc"][l, kt]), writes=["kstg"], slot=gk)
            b = self.bank()
            for hc in range(4):
                S.add("pe", lambda e, hc=hc, b=b: e.transpose(self.ps[:, b, hc * 128:(hc + 1) * 128],
                                                              self.kstg[:, hc * 128:(hc + 1) * 128], self.cst[:, 4, :]),
                      reads=["kstg", "cst"], writes=[("ps", b)])
            S.add("act", lambda e, kt=kt, b=b: e.activation(self.kctx[:, :, kt * 128:(kt + 1) * 128],
                                                            self.ps[:, b, :].rearrange("p (h c) -> p h c", h=4), AF.Identity),
                  reads=[("ps", b)], writes=["kctx"])
        gv = self.xgroup("cv")
        S.add("dve", lambda e: e.memset(self.vctx[:, :, :, 128:130], 1.0), writes=["vctx1"])
        for kt in range(2):
            S.add("pool", lambda e, kt=kt: e.dma_start(out=self.vctx[:, kt, :, 0:128], in_=d["cvc"][l, kt]),
                  writes=["vctx"], slot=gv)
        self.barrier()
        S.add("dve", lambda e: e.memset(self.VH[:, :, 128:130], 1.0), reads=["vx_in"], writes=["VH1"])
        self.bank_pool = [4, 5, 6, 7]
        for hc in range(4):
            gh = self.xgroup("kvh")
            S.add("sp", lambda e, hc=hc: e.dma_start(
                out=self.KH[:, :].rearrange("p (r c) -> p r c", r=4),
                in_=x_["kx_out"].rearrange("(r p) c -> p r c", p=128)[:, :, hc * 1024:(hc + 1) * 1024]),
                reads=["kx_out"], writes=["KH"], slot=gh)
            for r in range(4):
                S.add("sp", lambda e, hc=hc, r=r: e.dma_start(
                    out=self.VH[:, r * 8:(r + 1) * 8, 0:128],
                    in_=x_["vx_out"][r * 128:(r + 1) * 128, :].rearrange("p (t h v) -> p t h v", t=8, h=4)[:, :, hc, :]),
                    reads=["vx_out"], writes=["VH"], slot=gh)
            keyt = [("c", 0), ("c", 1)] + [("g", i) for i in range(32)]
            for qb in range(4):
                q0 = qb * 2
                qc = slice(q0 * 128, q0 * 128 + 256)
                for ji, (kind, kt) in enumerate(keyt):
                    kc_ = slice(kt * 128, (kt + 1) * 128)
                    bs = [self.bank(), self.bank()]
                    for sub in range(2):
                        lo, hi = sub * 64, sub * 64 + 64
                        ksrc = self.kctx[lo:hi, hc, kc_] if kind == "c" else self.KH[lo:hi, kc_]
                        S.add("pe", lambda e, sub=sub, lo=lo, hi=hi, bs=bs, ksrc=ksrc: e.matmul(
                            self.ps[:, bs[sub], 0:256], ksrc, self.QT[lo:hi, hc, qc], start=True, stop=True),
                            reads=["kctx" if kind == "c" else "KH", ("QT", hc)], writes=[("ps", bs[sub])])
                    i2 = ji % 2
                    for sub in range(2):
                        S.add("act", lambda e, sub=sub, bs=bs, i2=i2: e.activation(
                            self.ET[:, i2, sub * 256:(sub + 1) * 256], self.ps[:, bs[sub], 0:256], AF.Exp, scale=0.125),
                            reads=[("ps", bs[sub])], writes=[("ET", i2, sub)])
                    vsrc = self.vctx[:, kt, hc, :] if kind == "c" else self.VH[:, kt, :]
                    vkeys = ["vctx", "vctx1"] if kind == "c" else ["VH", "VH1"]
                    for qt in range(2):
                        for sub in range(2):
                            acc = qt * 2 + sub
                            S.add("pe", lambda e, qt=qt, sub=sub, acc=acc, i2=i2, vsrc=vsrc, ji=ji: e.matmul(
                                self.ps[:, acc, 0:130], self.ET[:, i2, sub * 256 + qt * 128: sub * 256 + (qt + 1) * 128],
                                vsrc, start=(ji == 0), stop=(ji == len(keyt) - 1)),
                                reads=[("ET", i2, sub)] + vkeys, writes=[("ps", acc)])
                self.attn_finalize(l, hc, q0)
        self.bank_pool = list(range(8))

    def attn_finalize(self, l, hc, q0):
        S = self.S
        for qt in range(2):
            t = q0 + qt
            a1, a2 = qt * 2, qt * 2 + 1
            S.add("dve", lambda e, a1=a1: e.reciprocal(self.sm8[:, 0:1], self.ps[:, a1, 128:129]),
                  reads=[("ps", a1)], writes=["sm8"])
            S.add("dve", lambda e, a2=a2: e.reciprocal(self.sm8[:, 1:2], self.ps[:, a2, 128:129]),
                  reads=[("ps", a2)], writes=["sm8"])
            S.add("dve", lambda e: e.tensor_tensor(self.sm8[:, 1:2], self.sm8[:, 1:2], self.lam[:, l:l + 1], ALU.mult),
                  reads=["sm8", "lam"], writes=["sm8"])
            S.add("dve", lambda e, a2=a2: e.tensor_scalar(self.tsm[:, 0, :], self.ps[:, a2, 0:128], self.sm8[:, 1:2], None, ALU.mult),
                  reads=[("ps", a2), "sm8"], writes=[("tsm", 0)])
            S.add("dve", lambda e, a1=a1, t=t, hc=hc: e.scalar_tensor_tensor(
                self.otokC[:, t, hc * 128:(hc + 1) * 128], self.ps[:, a1, 0:128], self.sm8[:, 0:1], self.tsm[:, 0, :],
                ALU.mult, ALU.add), reads=[("ps", a1), "sm8", ("tsm", 0)], writes=[("otokC", t)])
            S.add("act", lambda e, t=t, hc=hc: e.activation(
                self.tsm[:, 1, :], self.otokC[:, t, hc * 128:(hc + 1) * 128], AF.Square,
                accum_out=self.stat[:, t, hc, 1:2]), reads=[("otokC", t)], writes=["stat", ("tsm", 1)])

    def layer_norm(self, P, l, which, g, dT, sqT, tmpA, tmpB):
        S = self.S
        gc = slice(g * 512, (g + 1) * 512)
        xk = ("xT", g)
        for kc in range(8):
            S.add("act", lambda e, kc=kc: e.activation(sqT[:, kc, :], self.xT[:, kc, gc], AF.Identity), reads=[xk], writes=[("sqT", kc)])
        bm = self.bank()
        for kc in range(8):
            S.add("pe", lambda e, kc=kc: e.matmul(self.ps[:, bm, :], self.onesb[:, :], sqT[:, kc, :],
                                                  start=(kc == 0), stop=(kc == 7)),
                  reads=[("sqT", kc), "onesb"], writes=[("ps", bm)])
        for kc in range(8):
            S.add("dve", lambda e, kc=kc: e.tensor_tensor(dT[:, kc, :], self.xT[:, kc, gc], self.ps[:, bm, :], ALU.subtract),
                  reads=[xk, ("ps", bm)], writes=[("dT", kc)])
            S.add("act", lambda e, kc=kc: e.activation(sqT[:, kc, :], dT[:, kc, :], AF.Square),
                  reads=[("dT", kc)], writes=[("sqT", kc)])
        bv = self.bank()
        for kc in range(8):
            S.add("pe", lambda e, kc=kc: e.matmul(self.ps[:, bv, :], self.onesb[:, :], sqT[:, kc, :],
                                                  start=(kc == 0), stop=(kc == 7)),
                  reads=[("sqT", kc), "onesb"], writes=[("ps", bv)])
        S.add("act", lambda e: e.activation(tmpA[:], self.ps[:, bv, :], AF.Sqrt, bias=EPSA), reads=[("ps", bv)], writes=["tmpA"])
        S.add("dve", lambda e: e.reciprocal(tmpB[:], tmpA[:]), reads=["tmpA"], writes=["tmpB"])
        go = 48 + which * 8
        bo = 64 + which * 8
        for kc in range(8):
            S.add("dve", lambda e, kc=kc: e.tensor_tensor(dT[:, kc, :], dT[:, kc, :], tmpB[:], ALU.mult),
                  reads=[("dT", kc), "tmpB"], writes=[("dT", kc)])
            S.add("act", lambda e, kc=kc: e.activation(self.xT[:, kc, gc], dT[:, kc, :], AF.Identity,
                                                       scale=self.smallp[:, l, go + kc:go + kc + 1],
                                                       bias=self.smallp[:, l, bo + kc:bo + kc + 1]),
                  reads=[("dT", kc), "smallp"], writes=[xk])

    def merge_phase(self, P, l):
        S, d = self.S, self.d
        c = P["cond"]
        for oc in range(8):
            wg, kg = self.wload(d[f"wM{l}"][2 * oc], 3072)
            wb, kb = self.wload(d[f"wM{l}"][2 * oc + 1], 1536)
            for g in range(P["NG"]):
                gc = slice(g * 512, (g + 1) * 512)
                for br in range(3):
                    b = self.bank()
                    for kc in range(8):
                        S.add("pe", lambda e, b=b, kc=kc, br=br: e.matmul(
                            self.ps[:, b, :], wg[:, br * 1024 + kc * 128: br * 1024 + (kc + 1) * 128], self.hT[:, kc, gc],
                            start=(kc == 0), stop=(kc == 7)), reads=[kg, ("hT", g)], writes=[("ps", b)])
                    S.add("act", lambda e, b=b, br=br: e.activation(self.sig[:, br, :], self.ps[:, b, :], AF.Tanh, scale=0.5),
                          reads=[("ps", b)], writes=[("sig", br)])
                for br in range(3):
                    b = self.bank()
                    for kc in range(4):
                        S.add("pe", lambda e, b=b, kc=kc, br=br: e.matmul(
                            self.ps[:, b, :], wb[:, br * 512 + kc * 128: br * 512 + (kc + 1) * 128], self.yT[:, br, kc, gc],
                            start=(kc == 0), stop=(kc == 3)), reads=[kb, ("yT", br, g)], writes=[("ps", b)])
                    S.add("dve", lambda e, b=b, br=br: e.scalar_tensor_tensor(
                        self.tm3[:, br, :], self.sig[:, br, :], 1.0, self.ps[:, b, :], ALU.add, ALU.mult),
                        reads=[("sig", br), ("ps", b)], writes=[("tm3", br)])
                S.add("dve", lambda e: e.tensor_tensor(self.tm3[:, 0, :], self.tm3[:, 0, :], self.tm3[:, 1, :], ALU.add),
                      reads=[("tm3", 0), ("tm3", 1)], writes=[("tm3", 0)])
                S.add("dve", lambda e, oc=oc, gc=gc: e.tensor_tensor(self.mergedT[:, oc, gc], self.tm3[:, 0, :], self.tm3[:, 2, :], ALU.add),
                      reads=[("tm3", 0), ("tm3", 2)], writes=[("mergedT", g)])
        wo = [self.wload(d[f"wO{l}"][i]) for i in range(2)]
        for g in range(P["NG"]):
            gc = slice(g * 512, (g + 1) * 512)
            for oc in range(8):
                wv, wk = wo[oc // 4]
                ci = oc % 4
                b = self.bank()
                for kc in range(8):
                    S.add("pe", lambda e, b=b, kc=kc, wv=wv, ci=ci: e.matmul(
                        self.ps[:, b, :], wv[:, ci * 1024 + kc * 128: ci * 1024 + (kc + 1) * 128], self.mergedT[:, kc, gc],
                        start=(kc == 0), stop=(kc == 7)), reads=[wk, ("mergedT", g)], writes=[("ps", b)])
                S.add("dve", lambda e, b=b, oc=oc: e.scalar_tensor_tensor(
                    self.xT[:, oc, gc], self.ps[:, b, :], self.modT[:, l, 16 + oc, c:c + 1], self.xT[:, oc, gc],
                    ALU.mult, ALU.add), reads=[("ps", b), ("modT", l, 2), ("xT", g)], writes=[("xT", g)])
            self.layer_norm(P, l, 0, g, self.dT, self.sqT, self.tmpA, self.tmpB)

    def ffn(self, P, l):
        S, d = self.S, self.d
        c = P["cond"]
        T = P["T"]
        S.add("dve", lambda e: e.memset(self.aT[:, :, :], 0.0), writes=[("aT", 0), ("aT", 1)])
        wv = wk = None
        sample = bool(P.get("sample"))
        if sample:
            x_ = self.x_
            S.add("dve", lambda e: e.tensor_copy(self.hxs[:, 0:8].rearrange("p (k o) -> p k o", o=1), self.hT[:, :, 0:1]),
                  reads=[("hT", 0)], writes=["hxs"])
            S.add("dve", lambda e: e.tensor_copy(self.hxs[:, 8:16].rearrange("p (k o) -> p k o", o=1), self.hT[:, :, 1023:1024]),
                  reads=[("hT", 1)], writes=["hxs"])
            g1 = self.xgroup("hxw")
            S.add("sp", lambda e: e.dma_start(out=x_["hx_in"], in_=self.hxs[:, :]), reads=["hxs"], writes=["hx_in"], slot=g1)
            self.gather(x_["hx_in"], x_["hx_out"], "hx_in", "hx_out")
            g2 = self.xgroup("hxl")
            S.add("sp", lambda e: e.dma_start(out=self.hG[:, :, :], in_=x_["hx_out"].rearrange("(r p) c -> p r c", p=128)),
                  reads=["hx_out"], writes=["hG"], slot=g2)
            S.add("dve", lambda e: e.memset(self.hhf[:, :, :], 0.0), writes=["hhf"])
            for j in range(4):
                S.add("dve", lambda e, j=j: e.scalar_tensor_tensor(
                    self.hhf[:, :, 0], self.hG[:, j, 8:16], self.meta[:, 6 + j:7 + j], self.hhf[:, :, 0], ALU.mult, ALU.add),
                    reads=["hG", "meta", "hhf"], writes=["hhf"])
                S.add("dve", lambda e, j=j: e.scalar_tensor_tensor(
                    self.hhf[:, :, 1], self.hG[:, j, 0:8], self.meta[:, 10 + j:11 + j], self.hhf[:, :, 1], ALU.mult, ALU.add),
                    reads=["hG", "meta", "hhf"], writes=["hhf"])
            S.add("dve", lambda e: e.tensor_copy(self.hh[:, :, :], self.hhf[:, :, :]), reads=["hhf"], writes=["hh"])
        for fc in range(NFC):
            if fc % 2 == 0:
                wv, wk = self.wload(d[f"wU{l}"][fc // 2])
            ca, cb = (fc % 2) * 2, (fc % 2) * 2 + 1
            a2 = fc % 2
            bbs = {}
            for g in range(P["NG"]):
                gc = slice(g * 512, (g + 1) * 512)
                ba = self.bank()
                for kc in range(8):
                    S.add("pe", lambda e, b=ba, kc=kc, ci=ca, wv=wv, gc=gc: e.matmul(
                        self.ps[:, b, :], wv[:, ci * 1024 + kc * 128: ci * 1024 + (kc + 1) * 128], self.hT[:, kc, gc],
                        start=(kc == 0), stop=(kc == 7)), reads=[wk, ("hT", g)], writes=[("ps", ba)])
                for (s0, n, col0) in P["conv_rows"][g]:
                    S.add("act", lambda e, s0=s0, n=n, col0=col0, ba=ba: e.activation(
                        self.aT[:, a2, col0:col0 + n], self.ps[:, ba, s0:s0 + n], AF.Identity),
                        reads=[("ps", ba)], writes=[("aT", a2)])
            if sample:
                bh = self.bank()
                for kc in range(8):
                    S.add("pe", lambda e, kc=kc, wv=wv, bh=bh: e.matmul(
                        self.ps[:, bh, 0:2], wv[:, ca * 1024 + kc * 128: ca * 1024 + (kc + 1) * 128], self.hh[:, kc, :],
                        start=(kc == 0), stop=(kc == 7)), reads=[wk, "hh"], writes=[("ps", bh)])
                S.add("dve", lambda e, bh=bh: e.tensor_scalar(self.aT[:, a2, 0:1], self.ps[:, bh, 0:1], self.meta[:, 4:5], None, ALU.mult),
                      reads=[("ps", bh), "meta"], writes=[("aT", a2)])
                S.add("dve", lambda e, bh=bh: e.tensor_scalar(self.aT[:, a2, 1025:1026], self.ps[:, bh, 1:2], self.meta[:, 5:6], None, ALU.mult),
                      reads=[("ps", bh), "meta"], writes=[("aT", a2)])
            for g in range(P["NG"]):
                gc = slice(g * 512, (g + 1) * 512)
                bb = self.bank()
                for kc in range(8):
                    S.add("pe", lambda e, b=bb, kc=kc, ci=cb, wv=wv, gc=gc: e.matmul(
                        self.ps[:, b, :], wv[:, ci * 1024 + kc * 128: ci * 1024 + (kc + 1) * 128], self.hT[:, kc, gc],
                        start=(kc == 0), stop=(kc == 7)), reads=[wk, ("hT", g)], writes=[("ps", bb)])
                cw = 80
                for (s0, n, col0) in P["conv_rows"][g]:
                    S.add("dve", lambda e, s0=s0, n=n, col0=col0: e.tensor_scalar(
                        self.cT[:, a2, s0:s0 + n], self.aT[:, a2, col0 - 1:col0 - 1 + n],
                        self.smallp[:, l, cw + fc:cw + fc + 1], self.smallp[:, l, cw + 66 + fc:cw + 66 + fc + 1],
                        ALU.mult, ALU.add), reads=[("aT", a2), "smallp"], writes=[("cT", a2)])
                    for k in (1, 2):
                        S.add("dve", lambda e, s0=s0, n=n, col0=col0, k=k: e.scalar_tensor_tensor(
                            self.cT[:, a2, s0:s0 + n], self.aT[:, a2, col0 - 1 + k:col0 - 1 + k + n],
                            self.smallp[:, l, cw + k * NFC + fc:cw + k * NFC + fc + 1], self.cT[:, a2, s0:s0 + n],
                            ALU.mult, ALU.add), reads=[("aT", a2), "smallp", ("cT", a2)], writes=[("cT", a2)])
                S.add("act", lambda e: e.activation(self.glT[:, a2, :], self.cT[:, a2, :], AF.Gelu_apprx_tanh),
                      reads=[("cT", a2)], writes=[("glT", a2)])
                S.add("dve", lambda e, bb=bb, gc=gc: e.tensor_tensor(self.gT[:, fc, gc], self.glT[:, a2, :], self.ps[:, bb, :], ALU.mult),
                      reads=[("glT", a2), ("ps", bb)], writes=[("gT", g)])
        for oc in range(8):
            wv, wk = self.wload(d[f"wD{l}"][oc], NFC * 128)
            for g in range(P["NG"]):
                gc = slice(g * 512, (g + 1) * 512)
                b = self.bank()
                for fc in range(NFC):
                    S.add("pe", lambda e, b=b, fc=fc, wv=wv: e.matmul(
                        self.ps[:, b, :], wv[:, fc * 128:(fc + 1) * 128], self.gT[:, fc, gc],
                        start=(fc == 0), stop=(fc == NFC - 1)), reads=[wk, ("gT", g)], writes=[("ps", b)])
                S.add("dve", lambda e, b=b, oc=oc, gc=gc: e.scalar_tensor_tensor(
                    self.xT[:, oc, gc], self.ps[:, b, :], self.modT[:, l, 40 + oc, c:c + 1], self.xT[:, oc, gc],
                    ALU.mult, ALU.add), reads=[("ps", b), ("modT", l, 5), ("xT", g)], writes=[("xT", g)])
        for g in range(P["NG"]):
            self.layer_norm(P, l, 1, g, self.dT2, self.sqT2, self.tmpA2, self.tmpB2)

    def store_x(self, P):
        S = self.S
        dst = self.o[P["y"]]
        for t in range(P["NT"]):
            sl = t % 2
            for half in range(2):
                b = self.bank()
                for q in range(4):
                    kc = half * 4 + q
                    S.add("pe", lambda e, b=b, q=q, kc=kc, t=t: e.transpose(
                        self.ps[:, b, q * 128:(q + 1) * 128], self.xT[:, kc, t * 128:(t + 1) * 128], self.cst[:, 4, :]),
                        reads=[("xT", t // 4), "cst"], writes=[("ps", b)])
                S.add("act" if half == 0 else "dve",
                      (lambda e, b=b, half=half, sl=sl: e.activation(self.stage[:, sl, half * 512:(half + 1) * 512], self.ps[:, b, :], AF.Identity))
                      if half == 0 else
                      (lambda e, b=b, half=half, sl=sl: e.tensor_copy(self.stage[:, sl, half * 512:(half + 1) * 512], self.ps[:, b, :])),
                      reads=[("ps", b)], writes=[("stage", sl)])
            op = S.add("sp", lambda e, t=t, sl=sl: e.dma_start(out=dst[t * 128:(t + 1) * 128, :], in_=self.stage[:, sl, :]),
                       reads=[("stage", sl)], slot=self.s_io[sl])
            self.out_ops.append(op)

    def run_pass(self, P):
        stop = int(os.environ.get("KSTOP", "99"))
        if stop < 1:
            return
        self.load_x(P)
        self.barrier()
        for l in range(L):
            if stop < 2 + 10 * l:
                break
            self.modulate(P, l, 0)
            self.mixer_AB(P, l, 0)
            self.ada_deferred()
            self.barrier()
            if stop < 3 + 10 * l:
                break
            self.mixer_AB(P, l, 1)
            self.barrier()
            if stop < 4 + 10 * l:
                break
            self.mixer_C(P, l)
            self.barrier()
            if stop < 5 + 10 * l:
                break
            self.merge_phase(P, l)
            self.modulate(P, l, 1)
            self.barrier()
            if stop < 6 + 10 * l:
                break
            self.ffn(P, l)
            self.barrier()
        self.store_x(P)
        self.barrier()

    def build(self):
        self.declare()
        self.init_phase()
        PP = dict(T=512, NT=4, NG=1, cond=0, x="xp", y="yp", rope=False, states_out=True, kv_out=True,
                  seqs=[(0, 2), (2, 2)],
                  attn=[(0, 2, [0, 1]), (2, 2, [2, 3])],
                  conv_rows=[[(0, 256, 1), (256, 256, 515)]])
        if not os.environ.get("KNOPROMPT"):
            self.run_pass(PP)
        if self.do_sample:
            PS = dict(T=1024, NT=8, NG=2, cond=1, x="xs", y="ys", rope=True, states_out=False, kv_out=False,
                      seqs=[(0, 8)], sample=True, attn=[],
                      conv_rows=[[(0, 512, 1)], [(0, 512, 513)]])
            self.run_pass(PS)
        self.S.wait_for("sp", self.out_ops)
        global LAST_S
        LAST_S = self.S
        self.S.emit(self.nc, self.st)


_CACHE = {}


def _build_nc(do_sample=True):
    key = ("nc", do_sample)
    if key not in _CACHE:
        nc = bass.Bass("TRN2", target_bir_lowering=False)
        with contextlib.ExitStack() as st:
            b = Builder(nc, st, do_sample=do_sample)
            b.build()
        _CACHE[key] = nc
    return _CACHE[key]


def kernel(**inp):
    inp = {k: np.asarray(v) for k, v in inp.items()}
    W = _host_weights(inp)
    nc = _build_nc(not os.environ.get("KNOSAMPLE"))
    in_maps = []
    ncores = int(os.environ.get("KCORES", str(NCORES)))
    for c in range(ncores):
        b, r = c // 4, c % 4
        m = {k: v for k, v in W.items() if not k.startswith("wAfull")}
        for l in range(L):
            m[f"wAq{l}"] = np.ascontiguousarray(W[f"wAfull{l}"][3 * r:3 * r + 3])
        m["xp"] = np.ascontiguousarray(inp["x_prompt"][2 * c:2 * c + 2].reshape(512, D))
        m["xs"] = np.ascontiguousarray(inp["x_sample"][b, r * 1024:(r + 1) * 1024])
        cond = np.stack([inp["c_ctx"], inp["c"][b]], axis=-1)
        m["cond"] = np.ascontiguousarray(cond.reshape(8, 128, 2).transpose(1, 0, 2), dtype=np.float32)
        rc, rs = _rope_tables(r * 1024)
        m["ropec"], m["ropes"] = rc, rs
        meta = np.zeros((128, 16), np.float32)
        for s_ in range(4):
            meta[0:64, s_] = 1.0 if r == s_ else 0.0
            meta[64:128, s_] = 1.0 if r == 3 - s_ else 0.0
            meta[:, 6 + s_] = 1.0 if s_ == r - 1 else 0.0
            meta[:, 10 + s_] = 1.0 if s_ == r + 1 else 0.0
        meta[:, 4] = 1.0 if r > 0 else 0.0
        meta[:, 5] = 1.0 if r < 3 else 0.0
        m["meta"] = meta
        for nm, src in (("sret", inp["state_ret"]), ("sgla", inp["state_gla"])):
            a = src[b]
            m[nm] = np.ascontiguousarray(a.transpose(0, 1, 3, 2, 4).reshape(L, 128, 4, 128), dtype=np.float32)
        ck = inp["cache_diff_k"][b]
        m["ckc"] = np.ascontiguousarray(ck.reshape(L, 8, 2, 128, 64).transpose(0, 2, 3, 1, 4).reshape(L, 2, 128, 512), dtype=np.float32)
        cv = inp["cache_diff_v"][b]
        m["cvc"] = np.ascontiguousarray(cv.reshape(L, 4, 2, 128, 128).transpose(0, 2, 3, 1, 4), dtype=np.float32)
        in_maps.append(m)
    res = run_bass_kernel_spmd(nc, in_maps, core_ids=list(range(ncores)))
    R = list(res.results)
    while len(R) < NCORES:
        R.append(R[0])
    y_prompt = np.concatenate([R[c]["yp"].reshape(2, 256, D) for c in range(NCORES)], axis=0)
    y_sample = np.stack([np.concatenate([R[b * 4 + r]["ys"] for r in range(4)], axis=0) for b in range(2)], axis=0)
    ndk = np.concatenate([R[c]["ndk"] for c in range(NCORES)], axis=0)
    ndv = np.concatenate([R[c]["ndv"] for c in range(NCORES)], axis=0)
    nsr = np.concatenate([R[c]["nsr"] for c in range(NCORES)], axis=0)
    nsg = np.concatenate([R[c]["nsg"] for c in range(NCORES)], axis=0)
    f = np.float32
    return (y_prompt.astype(f), y_sample.astype(f), ndk.astype(f), ndv.astype(f), nsr.astype(f), nsg.astype(f))
```

```python
import contextlib
import os
import math
import numpy as np
import concourse.bass as bass
import concourse.mybir as mybir
from concourse.bass_utils import run_bass_kernel_spmd

F32 = mybir.dt.float32
BF16 = mybir.dt.bfloat16
AF = mybir.ActivationFunctionType
ALU = mybir.AluOpType

NCORES = 8
D = 1024
L = 2
DIN = 7680
DFF = 2816
NFC = 22
ALPHA = (2 * L) ** 0.25
EPS = 1e-5
EPSA = EPS / (ALPHA * ALPHA)
WSZ = 4096
NSLOT = 3
UW = 21504


class _Op:
    __slots__ = ("eng", "fn", "deps", "slot", "cnt", "need_inc", "inc")

    def __init__(self, eng, fn, slot, inc):
        self.eng = eng
        self.fn = fn
        self.deps = set()
        self.slot = slot
        self.cnt = 0
        self.need_inc = False
        self.inc = inc


class _Slot:
    def __init__(self, name, wait_all):
        self.name = name
        self.wait_all = wait_all
        self.total = 0
        self.sem = None


DEBUG_WAITS = None
LAST_S = None


class _Rec:
    def __init__(self):
        self.call = None

    def __getattr__(self, name):
        def f(*a, **k):
            self.call = (name, a, k)
            return None
        return f


def _freeze(fn):
    r = _Rec()
    fn(r)
    call = r.call
    if call is None:
        return lambda e: None
    name, a, k = call
    return lambda e: getattr(e, name)(*a, **k)


class Sched:
    ENGS = ("pe", "act", "dve", "pool", "sp")

    def __init__(self):
        self.ops = {e: [] for e in self.ENGS}
        self.lastw = {}
        self.readers = {}
        self.slots = []

    def slot(self, name, wait_all=False):
        s = _Slot(name, wait_all)
        self.slots.append(s)
        return s

    def add(self, eng, fn, reads=(), writes=(), slot=None, inc=16):
        op = _Op(eng, _freeze(fn), slot, inc)
        deps = op.deps
        for r in reads:
            w = self.lastw.get(r)
            if w is not None:
                deps.add(w)
            if isinstance(r, tuple) and r[0] == "ps":
                for rd in self.readers.get(r, {}).values():
                    if rd.eng != eng:
                        deps.add(rd)
        for k in writes:
            w = self.lastw.get(k)
            if w is not None:
                deps.add(w)
            for rd in self.readers.get(k, {}).values():
                deps.add(rd)
        rk = eng if slot is None else ("dma", id(op))
        for r in reads:
            self.readers.setdefault(r, {})[rk] = op
        for k in writes:
            self.lastw[k] = op
            self.readers[k] = {}
        if slot is not None:
            slot.total += inc
            op.cnt = slot.total
        self.ops[eng].append(op)
        return op

    def wait_for(self, eng, ops):
        op = _Op(eng, lambda e: None, None, 0)
        op.deps = set(o for o in ops if o is not None)
        self.ops[eng].append(op)
        return op

    def emit(self, nc, stack):
        for e in self.ENGS:
            for op in self.ops[e]:
                for d in op.deps:
                    if d.slot is None:
                        if d.eng == op.eng and op.slot is None and d.eng == "pe":
                            continue
                        d.need_inc = True
        esem = {}
        for e in self.ENGS:
            c = 0
            for op in self.ops[e]:
                if op.slot is None and op.need_inc:
                    c += 1
                    op.cnt = c
            if c > 0:
                esem[e] = stack.enter_context(nc.semaphore("es_" + e))
        for s in self.slots:
            if s.total > 0:
                s.sem = stack.enter_context(nc.semaphore("sl_" + s.name))
        block = stack.enter_context(nc.Block())

        def body(e):
            def run(eng):
                seen = {}
                for op in self.ops[e]:
                    waits = {}
                    for d in op.deps:
                        if d.slot is not None:
                            if d.slot is op.slot and d.slot.wait_all:
                                continue
                            key = d.slot
                            val = d.slot.total if d.slot.wait_all else d.cnt
                            sem = d.slot.sem
                        else:
                            if d.eng == e and op.slot is None and e == "pe":
                                continue
                            key = d.eng
                            val = d.cnt
                            sem = esem[d.eng]
                        if val > waits.get(key, (None, 0))[1]:
                            waits[key] = (sem, val)
                    for key in sorted(waits.keys(), key=lambda k: k if isinstance(k, str) else "~" + k.name):
                        sem, val = waits[key]
                        if seen.get(key, 0) >= val:
                            continue
                        seen[key] = val
                        eng.wait_ge(sem, val)
                        if DEBUG_WAITS is not None:
                            DEBUG_WAITS.append((e, len(DEBUG_WAITS), key if isinstance(key, str) else key.name, val))
                    inst = op.fn(eng)
                    if op.slot is not None:
                        inst.then_inc(op.slot.sem, op.inc)
                    elif op.need_inc:
                        inst.then_inc(esem[e], 1)
            return run

        block.tensor(body("pe"))
        block.scalar(body("act"))
        block.vector(body("dve"))
        block.gpsimd(body("pool"))
        block.sync(body("sp"))


PTS = np.cumsum([0, 256, 256, 512, 512, 256, 256, 512, 512, 512, 512, 512, 1024, 1024, 1024])
(O_AQ, O_AK, O_AV, O_AG, O_BQ, O_BK, O_BV, O_BR, O_CQ, O_CK, O_CV, O_MA, O_MB, O_MC) = [int(v) for v in PTS[:-1]]


def _fchunk(w, cols):
    sub = w[:, cols]
    return sub.reshape(8, 128, sub.shape[1]).transpose(1, 0, 2)


def _pack(chunks, per):
    out = []
    for i in range(0, len(chunks), per):
        blk = np.concatenate([c.reshape(128, -1) for c in chunks[i:i + per]], axis=1)
        assert blk.shape[1] <= WSZ
        if blk.shape[1] < WSZ:
            blk = np.concatenate([blk, np.zeros((128, WSZ - blk.shape[1]), np.float32)], axis=1)
        out.append(blk)
    return np.ascontiguousarray(np.stack(out, 0), dtype=np.float32)


def _host_weights(inp):
    W = {}
    ar = np.arange
    for l in range(L):
        w_in = inp["w_in"][l]
        ch = []
        for (oq, ok) in ((O_AQ, O_AK), (O_BQ, O_BK)):
            for o in (oq, ok):
                for h in range(4):
                    cols = np.concatenate([o + h * 64 + ar(64), o + h * 64 + ar(64)])
                    ch.append(_fchunk(w_in, cols))
        for o in (O_CQ, O_CK):
            for h in range(4):
                ch.append(_fchunk(w_in, o + h * 128 + ar(128)))
        W[f"wF{l}"] = _pack(ch, 4)
        ch = [_fchunk(w_in, o + ar(512)) for o in (O_AV, O_AG, O_BV, O_BR, O_CV, O_CK)]
        W[f"wT{l}"] = _pack(ch, 1)
        ch = []
        wb = inp["w_branch"][l]
        for oc in range(8):
            g = [_fchunk(w_in, o + oc * 128 + ar(128)).reshape(128, -1) for o in (O_MA, O_MB, O_MC)]
            ch.append(np.concatenate(g, axis=1))
            b = [wb[br][:, oc * 128:(oc + 1) * 128].reshape(4, 128, 128).transpose(1, 0, 2).reshape(128, -1)
                 for br in range(3)]
            ch.append(np.concatenate(b, axis=1))
        W[f"wM{l}"] = _pack(ch, 1)
        wo = inp["w_out"][l]
        W[f"wO{l}"] = _pack([_fchunk(wo, oc * 128 + ar(128)) for oc in range(8)], 4)
        wu = inp["ffn_w_up"][l]
        ch = []
        for fc in range(NFC):
            ch.append(_fchunk(wu, fc * 128 + ar(128)))
            ch.append(_fchunk(wu, DFF + fc * 128 + ar(128)))
        W[f"wU{l}"] = _pack(ch, 4)
        wd = inp["ffn_w_down"][l]
        ch = [wd[:, oc * 128:(oc + 1) * 128].reshape(NFC, 128, 128).transpose(1, 0, 2) for oc in range(8)]
        W[f"wD{l}"] = _pack(ch, 1)
        aw = inp["ada_w"][l]
        W[f"wAfull{l}"] = _pack([_fchunk(aw, j * 512 + ar(512)) for j in range(12)], 1)
    wa1 = np.zeros((128, L, 8, 32), np.float32)
    wa2 = np.zeros((33, L, 512), np.float32)
    for l in range(L):
        for e in range(2):
            wa1[:, l, :, e * 16:(e + 1) * 16] = inp["gla_wa1"][l, e].reshape(8, 128, 16).transpose(1, 0, 2)
            for h in range(4):
                c0 = h * 128 + e * 64
                wa2[e * 16:(e + 1) * 16, l, c0:c0 + 64] = inp["gla_wa2"][l, e][:, h * 64:(h + 1) * 64]
                wa2[32, l, c0:c0 + 64] = inp["gla_ba"][l, e][h * 64:(h + 1) * 64]
    W["wa1"] = wa1
    W["wa2"] = wa2
    sp = []
    for l in range(L):
        cols = [inp["ada_b"][l].reshape(48, 128).T,
                inp["ln_g"][l].reshape(16, 128).T,
                inp["ln_b"][l].reshape(16, 128).T,
                inp["ffn_conv_w"][l].reshape(3 * NFC, 128).T,
                inp["ffn_conv_b"][l].reshape(NFC, 128).T]
        sp.append(np.concatenate(cols, axis=1))
    W["smallp"] = np.ascontiguousarray(np.stack(sp, 1), dtype=np.float32)
    bc = []
    for l in range(L):
        row = np.concatenate([inp["gla_norm_g"][l], inp["diff_subln_g"][l],
                              inp["diff_lam"][l].reshape(-1), inp["ret_decay"][l].reshape(-1)])
        bc.append(np.broadcast_to(row[None, :], (128, row.shape[0])))
    W["bcast"] = np.ascontiguousarray(np.stack(bc, 1), dtype=np.float32)
    j = ar(128)[:, None]
    i = ar(128)[None, :]
    cs = np.zeros((128, 9, 128), np.float32)
    cs[:, 8] = 1.0
    cs[:, 0] = (j <= i)
    cs[:, 1] = (j >= i)
    cs[:, 2] = (j > i)
    cs[:, 3] = (j < i)
    cs[:, 4] = (j == i)
    m = ar(128)
    sw = np.where((m % 32) < 16, m + 16, m - 16)
    cs[sw, 5, m] = 1.0
    cs[:, 6] = (j == i) & (j < 64)
    cs[:, 7] = (j == i) & (j >= 64)
    W["consts"] = cs
    return W


def _rope_tables(t0):
    t = t0 + np.arange(1024)
    row = (t // 64).astype(np.float32)
    col = (t % 64).astype(np.float32)
    inv = (np.float32(10000.0) ** (-np.arange(16, dtype=np.float32) / np.float32(16))).astype(np.float32)
    cos = np.zeros((128, 1024), np.float32)
    sin = np.zeros((128, 1024), np.float32)
    for p in range(128):
        d = p % 64
        pos = row if d < 32 else col
        ang = (pos * inv[d % 16]).astype(np.float32)
        cos[p] = np.cos(ang)
        s = np.sin(ang)
        sin[p] = -s if (d % 32) < 16 else s
    return cos, sin


class Builder:
    def __init__(self, nc, st, do_sample=True):
        self.nc = nc
        self.st = st
        self.S = Sched()
        self.do_sample = do_sample
        self.wslot_i = 0
        self.out_ops = []
        self.out_seen = 0

    def dram_in(self, name, shape, dt=F32):
        return self.nc.dram_tensor(name, list(shape), dt, kind="ExternalInput").ap()

    def dram_out(self, name, shape, dt=F32):
        return self.nc.dram_tensor(name, list(shape), dt, kind="ExternalOutput").ap()

    def sb(self, name, shape, dt):
        return self.st.enter_context(self.nc.sbuf_tensor(name, list(shape), dt))

    def wload(self, src_blk, nelem=WSZ):
        s = self.wslot_i % NSLOT
        self.wslot_i += 1
        dst = self.wring[:, s, 0:nelem]
        key = ("w", s)
        self.S.add("pool", lambda e: e.dma_start(out=dst, in_=src_blk[:, 0:nelem]), writes=[key], slot=self.wsl[s])
        return self.wring[:, s, :], key

    def xgroup(self, name):
        self.xs_i += 1
        return self.S.slot("x%d_%s" % (self.xs_i, name), wait_all=True)

    def gather(self, src, dst, rk, wk):
        sl = self.xgroup("cc")
        groups = [[0, 1, 2, 3], [4, 5, 6, 7]] if not os.environ.get("KG4") else [[0, 1, 2, 3]]
        self.S.add("pool", lambda e: e.collective_compute("AllGather", ALU.bypass, replica_groups=groups,
                                                          ins=[src], outs=[dst]),
                   reads=[rk], writes=[wk], slot=sl, inc=1)

    def declare(self):
        nc = self.nc
        di = self.dram_in
        self.d = d = {}
        d["xp"] = di("xp", [512, D])
        d["xs"] = di("xs", [1024, D])
        d["cond"] = di("cond", [128, 8, 2])
        for l in range(L):
            d[f"wF{l}"] = di(f"wF{l}", [6, 128, WSZ])
            d[f"wT{l}"] = di(f"wT{l}", [6, 128, WSZ])
            d[f"wM{l}"] = di(f"wM{l}", [16, 128, WSZ])
            d[f"wO{l}"] = di(f"wO{l}", [2, 128, WSZ])
            d[f"wU{l}"] = di(f"wU{l}", [11, 128, WSZ])
            d[f"wD{l}"] = di(f"wD{l}", [8, 128, WSZ])
            d[f"wAq{l}"] = di(f"wAq{l}", [3, 128, WSZ])
        d["wa1"] = di("wa1", [128, L, 8, 32])
        d["wa2"] = di("wa2", [33, L, 512])
        d["smallp"] = di("smallp", [128, L, 168])
        d["bcast"] = di("bcast", [128, L, 520])
        d["consts"] = di("consts", [128, 9, 128])
        d["ropec"] = di("ropec", [128, 1024])
        d["ropes"] = di("ropes", [128, 1024])
        d["meta"] = di("meta", [128, 16])
        d["sret"] = di("sret", [L, 128, 4, 128])
        d["sgla"] = di("sgla", [L, 128, 4, 128])
        d["ckc"] = di("ckc", [L, 2, 128, 512])
        d["cvc"] = di("cvc", [L, 2, 128, 4, 128])
        dint = lambda name, shape, dt: nc.dram_tensor(name, list(shape), dt, addr_space="Local", kind="Internal").ap()
        self.x_ = x_ = {}
        x_["kx_in"] = dint("kx_in", [128, 4096], BF16)
        x_["kx_out"] = dint("kx_out", [512, 4096], BF16)
        x_["vx_in"] = dint("vx_in", [128, 4096], BF16)
        x_["vx_out"] = dint("vx_out", [512, 4096], BF16)
        x_["st_in"] = dint("st_in", [128, 516], F32)
        x_["st_out"] = dint("st_out", [512, 516], F32)
        x_["md_in"] = dint("md_in", [12, 512], F32)
        x_["md_out"] = dint("md_out", [48, 512], F32)
        x_["hx_in"] = dint("hx_in", [128, 16], BF16)
        x_["hx_out"] = dint("hx_out", [512, 16], BF16)
        do = self.dram_out
        self.o = o = {}
        o["yp"] = do("yp", [512, D])
        o["ys"] = do("ys", [1024, D])
        o["ndk"] = do("ndk", [2, L, 8, 256, 64])
        o["ndv"] = do("ndv", [2, L, 4, 256, 128])
        o["nsr"] = do("nsr", [2, L, 2, 4, 64, 128])
        o["nsg"] = do("nsg", [2, L, 2, 4, 64, 128])


        sb = self.sb
        self.xT = sb("xT", [128, 8, 1024], F32)
        self.hT = sb("hT", [128, 8, 1024], BF16)
        self.wring = sb("wring", [128, NSLOT, WSZ], BF16)
        self.wsl = [self.S.slot(f"w{s}") for s in range(NSLOT)]
        self.cst = sb("cst", [128, 9, 128], F32)
        self.cstb = sb("cstb", [128, 9, 128], BF16)
        self.onesb = sb("onesb", [128, 128], BF16)
        self.smallp = sb("smallp_sb", [128, L, 168], F32)
        self.bcast = sb("bcast_sb", [128, L, 520], F32)
        self.modT = sb("modT", [128, L, 48, 2], F32)
        self.condf = sb("condf", [128, 8, 2], F32)
        self.condb = sb("condb", [128, 8, 2], BF16)
        self.modtok = sb("modtok", [2, 512], F32)
        self.wa1 = sb("wa1_sb", [128, L, 8, 32], BF16)
        self.wa2 = sb("wa2_sb", [33, L, 512], BF16)
        self.yT = sb("yT", [128, 3, 4, 1024], BF16)
        if self.do_sample:
            self.ropec = sb("ropec_sb", [128, 1024], F32)
            self.ropes = sb("ropes_sb", [128, 1024], F32)
        self.lam = sb("lam", [128, 8], F32)
        self.retsp = sb("retsp", [128, L, 8], F32)
        self.subg = sb("subg", [128, 128], F32)
        self.ps = self.st.enter_context(nc.psum_tensor("ps", [128, 8, 512], F32))
        self.s_in = self.S.slot("init", wait_all=True)
        self.s_inp = self.S.slot("initp", wait_all=True)
        self.s_io = [self.S.slot(f"io{i}") for i in range(2)]
        self.s_st = [self.S.slot(f"st{i}") for i in range(2)]
        self.s_out = self.S.slot("out_misc")
        self.s_cc = self.S.slot("cc")
        self.s_xs = [self.S.slot(f"xs{i}") for i in range(4)]
        self.xs_i = 0
        self.meta = sb("meta_sb", [128, 16], F32)
        U = sb("U", [128, UW], F32)
        self.U = U

        def vb(off, shape):
            n = int(np.prod(shape)) // 2
            a = U[:, off:off + n].bitcast(BF16)
            return a if len(shape) == 1 else a.rearrange(
                "p (" + " ".join("abcd"[:len(shape)]) + ") -> p " + " ".join("abcd"[:len(shape)]),
                **{k: v for k, v in zip("abcd", shape)})

        def vf(off, shape):
            n = int(np.prod(shape))
            a = U[:, off:off + n]
            return a if len(shape) == 1 else a.rearrange(
                "p (" + " ".join("abcd"[:len(shape)]) + ") -> p " + " ".join("abcd"[:len(shape)]),
                **{k: v for k, v in zip("abcd", shape)})

        self.QT = vb(0, [4, 1024])
        self.otokAB = vb(0, [8, 512])
        self.dbf = vb(0, [4, 8, 128])
        self.stage = vf(0, [2, 1024])
        self.KT = vb(2048, [4, 1024])
        self.sstk = vb(2048, [4, 8, 128])
        self.vtok = vb(4096, [8, 512])
        self.gtok = vb(6144, [8, 512])
        self.otokC = vb(6144, [8, 512])
        self.qd = vb(8192, [4, 8, 128])
        self.Pm = vb(10240, [4, 8, 128])
        self.ropex = vb(10240, [512])
        self.ropet = vf(10496, [512])
        self.ropeu = vf(11008, [512])
        self.cstage = vf(10240, [2, 512])
        self.KH = vb(8192, [4096])
        self.VH = vb(12288, [32, 130])
        self.vctx = vb(10240, [2, 4, 130])
        self.kctx = vb(10760, [4, 256])
        self.kstg = vf(11272, [512])
        self.delta = vf(12288, [4, 8, 128])
        self.vaug = vb(12288, [8, 4, 130])
        self.ET = vb(12288 + 2080, [2, 512])
        o_ = 12288 + 4160
        self.s32 = vf(o_, [2, 4, 128]); o_ += 1024
        self.EqT = vf(o_, [4, 128]); o_ += 512
        self.EkT = vf(o_, [4, 128]); o_ += 512
        self.Ekd = vf(o_, [4, 128]); o_ += 512
        self.dcol = vf(o_, [4, 8]); o_ += 32
        self.latok = vf(o_, [512]); o_ += 512
        self.uT = vb(o_, [1024]); o_ += 512
        self.stat = vf(o_, [8, 4, 2]); o_ += 64
        self.rstd = vf(o_, [8, 4]); o_ += 32
        self.st6 = vf(o_, [6]); o_ += 8
        self.ksb = vb(o_, [2, 128]); o_ += 128
        self.kdtok = vb(o_, [2, 128]); o_ += 128
        self.ytok = vb(o_, [2, 512]); o_ += 512
        self.tsm = vf(o_, [2, 128]); o_ += 256
        self.sm8 = vf(o_, [16]); o_ += 16
        self.ggt = vb(o_, [512]); o_ += 256
        assert o_ <= UW, o_
        self.Sin = self.EqT
        self.Srun = self.EkT
        self.GL = self.Ekd
        self.sinb = self.uT[:, 0:512].rearrange("p (h v) -> p h v", h=4)
        self.cp = self.latok[:, 0:32].rearrange("p (h t) -> p h t", h=4)
        self.Dg = self.latok[:, 32:48].rearrange("p (s h) -> p s h", s=4)
        self.hxs = self.latok[:, 64:72].bitcast(BF16)
        self.hG = self.latok[:, 72:104].bitcast(BF16).rearrange("p (r c) -> p r c", r=4)
        self.hhf = self.latok[:, 104:120].rearrange("p (k c) -> p k c", k=8)
        self.hh = self.latok[:, 120:128].bitcast(BF16).rearrange("p (k c) -> p k c", k=8)
        self.mergedT = vb(0, [8, 1024])
        self.dT = vf(4096, [8, 512])
        self.sqT = vb(8192, [8, 512])
        self.tmpA = vf(10240, [512])
        self.tmpB = vf(10752, [512])
        self.sig = vf(11264, [3, 512])
        self.tm3 = vf(12800, [3, 512])
        self.gT = vb(0, [NFC, 1024])
        self.aT = vf(11264, [2, 1028])
        self.cT = vf(13320, [2, 512])
        self.glT = vf(14344, [2, 512])
        self.dT2 = vf(15368, [8, 512])
        self.tmpA2 = vf(19464, [512])
        self.tmpB2 = vf(19976, [512])
        self.sqT2 = vb(11264, [8, 512])
        assert 20488 <= UW
        self.bank_i = 0
        self.bank_pool = list(range(8))

    def bank(self):
        b = self.bank_pool[self.bank_i % len(self.bank_pool)]
        self.bank_i += 1
        return b

    def barrier(self, pool=False):
        S = self.S
        last = {}
        for e in S.ENGS:
            last[e] = None
            for op in reversed(S.ops[e]):
                if op.inc != 0:
                    last[e] = op
                    break
        dmas = [op for op in self.out_ops[self.out_seen:]]
        self.out_seen = len(self.out_ops)
        for e in (("pe", "act", "dve", "pool", "sp") if (pool or os.environ.get("KPOOLBAR")) else ("pe", "act", "dve", "sp")):
            deps = [last[o] for o in S.ENGS if (o != e or e != "pe") and last[o] is not None and last[o].slot is None]
            S.wait_for(e, deps + dmas)
        if os.environ.get("KNOCLR"):
            return
        keepw = lambda k: isinstance(k, tuple) and k[0] == "w"
        for k in list(S.lastw.keys()):
            if S.lastw[k].slot is None and not keepw(k):
                del S.lastw[k]
        for k in list(S.readers.keys()):
            if not keepw(k):
                S.readers[k] = {a: r for a, r in S.readers[k].items() if r.slot is not None}

    def psb(self, b):
        return self.ps[:, b, :].bitcast(BF16)

    def init_phase(self):
        S, d = self.S, self.d
        si = self.s_in
        S.add("sp", lambda e: e.dma_start(out=self.cst[:], in_=d["consts"]), writes=["cst"], slot=si)
        S.add("pool", lambda e: e.dma_start(out=self.cstb[:], in_=d["consts"]), writes=["cstb"], slot=self.s_inp)
        S.add("sp", lambda e: e.dma_start(out=self.smallp[:], in_=d["smallp"]), writes=["smallp"], slot=si)
        S.add("sp", lambda e: e.dma_start(out=self.bcast[:], in_=d["bcast"]), writes=["bcast"], slot=si)
        S.add("sp", lambda e: e.dma_start(out=self.condf[:], in_=d["cond"]), writes=["condf"], slot=si)
        S.add("pool", lambda e: e.dma_start(out=self.wa1[:], in_=d["wa1"]), writes=["wa1"], slot=self.s_inp)
        S.add("pool", lambda e: e.dma_start(out=self.wa2[:], in_=d["wa2"]), writes=["wa2"], slot=self.s_inp)
        if self.do_sample:
            S.add("sp", lambda e: e.dma_start(out=self.meta[:], in_=d["meta"]), writes=["meta"], slot=si)
            S.add("sp", lambda e: e.dma_start(out=self.ropec[:], in_=d["ropec"]), writes=["ropec"], slot=si)
            S.add("sp", lambda e: e.dma_start(out=self.ropes[:], in_=d["ropes"]), writes=["ropes"], slot=si)
        S.add("dve", lambda e: e.memset(self.onesb[:], 1.0 / 1024.0), writes=["onesb"])
        S.add("act", lambda e: e.activation(self.condb[:], self.condf[:], AF.Silu), reads=["condf"], writes=["condb"])
        self.s_mdw = self.S.slot("mdw")
        self.s_mdl = self.S.slot("mdl")
        for l in range(L):
            for i in range(3):
                wv, wk = self.wload(d[f"wAq{l}"][i])
                b = self.bank()
                for kc in range(8):
                    S.add("pe", lambda e, kc=kc, b=b, wv=wv: e.matmul(
                        self.ps[0:2, b, :], self.condb[:, kc, :], wv[:, kc * 512:(kc + 1) * 512],
                        start=(kc == 0), stop=(kc == 7)), reads=[wk, "condb"], writes=[("ps", b)])
                S.add("act", lambda e, b=b: e.activation(self.modtok[:], self.ps[0:2, b, :], AF.Identity),
                      reads=[("ps", b)], writes=["modtok"])
                r0 = (l * 3 + i) * 2
                S.add("sp", lambda e, r0=r0: e.dma_start(out=self.x_["md_in"][r0:r0 + 2, :], in_=self.modtok[:]),
                      reads=["modtok"], writes=["md_in"], slot=self.s_mdw)
        self.gather(self.x_["md_in"], self.x_["md_out"], "md_in", "md_out")
        self.ada_blocks(0, range(0, 12))
        self.ada_blocks(1, range(0, 12))
        self.ada_pending = []
        for l in range(L):
            lam_init = 0.8 - 0.6 * math.exp(-0.3 * l)
            lp = self.bcast[:, l, 256:512].rearrange("p (a b c) -> p a b c", a=2, b=2)
            S.add("dve", lambda e, lp=lp: e.tensor_tensor(self.tsm[:, 0, :].rearrange("p (a c) -> p a c", a=2),
                                                          lp[:, :, 0, :], lp[:, :, 1, :], ALU.mult),
                  reads=["bcast"], writes=["tsm"])
            S.add("dve", lambda e: e.reduce_sum(self.sm8[:, 0:2], self.tsm[:, 0, :].rearrange("p (a c) -> p a c", a=2),
                                                axis=mybir.AxisListType.X), reads=["tsm"], writes=["sm8"])
            S.add("act", lambda e: e.activation(self.sm8[:, 2:4], self.sm8[:, 0:2], AF.Exp), reads=["sm8"], writes=["sm8b"])
            S.add("dve", lambda e: e.tensor_tensor(self.sm8[:, 4:5], self.sm8[:, 3:4], self.sm8[:, 2:3], ALU.subtract),
                  reads=["sm8b"], writes=["sm8c"])
            S.add("dve", lambda e, l=l, li=lam_init: e.tensor_scalar(self.lam[:, l:l + 1], self.sm8[:, 4:5], -li, None, ALU.add),
                  reads=["sm8c"], writes=["lam"])
            S.add("act", lambda e, l=l: e.activation(self.sm8[:, 8:16], self.bcast[:, l, 512:520], AF.Exp, scale=-1.0),
                  reads=["bcast"], writes=["sm8d"])
            S.add("act", lambda e, l=l: e.activation(self.retsp[:, l, :], self.sm8[:, 8:16], AF.Ln, bias=1.0),
                  reads=["sm8d"], writes=["retsp"])

    def ada_blocks(self, l, js):
        S, d = self.S, self.d
        for j in js:
            row = (j // 3) * 12 + l * 6 + (j % 3) * 2
            S.add("sp", lambda e, row=row: e.dma_start(out=self.modtok[:], in_=self.x_["md_out"][row:row + 2, :]),
                  reads=["md_out"], writes=["modtok"], slot=self.s_mdl)
            b2 = self.bank()
            for q in range(4):
                S.add("pe", lambda e, q=q, b2=b2: e.transpose(
                    self.ps[:, b2, q * 2:(q + 1) * 2], self.modtok[0:2, q * 128:(q + 1) * 128],
                    self.cst[0:2, 4, 0:2]), reads=["modtok", "cst"], writes=[("ps", b2)])
            sec = j // 2
            S.add("dve", lambda e, l=l, j=j, b2=b2: e.tensor_tensor(
                self.modT[:, l, j * 4:(j + 1) * 4, :],
                self.ps[:, b2, 0:8].rearrange("p (a b) -> p a b", a=4),
                self.smallp[:, l, j * 4:(j + 1) * 4].rearrange("p (a o) -> p a o", o=1).broadcast_to([128, 4, 2]),
                ALU.add), reads=[("ps", b2), "smallp"], writes=[("modT", l, sec)])
            if j % 2 == 1:
                lo = sec * 8
                if sec in (1, 4):
                    S.add("dve", lambda e, l=l, lo=lo: e.tensor_scalar(self.modT[:, l, lo:lo + 8, :], self.modT[:, l, lo:lo + 8, :],
                                                                      1.0, None, ALU.add),
                          reads=[("modT", l, sec)], writes=[("modT", l, sec)])
                elif sec in (2, 5):
                    f = 0.5 / ALPHA if sec == 2 else 1.0 / ALPHA
                    S.add("dve", lambda e, l=l, lo=lo, f=f: e.tensor_scalar(self.modT[:, l, lo:lo + 8, :], self.modT[:, l, lo:lo + 8, :],
                                                                           f, None, ALU.mult),
                          reads=[("modT", l, sec)], writes=[("modT", l, sec)])

    def ada_deferred(self):
        for (l, js) in self.ada_pending:
            self.ada_blocks(l, js)
        self.ada_pending = []

    def load_x(self, P):
        S = self.S
        src = self.d[P["x"]]
        for t in range(P["NT"]):
            sl = t % 2
            S.add("sp", lambda e, t=t, sl=sl: e.dma_start(out=self.stage[:, sl, :], in_=src[t * 128:(t + 1) * 128, :]),
                  writes=[("stage", sl)], slot=self.s_io[sl])
            for half in range(2):
                b = self.bank()
                for q in range(4):
                    kc = half * 4 + q
                    S.add("pe", lambda e, b=b, q=q, kc=kc, sl=sl: e.transpose(
                        self.ps[:, b, q * 128:(q + 1) * 128], self.stage[:, sl, kc * 128:(kc + 1) * 128], self.cst[:, 4, :]),
                        reads=[("stage", sl), "cst"], writes=[("ps", b)])
                S.add("act" if half == 0 else "dve",
                      (lambda e, b=b, half=half, t=t: e.activation(
                          self.xT[:, half * 4:half * 4 + 4, t * 128:(t + 1) * 128],
                          self.ps[:, b, :].rearrange("p (a c) -> p a c", a=4), AF.Identity)) if half == 0 else
                      (lambda e, b=b, half=half, t=t: e.tensor_copy(
                          self.xT[:, half * 4:half * 4 + 4, t * 128:(t + 1) * 128],
                          self.ps[:, b, :].rearrange("p (a c) -> p a c", a=4))),
                      reads=[("ps", b)], writes=[("xT", t // 4)])

    def modulate(self, P, l, which):
        S = self.S
        c = P["cond"]
        sh0 = 0 if which == 0 else 24
        sc0 = 8 if which == 0 else 32
        for g in range(P["NG"]):
            for kc in range(8):
                S.add("dve", lambda e, g=g, kc=kc: e.tensor_scalar(
                    self.hT[:, kc, g * 512:(g + 1) * 512], self.xT[:, kc, g * 512:(g + 1) * 512],
                    self.modT[:, l, sc0 + kc, c:c + 1], self.modT[:, l, sh0 + kc, c:c + 1], ALU.mult, ALU.add),
                    reads=[("xT", g), ("modT", l, sh0 // 8), ("modT", l, sc0 // 8)], writes=[("hT", g)])

    def projF(self, P, wv, wk, ci, evac):
        S = self.S
        for g in range(P["NG"]):
            b = self.bank()
            for kc in range(8):
                S.add("pe", lambda e, b=b, kc=kc, g=g: e.matmul(
                    self.ps[:, b, :], wv[:, ci * 1024 + kc * 128: ci * 1024 + (kc + 1) * 128],
                    self.hT[:, kc, g * 512:(g + 1) * 512], start=(kc == 0), stop=(kc == 7)),
                    reads=[wk, ("hT", g)], writes=[("ps", b)])
            evac(g, b)

    def projT(self, P, wv, wk, evac):
        S = self.S
        for t in range(P["NT"]):
            b = self.bank()
            for kc in range(8):
                S.add("pe", lambda e, b=b, kc=kc, t=t: e.matmul(
                    self.ps[:, b, :], self.hT[:, kc, t * 128:(t + 1) * 128], wv[:, kc * 512:(kc + 1) * 512],
                    start=(kc == 0), stop=(kc == 7)),
                    reads=[wk, ("hT", t // 4)], writes=[("ps", b)])
            evac(t, b)

    def evac_rope(self, P, dst, key, h, g, b, scale):
        S = self.S
        gc = slice(g * 512, (g + 1) * 512)
        if not P["rope"]:
            S.add("act", lambda e: e.activation(dst[:, h, gc], self.ps[:, b, :], AF.Copy, scale=scale),
                  reads=[("ps", b)], writes=[(key, h)])
            return
        S.add("act", lambda e: e.activation(self.ropex[:, :], self.ps[:, b, :], AF.Copy, scale=scale),
              reads=[("ps", b)], writes=["ropex"])
        b2 = self.bank()
        S.add("pe", lambda e: e.matmul(self.ps[:, b2, :], self.cstb[:, 5, :], self.ropex[:, :], start=True, stop=True),
              reads=["ropex", "cstb"], writes=[("ps", b2)])
        S.add("dve", lambda e: e.tensor_tensor(self.ropet[:, :], self.ps[:, b2, :], self.ropes[:, gc], ALU.mult),
              reads=[("ps", b2), "ropes"], writes=["ropet"])
        S.add("dve", lambda e: e.tensor_tensor(self.ropeu[:, :], self.ropex[:, :], self.ropec[:, gc], ALU.mult),
              reads=["ropex", "ropec"], writes=["ropeu"])
        S.add("dve", lambda e: e.tensor_tensor(dst[:, h, gc], self.ropet[:, :], self.ropeu[:, :], ALU.add),
              reads=["ropet", "ropeu"], writes=[(key, h)])

    def decay_tiles(self, scale, t):
        S = self.S
        b = self.bank()
        for h in range(4):
            for dd, ci in ((0, 2), (1, 3)):
                c0 = h * 128 + dd * 64
                S.add("pe", lambda e, c0=c0, ci=ci: e.matmul(self.ps[:, b, c0:c0 + 64], self.cst[:, ci, :], self.latok[:, c0:c0 + 64],
                                                            start=True, stop=True),
                      reads=["latok", "cst"], writes=[("ps", b)])
        S.add("act", lambda e: e.activation(self.Ekd[:, :, :], self.ps[:, b, :].rearrange("p (h c) -> p h c", h=4), AF.Exp, scale=scale),
              reads=[("ps", b)], writes=["Ekd"])
        bx, by = self.bank(), self.bank()
        for h in range(4):
            S.add("pe", lambda e, h=h: e.matmul(self.ps[:, bx, h * 128:(h + 1) * 128], self.latok[:, h * 128:(h + 1) * 128],
                                               self.cst[:, 0, :], start=True, stop=True),
                  reads=["latok", "cst"], writes=[("ps", bx)])
            S.add("pe", lambda e, h=h: e.matmul(self.ps[:, by, h * 128:(h + 1) * 128], self.latok[:, h * 128:(h + 1) * 128],
                                               self.cst[:, 1, :], start=True, stop=True),
                  reads=["latok", "cst"], writes=[("ps", by)])
        for (dst, sc, key) in ((self.EqT, scale, "EqT"), (self.EkT, -scale, "EkT")):
            S.add("act", lambda e, dst=dst, sc=sc: e.activation(
                dst[0:64, :, :], self.ps[0:64, bx, :].rearrange("p (h c) -> p h c", h=4), AF.Exp, scale=sc),
                reads=[("ps", bx)], writes=[key])
            S.add("act", lambda e, dst=dst, sc=sc: e.activation(
                dst[64:128, :, :], self.ps[64:128, by, :].rearrange("p (h c) -> p h c", h=4), AF.Exp, scale=sc),
                reads=[("ps", by)], writes=[key])
        if t is not None:
            S.add("dve", lambda e: e.tensor_copy(self.dcol[0:64, :, t:t + 1], self.EqT[0:64, :, 127:128]),
                  reads=["EqT"], writes=["dcol"])
            S.add("dve", lambda e: e.tensor_copy(self.dcol[64:128, :, t:t + 1], self.EqT[64:128, :, 0:1]),
                  reads=["EqT"], writes=["dcol"])

    def compose_states(self, P, l, mix):
        S, x_ = self.S, self.x_
        dk = [("delta", h, t) for h in range(4) for t in range(8)]
        S.add("dve", lambda e: e.tensor_copy(self.cp[0:64, :, 0:1], self.dcol[0:64, :, 0:1]), reads=["dcol"], writes=["cp"])
        S.add("dve", lambda e: e.tensor_copy(self.cp[64:128, :, 7:8], self.dcol[64:128, :, 7:8]), reads=["dcol"], writes=["cp"])
        for t in range(1, 8):
            S.add("dve", lambda e, t=t: e.tensor_tensor(self.cp[0:64, :, t:t + 1], self.cp[0:64, :, t - 1:t],
                                                        self.dcol[0:64, :, t:t + 1], ALU.mult), reads=["cp", "dcol"], writes=["cp"])
            tb = 7 - t
            S.add("dve", lambda e, tb=tb: e.tensor_tensor(self.cp[64:128, :, tb:tb + 1], self.cp[64:128, :, tb + 1:tb + 2],
                                                          self.dcol[64:128, :, tb:tb + 1], ALU.mult), reads=["cp", "dcol"], writes=["cp"])
        g1 = self.xgroup("stw")
        stL = x_["st_in"][:, 0:512].rearrange("p (h v) -> p h v", h=4)
        S.add("sp", lambda e: e.dma_start(out=stL[0:64], in_=self.delta[0:64, :, 7, :]), reads=dk, writes=["st_in"], slot=g1)
        S.add("sp", lambda e: e.dma_start(out=stL[64:128], in_=self.delta[64:128, :, 0, :]), reads=dk, writes=["st_in"], slot=g1)
        S.add("sp", lambda e: e.dma_start(out=x_["st_in"][0:64, 512:516].rearrange("p (h o) -> p h o", o=1), in_=self.cp[0:64, :, 7:8], allow_slow_non_contiguous=True), reads=["cp"], writes=["st_in"], slot=g1)
        S.add("sp", lambda e: e.dma_start(out=x_["st_in"][64:128, 512:516].rearrange("p (h o) -> p h o", o=1), in_=self.cp[64:128, :, 0:1], allow_slow_non_contiguous=True), reads=["cp"], writes=["st_in"], slot=g1)
        self.gather(x_["st_in"], x_["st_out"], "st_in", "st_out")
        g2 = self.xgroup("stl")
        src0 = self.d["sret" if mix == 0 else "sgla"][l]
        S.add("sp", lambda e: e.dma_start(out=self.Srun[:, :, :], in_=src0), writes=["Srun"], slot=g2)
        for s_ in range(4):
            S.add("sp", lambda e, s_=s_: e.dma_start(out=self.Dg[0:64, s_, :], in_=x_["st_out"][s_ * 128:s_ * 128 + 64, 512:516]),
                  reads=["st_out"], writes=["Dg"], slot=g2)
            S.add("sp", lambda e, s_=s_: e.dma_start(out=self.Dg[64:128, s_, :],
                                                     in_=x_["st_out"][(3 - s_) * 128 + 64:(3 - s_) * 128 + 128, 512:516]),
                  reads=["st_out"], writes=["Dg"], slot=g2)
        S.add("dve", lambda e: e.memset(self.Sin[:, :, :], 0.0), writes=["Sin"])
        for s_ in range(4):
            g3 = self.xgroup("gl")
            S.add("sp", lambda e, s_=s_: e.dma_start(
                out=self.GL[0:64, :, :], in_=x_["st_out"][s_ * 128:s_ * 128 + 64, 0:512].rearrange("p (h v) -> p h v", h=4)),
                reads=["st_out"], writes=["GL"], slot=g3)
            S.add("sp", lambda e, s_=s_: e.dma_start(
                out=self.GL[64:128, :, :],
                in_=x_["st_out"][(3 - s_) * 128 + 64:(3 - s_) * 128 + 128, 0:512].rearrange("p (h v) -> p h v", h=4)),
                reads=["st_out"], writes=["GL"], slot=g3)
            S.add("dve", lambda e, s_=s_: e.scalar_tensor_tensor(
                self.Sin[:, :, :].rearrange("p h v -> p (h v)"), self.Srun[:, :, :].rearrange("p h v -> p (h v)"),
                self.meta[:, s_:s_ + 1], self.Sin[:, :, :].rearrange("p h v -> p (h v)"), ALU.mult, ALU.add),
                reads=["Srun", "meta", "Sin"], writes=["Sin"])
            for h in range(4):
                S.add("dve", lambda e, s_=s_, h=h: e.scalar_tensor_tensor(
                    self.Srun[:, h, :], self.Srun[:, h, :], self.Dg[:, s_, h:h + 1], self.GL[:, h, :], ALU.mult, ALU.add),
                    reads=["Srun", "Dg", "GL"], writes=["Srun"])
        for t in range(8):
            for h in range(4):
                S.add("dve", lambda e, t=t, h=h: e.scalar_tensor_tensor(
                    self.delta[:, h, t, :], self.Sin[:, h, :], self.cp[:, h, t:t + 1], self.delta[:, h, t, :], ALU.mult, ALU.add),
                    reads=["Sin", "cp", ("delta", h, t)], writes=[("delta", h, t)])
        S.add("dve", lambda e: e.tensor_copy(self.sinb[:, :, :], self.Sin[:, :, :]), reads=["Sin"], writes=["sinb"])

    def mixer_AB(self, P, l, mix):
        S, d = self.S, self.d
        NT, NG = P["NT"], P["NG"]
        qscale = 1.0 if mix == 0 else 0.125
        kscale = 0.125 if mix == 0 else 1.0
        ropeP = P if mix == 0 else dict(P, rope=False)
        wv, wk = self.wload(d[f"wT{l}"][2 * mix])
        self.projT(P, wv, wk, lambda t, b: S.add(
            "act", lambda e: e.activation(self.vtok[:, t, :], self.ps[:, b, :], AF.Identity), reads=[("ps", b)], writes=[("vtok", t)]))
        wv, wk = self.wload(d[f"wT{l}"][2 * mix + 1])
        self.projT(P, wv, wk, lambda t, b: S.add(
            "act", lambda e: e.activation(self.gtok[:, t, :], self.ps[:, b, :], AF.Silu),
            reads=[("ps", b)], writes=[("gtok", t)]))
        ksub = int(os.environ.get("KSUB", "99"))
        if ksub < 1:
            return
        if mix == 1:
            S.add("dve", lambda e: e.memset(self.uT[0:33, 0:P["T"]], 1.0), writes=["uT"])
            for g in range(NG):
                b = self.bank()
                for kc in range(8):
                    S.add("pe", lambda e, b=b, kc=kc, g=g: e.matmul(
                        self.ps[0:32, b, :], self.wa1[:, l, kc, :], self.hT[:, kc, g * 512:(g + 1) * 512],
                        start=(kc == 0), stop=(kc == 7)), reads=["wa1", ("hT", g)], writes=[("ps", b)])
                S.add("act", lambda e, b=b, g=g: e.activation(self.uT[0:32, g * 512:(g + 1) * 512], self.ps[0:32, b, :], AF.Identity),
                      reads=[("ps", b)], writes=["uT"])
        wv, wk = self.wload(d[f"wF{l}"][2 * mix])
        for h in range(4):
            self.projF(P, wv, wk, h, lambda g, b, h=h: self.evac_rope(ropeP, self.QT, "QT", h, g, b, qscale))
        wv, wk = self.wload(d[f"wF{l}"][2 * mix + 1])
        for h in range(4):
            self.projF(P, wv, wk, h, lambda g, b, h=h: self.evac_rope(ropeP, self.KT, "KT", h, g, b, kscale))
        if ropeP["rope"]:
            self.barrier()
        if ksub < 2:
            return
        if mix == 0:
            S.add("dve", lambda e: e.tensor_copy(
                self.latok[:, :].rearrange("p (h d k) -> p h d k", h=4, d=2),
                self.retsp[:, l, :].rearrange("p (d h o) -> p h d o", d=2, o=1).broadcast_to([128, 4, 2, 64])),
                reads=["retsp"], writes=["latok"])
            self.decay_tiles(-1.0, None)
            for t in range(NT):
                S.add("dve", lambda e, t=t: e.tensor_copy(self.dcol[0:64, :, t:t + 1], self.EqT[0:64, :, 127:128]),
                      reads=["EqT"], writes=["dcol"])
                S.add("dve", lambda e, t=t: e.tensor_copy(self.dcol[64:128, :, t:t + 1], self.EqT[64:128, :, 0:1]),
                      reads=["EqT"], writes=["dcol"])
        if ksub < 3:
            return
        it = 0
        for t in range(NT):
            tc_ = slice(t * 128, (t + 1) * 128)
            if mix == 1:
                b = self.bank()
                S.add("pe", lambda e, b=b, tc_=tc_: e.matmul(self.ps[:, b, :], self.uT[0:33, tc_], self.wa2[0:33, l, :],
                                                            start=True, stop=True),
                      reads=["uT", "wa2"], writes=[("ps", b)])
                S.add("act", lambda e, b=b: e.activation(self.latok[:, :], self.ps[:, b, :], AF.Exp, scale=-1.0),
                      reads=[("ps", b)], writes=["latok"])
                S.add("act", lambda e: e.activation(self.latok[:, :], self.latok[:, :], AF.Ln, bias=1.0),
                      reads=["latok"], writes=["latok"])
                self.decay_tiles(-1.0 / 16.0, t)
            for h in range(4):
                i2 = it % 2
                it += 1
                S.add("dve", lambda e, h=h, t=t, tc_=tc_: e.tensor_tensor(
                    self.qd[:, h, t, :], self.QT[:, h, tc_], self.EqT[:, h, :], ALU.mult),
                    reads=[("QT", h), "EqT"], writes=[("qd", h, t)])
                S.add("dve", lambda e, h=h, tc_=tc_, i2=i2: e.tensor_tensor(
                    self.ksb[:, i2, :], self.KT[:, h, tc_], self.EkT[:, h, :], ALU.mult),
                    reads=[("KT", h), "EkT"], writes=[("ksb", i2)])
                bt = self.bank()
                S.add("pe", lambda e, h=h, tc_=tc_, bt=bt: e.transpose(
                    self.psb(bt)[:, 0:128], self.KT[:, h, tc_], self.cstb[:, 4, :]),
                    reads=[("KT", h), "cstb"], writes=[("ps", bt)])
                S.add("dve", lambda e, h=h, bt=bt, i2=i2: e.tensor_tensor(
                    self.kdtok[:, i2, :], self.psb(bt)[:, 0:128], self.Ekd[:, h, :], ALU.mult),
                    reads=[("ps", bt), "Ekd"], writes=[("kdtok", i2)])
                bd = self.bank()
                S.add("pe", lambda e, h=h, t=t, bd=bd, i2=i2: e.matmul(
                    self.ps[:, bd, 0:128], self.kdtok[:, i2, :], self.vtok[:, t, h * 128:(h + 1) * 128], start=True, stop=True),
                    reads=[("kdtok", i2), ("vtok", t)], writes=[("ps", bd)])
                S.add("act", lambda e, h=h, t=t, bd=bd: e.activation(self.delta[:, h, t, 0:128], self.ps[:, bd, 0:128], AF.Identity),
                      reads=[("ps", bd)], writes=[("delta", h, t)])
                b1, b2 = self.bank(), self.bank()
                S.add("pe", lambda e, h=h, t=t, b1=b1, i2=i2: e.matmul(
                    self.ps[:, b1, 0:128], self.ksb[0:64, i2, :], self.qd[0:64, h, t, :], start=True, stop=True),
                    reads=[("ksb", i2), ("qd", h, t)], writes=[("ps", b1)])
                S.add("pe", lambda e, h=h, t=t, b2=b2, i2=i2: e.matmul(
                    self.ps[:, b2, 0:128], self.ksb[64:128, i2, :], self.qd[64:128, h, t, :], start=True, stop=True),
                    reads=[("ksb", i2), ("qd", h, t)], writes=[("ps", b2)])
                S.add("dve", lambda e, b1=b1, i2=i2: e.tensor_tensor(
                    self.tsm[:, i2, :], self.ps[:, b1, 0:128], self.cst[:, 0, :], ALU.mult),
                    reads=[("ps", b1), "cst"], writes=[("tsm", i2)])
                S.add("dve", lambda e, b2=b2: e.tensor_tensor(
                    self.ps[:, b2, 128:256], self.ps[:, b2, 0:128], self.cst[:, 1, :], ALU.mult),
                    reads=[("ps", b2), "cst"], writes=[("ps", b2)])
                S.add("dve", lambda e, h=h, t=t, b2=b2, i2=i2: e.tensor_tensor(
                    self.Pm[:, h, t, :], self.ps[:, b2, 128:256], self.tsm[:, i2, :], ALU.add),
                    reads=[("ps", b2), ("tsm", i2)], writes=[("Pm", h, t)])
        if ksub < 4:
            return
        self.barrier()
        out_name = "nsr" if mix == 0 else "nsg"
        for si, (t0, n) in enumerate(P["seqs"]):
            for c_ in range(1, n):
                tf = t0 + c_
                tb = t0 + n - 1 - c_
                for h in range(4 if os.environ.get("KX", "0") != "5" else 0):
                    S.add("dve", lambda e, h=h, tf=tf: e.scalar_tensor_tensor(
                        self.delta[0:64, h, tf, 0:128], self.delta[0:64, h, tf - 1, 0:128], self.dcol[0:64, h, tf:tf + 1],
                        self.delta[0:64, h, tf, 0:128], ALU.mult, ALU.add),
                        reads=[("delta", h, tf - 1), "dcol", ("delta", h, tf)], writes=[("delta", h, tf)])
                    S.add("dve", lambda e, h=h, tb=tb: e.scalar_tensor_tensor(
                        self.delta[64:128, h, tb, 0:128], self.delta[64:128, h, tb + 1, 0:128], self.dcol[64:128, h, tb:tb + 1],
                        self.delta[64:128, h, tb, 0:128], ALU.mult, ALU.add),
                        reads=[("delta", h, tb + 1), "dcol", ("delta", h, tb)], writes=[("delta", h, tb)])
            if P.get("sample"):
                self.compose_states(P, l, mix)
            if P["states_out"] and not os.environ.get("KNOST"):
                for dr, tt in ((0, t0 + n - 1), (1, t0)):
                    op = S.add("sp", lambda e, si=si, dr=dr, tt=tt: e.dma_start(
                        out=self.o[out_name][si, l, dr].rearrange("h k v -> k h v"),
                        in_=self.delta[dr * 64:(dr + 1) * 64, :, tt, 0:128]),
                        reads=[("delta", h_, tt) for h_ in range(4)], slot=self.s_st[si % 2])
                    self.out_ops.append(op)
            if os.environ.get("KX") == "7":
                ba = self.bank()
                S.add("pe", lambda e, ba=ba: e.matmul(self.ps[:, ba, 0:128], self.kdtok[:, 0, :], self.vtok[:, 0, 0:128],
                                                      start=True, stop=True),
                      reads=[("kdtok", 0), ("vtok", 0)], writes=[("ps", ba)])
                continue
            if os.environ.get("KX") == "8":
                ba = self.bank()
                S.add("pe", lambda e, ba=ba: e.matmul(self.ps[:, ba, 0:128], self.cstb[:, 6, :], self.vtok[:, 0, 0:128],
                                                      start=True, stop=True),
                      reads=["cstb", ("vtok", 0)], writes=[("ps", ba)])
                continue
            for c_ in range(n):
                t = t0 + c_
                for h in range(4):
                    kv = os.environ.get("KV", "0")
                    if kv == "1":
                        S.add("dve", lambda e, t=t, h=h: e.tensor_tensor(self.dbf[:, h, t, :], self.cst[:, 0, :], self.cst[:, 8, :], ALU.mult),
                              reads=["cst"], writes=[("QT", h)])
                    elif kv == "3":
                        S.add("dve", lambda e, t=t, h=h: e.tensor_tensor(self.dbf[:, h, t, :], self.cst[:, 0, :], self.cst[:, 8, :], ALU.mult),
                              reads=["cst"], writes=[("tsm", 0)])
                    elif kv == "4":
                        S.add("dve", lambda e, t=t, h=h: e.tensor_tensor(self.tsm[:, 0, :], self.cst[:, 0, :], self.cst[:, 8, :], ALU.mult),
                              reads=["cst"], writes=[("QT", h)])
                    elif kv == "2":
                        S.add("dve", lambda e, t=t, h=h: e.tensor_tensor(self.tsm[:, 0, :], self.delta[:, h, t, :], self.cst[:, 8, :], ALU.mult),
                              reads=[("delta", h, t), "cst"], writes=[("tsm", 0)])
                    else:
                        S.add("dve", lambda e, t=t, h=h: e.tensor_tensor(self.dbf[:, h, t, :], self.delta[:, h, t, :], self.cst[:, 8, :], ALU.mult),
                              reads=[("delta", h, t), "cst"], writes=[("dbf", h)])
            if os.environ.get("KX") == "9":
                continue
            if os.environ.get("KX") == "10":
                ba = self.bank()
                S.add("pe", lambda e, ba=ba: e.matmul(self.ps[:, ba, 0:128], self.cstb[:, 6, :], self.dbf[:, 0, t0, :],
                                                      start=True, stop=True),
                      reads=["cstb", ("QT", 0)], writes=[("ps", ba)])
                continue
            for c_ in range(n):
                t = t0 + c_
                ba = self.bank()
                for h in range(4 if os.environ.get("KX", "0") != "4" else 0):
                    hasf, hasb = c_ > 0, c_ < n - 1
                    if P.get("sample") and not hasf:
                        S.add("pe", lambda e, h=h, ba=ba: e.matmul(
                            self.ps[:, ba, h * 128:(h + 1) * 128], self.cstb[:, 6, :], self.sinb[:, h, :],
                            start=True, stop=False), reads=["sinb", "cstb"], writes=[("ps", ba)])
                        hasf = True
                    if P.get("sample") and not hasb:
                        S.add("pe", lambda e, h=h, t=t, ba=ba: e.matmul(
                            self.ps[:, ba, h * 128:(h + 1) * 128], self.cstb[:, 6, :], self.dbf[:, h, t - 1, :],
                            start=True, stop=False), reads=[("dbf", h), "cstb"], writes=[("ps", ba)])
                        S.add("pe", lambda e, h=h, ba=ba: e.matmul(
                            self.ps[:, ba, h * 128:(h + 1) * 128], self.cstb[:, 7, :], self.sinb[:, h, :],
                            start=False, stop=True), reads=["sinb", "cstb"], writes=[("ps", ba)])
                        continue
                    if P.get("sample") and c_ == 0:
                        S.add("pe", lambda e, h=h, t=t, ba=ba: e.matmul(
                            self.ps[:, ba, h * 128:(h + 1) * 128], self.cstb[:, 7, :], self.dbf[:, h, t + 1, :],
                            start=False, stop=True), reads=[("dbf", h), "cstb"], writes=[("ps", ba)])
                        continue
                    if hasf:
                        S.add("pe", lambda e, h=h, t=t, ba=ba, hasb=hasb: e.matmul(
                            self.ps[:, ba, h * 128:(h + 1) * 128], self.cstb[:, 6, :], self.dbf[:, h, t - 1, :],
                            start=True, stop=not hasb), reads=[("dbf", h), "cstb"], writes=[("ps", ba)])
                    if hasb:
                        S.add("pe", lambda e, h=h, t=t, ba=ba, hasf=hasf: e.matmul(
                            self.ps[:, ba, h * 128:(h + 1) * 128], self.cstb[:, 7, :], self.dbf[:, h, t + 1, :],
                            start=not hasf, stop=True), reads=[("dbf", h), "cstb"], writes=[("ps", ba)])
                kx = os.environ.get("KX", "0")
                if kx == "0":
                    S.add("act", lambda e, t=t, ba=ba: e.activation(
                        self.sstk[:, :, t, :], self.ps[:, ba, :].rearrange("p (h v) -> p h v", h=4), AF.Identity),
                        reads=[("ps", ba)], writes=[("KT", h_) for h_ in range(4)])
                elif kx == "2":
                    S.add("act", lambda e, t=t, ba=ba: e.activation(
                        self.Pm[:, :, t, :], self.ps[:, ba, :].rearrange("p (h v) -> p h v", h=4), AF.Identity),
                        reads=[("ps", ba)], writes=[("Pm", h_, t) for h_ in range(4)])
                elif kx == "3":
                    S.add("dve", lambda e, t=t, ba=ba: e.tensor_copy(
                        self.sstk[:, :, t, :], self.ps[:, ba, :].rearrange("p (h v) -> p h v", h=4)),
                        reads=[("ps", ba)], writes=[("KT", h_) for h_ in range(4)])
        if ksub < 5 or os.environ.get("KNOP2"):
            return
        if os.environ.get("KPAD"):
            S.add("dve", lambda e: e.memset(self.tsm[:, 0, :], 0.0), writes=[("tsm", 0)])
        if not os.environ.get("KNOBAR"):
            self.barrier()
        for t in range(int(os.environ.get("KP2T", NT))):
            b = self.bank()
            for h in range(4):
                S.add("pe", lambda e, h=h, t=t, b=b: e.matmul(
                    self.ps[:, b, h * 128:(h + 1) * 128], self.qd[:, h, t, :], self.sstk[:, h, t, :],
                    start=True, stop=False), reads=[("qd", h, t), ("KT", h), ("Pm", h, t), ("vtok", t)], writes=[("ps", b)])
                S.add("pe", lambda e, h=h, t=t, b=b: e.matmul(
                    self.ps[:, b, h * 128:(h + 1) * 128], self.Pm[:, h, t, :], self.vtok[:, t, h * 128:(h + 1) * 128],
                    start=False, stop=True), reads=[("Pm", h, t), ("vtok", t)], writes=[("ps", b)])
            kp2 = int(os.environ.get("KP2", "3"))
            if kp2 & 1:
              S.add("act", lambda e, t=t, b=b: e.activation(self.otokAB[:, t, :], self.ps[:, b, :], AF.Identity),
                  reads=[("ps", b)], writes=[("QT", t // 2) if not os.environ.get("KOT") else ("otokAB", t)])
            for h in range(4 if (kp2 & 2) else 0):
                if mix == 0:
                    S.add("dve", lambda e, h=h, b=b: e.bn_stats(self.st6[:, :], self.ps[:, b, h * 128:(h + 1) * 128]),
                          reads=[("ps", b)], writes=["st6"])
                    S.add("dve", lambda e, h=h, t=t: e.bn_aggr(self.stat[:, t, h, :], self.st6[:, :]),
                          reads=["st6"], writes=["stat"])
                else:
                    S.add("act", lambda e, h=h, t=t, b=b: e.activation(
                        self.tsm[:, 0, :], self.ps[:, b, h * 128:(h + 1) * 128], AF.Square,
                        accum_out=self.stat[:, t, h, 1:2]), reads=[("ps", b)], writes=["stat", ("tsm", 0)])
        if ksub < 6:
            return
        self.norm_out(P, l, mix, self.otokAB, lambda t: ("QT", t // 2))

    def norm_out(self, P, l, br, otok, okey):
        S = self.S
        NT = P["NT"]
        if br == 0:
            sc, bias = 1.0, EPS
        else:
            sc, bias = 1.0 / 128.0, EPS
        S.add("act", lambda e: e.activation(self.rstd[:, 0:NT, :], self.stat[:, 0:NT, :, 1], AF.Sqrt, bias=bias, scale=sc),
              reads=["stat"], writes=["rstd"])
        S.add("dve", lambda e: e.reciprocal(self.rstd[:, 0:NT, :], self.rstd[:, 0:NT, :]), reads=["rstd"], writes=["rstd"])
        for t in range(NT):
            i2 = t % 2
            if br == 1:
                S.add("dve", lambda e, t=t: e.tensor_tensor(
                    self.ggt[:].rearrange("p (h v) -> p h v", h=4), self.gtok[:, t, :].rearrange("p (h v) -> p h v", h=4),
                    self.bcast[:, l, 0:128].rearrange("p (o v) -> p o v", o=1).broadcast_to([128, 4, 128]), ALU.mult),
                    reads=[("gtok", t), "bcast"], writes=["ggt"])
            for h in range(4):
                hs = slice(h * 128, (h + 1) * 128)
                if br == 0:
                    S.add("dve", lambda e, t=t, hs=hs, h=h, i2=i2: e.scalar_tensor_tensor(
                        self.tsm[:, 1, :], otok[:, t, hs], self.stat[:, t, h, 0:1], self.gtok[:, t, hs],
                        ALU.subtract, ALU.mult), reads=[okey(t), "stat", ("gtok", t)], writes=[("tsm", 1)])
                    S.add("dve", lambda e, t=t, hs=hs, h=h, i2=i2: e.tensor_scalar(
                        self.ytok[:, i2, hs], self.tsm[:, 1, :], self.rstd[:, t, h:h + 1], None, ALU.mult),
                        reads=[("tsm", 1), "rstd"], writes=[("ytok", i2)])
                elif br == 1:
                    S.add("dve", lambda e, t=t, hs=hs, h=h, i2=i2: e.scalar_tensor_tensor(
                        self.ytok[:, i2, hs], otok[:, t, hs], self.rstd[:, t, h:h + 1], self.ggt[:, hs],
                        ALU.mult, ALU.mult), reads=[okey(t), "rstd", "ggt"], writes=[("ytok", i2)])
                else:
                    S.add("dve", lambda e, t=t, hs=hs, h=h, i2=i2: e.scalar_tensor_tensor(
                        self.ytok[:, i2, hs], otok[:, t, hs], self.rstd[:, t, h:h + 1], self.subg[:, :],
                        ALU.mult, ALU.mult), reads=[okey(t), "rstd", "subg"], writes=[("ytok", i2)])
            bt = self.bank()
            for h in range(4):
                S.add("pe", lambda e, h=h, bt=bt, i2=i2: e.transpose(
                    self.psb(bt)[:, h * 128:(h + 1) * 128], self.ytok[:, i2, h * 128:(h + 1) * 128], self.cstb[:, 4, :]),
                    reads=[("ytok", i2), "cstb"], writes=[("ps", bt)])
            S.add("act", lambda e, t=t, bt=bt: e.activation(
                self.yT[:, br, :, t * 128:(t + 1) * 128], self.psb(bt)[:, 0:512].rearrange("p (h c) -> p h c", h=4), AF.Identity),
                reads=[("ps", bt)], writes=[("yT", br, t // 4)])

    def mixer_C(self, P, l):
        S, d = self.S, self.d
        NT = P["NT"]
        lam_init = 0.8 - 0.6 * math.exp(-0.3 * l)
        S.add("dve", lambda e: e.tensor_scalar(self.subg[:, :], self.bcast[:, l, 128:256], 1.0 - lam_init, None, ALU.mult),
              reads=["bcast"], writes=["subg"])
        S.add("dve", lambda e: e.memset(self.vaug[:, :, :, 128:130], 1.0), writes=["vaug1"])
        wv, wk = self.wload(d[f"wT{l}"][4])

        def ev_v(t, b):
            S.add("act", lambda e: e.activation(self.vaug[:, t, :, 0:128], self.ps[:, b, :].rearrange("p (h v) -> p h v", h=4), AF.Identity),
                  reads=[("ps", b)], writes=[("vaug", t)])
            if P["kv_out"]:
                sl = t % 2
                S.add("dve", lambda e: e.tensor_copy(self.cstage[:, sl, :], self.ps[:, b, :]),
                      reads=[("ps", b)], writes=[("cstage", sl)])
                si, tt = t // 2, t % 2
                op = S.add("sp", lambda e: e.dma_start(
                    out=self.o["ndv"][si, l, :, tt * 128:(tt + 1) * 128, :].rearrange("h t v -> t h v"),
                    in_=self.cstage[:, sl, :].rearrange("p (h v) -> p h v", h=4)),
                    reads=[("cstage", sl)], slot=self.s_st[sl])
                self.out_ops.append(op)
        self.projT(P, wv, wk, ev_v)
        if P["kv_out"]:
            wv, wk = self.wload(d[f"wT{l}"][5])

            def ev_k(t, b):
                sl = t % 2
                S.add("dve", lambda e: e.tensor_copy(self.cstage[:, sl, :], self.ps[:, b, :]),
                      reads=[("ps", b)], writes=[("cstage", sl)])
                si, tt = t // 2, t % 2
                op = S.add("sp", lambda e: e.dma_start(
                    out=self.o["ndk"][si, l, :, tt * 128:(tt + 1) * 128, :].rearrange("g t k -> t g k"),
                    in_=self.cstage[:, sl, :].rearrange("p (g k) -> p g k", g=8)),
                    reads=[("cstage", sl)], slot=self.s_st[sl])
                self.out_ops.append(op)
            self.projT(P, wv, wk, ev_k)
        wv, wk = self.wload(d[f"wF{l}"][4])
        for h in range(4):
            self.projF(P, wv, wk, h, lambda g, b, h=h: self.evac_rope(P, self.QT, "QT", h, g, b, 1.0))
        wv, wk = self.wload(d[f"wF{l}"][5])
        for h in range(4):
            self.projF(P, wv, wk, h, lambda g, b, h=h: self.evac_rope(P, self.KT, "KT", h, g, b, 1.0))
        self.barrier(pool=bool(P.get("sample")))
        if P.get("sample"):
            self.attn_sample(P, l)
            self.barrier()
            self.norm_out(P, l, 2, self.otokC, lambda t: ("otokC", t))
            return
        self.bank_pool = [4, 5, 6, 7]
        for (q0, nq, keyt) in P["attn"]:
            qc = slice(q0 * 128, q0 * 128 + 256)
            for hc in range(4):
                for ji, kt in enumerate(keyt):
                    kc_ = slice(kt * 128, (kt + 1) * 128)
                    bs = [self.bank(), self.bank()]
                    for sub in range(2):
                        lo, hi = sub * 64, sub * 64 + 64
                        S.add("pe", lambda e, sub=sub, lo=lo, hi=hi, bs=bs, kc_=kc_, hc=hc: e.matmul(
                            self.ps[:, bs[sub], 0:256], self.KT[lo:hi, hc, kc_], self.QT[lo:hi, hc, qc],
                            start=True, stop=True), reads=[("KT", hc), ("QT", hc)], writes=[("ps", bs[sub])])
                    i2 = ji % 2
                    for sub in range(2):
                        S.add("act", lambda e, sub=sub, bs=bs, i2=i2: e.activation(
                            self.ET[:, i2, sub * 256:(sub + 1) * 256], self.ps[:, bs[sub], 0:256], AF.Exp, scale=0.125),
                            reads=[("ps", bs[sub])], writes=[("ET", i2, sub)])
                    for qt in range(2):
                        for sub in range(2):
                            acc = qt * 2 + sub
                            S.add("pe", lambda e, qt=qt, sub=sub, acc=acc, i2=i2, kt=kt, hc=hc, ji=ji: e.matmul(
                                self.ps[:, acc, 0:130], self.ET[:, i2, sub * 256 + qt * 128: sub * 256 + (qt + 1) * 128],
                                self.vaug[:, kt, hc, :], start=(ji == 0), stop=(ji == len(keyt) - 1)),
                                reads=[("ET", i2, sub), ("vaug", kt), "vaug1"], writes=[("ps", acc)])
                for qt in range(2):
                    t = q0 + qt
                    a1, a2 = qt * 2, qt * 2 + 1
                    S.add("dve", lambda e, a1=a1: e.reciprocal(self.sm8[:, 0:1], self.ps[:, a1, 128:129]),
                          reads=[("ps", a1)], writes=["sm8"])
                    S.add("dve", lambda e, a2=a2: e.reciprocal(self.sm8[:, 1:2], self.ps[:, a2, 128:129]),
                          reads=[("ps", a2)], writes=["sm8"])
                    S.add("dve", lambda e: e.tensor_tensor(self.sm8[:, 1:2], self.sm8[:, 1:2], self.lam[:, l:l + 1], ALU.mult),
                          reads=["sm8", "lam"], writes=["sm8"])
                    S.add("dve", lambda e, a2=a2: e.tensor_scalar(self.tsm[:, 0, :], self.ps[:, a2, 0:128], self.sm8[:, 1:2], None, ALU.mult),
                          reads=[("ps", a2), "sm8"], writes=[("tsm", 0)])
                    S.add("dve", lambda e, a1=a1, t=t, hc=hc: e.scalar_tensor_tensor(
                        self.otokC[:, t, hc * 128:(hc + 1) * 128], self.ps[:, a1, 0:128], self.sm8[:, 0:1], self.tsm[:, 0, :],
                        ALU.mult, ALU.add), reads=[("ps", a1), "sm8", ("tsm", 0)], writes=[("otokC", t)])
                    S.add("act", lambda e, t=t, hc=hc: e.activation(
                        self.tsm[:, 1, :], self.otokC[:, t, hc * 128:(hc + 1) * 128], AF.Square,
                        accum_out=self.stat[:, t, hc, 1:2]), reads=[("otokC", t)], writes=["stat", ("tsm", 1)])
        self.bank_pool = list(range(8))
        self.barrier()
        self.norm_out(P, l, 2, self.otokC, lambda t: ("otokC", t))

    def attn_sample(self, P, l):
        S, d, x_ = self.S, self.d, self.x_
        g1 = self.xgroup("kvw")
        S.add("sp", lambda e: e.dma_start(out=x_["kx_in"].rearrange("p (h t) -> p h t", h=4), in_=self.KT[:, :, :]),
              reads=[("KT", h) for h in range(4)], writes=["kx_in"], slot=g1)
        S.add("sp", lambda e: e.dma_start(out=x_["vx_in"].rearrange("p (t h v) -> p t h v", t=8, h=4), in_=self.vaug[:, :, :, 0:128]),
              reads=[("vaug", t) for t in range(8)], writes=["vx_in"], slot=g1)
        self.gather(x_["kx_in"], x_["kx_out"], "kx_in", "kx_out")
        self.gather(x_["vx_in"], x_["vx_out"], "vx_in", "vx_out")
        for kt in range(2):
            gk = self.xgroup("ck")
            S.add("sp", lambda e, kt=kt: e.dma_start(out=self.kstg[:, :], in_=d["ckc"][l, kt]), writes=["kstg"], slot=gk)
            b = self.bank()
            for hc in range(4):
                S.add("pe", lambda e, hc=hc, b=b: e.transpose(self.ps[:, b, hc * 128:(hc + 1) * 128],
                                                              self.kstg[:, hc * 128:(hc + 1) * 128], self.cst[:, 4, :]),
                      reads=["kstg", "cst"], writes=[("ps", b)])
            S.add("act", lambda e, kt=kt, b=b: e.activation(self.kctx[:, :, kt * 128:(kt + 1) * 128],
                                                            self.ps[:, b, :].rearrange("p (h c) -> p h c", h=4), AF.Identity),
                  reads=[("ps", b)], writes=["kctx"])
        gv = self.xgroup("cv")
        S.add("dve", lambda e: e.memset(self.vctx[:, :, :, 128:130], 1.0), writes=["vctx1"])
        for kt in range(2):
            S.add("pool", lambda e, kt=kt: e.dma_start(out=self.vctx[:, kt, :, 0:128], in_=d["cvc"][l, kt]),
                  writes=["vctx"], slot=gv)
        self.barrier()
        S.add("dve", lambda e: e.memset(self.VH[:, :, 128:130], 1.0), reads=["vx_in"], writes=["VH1"])
        self.bank_pool = [4, 5, 6, 7]
        for hc in range(4):
            gh = self.xgroup("kvh")
            S.add("sp", lambda e, hc=hc: e.dma_start(
                out=self.KH[:, :].rearrange("p (r c) -> p r c", r=4),
                in_=x_["kx_out"].rearrange("(r p) c -> p r c", p=128)[:, :, hc * 1024:(hc + 1) * 1024]),
                reads=["kx_out"], writes=["KH"], slot=gh)
            for r in range(4):
                S.add("sp", lambda e, hc=hc, r=r: e.dma_start(
                    out=self.VH[:, r * 8:(r + 1) * 8, 0:128],
                    in_=x_["vx_out"][r * 128:(r + 1) * 128, :].rearrange("p (t h v) -> p t h v", t=8, h=4)[:, :, hc, :]),
                    reads=["vx_out"], writes=["VH"], slot=gh)
            keyt = [("c", 0), ("c", 1)] + [("g", i) for i in range(32)]
            for qb in range(4):
                q0 = qb * 2
                qc = slice(q0 * 128, q0 * 128 + 256)
                def qk_exp(ji, kind, kt):
                    kc_ = slice(kt * 128, (kt + 1) * 128)
                    bs = [self.bank(), self.bank()]
                    for sub in range(2):
                        lo, hi = sub * 64, sub * 64 + 64
                        ksrc = self.kctx[lo:hi, hc, kc_] if kind == "c" else self.KH[lo:hi, kc_]
                        S.add("pe", lambda e, sub=sub, lo=lo, hi=hi, bs=bs, ksrc=ksrc: e.matmul(
                            self.ps[:, bs[sub], 0:256], ksrc, self.QT[lo:hi, hc, qc], start=True, stop=True),
                            reads=["kctx" if kind == "c" else "KH", ("QT", hc)], writes=[("ps", bs[sub])])
                    i2 = ji % 2
                    for sub in range(2):
                        S.add("act", lambda e, sub=sub, bs=bs, i2=i2: e.activation(
                            self.ET[:, i2, sub * 256:(sub + 1) * 256], self.ps[:, bs[sub], 0:256], AF.Exp, scale=0.125),
                            reads=[("ps", bs[sub])], writes=[("ET", i2, sub)])

                def av(ji, kind, kt):
                    i2 = ji % 2
                    vsrc = self.vctx[:, kt, hc, :] if kind == "c" else self.VH[:, kt, :]
                    vkeys = ["vctx", "vctx1"] if kind == "c" else ["VH", "VH1"]
                    for qt in range(2):
                        for sub in range(2):
                            acc = qt * 2 + sub
                            S.add("pe", lambda e, qt=qt, sub=sub, acc=acc, i2=i2, vsrc=vsrc, ji=ji: e.matmul(
                                self.ps[:, acc, 0:130], self.ET[:, i2, sub * 256 + qt * 128: sub * 256 + (qt + 1) * 128],
                                vsrc, start=(ji == 0), stop=(ji == len(keyt) - 1)),
                                reads=[("ET", i2, sub)] + vkeys, writes=[("ps", acc)])

                prev = None
                for ji, (kind, kt) in enumerate(keyt):
                    qk_exp(ji, kind, kt)
                    if prev is not None:
                        av(*prev)
                    prev = (ji, kind, kt)
                av(*prev)
                self.attn_finalize(l, hc, q0)
        self.bank_pool = list(range(8))

    def attn_finalize(self, l, hc, q0):
        S = self.S
        for qt in range(2):
            t = q0 + qt
            a1, a2 = qt * 2, qt * 2 + 1
            S.add("dve", lambda e, a1=a1: e.reciprocal(self.sm8[:, 0:1], self.ps[:, a1, 128:129]),
                  reads=[("ps", a1)], writes=["sm8"])
            S.add("dve", lambda e, a2=a2: e.reciprocal(self.sm8[:, 1:2], self.ps[:, a2, 128:129]),
                  reads=[("ps", a2)], writes=["sm8"])
            S.add("dve", lambda e: e.tensor_tensor(self.sm8[:, 1:2], self.sm8[:, 1:2], self.lam[:, l:l + 1], ALU.mult),
                  reads=["sm8", "lam"], writes=["sm8"])
            S.add("dve", lambda e, a2=a2: e.tensor_scalar(self.tsm[:, 0, :], self.ps[:, a2, 0:128], self.sm8[:, 1:2], None, ALU.mult),
                  reads=[("ps", a2), "sm8"], writes=[("tsm", 0)])
            S.add("dve", lambda e, a1=a1, t=t, hc=hc: e.scalar_tensor_tensor(
                self.otokC[:, t, hc * 128:(hc + 1) * 128], self.ps[:, a1, 0:128], self.sm8[:, 0:1], self.tsm[:, 0, :],
                ALU.mult, ALU.add), reads=[("ps", a1), "sm8", ("tsm", 0)], writes=[("otokC", t)])
            S.add("act", lambda e, t=t, hc=hc: e.activation(
                self.tsm[:, 1, :], self.otokC[:, t, hc * 128:(hc + 1) * 128], AF.Square,
                accum_out=self.stat[:, t, hc, 1:2]), reads=[("otokC", t)], writes=["stat", ("tsm", 1)])

    def layer_norm(self, P, l, which, g, dT, sqT, tmpA, tmpB):
        S = self.S
        gc = slice(g * 512, (g + 1) * 512)
        xk = ("xT", g)
        for kc in range(8):
            S.add("act", lambda e, kc=kc: e.activation(sqT[:, kc, :], self.xT[:, kc, gc], AF.Identity), reads=[xk], writes=[("sqT", kc)])
        bm = self.bank()
        for kc in range(8):
            S.add("pe", lambda e, kc=kc: e.matmul(self.ps[:, bm, :], self.onesb[:, :], sqT[:, kc, :],
                                                  start=(kc == 0), stop=(kc == 7)),
                  reads=[("sqT", kc), "onesb"], writes=[("ps", bm)])
        for kc in range(8):
            S.add("dve", lambda e, kc=kc: e.tensor_tensor(dT[:, kc, :], self.xT[:, kc, gc], self.ps[:, bm, :], ALU.subtract),
                  reads=[xk, ("ps", bm)], writes=[("dT", kc)])
            S.add("act", lambda e, kc=kc: e.activation(sqT[:, kc, :], dT[:, kc, :], AF.Square),
                  reads=[("dT", kc)], writes=[("sqT", kc)])
        bv = self.bank()
        for kc in range(8):
            S.add("pe", lambda e, kc=kc: e.matmul(self.ps[:, bv, :], self.onesb[:, :], sqT[:, kc, :],
                                                  start=(kc == 0), stop=(kc == 7)),
                  reads=[("sqT", kc), "onesb"], writes=[("ps", bv)])
        S.add("act", lambda e: e.activation(tmpA[:], self.ps[:, bv, :], AF.Sqrt, bias=EPSA), reads=[("ps", bv)], writes=["tmpA"])
        S.add("dve", lambda e: e.reciprocal(tmpB[:], tmpA[:]), reads=["tmpA"], writes=["tmpB"])
        go = 48 + which * 8
        bo = 64 + which * 8
        for kc in range(8):
            S.add("dve", lambda e, kc=kc: e.tensor_tensor(dT[:, kc, :], dT[:, kc, :], tmpB[:], ALU.mult),
                  reads=[("dT", kc), "tmpB"], writes=[("dT", kc)])
            S.add("act", lambda e, kc=kc: e.activation(self.xT[:, kc, gc], dT[:, kc, :], AF.Identity,
                                                       scale=self.smallp[:, l, go + kc:go + kc + 1],
                                                       bias=self.smallp[:, l, bo + kc:bo + kc + 1]),
                  reads=[("dT", kc), "smallp"], writes=[xk])

    def merge_phase(self, P, l):
        S, d = self.S, self.d
        c = P["cond"]
        for oc in range(8):
            wg, kg = self.wload(d[f"wM{l}"][2 * oc], 3072)
            wb, kb = self.wload(d[f"wM{l}"][2 * oc + 1], 1536)
            for g in range(P["NG"]):
                gc = slice(g * 512, (g + 1) * 512)
                for br in range(3):
                    b = self.bank()
                    for kc in range(8):
                        S.add("pe", lambda e, b=b, kc=kc, br=br: e.matmul(
                            self.ps[:, b, :], wg[:, br * 1024 + kc * 128: br * 1024 + (kc + 1) * 128], self.hT[:, kc, gc],
                            start=(kc == 0), stop=(kc == 7)), reads=[kg, ("hT", g)], writes=[("ps", b)])
                    S.add("act", lambda e, b=b, br=br: e.activation(self.sig[:, br, :], self.ps[:, b, :], AF.Tanh, scale=0.5),
                          reads=[("ps", b)], writes=[("sig", br)])
                for br in range(3):
                    b = self.bank()
                    for kc in range(4):
                        S.add("pe", lambda e, b=b, kc=kc, br=br: e.matmul(
                            self.ps[:, b, :], wb[:, br * 512 + kc * 128: br * 512 + (kc + 1) * 128], self.yT[:, br, kc, gc],
                            start=(kc == 0), stop=(kc == 3)), reads=[kb, ("yT", br, g)], writes=[("ps", b)])
                    S.add("dve", lambda e, b=b, br=br: e.scalar_tensor_tensor(
                        self.tm3[:, br, :], self.sig[:, br, :], 1.0, self.ps[:, b, :], ALU.add, ALU.mult),
                        reads=[("sig", br), ("ps", b)], writes=[("tm3", br)])
                S.add("dve", lambda e: e.tensor_tensor(self.tm3[:, 0, :], self.tm3[:, 0, :], self.tm3[:, 1, :], ALU.add),
                      reads=[("tm3", 0), ("tm3", 1)], writes=[("tm3", 0)])
                S.add("dve", lambda e, oc=oc, gc=gc: e.tensor_tensor(self.mergedT[:, oc, gc], self.tm3[:, 0, :], self.tm3[:, 2, :], ALU.add),
                      reads=[("tm3", 0), ("tm3", 2)], writes=[("mergedT", g)])
        wo = [self.wload(d[f"wO{l}"][i]) for i in range(2)]
        for g in range(P["NG"]):
            gc = slice(g * 512, (g + 1) * 512)
            for oc in range(8):
                wv, wk = wo[oc // 4]
                ci = oc % 4
                b = self.bank()
                for kc in range(8):
                    S.add("pe", lambda e, b=b, kc=kc, wv=wv, ci=ci: e.matmul(
                        self.ps[:, b, :], wv[:, ci * 1024 + kc * 128: ci * 1024 + (kc + 1) * 128], self.mergedT[:, kc, gc],
                        start=(kc == 0), stop=(kc == 7)), reads=[wk, ("mergedT", g)], writes=[("ps", b)])
                S.add("dve", lambda e, b=b, oc=oc: e.scalar_tensor_tensor(
                    self.xT[:, oc, gc], self.ps[:, b, :], self.modT[:, l, 16 + oc, c:c + 1], self.xT[:, oc, gc],
                    ALU.mult, ALU.add), reads=[("ps", b), ("modT", l, 2), ("xT", g)], writes=[("xT", g)])
            self.layer_norm(P, l, 0, g, self.dT, self.sqT, self.tmpA, self.tmpB)

    def ffn(self, P, l):
        S, d = self.S, self.d
        c = P["cond"]
        T = P["T"]
        S.add("dve", lambda e: e.memset(self.aT[:, :, :], 0.0), writes=[("aT", 0), ("aT", 1)])
        wv = wk = None
        sample = bool(P.get("sample"))
        if sample:
            x_ = self.x_
            S.add("dve", lambda e: e.tensor_copy(self.hxs[:, 0:8].rearrange("p (k o) -> p k o", o=1), self.hT[:, :, 0:1]),
                  reads=[("hT", 0)], writes=["hxs"])
            S.add("dve", lambda e: e.tensor_copy(self.hxs[:, 8:16].rearrange("p (k o) -> p k o", o=1), self.hT[:, :, 1023:1024]),
                  reads=[("hT", 1)], writes=["hxs"])
            g1 = self.xgroup("hxw")
            S.add("sp", lambda e: e.dma_start(out=x_["hx_in"], in_=self.hxs[:, :]), reads=["hxs"], writes=["hx_in"], slot=g1)
            self.gather(x_["hx_in"], x_["hx_out"], "hx_in", "hx_out")
            g2 = self.xgroup("hxl")
            S.add("sp", lambda e: e.dma_start(out=self.hG[:, :, :], in_=x_["hx_out"].rearrange("(r p) c -> p r c", p=128)),
                  reads=["hx_out"], writes=["hG"], slot=g2)
            S.add("dve", lambda e: e.memset(self.hhf[:, :, :], 0.0), writes=["hhf"])
            for j in range(4):
                S.add("dve", lambda e, j=j: e.scalar_tensor_tensor(
                    self.hhf[:, :, 0], self.hG[:, j, 8:16], self.meta[:, 6 + j:7 + j], self.hhf[:, :, 0], ALU.mult, ALU.add),
                    reads=["hG", "meta", "hhf"], writes=["hhf"])
                S.add("dve", lambda e, j=j: e.scalar_tensor_tensor(
                    self.hhf[:, :, 1], self.hG[:, j, 0:8], self.meta[:, 10 + j:11 + j], self.hhf[:, :, 1], ALU.mult, ALU.add),
                    reads=["hG", "meta", "hhf"], writes=["hhf"])
            S.add("dve", lambda e: e.tensor_copy(self.hh[:, :, :], self.hhf[:, :, :]), reads=["hhf"], writes=["hh"])
        for fc in range(NFC):
            if fc % 2 == 0:
                wv, wk = self.wload(d[f"wU{l}"][fc // 2])
            ca, cb = (fc % 2) * 2, (fc % 2) * 2 + 1
            a2 = fc % 2
            bbs = {}
            for g in range(P["NG"]):
                gc = slice(g * 512, (g + 1) * 512)
                ba = self.bank()
                for kc in range(8):
                    S.add("pe", lambda e, b=ba, kc=kc, ci=ca, wv=wv, gc=gc: e.matmul(
                        self.ps[:, b, :], wv[:, ci * 1024 + kc * 128: ci * 1024 + (kc + 1) * 128], self.hT[:, kc, gc],
                        start=(kc == 0), stop=(kc == 7)), reads=[wk, ("hT", g)], writes=[("ps", ba)])
                for (s0, n, col0) in P["conv_rows"][g]:
                    S.add("act", lambda e, s0=s0, n=n, col0=col0, ba=ba: e.activation(
                        self.aT[:, a2, col0:col0 + n], self.ps[:, ba, s0:s0 + n], AF.Identity),
                        reads=[("ps", ba)], writes=[("aT", a2)])
            if sample:
                bh = self.bank()
                for kc in range(8):
                    S.add("pe", lambda e, kc=kc, wv=wv, bh=bh: e.matmul(
                        self.ps[:, bh, 0:2], wv[:, ca * 1024 + kc * 128: ca * 1024 + (kc + 1) * 128], self.hh[:, kc, :],
                        start=(kc == 0), stop=(kc == 7)), reads=[wk, "hh"], writes=[("ps", bh)])
                S.add("dve", lambda e, bh=bh: e.tensor_scalar(self.aT[:, a2, 0:1], self.ps[:, bh, 0:1], self.meta[:, 4:5], None, ALU.mult),
                      reads=[("ps", bh), "meta"], writes=[("aT", a2)])
                S.add("dve", lambda e, bh=bh: e.tensor_scalar(self.aT[:, a2, 1025:1026], self.ps[:, bh, 1:2], self.meta[:, 5:6], None, ALU.mult),
                      reads=[("ps", bh), "meta"], writes=[("aT", a2)])
            for g in range(P["NG"]):
                gc = slice(g * 512, (g + 1) * 512)
                bb = self.bank()
                for kc in range(8):
                    S.add("pe", lambda e, b=bb, kc=kc, ci=cb, wv=wv, gc=gc: e.matmul(
                        self.ps[:, b, :], wv[:, ci * 1024 + kc * 128: ci * 1024 + (kc + 1) * 128], self.hT[:, kc, gc],
                        start=(kc == 0), stop=(kc == 7)), reads=[wk, ("hT", g)], writes=[("ps", bb)])
                cw = 80
                for (s0, n, col0) in P["conv_rows"][g]:
                    S.add("dve", lambda e, s0=s0, n=n, col0=col0: e.tensor_scalar(
                        self.cT[:, a2, s0:s0 + n], self.aT[:, a2, col0 - 1:col0 - 1 + n],
                        self.smallp[:, l, cw + fc:cw + fc + 1], self.smallp[:, l, cw + 66 + fc:cw + 66 + fc + 1],
                        ALU.mult, ALU.add), reads=[("aT", a2), "smallp"], writes=[("cT", a2)])
                    for k in (1, 2):
                        S.add("dve", lambda e, s0=s0, n=n, col0=col0, k=k: e.scalar_tensor_tensor(
                            self.cT[:, a2, s0:s0 + n], self.aT[:, a2, col0 - 1 + k:col0 - 1 + k + n],
                            self.smallp[:, l, cw + k * NFC + fc:cw + k * NFC + fc + 1], self.cT[:, a2, s0:s0 + n],
                            ALU.mult, ALU.add), reads=[("aT", a2), "smallp", ("cT", a2)], writes=[("cT", a2)])
                S.add("act", lambda e: e.activation(self.glT[:, a2, :], self.cT[:, a2, :], AF.Gelu_apprx_tanh),
                      reads=[("cT", a2)], writes=[("glT", a2)])
                S.add("dve", lambda e, bb=bb, gc=gc: e.tensor_tensor(self.gT[:, fc, gc], self.glT[:, a2, :], self.ps[:, bb, :], ALU.mult),
                      reads=[("glT", a2), ("ps", bb)], writes=[("gT", g)])
        for oc in range(8):
            wv, wk = self.wload(d[f"wD{l}"][oc], NFC * 128)
            for g in range(P["NG"]):
                gc = slice(g * 512, (g + 1) * 512)
                b = self.bank()
                for fc in range(NFC):
                    S.add("pe", lambda e, b=b, fc=fc, wv=wv: e.matmul(
                        self.ps[:, b, :], wv[:, fc * 128:(fc + 1) * 128], self.gT[:, fc, gc],
                        start=(fc == 0), stop=(fc == NFC - 1)), reads=[wk, ("gT", g)], writes=[("ps", b)])
                S.add("dve", lambda e, b=b, oc=oc, gc=gc: e.scalar_tensor_tensor(
                    self.xT[:, oc, gc], self.ps[:, b, :], self.modT[:, l, 40 + oc, c:c + 1], self.xT[:, oc, gc],
                    ALU.mult, ALU.add), reads=[("ps", b), ("modT", l, 5), ("xT", g)], writes=[("xT", g)])
        for g in range(P["NG"]):
            self.layer_norm(P, l, 1, g, self.dT2, self.sqT2, self.tmpA2, self.tmpB2)

    def store_x(self, P):
        S = self.S
        dst = self.o[P["y"]]
        for t in range(P["NT"]):
            sl = t % 2
            for half in range(2):
                b = self.bank()
                for q in range(4):
                    kc = half * 4 + q
                    S.add("pe", lambda e, b=b, q=q, kc=kc, t=t: e.transpose(
                        self.ps[:, b, q * 128:(q + 1) * 128], self.xT[:, kc, t * 128:(t + 1) * 128], self.cst[:, 4, :]),
                        reads=[("xT", t // 4), "cst"], writes=[("ps", b)])
                S.add("act" if half == 0 else "dve",
                      (lambda e, b=b, half=half, sl=sl: e.activation(self.stage[:, sl, half * 512:(half + 1) * 512], self.ps[:, b, :], AF.Identity))
                      if half == 0 else
                      (lambda e, b=b, half=half, sl=sl: e.tensor_copy(self.stage[:, sl, half * 512:(half + 1) * 512], self.ps[:, b, :])),
                      reads=[("ps", b)], writes=[("stage", sl)])
            op = S.add("sp", lambda e, t=t, sl=sl: e.dma_start(out=dst[t * 128:(t + 1) * 128, :], in_=self.stage[:, sl, :]),
                       reads=[("stage", sl)], slot=self.s_io[sl])
            self.out_ops.append(op)

    def run_pass(self, P):
        stop = int(os.environ.get("KSTOP", "99"))
        if stop < 1:
            return
        self.load_x(P)
        self.barrier()
        for l in range(L):
            if stop < 2 + 10 * l:
                break
            self.modulate(P, l, 0)
            self.mixer_AB(P, l, 0)
            self.ada_deferred()
            self.barrier()
            if stop < 3 + 10 * l:
                break
            self.mixer_AB(P, l, 1)
            self.barrier()
            if stop < 4 + 10 * l:
                break
            self.mixer_C(P, l)
            self.barrier()
            if stop < 5 + 10 * l:
                break
            self.merge_phase(P, l)
            self.modulate(P, l, 1)
            self.barrier()
            if stop < 6 + 10 * l:
                break
            self.ffn(P, l)
            self.barrier()
        self.store_x(P)
        self.barrier()

    def build(self):
        self.declare()
        self.init_phase()
        PP = dict(T=512, NT=4, NG=1, cond=0, x="xp", y="yp", rope=False, states_out=True, kv_out=True,
                  seqs=[(0, 2), (2, 2)],
                  attn=[(0, 2, [0, 1]), (2, 2, [2, 3])],
                  conv_rows=[[(0, 256, 1), (256, 256, 515)]])
        if not os.environ.get("KNOPROMPT"):
            self.run_pass(PP)
        if self.do_sample:
            PS = dict(T=1024, NT=8, NG=2, cond=1, x="xs", y="ys", rope=True, states_out=False, kv_out=False,
                      seqs=[(0, 8)], sample=True, attn=[],
                      conv_rows=[[(0, 512, 1)], [(0, 512, 513)]])
            self.run_pass(PS)
        self.S.wait_for("sp", self.out_ops)
        global LAST_S
        LAST_S = self.S
        self.S.emit(self.nc, self.st)


_CACHE = {}


def _build_nc(do_sample=True):
    key = ("nc", do_sample)
    if key not in _CACHE:
        nc = bass.Bass("TRN2", target_bir_lowering=False)
        with contextlib.ExitStack() as st:
            b = Builder(nc, st, do_sample=do_sample)
            b.build()
        _CACHE[key] = nc
    return _CACHE[key]


def kernel(**inp):
    inp = {k: np.asarray(v) for k, v in inp.items()}
    W = _host_weights(inp)
    nc = _build_nc(not os.environ.get("KNOSAMPLE"))
    in_maps = []
    ncores = int(os.environ.get("KCORES", str(NCORES)))
    for c in range(ncores):
        b, r = c // 4, c % 4
        m = {k: v for k, v in W.items() if not k.startswith("wAfull")}
        for l in range(L):
            m[f"wAq{l}"] = np.ascontiguousarray(W[f"wAfull{l}"][3 * r:3 * r + 3])
        m["xp"] = np.ascontiguousarray(inp["x_prompt"][2 * c:2 * c + 2].reshape(512, D))
        m["xs"] = np.ascontiguousarray(inp["x_sample"][b, r * 1024:(r + 1) * 1024])
        cond = np.stack([inp["c_ctx"], inp["c"][b]], axis=-1)
        m["cond"] = np.ascontiguousarray(cond.reshape(8, 128, 2).transpose(1, 0, 2), dtype=np.float32)
        rc, rs = _rope_tables(r * 1024)
        m["ropec"], m["ropes"] = rc, rs
        meta = np.zeros((128, 16), np.float32)
        for s_ in range(4):
            meta[0:64, s_] = 1.0 if r == s_ else 0.0
            meta[64:128, s_] = 1.0 if r == 3 - s_ else 0.0
            meta[:, 6 + s_] = 1.0 if s_ == r - 1 else 0.0
            meta[:, 10 + s_] = 1.0 if s_ == r + 1 else 0.0
        meta[:, 4] = 1.0 if r > 0 else 0.0
        meta[:, 5] = 1.0 if r < 3 else 0.0
        m["meta"] = meta
        for nm, src in (("sret", inp["state_ret"]), ("sgla", inp["state_gla"])):
            a = src[b]
            m[nm] = np.ascontiguousarray(a.transpose(0, 1, 3, 2, 4).reshape(L, 128, 4, 128), dtype=np.float32)
        ck = inp["cache_diff_k"][b]
        m["ckc"] = np.ascontiguousarray(ck.reshape(L, 8, 2, 128, 64).transpose(0, 2, 3, 1, 4).reshape(L, 2, 128, 512), dtype=np.float32)
        cv = inp["cache_diff_v"][b]
        m["cvc"] = np.ascontiguousarray(cv.reshape(L, 4, 2, 128, 128).transpose(0, 2, 3, 1, 4), dtype=np.float32)
        in_maps.append(m)
    res = run_bass_kernel_spmd(nc, in_maps, core_ids=list(range(ncores)))
    R = list(res.results)
    while len(R) < NCORES:
        R.append(R[0])
    y_prompt = np.concatenate([R[c]["yp"].reshape(2, 256, D) for c in range(NCORES)], axis=0)
    y_sample = np.stack([np.concatenate([R[b * 4 + r]["ys"] for r in range(4)], axis=0) for b in range(2)], axis=0)
    ndk = np.concatenate([R[c]["ndk"] for c in range(NCORES)], axis=0)
    ndv = np.concatenate([R[c]["ndv"] for c in range(NCORES)], axis=0)
    nsr = np.concatenate([R[c]["nsr"] for c in range(NCORES)], axis=0)
    nsg = np.concatenate([R[c]["nsg"] for c in range(NCORES)], axis=0)
    f = np.float32
    return (y_prompt.astype(f), y_sample.astype(f), ndk.astype(f), ndv.astype(f), nsr.astype(f), nsg.astype(f))
```
